# Optimizing a Trainium2 kernel written in Bass

```python
import math
import jax
import jax.numpy as jnp
from jax import lax
import numpy as np

D_MODEL = 1024
BATCH = 8
SEQ = 4096
DEPTH = 2

GRID_W = 64
CTX_LEN = 256
N_MIXERS = 4
MIX_WIDTH = D_MODEL
GROUP_WIDTH = MIX_WIDTH // N_MIXERS
HEAD_DIM = 64
GROUP_HEADS = GROUP_WIDTH // HEAD_DIM

CM_CH = GROUP_WIDTH
CM_KERNEL = 31
DA_HEADS = GROUP_HEADS
DA_QK_DIM = HEAD_DIM // 2
DA_V_DIM = HEAD_DIM
ROPE_THETA = 10000.0
DN_HEADS = GROUP_HEADS
DN_K_DIM = HEAD_DIM
DN_V_DIM = HEAD_DIM
DN_SHORT_CONV = 5
DN_QKV = DN_HEADS * (2 * DN_K_DIM + DN_V_DIM)
GLA_HEADS = GROUP_HEADS
GLA_K_DIM = HEAD_DIM // 2
GLA_V_DIM = HEAD_DIM
GLA_GATE_RANK = 16
GLA_TAU = 16.0

CHUNK = 64
Q_BLOCK = 128
D_FF = ((8 * D_MODEL // 3 + 255) // 256) * 256
FFN_KERNEL = 3
EPS = 1e-6

CM_COLS = 2 * CM_CH
DA_COLS = 2 * DA_HEADS * 2 * DA_QK_DIM + DA_HEADS * DA_V_DIM
DN_COLS = DN_QKV + 4 * DN_HEADS + DN_HEADS * DN_V_DIM
GLA_COLS = GLA_HEADS * (2 * GLA_K_DIM + GLA_V_DIM) + 2 * GLA_GATE_RANK + GLA_HEADS * GLA_V_DIM
IN_COLS = CM_COLS + DA_COLS + DN_COLS + GLA_COLS
IN_SPLITS = [CM_COLS, CM_COLS + DA_COLS, CM_COLS + DA_COLS + DN_COLS]

kernel_name = 'hybrid_parallel_group_flow_block'

F32 = jnp.float32


def rms_norm(x, g):
    xf = x.astype(F32)
    y = xf * lax.rsqrt(jnp.mean(xf * xf, axis=-1, keepdims=True) + EPS)
    return (y * g.astype(F32)).astype(x.dtype)


def layer_norm(x, g, b):
    xf = x.astype(F32)
    mu = jnp.mean(xf, axis=-1, keepdims=True)
    var = jnp.mean(jnp.square(xf - mu), axis=-1, keepdims=True)
    y = (xf - mu) * lax.rsqrt(var + EPS)
    return (y * g.astype(F32) + b.astype(F32)).astype(x.dtype)


def l2_norm(x):
    xf = x.astype(F32)
    return xf * lax.rsqrt(jnp.sum(xf * xf, axis=-1, keepdims=True) + EPS)


def modulate(x, shift, scale):
    return x * (1 + scale[..., None, :]) + shift[..., None, :]


def dwconv(x, w):
    pad = w.shape[0] // 2
    return lax.conv_general_dilated(
        x, w[:, None, :].astype(x.dtype), (1,), [(pad, pad)],
        dimension_numbers=('NWC', 'WIO', 'NWC'), feature_group_count=x.shape[-1])


def axial_rope(n):
    rows = n // GRID_W
    row = jnp.repeat(jnp.arange(rows, dtype=F32), GRID_W)
    col = jnp.tile(jnp.arange(GRID_W, dtype=F32), rows)
    nf = DA_QK_DIM // 4
    inv = ROPE_THETA ** (-jnp.arange(nf, dtype=F32) / nf)
    ang = jnp.concatenate([row[:, None] * inv, col[:, None] * inv], axis=-1)
    return jnp.cos(ang), jnp.sin(ang)


def apply_rope(x, cos, sin):
    c = cos[:, None, None, :].astype(x.dtype)
    s = sin[:, None, None, :].astype(x.dtype)
    x1, x2 = jnp.split(x, 2, axis=-1)
    return jnp.concatenate([x1 * c - x2 * s, x1 * s + x2 * c], axis=-1)


def _to_heads(t, n_heads):
    b, l = t.shape[:2]
    return jnp.moveaxis(t.reshape(b, l, n_heads, -1), 2, 1)


def _chunks(t):
    n = t.shape[2] // CHUNK
    return t.reshape(t.shape[:2] + (n, CHUNK) + t.shape[3:])


def _gated_head_out(o, gate, g, n_heads):
    o = jnp.moveaxis(o, 1, 2)
    b, l = o.shape[:2]
    y = rms_norm(o, g) * jax.nn.silu(gate.reshape(b, l, n_heads, -1).astype(F32))
    return y.reshape(b, l, -1).astype(gate.dtype)


def gated_delta_chunked(q, k, v, g, beta, s0, with_out):
    q = _chunks(q * (q.shape[-1] ** -0.5))
    k, v, g, beta = _chunks(k), _chunks(v), _chunks(g), _chunks(beta)
    gcum = jnp.cumsum(g, axis=-1)
    causal = jnp.tril(jnp.ones((CHUNK, CHUNK), bool))
    strict = jnp.tril(jnp.ones((CHUNK, CHUNK), bool), -1)
    diff = gcum[..., :, None] - gcum[..., None, :]
    decay = jnp.where(causal, jnp.exp(jnp.where(causal, diff, 0.0)), 0.0)
    kb = k * beta[..., None]
    a = jnp.where(strict, jnp.einsum('bhnid,bhnjd->bhnij', kb, k) * decay, 0.0)
    eye = jnp.eye(CHUNK, dtype=F32)
    t_inv = lax.linalg.triangular_solve(eye + a, jnp.broadcast_to(eye, a.shape),
                                        left_side=True, lower=True, unit_diagonal=True)
    u = t_inv @ (v * beta[..., None])
    w = t_inv @ (kb * jnp.exp(gcum)[..., None])
    k_end = k * jnp.exp(gcum[..., -1:] - gcum)[..., None]
    g_end = jnp.exp(gcum[..., -1])
    xs = [u, w, k_end, g_end]
    if with_out:
        q_in = q * jnp.exp(gcum)[..., None]
        a_qk = jnp.where(causal, jnp.einsum('bhnid,bhnjd->bhnij', q, k) * decay, 0.0)
        xs += [q_in, a_qk]
    xs = tuple(jnp.moveaxis(t, 2, 0) for t in xs)

    def step(s, xi):
        v_new = xi[0] - xi[1] @ s
        s_new = s * xi[3][..., None, None] + jnp.einsum('bhck,bhcv->bhkv', xi[2], v_new)
        if with_out:
            return s_new, xi[4] @ s + xi[5] @ v_new
        return s_new, None

    s_fin, o = lax.scan(step, s0, xs)
    if with_out:
        o = jnp.moveaxis(o, 0, 2)
        o = o.reshape(o.shape[:2] + (-1, o.shape[-1]))
    return s_fin, o


def gla_chunked(q, k, v, gk, s0, with_out):
    q = _chunks(q * (q.shape[-1] ** -0.5))
    k, v, gk = _chunks(k), _chunks(v), _chunks(gk)
    b = jnp.cumsum(gk, axis=3)
    b_end = b[..., -1:, :]
    k_end = k * jnp.exp(b_end - b)
    decay_end = jnp.exp(b_end[..., 0, :])
    xs = [k_end, v, decay_end]
    if with_out:
        causal = jnp.tril(jnp.ones((CHUNK, CHUNK), bool))
        q_in = q * jnp.exp(b)
        a_qk = jnp.where(causal, jnp.einsum('bhnid,bhnjd->bhnij', q_in, k * jnp.exp(-b)), 0.0)
        xs += [q_in, a_qk]
    xs = tuple(jnp.moveaxis(t, 2, 0) for t in xs)

    def step(s, xi):
        s_new = s * xi[2][..., :, None] + jnp.einsum('bhck,bhcv->bhkv', xi[0], xi[1])
        if with_out:
            return s_new, xi[3] @ s + xi[4] @ xi[1]
        return s_new, None

    s_fin, o = lax.scan(step, s0, xs)
    if with_out:
        o = jnp.moveaxis(o, 0, 2)
        o = o.reshape(o.shape[:2] + (-1, o.shape[-1]))
    return s_fin, o


def two_segment_scan(chunk_fn, ctx_in, lat_in, s0, reverse, with_ctx_out):
    flip = (lambda t: jnp.flip(t, axis=2)) if reverse else (lambda t: t)
    s_ctx, o_ctx = chunk_fn(*[flip(t) for t in ctx_in], s0, with_ctx_out)
    _, o_lat = chunk_fn(*[flip(t) for t in lat_in], s_ctx, True)
    return (flip(o_ctx) if with_ctx_out else None), flip(o_lat)


def conv_module_mixer(p_ctx, p_lat, conv_w, conv_b, ln_g, ln_b, with_ctx_out):
    def run(p):
        a, gate = jnp.split(p, 2, axis=-1)
        y = a * jax.nn.sigmoid(gate)
        y = dwconv(y, conv_w) + conv_b.astype(y.dtype)
        return jax.nn.silu(layer_norm(y, ln_g, ln_b))
    return (run(p_ctx) if with_ctx_out else None), run(p_lat)


def diff_attention_mixer(p_ctx, p_lat, cos, sin, qn_g, kn_g, lam_p, subln_g, layer_idx, with_ctx_out):
    nq = DA_HEADS * 2 * DA_QK_DIM

    def heads(p):
        b, l = p.shape[:2]
        q, k, v = jnp.split(p, [nq, 2 * nq], axis=-1)
        q = rms_norm(q.reshape(b, l, DA_HEADS, 2, DA_QK_DIM), qn_g)
        k = rms_norm(k.reshape(b, l, DA_HEADS, 2, DA_QK_DIM), kn_g)
        return q, k, v.reshape(b, l, DA_HEADS, DA_V_DIM)

    qc, kc, vc = heads(p_ctx)
    ql, kl, vl = heads(p_lat)
    ql = apply_rope(ql, cos, sin)
    kl = apply_rope(kl, cos, sin)
    lam_init = 0.8 - 0.6 * math.exp(-0.3 * layer_idx)
    lp = lam_p.astype(F32)
    lam = jnp.exp(jnp.sum(lp[0] * lp[1])) - jnp.exp(jnp.sum(lp[2] * lp[3])) + lam_init
    scale = DA_QK_DIM ** -0.5

    def attend(qb, kk, vv):
        s = jnp.einsum('bqhmd,bkhmd->bhmqk', qb, kk).astype(F32) * scale
        pr = jax.nn.softmax(s, axis=-1)
        a = pr[:, :, 0] - lam * pr[:, :, 1]
        return jnp.einsum('bhqk,bkhd->bqhd', a.astype(vv.dtype), vv)

    def finish(o):
        y = rms_norm(o, subln_g) * (1 - lam_init)
        return y.reshape(o.shape[0], o.shape[1], -1)

    k_all = jnp.concatenate([kc, kl], axis=1)
    v_all = jnp.concatenate([vc, vl], axis=1)
    b, n = ql.shape[:2]
    nb = n // Q_BLOCK
    q_blocks = jnp.moveaxis(ql.reshape(b, nb, Q_BLOCK, DA_HEADS, 2, DA_QK_DIM), 1, 0)
    ol = lax.map(lambda qb: attend(qb, k_all, v_all), q_blocks)
    ol = jnp.moveaxis(ol, 0, 1).reshape(b, n, DA_HEADS, DA_V_DIM)
    oc = finish(attend(qc, kc, vc)) if with_ctx_out else None
    return oc, finish(ol)


def gated_deltanet_mixer(p_ctx, p_lat, conv_w, a_log, dt_bias, onorm_g, with_ctx_out):
    nk = DN_HEADS * DN_K_DIM

    def prep(p):
        b, l = p.shape[:2]
        qkv, beta_raw, a_raw, gate = jnp.split(p, [DN_QKV, DN_QKV + 2 * DN_HEADS, DN_QKV + 4 * DN_HEADS], axis=-1)
        qkv = jax.nn.silu(dwconv(qkv, conv_w))
        q, k, v = jnp.split(qkv, [nk, 2 * nk], axis=-1)
        q = l2_norm(_to_heads(q, DN_HEADS))
        k = l2_norm(_to_heads(k, DN_HEADS))
        v = _to_heads(v, DN_HEADS).astype(F32)
        beta = jax.nn.sigmoid(beta_raw.astype(F32)).reshape(b, l, 2, DN_HEADS)
        g = -jnp.exp(a_log.astype(F32)) * jax.nn.softplus(
            a_raw.astype(F32).reshape(b, l, 2, DN_HEADS) + dt_bias.astype(F32))
        return q, k, v, jnp.transpose(g, (0, 2, 3, 1)), jnp.transpose(beta, (0, 2, 3, 1)), gate

    qc, kc, vc, gc, bc, gate_c = prep(p_ctx)
    ql, kl, vl, gl, bl, gate_l = prep(p_lat)
    s0 = jnp.zeros((ql.shape[0], DN_HEADS, DN_K_DIM, DN_V_DIM), F32)
    outs = [two_segment_scan(gated_delta_chunked, (qc, kc, vc, gc[:, d], bc[:, d]),
                             (ql, kl, vl, gl[:, d], bl[:, d]), s0, d == 1, with_ctx_out)
            for d in range(2)]
    ol = _gated_head_out(outs[0][1] + outs[1][1], gate_l, onorm_g, DN_HEADS)
    oc = _gated_head_out(outs[0][0] + outs[1][0], gate_c, onorm_g, DN_HEADS) if with_ctx_out else None
    return oc, ol


def gla_mixer(p_ctx, p_lat, w2, b2, onorm_g, with_ctx_out):
    nk = GLA_HEADS * GLA_K_DIM
    nv = GLA_HEADS * GLA_V_DIM

    def prep(p):
        b, l = p.shape[:2]
        q, k, v, lr, gate = jnp.split(p, [nk, 2 * nk, 2 * nk + nv, 2 * nk + nv + 2 * GLA_GATE_RANK], axis=-1)
        lr = lr.reshape(b, l, 2, GLA_GATE_RANK).astype(F32)
        z = jnp.einsum('blmr,mrk->bmlk', lr, w2.astype(F32)) + b2.astype(F32)[None, :, None, :]
        gk = jax.nn.log_sigmoid(z) / GLA_TAU
        gk = jnp.transpose(gk.reshape(b, 2, l, GLA_HEADS, GLA_K_DIM), (0, 1, 3, 2, 4))
        q = _to_heads(q, GLA_HEADS).astype(F32)
        k = _to_heads(k, GLA_HEADS).astype(F32)
        v = _to_heads(v, GLA_HEADS).astype(F32)
        return q, k, v, gk, gate

    qc, kc, vc, gkc, gate_c = prep(p_ctx)
    ql, kl, vl, gkl, gate_l = prep(p_lat)
    s0 = jnp.zeros((ql.shape[0], GLA_HEADS, GLA_K_DIM, GLA_V_DIM), F32)
    outs = [two_segment_scan(gla_chunked, (qc, kc, vc, gkc[:, d]), (ql, kl, vl, gkl[:, d]),
                             s0, d == 1, with_ctx_out)
            for d in range(2)]
    ol = _gated_head_out(outs[0][1] + outs[1][1], gate_l, onorm_g, GLA_HEADS)
    oc = _gated_head_out(outs[0][0] + outs[1][0], gate_c, onorm_g, GLA_HEADS) if with_ctx_out else None
    return oc, ol


def conv_ffn(h, w_up, conv_w, w_down):
    u = dwconv(h @ w_up, conv_w)
    a, g = jnp.split(u, 2, axis=-1)
    return (jax.nn.silu(g) * a) @ w_down


def trunk_layer(x, xc, c_act, cc_act, lp, cos, sin, layer_idx, with_ctx_out):
    mod = c_act @ lp['w_mod'] + lp['b_mod']
    modc = cc_act @ lp['w_mod'] + lp['b_mod']
    sh1, sc1, g1, sh2, sc2, g2 = jnp.split(mod, 6, axis=-1)
    sh1c, sc1c, g1c, sh2c, sc2c, g2c = jnp.split(modc, 6, axis=-1)

    h = modulate(rms_norm(x, lp['norm1_g']), sh1, sc1)
    hc = modulate(rms_norm(xc, lp['norm1_g']), sh1c, sc1c)
    pa, pb, pcn, pd = jnp.split(h @ lp['w_in'], IN_SPLITS, axis=-1)
    pac, pbc, pcc, pdc = jnp.split(hc @ lp['w_in'], IN_SPLITS, axis=-1)

    ya_c, ya = conv_module_mixer(pac, pa, lp['cm_conv_w'], lp['cm_conv_b'], lp['cm_ln_g'], lp['cm_ln_b'], with_ctx_out)
    yb_c, yb = diff_attention_mixer(pbc, pb, cos, sin, lp['da_qnorm_g'], lp['da_knorm_g'], lp['da_lambda'],
                                    lp['da_subln_g'], layer_idx, with_ctx_out)
    yc_c, yc = gated_deltanet_mixer(pcc, pcn, lp['dn_conv_w'], lp['dn_a_log'], lp['dn_dt_bias'],
                                    lp['dn_onorm_g'], with_ctx_out)
    yd_c, yd = gla_mixer(pdc, pd, lp['gla_w2'], lp['gla_b2'], lp['gla_onorm_g'], with_ctx_out)

    x = x + g1[:, None, :] * (jnp.concatenate([ya, yb, yc, yd], axis=-1) @ lp['w_out'])
    x = x + g2[:, None, :] * conv_ffn(modulate(rms_norm(x, lp['norm2_g']), sh2, sc2),
                                      lp['ffn_w_up'], lp['ffn_conv_w'], lp['ffn_w_down'])
    if with_ctx_out:
        xc = xc + g1c * (jnp.concatenate([ya_c, yb_c, yc_c, yd_c], axis=-1) @ lp['w_out'])
        xc = xc + g2c * conv_ffn(modulate(rms_norm(xc, lp['norm2_g']), sh2c, sc2c),
                                 lp['ffn_w_up'], lp['ffn_conv_w'], lp['ffn_w_down'])
    return x, xc


def setup_inputs(seed: int = 0) -> dict:
    key = jax.random.key(seed)
    k = jax.random.split(key, 28)
    D, L = D_MODEL, DEPTH

    def nrm(i, shape, scale):
        return jax.random.normal(k[i], shape, F32) * scale

    def gain(i, shape):
        return 1.0 + 0.02 * jax.random.normal(k[i], shape, F32)

    dt = jnp.exp(jax.random.uniform(k[20], (L, 2, DN_HEADS), F32, math.log(1e-3), math.log(1e-1)))
    return {
        'x': nrm(0, (BATCH, SEQ, D), 1.0),
        'c': nrm(1, (BATCH, D), 1.0),
        'ctx': nrm(2, (BATCH, CTX_LEN, D), 1.0),
        'c_ctx': nrm(3, (D,), 1.0),
        'w_mod': nrm(4, (L, D, 6 * D), 0.5 * D ** -0.5),
        'b_mod': nrm(5, (L, 6 * D), 0.01),
        'norm1_g': gain(6, (L, D)),
        'norm2_g': gain(7, (L, D)),
        'w_in': nrm(8, (L, D, IN_COLS), D ** -0.5),
        'w_out': nrm(9, (L, MIX_WIDTH, D), MIX_WIDTH ** -0.5),
        'cm_conv_w': nrm(10, (L, CM_KERNEL, CM_CH), CM_KERNEL ** -0.5),
        'cm_conv_b': nrm(11, (L, CM_CH), 0.01),
        'cm_ln_g': gain(12, (L, CM_CH)),
        'cm_ln_b': nrm(13, (L, CM_CH), 0.01),
        'da_qnorm_g': gain(14, (L, DA_QK_DIM)),
        'da_knorm_g': gain(15, (L, DA_QK_DIM)),
        'da_lambda': nrm(16, (L, 4, DA_QK_DIM), 0.1),
        'da_subln_g': gain(17, (L, DA_V_DIM)),
        'dn_conv_w': nrm(18, (L, DN_SHORT_CONV, DN_QKV), DN_SHORT_CONV ** -0.5),
        'dn_a_log': jnp.log(jax.random.uniform(k[19], (L, 2, DN_HEADS), F32, 1.0, 16.0)),
        'dn_dt_bias': dt + jnp.log(-jnp.expm1(-dt)),
        'dn_onorm_g': gain(21, (L, DN_V_DIM)),
        'gla_w2': nrm(22, (L, 2, GLA_GATE_RANK, GLA_HEADS * GLA_K_DIM), GLA_GATE_RANK ** -0.5),
        'gla_b2': nrm(23, (L, 2, GLA_HEADS * GLA_K_DIM), 0.01),
        'gla_onorm_g': gain(24, (L, GLA_V_DIM)),
        'ffn_w_up': nrm(25, (L, D, 2 * D_FF), D ** -0.5),
        'ffn_conv_w': nrm(26, (L, FFN_KERNEL, 2 * D_FF), FFN_KERNEL ** -0.5),
        'ffn_w_down': nrm(27, (L, D_FF, D), D_FF ** -0.5),
    }


def reference(x, c, ctx, c_ctx, w_mod, b_mod, norm1_g, norm2_g, w_in, w_out,
              cm_conv_w, cm_conv_b, cm_ln_g, cm_ln_b,
              da_qnorm_g, da_knorm_g, da_lambda, da_subln_g,
              dn_conv_w, dn_a_log, dn_dt_bias, dn_onorm_g,
              gla_w2, gla_b2, gla_onorm_g,
              ffn_w_up, ffn_conv_w, ffn_w_down):
    cos, sin = axial_rope(x.shape[1])
    c_act = jax.nn.silu(c)
    cc_act = jax.nn.silu(c_ctx)
    xc = ctx
    for l in range(DEPTH):
        lp = {
            'w_mod': w_mod[l], 'b_mod': b_mod[l], 'norm1_g': norm1_g[l], 'norm2_g': norm2_g[l],
            'w_in': w_in[l], 'w_out': w_out[l],
            'cm_conv_w': cm_conv_w[l], 'cm_conv_b': cm_conv_b[l], 'cm_ln_g': cm_ln_g[l], 'cm_ln_b': cm_ln_b[l],
            'da_qnorm_g': da_qnorm_g[l], 'da_knorm_g': da_knorm_g[l], 'da_lambda': da_lambda[l],
            'da_subln_g': da_subln_g[l],
            'dn_conv_w': dn_conv_w[l], 'dn_a_log': dn_a_log[l], 'dn_dt_bias': dn_dt_bias[l],
            'dn_onorm_g': dn_onorm_g[l],
            'gla_w2': gla_w2[l], 'gla_b2': gla_b2[l], 'gla_onorm_g': gla_onorm_g[l],
            'ffn_w_up': ffn_w_up[l], 'ffn_conv_w': ffn_conv_w[l], 'ffn_w_down': ffn_w_down[l],
        }
        x, xc = trunk_layer(x, xc, c_act, cc_act, lp, cos, sin, l, l < DEPTH - 1)
    return x
```

```python
import math
import numpy as np
from contextlib import ExitStack
import concourse.bass as bass
import concourse.mybir as mybir
from concourse.bass_utils import run_bass_kernel_spmd

F32 = mybir.dt.float32
BF16 = mybir.dt.bfloat16
ALU = mybir.AluOpType
AF = mybir.ActivationFunctionType
AX = mybir.AxisListType

D = 1024
SEQ = 4096
CTX = 256
T = SEQ + CTX
NT = T // 128
DEPTH = 2
INC = 3120
DFF = 2816
EPS = 1e-6
CMP = 15
TP = CMP + CTX + 2 * CMP + SEQ + CMP
CM_CTX0 = CMP
CM_LAT0 = CMP + CTX + 2 * CMP
DNP = 2
TPD = DNP + CTX + 2 * DNP + SEQ + DNP
DN_CTX0 = DNP
DN_LAT0 = DNP + CTX + 2 * DNP
NTM = 1072
FB = 510
NFB = 9
FPAD = 1 + NFB * FB + 1
F_CTX0 = 1
F_LAT0 = 1 + CTX + 2


class Buf:
    __slots__ = ("name", "lw", "rd", "ps")

    def __init__(self, name=""):
        self.name = name
        self.lw = None
        self.rd = []
        self.ps = False


class FW:
    NDMA = 40

    def __init__(self, nc, stack):
        self.nc = nc
        self.eng = {"pe": nc.tensor, "act": nc.scalar, "dve": nc.vector,
                    "pool": nc.gpsimd, "sp": nc.sync}
        self.sem = {}
        self.cnt = {}
        for e in self.eng:
            self.sem[e] = stack.enter_context(nc.semaphore("s_" + e))
            self.cnt[e] = 0
        self.dsem = [stack.enter_context(nc.semaphore("d%d" % i)) for i in range(self.NDMA)]
        self.dcnt = [0] * self.NDMA
        self.dnext = 0
        self.seen = {e: {} for e in self.eng}

    def buf(self, name=""):
        return Buf(name)

    def bufs(self, n, name=""):
        return [Buf(name + str(i)) for i in range(n)]

    def _semobj(self, key):
        return self.sem[key] if isinstance(key, str) else self.dsem[key]

    def _wait(self, e, ev):
        if ev is None:
            return
        key, val = ev
        if key == "pe" and e == "pe":
            return
        if key == e and val <= self.cnt[e] - 6:
            return
        if self.seen[e].get(key, 0) >= val:
            return
        self.seen[e][key] = val
        self.eng[e].wait_ge(self._semobj(key), val)

    def psum(self, *bufs):
        for b in bufs:
            if isinstance(b, (list, tuple)):
                self.psum(*b)
            else:
                b.ps = True

    def _deps(self, e, reads, writes):
        for b in reads:
            self._wait(e, b.lw)
            if b.ps:
                for ev in b.rd:
                    if ev[0] != e:
                        self._wait(e, ev)
        for b in writes:
            self._wait(e, b.lw)
            for ev in b.rd:
                self._wait(e, ev)

    def _commit(self, ev, reads, writes):
        for b in reads:
            b.rd.append(ev)
            if len(b.rd) > 48:
                best = {}
                for k, v in b.rd:
                    if best.get(k, 0) < v:
                        best[k] = v
                b.rd = list(best.items())
        for b in writes:
            b.lw = ev
            b.rd = []

    def op(self, e, fn, reads=(), writes=()):
        self._deps(e, reads, writes)
        ins = fn()
        self.cnt[e] += 1
        ins.then_inc(self.sem[e], 1)
        self._commit((e, self.cnt[e]), reads, writes)
        return ins

    def dma(self, out, in_, reads=(), writes=(), q="sp", **kw):
        k = self.dnext
        self.dnext = (self.dnext + 1) % self.NDMA
        if self.dcnt[k] > 0:
            self._wait(q, (k, self.dcnt[k]))
        self._deps(q, reads, writes)
        ins = self.eng[q].dma_start(out=out, in_=in_, **kw)
        self.dcnt[k] += 16
        ins.then_inc(self.dsem[k], 16)
        self._commit((k, self.dcnt[k]), reads, writes)
        return ins

    def barrier(self):
        for e in self.eng:
            for f in self.eng:
                if f != e and self.cnt[f] > 0:
                    self._wait(e, (f, self.cnt[f]))
            if self.cnt[e] > 0 and e != "pe" and self.seen[e].get(e, 0) < self.cnt[e]:
                self.seen[e][e] = self.cnt[e]
                self.eng[e].wait_ge(self.sem[e], self.cnt[e])
            for k in range(self.NDMA):
                if self.dcnt[k] > 0:
                    self._wait(e, (k, self.dcnt[k]))


def _consts():
    c = {}
    idx = np.arange(128)
    same = (idx[:, None] // 64) == (idx[None, :] // 64)
    le = idx[:, None] <= idx[None, :]
    ge = idx[:, None] >= idx[None, :]
    c["ident"] = np.eye(128)
    c["ones"] = np.ones((128, 128))
    c["negones"] = -np.ones((128, 128))
    c["blk"] = same.astype(np.float64)
    for d, (a_le, name) in enumerate(((le, "f"), (ge, "r"))):
        tri = (same & a_le).astype(np.float64)
        c["tri_" + name] = tri
        c["tris_" + name] = -tri / 16.0
        causal = tri.T
        c["negc_" + name] = np.where(causal > 0, 0.0, -30000.0)
        c["negcT_" + name] = np.where(tri > 0, 0.0, -30000.0)
        c["strict_" + name] = causal * (1 - np.eye(128))
        c["cT4_" + name] = np.tile(tri, (1, 4))
    c["blks"] = -same.astype(np.float64) / 16.0
    c["blk32"] = ((idx[:, None] // 32) == (idx[None, :] // 32)).astype(np.float64) / 32.0
    c["blk64"] = same.astype(np.float64)
    c["div256"] = np.ones((128, 128)) / 256.0
    perm = np.zeros((128, 128))
    for m in range(128):
        k = m + 16 if (m % 32) < 16 else m - 16
        perm[k, m] = 1.0
    c["perm"] = perm
    c["bmask4"] = ((idx[:, None] // 32) == (np.arange(256)[None, :] // 64)).astype(np.float64)
    c["bmask2"] = same.astype(np.float64)
    c["hm32"] = ((idx[:, None] // 32) == np.arange(4)[None, :]).astype(np.float64)
    ci = np.zeros((128, 2)); ci[:64, 0] = 1; ci[64:, 1] = 1
    c["chunkind"] = ci
    sel = np.zeros((128, 256)); sel[0, :128] = 1; sel[1, 128:] = 1
    c["sel"] = sel
    names = list(c.keys())
    offs = {}
    o = 0
    for n in names:
        offs[n] = (o, c[n].shape[1])
        o += c[n].shape[1]
    arr = np.concatenate([c[n] for n in names], axis=1).astype(np.float32)
    return arr, offs


def _rope_tables():
    rows = SEQ // 64
    row = np.repeat(np.arange(rows, dtype=np.float32), 64)
    col = np.tile(np.arange(64, dtype=np.float32), rows)
    nf = 8
    inv = (np.float32(10000.0) ** (-np.arange(nf, dtype=np.float32) / nf)).astype(np.float32)
    ang = np.concatenate([row[:, None] * inv, col[:, None] * inv], axis=-1).astype(np.float32)
    cos = np.cos(ang).astype(np.float32)
    sin = np.sin(ang).astype(np.float32)
    p = np.arange(128)
    ct = cos[:, p % 16].T
    st = sin[:, p % 16].T * np.where((p % 32) < 16, -1.0, 1.0)[:, None]
    return np.ascontiguousarray(np.concatenate([ct, st], axis=1).astype(np.float32))


CONSTS, COFF = _consts()
NCONST = CONSTS.shape[1]

COLS = {}
_o = 0
for _n, _w in (("b_mod", 48), ("n1g", 8), ("n2g", 8), ("cm_w", 62), ("cm_b", 2), ("cm_lg", 2), ("cm_lb", 2),
               ("qg", 1), ("kg", 1), ("dn_w", 30), ("ffn_w", 132), ("gla_b2", 2), ("ccol", 16)):
    COLS[_n] = (_o, _w)
    _o += _w
NCOL = _o
ROWS = {}
_o = 0
for _n, _w in (("subln", 64), ("dn_on", 64), ("gla_on", 64), ("a_log", 8), ("dt_b", 8), ("lam", 128),
               ("b_g1", 1024), ("b_g2", 1024), ("gla_b2r", 256)):
    ROWS[_n] = (_o, _w)
    _o += _w
NROW = _o


def _pack_params(inp, l, cvec):
    cols = np.zeros((128, NCOL), np.float32)

    def put(name, a):
        o, w = COLS[name]
        assert a.shape == (128, w), (name, a.shape)
        cols[:, o:o + w] = a

    put("b_mod", inp["b_mod"][l].reshape(48, 128).T)
    put("n1g", inp["norm1_g"][l].reshape(8, 128).T)
    put("n2g", inp["norm2_g"][l].reshape(8, 128).T)
    put("cm_w", inp["cm_conv_w"][l].reshape(31, 2, 128).transpose(2, 1, 0).reshape(128, 62))
    put("cm_b", inp["cm_conv_b"][l].reshape(2, 128).T)
    put("cm_lg", inp["cm_ln_g"][l].reshape(2, 128).T)
    put("cm_lb", inp["cm_ln_b"][l].reshape(2, 128).T)
    put("qg", np.tile(inp["da_qnorm_g"][l], 4)[:, None])
    put("kg", np.tile(inp["da_knorm_g"][l], 4)[:, None])
    put("dn_w", inp["dn_conv_w"][l].reshape(5, 6, 128).transpose(2, 1, 0).reshape(128, 30))
    put("ffn_w", inp["ffn_conv_w"][l].reshape(3, 44, 128).transpose(2, 1, 0).reshape(128, 132))
    put("gla_b2", inp["gla_b2"][l].T)
    put("ccol", cvec.reshape(2, 8, 128).transpose(2, 1, 0).reshape(128, 16))
    rows = np.zeros((1, NROW), np.float32)

    def putr(name, a):
        o, w = ROWS[name]
        rows[0, o:o + w] = a.reshape(-1)

    putr("subln", inp["da_subln_g"][l])
    putr("dn_on", inp["dn_onorm_g"][l])
    putr("gla_on", inp["gla_onorm_g"][l])
    putr("a_log", inp["dn_a_log"][l])
    putr("dt_b", inp["dn_dt_bias"][l])
    putr("lam", inp["da_lambda"][l])
    putr("b_g1", inp["b_mod"][l][2048:3072])
    putr("b_g2", inp["b_mod"][l][5120:6144])
    putr("gla_b2r", inp["gla_b2"][l])
    w2p = np.zeros((2, 32, 128), np.float32)
    w2p[0, 0:16] = inp["gla_w2"][l][0]
    w2p[1, 16:32] = inp["gla_w2"][l][1]
    return cols, rows, w2p


PHASES = None


class Builder:
    def __init__(self, dbg=False, phases=None):
        self.dbg = dbg
        self.phases = phases
        self.nc = bass.Bass("TRN2", target_bir_lowering=False)
        self.scr = {}

    def din(self, name, shape, dt=F32):
        return self.nc.dram_tensor(name, list(shape), dt, kind="ExternalInput").ap()

    def dscr(self, name, shape, dt=F32):
        kind = "ExternalOutput" if self.dbg else "Internal"
        if name in getattr(self, "dbg_inputs", ()):
            kind = "ExternalInput"
        t = self.nc.dram_tensor(name, list(shape), dt, kind=kind).ap()
        self.scr[name] = (t, Buf(name))
        return t

    def uniq(self, n):
        self._u = getattr(self, "_u", 0) + 1
        return "%s_%d" % (n, self._u)

    def want(self, ph):
        return self.phases is None or ph in self.phases

    def build(self):
        nc = self.nc
        I = {}
        I["xin"] = self.din("xin", [T, D])
        I["consts"] = self.din("consts", [128, NCONST])
        I["rope"] = self.din("rope", [128, 2 * SEQ])
        I["cols"] = self.din("cols", [DEPTH, 128, NCOL])
        I["rows"] = self.din("rows", [DEPTH, 1, NROW])
        I["w2p"] = self.din("w2p", [DEPTH, 2, 32, 128])
        I["w_mod"] = self.din("w_mod", [DEPTH, D, 6 * D])
        I["w_in"] = self.din("w_in", [DEPTH, D, INC])
        I["w_out"] = self.din("w_out", [DEPTH, D, D])
        I["w_up"] = self.din("w_up", [DEPTH, D, 2 * DFF])
        I["w_down"] = self.din("w_down", [DEPTH, DFF, D])
        self.I = I
        self.out = nc.dram_tensor("out", [SEQ, D], F32, kind="ExternalOutput").ap()
        self.OUTB = Buf("out")
        self.dscr("xres", [T, D])
        self.dscr("x1", [T, D])
        self.dscr("xa", [T, D])
        self.dscr("cmY", [256, TP])
        self.dscr("qT", [256, T])
        self.dscr("kT", [256, T])
        self.dscr("dnqkv", [768, TPD])
        self.dscr("glaqT", [128, T])
        self.dscr("glakT", [128, T])
        self.dscr("glalrT", [32, T])
        self.dscr("tokmaj", [T, NTM])
        self.dscr("mixT", [D, T], BF16)
        self.dscr("dn_qT", [256, T])
        self.dscr("dn_kT", [256, T])
        self.dscr("dn_ktok", [T, 256])
        self.dscr("dn_vtok", [T, 256])
        self.dscr("ofwd", [T, 256])
        self.dscr("h2T", [8 * 128, FPAD], BF16)
        with ExitStack() as top:
            self.fw = FW(nc, top)
            fw = self.fw
            self.cst = top.enter_context(nc.sbuf_tensor("cst", [128, NCONST], F32))
            self.CST = Buf("cst")
            fw.dma(self.cst[:], I["consts"][:, :], writes=[self.CST])
            self.colp = top.enter_context(nc.sbuf_tensor("colp", [128, NCOL], F32))
            self.COLP = Buf("colp")
            self.rowp = top.enter_context(nc.sbuf_tensor("rowp", [128, NROW], F32))
            self.ROWP = Buf("rowp")
            self.modc = top.enter_context(nc.sbuf_tensor("modc", [128, 96], F32))
            self.MODC = Buf("modc")
            self.a1 = top.enter_context(nc.sbuf_tensor("a1", [128, 32], F32))
            self.A1 = Buf("a1")
            self.gb = top.enter_context(nc.sbuf_tensor("gb", [128, 4, D], F32))
            self.GB = Buf("gb")
            for l in range(getattr(self, 'depth_run', DEPTH)):
                self.layer(l)
            fw.barrier()
        return nc

    def C(self, name, rows=128):
        o, w = COFF[name]
        return self.cst[0:rows, o:o + w]

    def col(self, name, j=0, n=1):
        o, w = COLS[name]
        return self.colp[:, o + j:o + j + n]

    def row(self, name, j=0, n=None):
        o, w = ROWS[name]
        if n is None:
            n = w
        return self.rowp[:, o + j:o + j + n]

    def layer(self, l):
        fw, nc, I = self.fw, self.nc, self.I
        fw.barrier()
        fw.dma(self.colp[:], I["cols"][l, :, :], writes=[self.COLP])
        fw.dma(self.rowp[:], I["rows"][l, :, :].partition_broadcast(128), writes=[self.ROWP])
        self.phase0(l)
        xsrc = I["xin"] if l == 0 else self.scr["xres"][0]
        if self.want("A"):
            self.phaseA(l, xsrc)
        if self.want("B"):
            self.phaseB(l)
        if self.want("C"):
            self.phaseC(l)
        if not (self.want("D") and self.want("E")):
            self.zero_mix(l)
        if self.want("D"):
            self.phaseD(l)
        if self.want("E"):
            self.phaseE(l)
        if self.want("F"):
            self.phaseF(l, xsrc)

    def phase0(self, l):
        fw, nc, I = self.fw, self.nc, self.I
        with ExitStack() as st:
            T_ = lambda n, s, d=F32: st.enter_context(nc.sbuf_tensor(self.uniq(n), s, d))
            P_ = lambda n, s, d=F32: st.enter_context(nc.psum_tensor(self.uniq(n), s, d))
            cact = T_("cact", [128, 16])
            CACT = Buf()
            wm = [T_("wm%d" % i, [128, 8, 1024]) for i in range(2)]
            WM = fw.bufs(2)
            mps = P_("mps", [128, 96])
            MPS = Buf()
            rps = [P_("rps%d" % i, [128, 512]) for i in range(2)]
            RPS = fw.bufs(2)
            grow = T_("grow", [2, 2, 1024])
            GROW = Buf()
            gps = [P_("gps%d" % i, [128, 512]) for i in range(2)]
            GPS = fw.bufs(2)
            fw.psum(MPS, RPS, GPS)
            o, w = COLS["ccol"]
            fw.op("act", lambda: nc.scalar.activation(out=cact[:], in_=self.colp[:, o:o + 16], func=AF.Silu),
                  reads=[self.COLP], writes=[CACT])
            ob, _ = COLS["b_mod"]
            for comp in range(6):
                b = comp % 2
                fw.dma(wm[b][:], I["w_mod"][l, :, comp * 1024:(comp + 1) * 1024].rearrange("(k p) c -> p k c", p=128),
                       writes=[WM[b]])
                for jj in range(8):
                    j = comp * 8 + jj
                    for k in range(8):
                        fw.op("pe", lambda: nc.tensor.matmul(mps[:, 2 * j:2 * j + 2], lhsT=wm[b][:, k, jj * 128:(jj + 1) * 128],
                                                             rhs=cact[:, 2 * k:2 * k + 2], start=(k == 0), stop=(k == 7)),
                              reads=[WM[b], CACT], writes=[MPS])
                if comp in (2, 5):
                    which = 0 if comp == 2 else 1
                    bname = "b_g1" if comp == 2 else "b_g2"
                    for hc in range(2):
                        for k in range(8):
                            fw.op("pe", lambda: nc.tensor.matmul(rps[hc][0:2, :], lhsT=cact[:, 2 * k:2 * k + 2],
                                                                 rhs=wm[b][:, k, hc * 512:(hc + 1) * 512],
                                                                 start=(k == 0), stop=False),
                                  reads=[WM[b], CACT], writes=[RPS[hc]])
                        ro, _ = ROWS[bname]
                        fw.op("pe", lambda: nc.tensor.matmul(rps[hc][0:2, :], lhsT=self.C("ones")[0:1, 0:2],
                                                             rhs=self.rowp[0:1, ro + hc * 512:ro + (hc + 1) * 512],
                                                             start=False, stop=True),
                              reads=[self.ROWP, self.CST], writes=[RPS[hc]])
                        fw.op("dve", lambda: nc.vector.tensor_copy(out=grow[:, which, hc * 512:(hc + 1) * 512], in_=rps[hc][0:2, :]),
                              reads=[RPS[hc]], writes=[GROW])
            for s in range(2):
                fw.op("dve", lambda: nc.vector.tensor_tensor(out=self.modc[:].rearrange("p (j s) -> p j s", s=2)[:, :, s],
                                                             in0=mps[:].rearrange("p (j s) -> p j s", s=2)[:, :, s],
                                                             in1=self.colp[:, ob:ob + 48], op=ALU.add),
                      reads=[MPS, self.COLP], writes=[self.MODC])
            for which, (gname, scbase) in enumerate((("n1g", 8), ("n2g", 32))):
                go, _ = COLS[gname]
                for s in range(2):
                    fw.op("dve", lambda: nc.vector.scalar_tensor_tensor(
                        out=self.a1[:, which * 16:(which + 1) * 16].rearrange("p (k s) -> p k s", s=2)[:, :, s],
                        in0=self.modc[:].rearrange("p (j s) -> p j s", s=2)[:, scbase:scbase + 8, s],
                        scalar=1.0, in1=self.colp[:, go:go + 8], op0=ALU.add, op1=ALU.mult),
                        reads=[self.MODC, self.COLP], writes=[self.A1])
            so, _ = COFF["sel"]
            for which in range(2):
                for s in range(2):
                    for hc in range(2):
                        pi = hc
                        fw.op("pe", lambda: nc.tensor.matmul(gps[pi][:, :], lhsT=self.cst[0:2, so + s * 128:so + (s + 1) * 128],
                                                             rhs=grow[:, which, hc * 512:(hc + 1) * 512], start=True, stop=True),
                              reads=[GROW, self.CST], writes=[GPS[pi]])
                        fw.op("act", lambda: nc.scalar.copy(out=self.gb[:, which * 2 + s, hc * 512:(hc + 1) * 512], in_=gps[pi][:, :]),
                              reads=[GPS[pi]], writes=[self.GB])
            fw.barrier()

    def shift(self, k, s):
        raise NotImplementedError

    def norm_mod_T(self, st, which, xt, XT, s, hT, HT, hoff, tag, pst, PST, ident16, ID16):
        fw, nc = self.fw, self.nc
        sq, SQ, ssq, SSQ, xs, XS = self._nm_tmp
        fw.op("act", lambda: nc.scalar.activation(out=sq[:], in_=xt, func=AF.Square, accum_out=ssq[:, 0:1]),
              reads=[XT], writes=[SQ, SSQ])
        fw.op("act", lambda: nc.scalar.activation(out=ssq[:, 1:2], in_=ssq[:, 0:1], func=AF.Sqrt, bias=self.epsc[:, 0:1], scale=1.0 / D),
              reads=[SSQ], writes=[SSQ])
        fw.op("dve", lambda: nc.vector.reciprocal(out=ssq[:, 2:3], in_=ssq[:, 1:2]), reads=[SSQ], writes=[SSQ])
        fw.op("dve", lambda: nc.vector.tensor_scalar(out=xs[:], in0=xt, scalar1=ssq[:, 2:3], scalar2=None, op0=ALU.mult),
              reads=[XT, SSQ], writes=[XS])
        for k in range(8):
            fw.op("pe", lambda: nc.tensor.transpose(pst[:, k * 128:(k + 1) * 128], xs[:, k * 128:(k + 1) * 128], ident16[:]),
                  reads=[XS, ID16], writes=[PST])
        shbase = 0 if which == 0 else 24
        for k in range(8):
            eng = "act" if k % 2 == 0 else "dve"
            acol = self.a1[:, which * 16 + 2 * k + s:which * 16 + 2 * k + s + 1]
            shcol = self.modc[:, 2 * (shbase + k) + s:2 * (shbase + k) + s + 1]
            if eng == "act":
                fw.op("act", lambda: nc.scalar.activation(out=hT[:, k, hoff:hoff + 128], in_=pst[:, k * 128:(k + 1) * 128],
                                                          func=AF.Identity, bias=shcol, scale=acol),
                      reads=[PST, self.A1, self.MODC], writes=[HT])
            else:
                fw.op("dve", lambda: nc.vector.tensor_scalar(out=hT[:, k, hoff:hoff + 128], in0=pst[:, k * 128:(k + 1) * 128],
                                                             scalar1=acol, scalar2=shcol, op0=ALU.mult, op1=ALU.add),
                      reads=[PST, self.A1, self.MODC], writes=[HT])

    def common_tiles(self, st):
        fw, nc = self.fw, self.nc
        T_ = lambda n, s, d=F32: st.enter_context(nc.sbuf_tensor(self.uniq(n), s, d))
        self.epsc = T_("epsc", [128, 1])
        self.EPSC = Buf()
        fw.op("pool", lambda: nc.gpsimd.memset(self.epsc[:], EPS), writes=[self.EPSC])
        ident16 = T_("ident16", [128, 128], BF16)
        ID16 = Buf()
        fw.op("dve", lambda: nc.vector.tensor_copy(out=ident16[:], in_=self.C("ident")), reads=[self.CST], writes=[ID16])
        sq = T_("nm_sq", [128, 1024], BF16)
        ssq = T_("nm_ssq", [128, 4])
        xs = T_("nm_xs", [128, 1024], BF16)
        self._nm_tmp = (sq, Buf(), ssq, Buf(), xs, Buf())
        return ident16, ID16

    def phaseA(self, l, xsrc):
        fw, nc, I = self.fw, self.nc, self.I
        S = self.scr
        with ExitStack() as st:
            T_ = lambda n, s, d=F32: st.enter_context(nc.sbuf_tensor(self.uniq(n), s, d))
            P_ = lambda n, s, d=F32: st.enter_context(nc.psum_tensor(self.uniq(n), s, d))
            ident16, ID16 = self.common_tiles(st)
            win = T_("win", [128, 8, INC], BF16)
            WIN = Buf()
            stg = [T_("wstg%d" % i, [128, 8, 390]) for i in range(2)]
            STG = fw.bufs(2)
            for cb in range(8):
                b = cb % 2
                fw.dma(stg[b][:], I["w_in"][l, :, cb * 390:(cb + 1) * 390].rearrange("(k p) c -> p k c", p=128), writes=[STG[b]])
                fw.op("pool", lambda: nc.gpsimd.tensor_copy(out=win[:, :, cb * 390:(cb + 1) * 390], in_=stg[b][:]),
                      reads=[STG[b]], writes=[WIN])
            xt = [T_("xt%d" % i, [128, 2, 1024]) for i in range(2)]
            XT = fw.bufs(2)
            hT = [T_("hT%d" % i, [128, 8, 256], BF16) for i in range(2)]
            HT = fw.bufs(2)
            pst = [P_("pst%d" % i, [128, 1024], BF16) for i in range(2)]
            PST = fw.bufs(2)
            mp = [P_("mp%d" % i, [128, 512]) for i in range(4)]
            MP = fw.bufs(4)
            fw.psum(PST, MP)
            fo = [T_("fo%d" % i, [128, 17, 256]) for i in range(2)]
            FO = fw.bufs(2)
            to = [T_("to%d" % i, [128, 2, NTM]) for i in range(2)]
            TO = fw.bufs(2)
            sg = [T_("sg%d" % i, [128, 256]) for i in range(2)]
            SG = fw.bufs(2)
            fchunks = [(0, 128), (128, 128), (256, 128), (384, 128),
                       (512, 128), (640, 128), (768, 128), (896, 128)]
            fchunks += [(1280 + 128 * i, 128) for i in range(6)]
            fchunks += [(2320, 128), (2448, 128), (2832, 32)]
            tpieces = [(1024, 256, 0), (2048, 272, 256), (2576, 272, 528), (2848, 272, 800)]
            mpi = 0
            ngroups = T // 256
            for g in range(ngroups):
                b = g % 2
                s = 1 if g == 0 else 0
                fw.dma(xt[b][:], xsrc[g * 256:(g + 1) * 256, :].rearrange("(a p) c -> p a c", p=128), writes=[XT[b]])
                for a in range(2):
                    self.norm_mod_T(st, 0, xt[b][:, a, :], XT[b], s, hT[b], HT[b], a * 128, "A", pst[a], PST[a], ident16, ID16)
                if g == 0:
                    tok0 = 0
                    cmpos = CM_CTX0
                    dnpos = DN_CTX0
                else:
                    tok0 = g * 256
                    cmpos = CM_LAT0 + (g - 1) * 256
                    dnpos = DN_LAT0 + (g - 1) * 256
                for ci, (c0, ncol) in enumerate(fchunks):
                    pi = mpi % 4
                    mpi += 1
                    for k in range(8):
                        fw.op("pe", lambda: nc.tensor.matmul(mp[pi][0:ncol, 0:256], lhsT=win[:, k, c0:c0 + ncol], rhs=hT[b][:, k, :],
                                                             start=(k == 0), stop=(k == 7)),
                              reads=[WIN, HT[b]], writes=[MP[pi]])
                    if ci in (2, 3):
                        fw.op("act", lambda: nc.scalar.activation(out=sg[ci - 2][:], in_=mp[pi][:, 0:256], func=AF.Sigmoid),
                              reads=[MP[pi]], writes=[SG[ci - 2]])
                        fw.op("dve", lambda: nc.vector.tensor_tensor(out=fo[b][:, ci - 2, :], in0=fo[b][:, ci - 2, :], in1=sg[ci - 2][:], op=ALU.mult),
                              reads=[SG[ci - 2], FO[b]], writes=[FO[b]])
                    else:
                        eng = "act" if ci % 2 == 0 else "dve"
                        if eng == "act":
                            fw.op("act", lambda: nc.scalar.copy(out=fo[b][0:ncol, ci, :], in_=mp[pi][0:ncol, 0:256]),
                                  reads=[MP[pi]], writes=[FO[b]])
                        else:
                            fw.op("dve", lambda: nc.vector.tensor_copy(out=fo[b][0:ncol, ci, :], in_=mp[pi][0:ncol, 0:256]),
                                  reads=[MP[pi]], writes=[FO[b]])
                fw.dma(S["cmY"][0][:, cmpos:cmpos + 256].rearrange("(c p) t -> p c t", p=128), fo[b][:, 0:2, :], reads=[FO[b]], writes=[S["cmY"][1]])
                fw.dma(S["qT"][0][:, tok0:tok0 + 256].rearrange("(c p) t -> p c t", p=128), fo[b][:, 4:6, :], reads=[FO[b]], writes=[S["qT"][1]])
                fw.dma(S["kT"][0][:, tok0:tok0 + 256].rearrange("(c p) t -> p c t", p=128), fo[b][:, 6:8, :], reads=[FO[b]], writes=[S["kT"][1]])
                fw.dma(S["dnqkv"][0][:, dnpos:dnpos + 256].rearrange("(c p) t -> p c t", p=128), fo[b][:, 8:14, :], reads=[FO[b]], writes=[S["dnqkv"][1]])
                fw.dma(S["glaqT"][0][:, tok0:tok0 + 256], fo[b][:, 14, :], reads=[FO[b]], writes=[S["glaqT"][1]])
                fw.dma(S["glakT"][0][:, tok0:tok0 + 256], fo[b][:, 15, :], reads=[FO[b]], writes=[S["glakT"][1]])
                fw.dma(S["glalrT"][0][:, tok0:tok0 + 256], fo[b][0:32, 16, :], reads=[FO[b]], writes=[S["glalrT"][1]])
                for a in range(2):
                    for (c0, ncol, o0) in tpieces:
                        pi = mpi % 4
                        mpi += 1
                        for k in range(8):
                            fw.op("pe", lambda: nc.tensor.matmul(mp[pi][:, 0:ncol], lhsT=hT[b][:, k, a * 128:(a + 1) * 128], rhs=win[:, k, c0:c0 + ncol],
                                                                 start=(k == 0), stop=(k == 7)),
                                  reads=[WIN, HT[b]], writes=[MP[pi]])
                        eng = "act" if (pi % 2 == 0) else "dve"
                        if eng == "act":
                            fw.op("act", lambda: nc.scalar.copy(out=to[b][:, a, o0:o0 + ncol], in_=mp[pi][:, 0:ncol]), reads=[MP[pi]], writes=[TO[b]])
                        else:
                            fw.op("dve", lambda: nc.vector.tensor_copy(out=to[b][:, a, o0:o0 + ncol], in_=mp[pi][:, 0:ncol]), reads=[MP[pi]], writes=[TO[b]])
                fw.dma(S["tokmaj"][0][tok0:tok0 + 256, :].rearrange("(a p) c -> p a c", p=128), to[b][:], reads=[TO[b]], writes=[S["tokmaj"][1]])
            fw.barrier()

    def zero_mix(self, l):
        fw, nc = self.fw, self.nc
        S = self.scr
        with ExitStack() as st:
            z = st.enter_context(nc.sbuf_tensor(self.uniq("zmix"), [128, T], BF16))
            Z = Buf()
            fw.op("pool", lambda: nc.gpsimd.memset(z[:], 0.0), writes=[Z])
            for c in range(4, 8):
                if (c < 6 and not self.want("E")) or (c >= 6 and not self.want("D")):
                    fw.dma(S["mixT"][0][c * 128:(c + 1) * 128, :], z[:], reads=[Z], writes=[S["mixT"][1]])
            fw.barrier()

    def phaseB(self, l):
        fw, nc = self.fw, self.nc
        S = self.scr
        with ExitStack() as st:
            T_ = lambda n, s, d=F32: st.enter_context(nc.sbuf_tensor(self.uniq(n), s, d))
            P_ = lambda n, s, d=F32: st.enter_context(nc.psum_tensor(self.uniq(n), s, d))
            self.common_tiles(st)
            Y = T_("cmy", [128, 2, TP])
            YB = fw.bufs(2)
            accA = T_("accA", [128, 2, TP])
            AA = fw.bufs(2)
            L = TP - 2 * CMP
            wo, _ = COLS["cm_w"]
            bo, _ = COLS["cm_b"]
            for c in range(2):
                fw.dma(Y[:, c, :], S["cmY"][0][c * 128:(c + 1) * 128, :], reads=[S["cmY"][1]], writes=[YB[c]])
                for (a, b) in ((0, CMP), (CM_CTX0 + CTX, CM_LAT0), (CM_LAT0 + SEQ, TP)):
                    fw.op("pool", lambda: nc.gpsimd.memset(Y[:, c, a:b], 0.0), writes=[YB[c]])
            for c in range(2):
                for j in range(31):
                    wcol = self.colp[:, wo + c * 31 + j:wo + c * 31 + j + 1]
                    e, eng, acc, AC, first = "dve", nc.vector, accA, AA[c], (j == 0)
                    if first:
                        fw.op(e, lambda: eng.tensor_scalar(out=acc[:, c, CMP:CMP + L], in0=Y[:, c, j:j + L], scalar1=wcol, scalar2=None, op0=ALU.mult),
                              reads=[YB[c], self.COLP], writes=[AC])
                    else:
                        fw.op(e, lambda: eng.scalar_tensor_tensor(out=acc[:, c, CMP:CMP + L], in0=Y[:, c, j:j + L], scalar=wcol, in1=acc[:, c, CMP:CMP + L],
                                                                  op0=ALU.mult, op1=ALU.add), reads=[YB[c], self.COLP, AC], writes=[AC])
                fw.op("pool", lambda: nc.gpsimd.tensor_scalar(out=accA[:, c, CMP:CMP + L], in0=accA[:, c, CMP:CMP + L], scalar1=self.colp[:, bo + c:bo + c + 1],
                                                              scalar2=None, op0=ALU.add),
                      reads=[AA[c], self.COLP], writes=[AA[c]])
            sq = T_("lnsq", [128, 2, 512])
            SQ = Buf()
            msq = T_("msq", [128, 512])
            MSQ = Buf()
            var = T_("var", [128, 512])
            VAR = Buf()
            tt = [T_("lnt%d" % i, [128, 512]) for i in range(2)]
            TT = fw.bufs(2)
            ob = [T_("lno%d" % i, [128, 512], BF16) for i in range(2)]
            OB = fw.bufs(2)
            mps = P_("lnm", [128, 512])
            MPS = Buf()
            eps_ = P_("lne", [128, 512])
            EPS_ = Buf()
            fw.psum(MPS, EPS_)
            lg, _ = COLS["cm_lg"]
            lb, _ = COLS["cm_lb"]
            blocks = [(CM_CTX0, 256, 0)] + [(CM_LAT0 + 512 * k, 512, CTX + 512 * k) for k in range(8)]
            for (p0, n, tok0) in blocks:
                for c in range(2):
                    fw.op("act", lambda: nc.scalar.activation(out=sq[:, c, 0:n], in_=accA[:, c, p0:p0 + n], func=AF.Square), reads=[AA[c]], writes=[SQ])
                for c in range(2):
                    fw.op("pe", lambda: nc.tensor.matmul(mps[:, 0:n], lhsT=self.C("div256"), rhs=accA[:, c, p0:p0 + n], start=(c == 0), stop=(c == 1)),
                          reads=[self.CST, AA[c]], writes=[MPS])
                for c in range(2):
                    fw.op("pe", lambda: nc.tensor.matmul(eps_[:, 0:n], lhsT=self.C("div256"), rhs=sq[:, c, 0:n], start=(c == 0), stop=(c == 1)),
                          reads=[self.CST, SQ], writes=[EPS_])
                fw.op("act", lambda: nc.scalar.activation(out=msq[:, 0:n], in_=mps[:, 0:n], func=AF.Square), reads=[MPS], writes=[MSQ])
                fw.op("dve", lambda: nc.vector.tensor_tensor(out=var[:, 0:n], in0=eps_[:, 0:n], in1=msq[:, 0:n], op=ALU.subtract), reads=[EPS_, MSQ], writes=[VAR])
                fw.op("act", lambda: nc.scalar.activation(out=var[:, 0:n], in_=var[:, 0:n], func=AF.Sqrt, bias=self.epsc[:, 0:1], scale=1.0), reads=[VAR, self.EPSC], writes=[VAR])
                fw.op("dve", lambda: nc.vector.reciprocal(out=var[:, 0:n], in_=var[:, 0:n]), reads=[VAR], writes=[VAR])
                for c in range(2):
                    fw.op("dve", lambda: nc.vector.tensor_tensor(out=tt[c][:, 0:n], in0=accA[:, c, p0:p0 + n], in1=mps[:, 0:n], op=ALU.subtract),
                          reads=[AA[c], MPS], writes=[TT[c]])
                    fw.op("pool", lambda: nc.gpsimd.tensor_tensor(out=tt[c][:, 0:n], in0=tt[c][:, 0:n], in1=var[:, 0:n], op=ALU.mult), reads=[TT[c], VAR], writes=[TT[c]])
                    fw.op("act", lambda: nc.scalar.activation(out=ob[c][:, 0:n], in_=tt[c][:, 0:n], func=AF.Silu, bias=self.colp[:, lb + c:lb + c + 1],
                                                              scale=self.colp[:, lg + c:lg + c + 1]), reads=[TT[c], self.COLP], writes=[OB[c]])
                    fw.dma(S["mixT"][0][c * 128:(c + 1) * 128, tok0:tok0 + n], ob[c][:, 0:n], reads=[OB[c]], writes=[S["mixT"][1]])
            fw.barrier()

    def phaseC(self, l):
        fw, nc, I = self.fw, self.nc, self.I
        S = self.scr
        lam_init = 0.8 - 0.6 * math.exp(-0.3 * l)
        scale = 32.0 ** -0.5
        with ExitStack() as st:
            T_ = lambda n, s, d=F32: st.enter_context(nc.sbuf_tensor(self.uniq(n), s, d))
            P_ = lambda n, s, d=F32: st.enter_context(nc.psum_tensor(self.uniq(n), s, d))
            ident16, ID16 = self.common_tiles(st)
            QT = T_("QT", [128, 2, T], BF16)
            KT = T_("KT", [128, 2, T], BF16)
            QTB, KTB = Buf(), Buf()
            V = T_("V", [128, NT, 4, 65], BF16)
            VB = Buf()
            fw.op("pool", lambda: nc.gpsimd.memset(V[:].rearrange("p a b c -> p (a b c)"), 1.0), writes=[VB])
            blocks = [(0, 256)] + [(CTX + 512 * k, 512) for k in range(8)]
            with ExitStack() as st2:
                T2 = lambda n, s, d=F32: st2.enter_context(nc.sbuf_tensor(self.uniq(n), s, d))
                P2 = lambda n, s, d=F32: st2.enter_context(nc.psum_tensor(self.uniq(n), s, d))
                raw = [T2("raw%d" % i, [128, 512]) for i in range(2)]
                RAW = fw.bufs(2)
                cs = [T2("cs%d" % i, [128, 2, 512]) for i in range(2)]
                CS = fw.bufs(2)
                sq = T2("sq", [128, 512]); SQ = Buf()
                rs = T2("rs", [128, 512]); RS = Buf()
                qn = T2("qn", [128, 512]); QN = Buf()
                r1 = T2("r1", [128, 512]); R1 = Buf()
                r2 = T2("r2", [128, 512]); R2 = Buf()
                ssp = P2("ssp", [128, 512]); SSP = Buf()
                swp = P2("swp", [128, 512]); SWP = Buf()
                fw.psum(SSP, SWP)
                vst = [T2("vst%d" % i, [128, 256]) for i in range(2)]
                VST = fw.bufs(2)
                it = 0
                for (name, dst, DST, gname) in (("qT", QT, QTB, "qg"), ("kT", KT, KTB, "kg")):
                    go, _ = COLS[gname]
                    for c in range(2):
                        for (t0, n) in blocks:
                            b = it % 2
                            it += 1
                            fw.dma(raw[b][:, 0:n], S[name][0][c * 128:(c + 1) * 128, t0:t0 + n], reads=[S[name][1]], writes=[RAW[b]])
                            fw.op("act", lambda: nc.scalar.activation(out=sq[:, 0:n], in_=raw[b][:, 0:n], func=AF.Square), reads=[RAW[b]], writes=[SQ])
                            fw.op("pe", lambda: nc.tensor.matmul(ssp[:, 0:n], lhsT=self.C("blk32"), rhs=sq[:, 0:n], start=True, stop=True), reads=[self.CST, SQ], writes=[SSP])
                            fw.op("act", lambda: nc.scalar.activation(out=rs[:, 0:n], in_=ssp[:, 0:n], func=AF.Sqrt, bias=self.epsc[:, 0:1], scale=1.0), reads=[SSP, self.EPSC], writes=[RS])
                            fw.op("dve", lambda: nc.vector.reciprocal(out=rs[:, 0:n], in_=rs[:, 0:n]), reads=[RS], writes=[RS])
                            if t0 < CTX:
                                fw.op("dve", lambda: nc.vector.scalar_tensor_tensor(out=dst[:, c, t0:t0 + n], in0=raw[b][:, 0:n], scalar=self.colp[:, go:go + 1], in1=rs[:, 0:n],
                                                                                    op0=ALU.mult, op1=ALU.mult), reads=[RAW[b], RS, self.COLP], writes=[DST])
                                continue
                            fw.op("dve", lambda: nc.vector.scalar_tensor_tensor(out=qn[:, 0:n], in0=raw[b][:, 0:n], scalar=self.colp[:, go:go + 1], in1=rs[:, 0:n],
                                                                                op0=ALU.mult, op1=ALU.mult), reads=[RAW[b], RS, self.COLP], writes=[QN])
                            lt = t0 - CTX
                            fw.dma(cs[b][:, 0, 0:n], I["rope"][:, lt:lt + n], writes=[CS[b]])
                            fw.dma(cs[b][:, 1, 0:n], I["rope"][:, SEQ + lt:SEQ + lt + n], writes=[CS[b]])
                            fw.op("pe", lambda: nc.tensor.matmul(swp[:, 0:n], lhsT=self.C("perm"), rhs=qn[:, 0:n], start=True, stop=True), reads=[self.CST, QN], writes=[SWP])
                            fw.op("pool", lambda: nc.gpsimd.tensor_tensor(out=r1[:, 0:n], in0=qn[:, 0:n], in1=cs[b][:, 0, 0:n], op=ALU.mult), reads=[QN, CS[b]], writes=[R1])
                            fw.op("dve", lambda: nc.vector.tensor_tensor(out=r2[:, 0:n], in0=swp[:, 0:n], in1=cs[b][:, 1, 0:n], op=ALU.mult), reads=[SWP, CS[b]], writes=[R2])
                            fw.op("pool", lambda: nc.gpsimd.tensor_tensor(out=dst[:, c, t0:t0 + n], in0=r1[:, 0:n], in1=r2[:, 0:n], op=ALU.add), reads=[R1, R2], writes=[DST])
                for t in range(NT):
                    b = t % 2
                    fw.dma(vst[b][:], S["tokmaj"][0][t * 128:(t + 1) * 128, 0:256], reads=[S["tokmaj"][1]], writes=[VST[b]])
                    fw.op("pool", lambda: nc.gpsimd.tensor_copy(out=V[:, t, :, 0:64], in_=vst[b][:].rearrange("p (h d) -> p h d", h=4)), reads=[VST[b]], writes=[VB])
                fw.barrier()
            lt_ = T_("lamt", [128, 8]); LT = Buf()
            lp = T_("lamp", [128, 64]); LP = Buf()
            lo, _ = ROWS["lam"]
            for i in range(2):
                fw.op("dve", lambda: nc.vector.tensor_tensor(out=lp[:, i * 32:(i + 1) * 32], in0=self.rowp[:, lo + 64 * i:lo + 64 * i + 32], in1=self.rowp[:, lo + 64 * i + 32:lo + 64 * i + 64], op=ALU.mult),
                      reads=[self.ROWP], writes=[LP])
                fw.op("dve", lambda: nc.vector.reduce_sum(out=lt_[:, i:i + 1], in_=lp[:, i * 32:(i + 1) * 32], axis=AX.X), reads=[LP], writes=[LT])
            fw.op("act", lambda: nc.scalar.activation(out=lt_[:, 2:4], in_=lt_[:, 0:2], func=AF.Exp), reads=[LT], writes=[LT])
            fw.op("dve", lambda: nc.vector.scalar_tensor_tensor(out=lt_[:, 4:5], in0=lt_[:, 3:4], scalar=-lam_init, in1=lt_[:, 2:3], op0=ALU.add, op1=ALU.subtract), reads=[LT], writes=[LT])
            subg = T_("subg", [128, 64]); SUBG = Buf()
            so, _ = ROWS["subln"]
            fw.op("dve", lambda: nc.vector.tensor_scalar(out=subg[:], in0=self.rowp[:, so:so + 64], scalar1=(1.0 - lam_init), scalar2=None, op0=ALU.mult), reads=[self.ROWP], writes=[SUBG])
            identf = self.C("ident")
            scps = [P_("scps%d" % i, [128, 512]) for i in range(2)]
            SCPS = fw.bufs(2)
            otps = [P_("otps%d" % i, [128, 512]) for i in range(2)]
            OTPS = fw.bufs(2)
            trps = [P_("trps%d" % i, [128, 4, 65]) for i in range(2)]
            TRPS = fw.bufs(2)
            ytps = [P_("ytps%d" % i, [128, 512], BF16) for i in range(2)]
            YTPS = fw.bufs(2)
            fw.psum(SCPS, OTPS, TRPS, YTPS)
            pb = [T_("pb%d" % i, [128, 512], BF16) for i in range(3)]
            PB = fw.bufs(3)
            otsb = [T_("otsb%d" % i, [128, 512]) for i in range(2)]
            OTSB = fw.bufs(2)
            rcp = T_("rcp", [128, 2, 4]); RCP = Buf()
            o0 = T_("o0", [128, 64]); O0 = Buf()
            o1 = T_("o1", [128, 64]); O1 = Buf()
            av = T_("av", [128, 64]); AV = Buf()
            junk = T_("junk", [128, 64]); JUNK = Buf()
            ssa = T_("ssa", [128, 4]); SSA = Buf()
            yb = [T_("yb%d" % i, [128, 4, 256], BF16) for i in range(2)]
            YB = fw.bufs(2)
            ybT = [T_("ybT%d" % i, [128, 2, 512], BF16) for i in range(2)]
            YBT = fw.bufs(2)
            qblocks = [(0, 256, 2)] + [(CTX + 512 * k, 512, NT) for k in range(8)]
            pit = 0
            for qi, (q0, nq, nkt) in enumerate(qblocks):
                nsub = nq // 128
                qb = qi % 2
                for h in range(4):
                    ch = h // 2
                    for m in range(2):
                        r = (h % 2) * 2 + m
                        for kt in range(nkt):
                            si = kt % 2
                            fw.op("pe", lambda: nc.tensor.matmul(scps[si][:, 0:nq], lhsT=KT[32 * r:32 * r + 32, ch, kt * 128:(kt + 1) * 128],
                                                                 rhs=QT[32 * r:32 * r + 32, ch, q0:q0 + nq], start=True, stop=True, tile_position=(32 * r, 0)),
                                  reads=[KTB, QTB], writes=[SCPS[si]])
                            pi = pit % 3
                            pit += 1
                            fw.op("act", lambda: nc.scalar.activation(out=pb[pi][:, 0:nq], in_=scps[si][:, 0:nq], func=AF.Exp, scale=scale), reads=[SCPS[si]], writes=[PB[pi]])
                            fw.op("pe", lambda: nc.tensor.matmul(otps[m][0:65, 0:nq], lhsT=V[:, kt, h, :], rhs=pb[pi][:, 0:nq], start=(kt == 0), stop=(kt == nkt - 1)),
                                  reads=[VB, PB[pi]], writes=[OTPS[m]])
                        fw.op("dve", lambda: nc.vector.tensor_copy(out=otsb[m][0:65, 0:nq], in_=otps[m][0:65, 0:nq]), reads=[OTPS[m]], writes=[OTSB[m]])
                        for sub in range(nsub):
                            fw.op("pe", lambda: nc.tensor.transpose(trps[m][:, sub, :], otsb[m][0:65, sub * 128:(sub + 1) * 128], identf[0:65, 0:65]),
                                  reads=[OTSB[m], self.CST], writes=[TRPS[m]])
                        fw.op("dve", lambda: nc.vector.reciprocal(out=rcp[:, m, 0:nsub], in_=trps[m][:, 0:nsub, 64]), reads=[TRPS[m]], writes=[RCP])
                    for sub in range(nsub):
                        fw.op("dve", lambda: nc.vector.tensor_scalar(out=o0[:], in0=trps[0][:, sub, 0:64], scalar1=rcp[:, 0, sub:sub + 1], scalar2=None, op0=ALU.mult),
                              reads=[TRPS[0], RCP], writes=[O0])
                        fw.op("dve", lambda: nc.vector.tensor_scalar(out=o1[:], in0=trps[1][:, sub, 0:64], scalar1=rcp[:, 1, sub:sub + 1], scalar2=None, op0=ALU.mult),
                              reads=[TRPS[1], RCP], writes=[O1])
                        fw.op("dve", lambda: nc.vector.scalar_tensor_tensor(out=av[:], in0=o1[:], scalar=lt_[:, 4:5], in1=o0[:], op0=ALU.mult, op1=ALU.add),
                              reads=[O0, O1, LT], writes=[AV])
                        fw.op("act", lambda: nc.scalar.activation(out=junk[:], in_=av[:], func=AF.Square, accum_out=ssa[:, 0:1]), reads=[AV], writes=[JUNK, SSA])
                        fw.op("act", lambda: nc.scalar.activation(out=ssa[:, 1:2], in_=ssa[:, 0:1], func=AF.Sqrt, bias=self.epsc[:, 0:1], scale=1.0 / 64), reads=[SSA, self.EPSC], writes=[SSA])
                        fw.op("dve", lambda: nc.vector.reciprocal(out=ssa[:, 2:3], in_=ssa[:, 1:2]), reads=[SSA], writes=[SSA])
                        fw.op("dve", lambda: nc.vector.scalar_tensor_tensor(out=yb[qb][:, sub, h * 64:(h + 1) * 64], in0=av[:], scalar=ssa[:, 2:3], in1=subg[:], op0=ALU.mult, op1=ALU.mult),
                              reads=[AV, SSA, SUBG], writes=[YB[qb]])
                for sub in range(nsub):
                    for c2 in range(2):
                        fw.op("pe", lambda: nc.tensor.transpose(ytps[c2][:, sub * 128:(sub + 1) * 128], yb[qb][:, sub, c2 * 128:(c2 + 1) * 128], ident16[:]),
                              reads=[YB[qb], ID16], writes=[YTPS[c2]])
                for c2 in range(2):
                    fw.op("act", lambda: nc.scalar.copy(out=ybT[qb][:, c2, 0:nq], in_=ytps[c2][:, 0:nq]), reads=[YTPS[c2]], writes=[YBT[qb]])
                    fw.dma(S["mixT"][0][256 + c2 * 128:256 + (c2 + 1) * 128, q0:q0 + nq], ybT[qb][:, c2, 0:nq], reads=[YBT[qb]], writes=[S["mixT"][1]])
            fw.barrier()

    def gated_out(self, st, tiles, osb, OSB, gate_ap, GATE, on_name, row0, t):
        fw, nc = self.fw, self.nc
        (sq, SQ, ssq, SSQ, sg, SG, y, Y, yf, YF, ytps, YTPS, yT, YT, ident16, ID16) = tiles
        S = self.scr
        go, _ = ROWS[on_name]
        fw.op("pool", lambda: nc.gpsimd.tensor_tensor(out=sq[:], in0=osb[:], in1=osb[:], op=ALU.mult), reads=[OSB], writes=[SQ])
        fw.op("dve", lambda: nc.vector.reduce_sum(out=ssq[:, 0:4], in_=sq[:].rearrange("p (h d) -> p h d", h=4), axis=AX.X), reads=[SQ], writes=[SSQ])
        fw.op("act", lambda: nc.scalar.activation(out=ssq[:, 4:8], in_=ssq[:, 0:4], func=AF.Sqrt, bias=self.epsc[:, 0:1], scale=1.0 / 64), reads=[SSQ, self.EPSC], writes=[SSQ])
        fw.op("dve", lambda: nc.vector.reciprocal(out=ssq[:, 8:12], in_=ssq[:, 4:8]), reads=[SSQ], writes=[SSQ])
        fw.op("act", lambda: nc.scalar.activation(out=sg[:], in_=gate_ap, func=AF.Silu), reads=[GATE], writes=[SG])
        for h in range(4):
            fw.op("dve", lambda: nc.vector.scalar_tensor_tensor(out=y[:, h * 64:(h + 1) * 64], in0=osb[:, h * 64:(h + 1) * 64], scalar=ssq[:, 8 + h:9 + h],
                                                                in1=self.rowp[:, go:go + 64], op0=ALU.mult, op1=ALU.mult), reads=[OSB, SSQ, self.ROWP], writes=[Y])
        fw.op("pool", lambda: nc.gpsimd.tensor_tensor(out=yf[:], in0=y[:], in1=sg[:], op=ALU.mult), reads=[Y, SG], writes=[YF])
        for c2 in range(2):
            fw.op("pe", lambda: nc.tensor.transpose(ytps[:, c2 * 128:(c2 + 1) * 128], yf[:, c2 * 128:(c2 + 1) * 128], ident16[:]), reads=[YF, ID16], writes=[YTPS])
        fw.op("act", lambda: nc.scalar.copy(out=yT[:], in_=ytps[:, 0:256]), reads=[YTPS], writes=[YT])
        fw.dma(S["mixT"][0][row0:row0 + 256, t * 128:(t + 1) * 128].rearrange("(c p) t -> p c t", p=128), yT[:].rearrange("p (c t) -> p c t", c=2), reads=[YT], writes=[S["mixT"][1]])

    def gated_tiles(self, st, ident16, ID16):
        nc = self.nc
        T_ = lambda n, s, d=F32: st.enter_context(nc.sbuf_tensor(self.uniq(n), s, d))
        P_ = lambda n, s, d=F32: st.enter_context(nc.psum_tensor(self.uniq(n), s, d))
        ytb = Buf()
        ytb.ps = True
        return (T_("g_sq", [128, 256]), Buf(), T_("g_ssq", [128, 12]), Buf(), T_("g_sg", [128, 256]), Buf(), T_("g_y", [128, 256]), Buf(),
                T_("g_yf", [128, 256], BF16), Buf(), P_("g_ytps", [128, 1024], BF16), ytb, T_("g_yT", [128, 256], BF16), Buf(), ident16, ID16)

    def phaseD(self, l):
        fw, nc, I = self.fw, self.nc, self.I
        S = self.scr
        scale = 32.0 ** -0.5
        with ExitStack() as st:
            T_ = lambda n, s, d=F32: st.enter_context(nc.sbuf_tensor(self.uniq(n), s, d))
            P_ = lambda n, s, d=F32: st.enter_context(nc.psum_tensor(self.uniq(n), s, d))
            ident16, ID16 = self.common_tiles(st)
            gt = self.gated_tiles(st, ident16, ID16)
            w2 = T_("w2", [32, 2, 128]); W2 = Buf()
            fw.dma(w2[:], I["w2p"][l].rearrange("d r c -> r d c"), writes=[W2])
            Sst = T_("Sst", [128, 256]); SST = Buf()
            qT = T_("qT", [128, 128]); QTB = Buf()
            kT = T_("kT", [128, 128]); KTB = Buf()
            lrT = T_("lrT", [32, 128]); LRT = Buf()
            tm = T_("tm", [128, 544]); TM = Buf()
            z = T_("z", [128, 128]); Z = Buf()
            ebT = T_("ebT", [128, 128]); EBT = Buf()
            enbT = T_("enbT", [128, 128]); ENBT = Buf()
            qin = T_("qin", [128, 128]); QIN = Buf()
            kn = T_("kn", [128, 4, 128]); KN = Buf()
            ktok = T_("ktok", [128, 128]); KTOK = Buf()
            bcs = T_("bcs", [128, 128]); BCS = Buf()
            kend = T_("kend", [128, 2, 128]); KEND = Buf()
            aqk = T_("aqk", [128, 2, 512]); AQK = Buf()
            tmp = T_("tmp", [128, 256]); TMP = Buf()
            osb = T_("osb", [128, 256]); OSB = Buf()
            of = T_("of", [128, 256]); OF = Buf()
            zk = P_("zk", [128, 512]); ZP = Buf(); KP = ZP
            b3 = P_("b3", [128, 512]); B3 = [Buf()] * 3
            aq = P_("aq", [128, 512]); AQ = Buf()
            ops = P_("ops", [128, 512]); OPS = [Buf()] * 2
            sp_ = P_("sp", [128, 512]); sp = sp_[:, 0:256]; SP = Buf()
            fw.psum(ZP, B3, AQ, OPS, SP)
            onescol = self.C("ones")[:, 0:1]
            b2o, _ = ROWS["gla_b2r"]
            for d in range(2):
                nm = "f" if d == 0 else "r"
                order = list(range(NT)) if d == 0 else [1, 0] + list(range(NT - 1, 1, -1))
                if getattr(self, "nt_dbg", None):
                    order = [t for t in order if t < self.nt_dbg]
                fw.op("pool", lambda: nc.gpsimd.memset(Sst[:], 0.0), writes=[SST])
                for t in order:
                    tk = slice(t * 128, (t + 1) * 128)
                    fw.dma(qT[:], S["glaqT"][0][:, tk], reads=[S["glaqT"][1]], writes=[QTB])
                    fw.dma(kT[:], S["glakT"][0][:, tk], reads=[S["glakT"][1]], writes=[KTB])
                    fw.dma(lrT[:], S["glalrT"][0][:, tk], reads=[S["glalrT"][1]], writes=[LRT])
                    fw.dma(tm[:], S["tokmaj"][0][tk, 528:1072], reads=[S["tokmaj"][1]], writes=[TM])
                    fw.op("pe", lambda: nc.tensor.matmul(zk[:, 0:128], lhsT=lrT[:, :], rhs=w2[:, d, :], start=True, stop=True), reads=[LRT, W2], writes=[ZP])
                    fw.op("dve", lambda: nc.vector.tensor_tensor(out=z[:], in0=zk[:, 0:128], in1=self.rowp[:, b2o + d * 128:b2o + (d + 1) * 128], op=ALU.add), reads=[ZP, self.ROWP], writes=[Z])
                    fw.op("act", lambda: nc.scalar.activation(out=z[:], in_=z[:], func=AF.Exp, scale=-1.0), reads=[Z], writes=[Z])
                    fw.op("act", lambda: nc.scalar.activation(out=z[:], in_=z[:], func=AF.Ln, bias=onescol, scale=1.0), reads=[Z, self.CST], writes=[Z])
                    fw.op("pe", lambda: nc.tensor.matmul(b3[:, 0:128], lhsT=self.C("tris_" + nm), rhs=z[:], start=True, stop=True), reads=[self.CST, Z], writes=[B3[0]])
                    fw.op("pe", lambda: nc.tensor.matmul(b3[:, 128:256], lhsT=z[:], rhs=self.C("tris_" + nm), start=True, stop=True), reads=[self.CST, Z], writes=[B3[1]])
                    fw.op("pe", lambda: nc.tensor.matmul(b3[:, 256:384], lhsT=self.C("blks"), rhs=z[:], start=True, stop=True), reads=[self.CST, Z], writes=[B3[2]])
                    fw.op("act", lambda: nc.scalar.activation(out=ebT[:], in_=b3[:, 128:256], func=AF.Exp), reads=[B3[1]], writes=[EBT])
                    fw.op("act", lambda: nc.scalar.activation(out=enbT[:], in_=b3[:, 128:256], func=AF.Exp, scale=-1.0), reads=[B3[1]], writes=[ENBT])
                    fw.op("dve", lambda: nc.vector.scalar_tensor_tensor(out=qin[:], in0=qT[:], scalar=scale, in1=ebT[:], op0=ALU.mult, op1=ALU.mult), reads=[QTB, EBT], writes=[QIN])
                    hmo, _ = COFF["hm32"]
                    for h in range(4):
                        fw.op("dve", lambda: nc.vector.scalar_tensor_tensor(out=kn[:, h, :], in0=kT[:], scalar=self.cst[:, hmo + h:hmo + h + 1], in1=enbT[:], op0=ALU.mult, op1=ALU.mult),
                              reads=[KTB, ENBT, self.CST], writes=[KN])
                    fw.op("pe", lambda: nc.tensor.transpose(zk[:, 128:256], kT[:], self.C("ident")), reads=[KTB, self.CST], writes=[KP])
                    fw.op("act", lambda: nc.scalar.copy(out=ktok[:], in_=zk[:, 128:256]), reads=[KP], writes=[KTOK])
                    fw.op("act", lambda: nc.scalar.copy(out=bcs[:], in_=b3[:, 0:128]), reads=[B3[0]], writes=[BCS])
                    fw.op("dve", lambda: nc.vector.tensor_tensor(out=bcs[:], in0=b3[:, 256:384], in1=bcs[:], op=ALU.subtract), reads=[B3[2], BCS], writes=[BCS])
                    fw.op("act", lambda: nc.scalar.activation(out=bcs[:], in_=bcs[:], func=AF.Exp), reads=[BCS], writes=[BCS])
                    cio, _ = COFF["chunkind"]
                    for c in range(2):
                        fw.op("dve", lambda: nc.vector.scalar_tensor_tensor(out=kend[:, c, :], in0=ktok[:], scalar=self.cst[:, cio + c:cio + c + 1], in1=bcs[:], op0=ALU.mult, op1=ALU.mult),
                              reads=[KTOK, BCS, self.CST], writes=[KEND])
                    if getattr(self, "d_level", 9) < 2:
                        continue
                    for h in range(4):
                        fw.op("pe", lambda: nc.tensor.matmul(aq[:, h * 128:(h + 1) * 128], lhsT=kn[:, h, :], rhs=qin[:], start=True, stop=True), reads=[KN, QIN], writes=[AQ])
                    for c in range(2):
                        fw.op("dve", lambda: nc.vector.scalar_tensor_tensor(out=aqk[:, c, :], in0=aq[:], scalar=self.cst[:, cio + c:cio + c + 1], in1=self.C("cT4_" + nm), op0=ALU.mult, op1=ALU.mult),
                              reads=[AQ, self.CST], writes=[AQK])
                    if getattr(self, "d_level", 9) < 3:
                        continue
                    for c in ((0, 1) if d == 0 else (1, 0)):
                        r0 = 64 * c
                        dcol = (63 + 64 * c) if d == 0 else (64 * c)
                        for h in range(4):
                            fw.op("pe", lambda: nc.tensor.matmul(ops[:, c * 256 + h * 64:c * 256 + (h + 1) * 64], lhsT=qin[:], rhs=Sst[:, h * 64:(h + 1) * 64], start=True, stop=False),
                                  reads=[QIN, SST], writes=[OPS[c]])
                            fw.op("pe", lambda: nc.tensor.matmul(ops[:, c * 256 + h * 64:c * 256 + (h + 1) * 64], lhsT=aqk[:, c, h * 128:(h + 1) * 128],
                                                                 rhs=tm[:, h * 64:(h + 1) * 64], start=False, stop=True), reads=[AQK, TM], writes=[OPS[c]])
                        fw.op("pe", lambda: nc.tensor.matmul(sp, lhsT=kend[:, c, :], rhs=tm[:, 0:256], start=True, stop=True), reads=[KEND, TM], writes=[SP])
                        fw.op("dve", lambda: nc.vector.tensor_tensor(out=tmp[:], in0=sp, in1=self.C("bmask4"), op=ALU.mult), reads=[SP, self.CST], writes=[TMP])
                        fw.op("dve", lambda: nc.vector.scalar_tensor_tensor(out=Sst[:], in0=Sst[:], scalar=ebT[:, dcol:dcol + 1], in1=tmp[:], op0=ALU.mult, op1=ALU.add),
                              reads=[SST, EBT, TMP], writes=[SST])
                        fw.op("act", lambda: nc.scalar.copy(out=osb[r0:r0 + 64, :], in_=ops[r0:r0 + 64, c * 256:(c + 1) * 256]), reads=[OPS[c]], writes=[OSB])
                    if getattr(self, "d_level", 9) < 4:
                        continue
                    if d == 0:
                        fw.dma(S["ofwd"][0][tk, :], osb[:], reads=[OSB], writes=[S["ofwd"][1]])
                    else:
                        fw.dma(of[:], S["ofwd"][0][tk, :], reads=[S["ofwd"][1]], writes=[OF])
                        fw.op("pool", lambda: nc.gpsimd.tensor_tensor(out=osb[:], in0=osb[:], in1=of[:], op=ALU.add), reads=[OSB, OF], writes=[OSB])
                        self.gated_out(st, gt, osb, OSB, tm[:, 288:544], TM, "gla_on", 768, t)
            fw.barrier()

    def phaseE(self, l):
        fw, nc, I = self.fw, self.nc, self.I
        S = self.scr
        with ExitStack() as st:
            T_ = lambda n, s, d=F32: st.enter_context(nc.sbuf_tensor(self.uniq(n), s, d))
            P_ = lambda n, s, d=F32: st.enter_context(nc.psum_tensor(self.uniq(n), s, d))
            self.common_tiles(st)
            raw = [T_("dnraw%d" % i, [128, TPD]) for i in range(2)]
            RAW = fw.bufs(2)
            acc = [T_("dnacc%d" % i, [128, TPD]) for i in range(2)]
            ACC = fw.bufs(2)
            sq = T_("dnsq", [128, 512]); SQ = Buf()
            rs = T_("dnrs", [128, 512]); RS = Buf()
            nrm = [T_("dnnrm%d" % i, [128, 512]) for i in range(2)]
            NRM = fw.bufs(2)
            tk_ = [T_("dntk%d" % i, [128, 512]) for i in range(2)]
            TK = fw.bufs(2)
            ssp = P_("dnssp", [128, 512]); SSP = Buf()
            trp = [P_("dntrp%d" % i, [128, 512]) for i in range(2)]
            TRP = fw.bufs(2)
            fw.psum(SSP, TRP)
            L = TPD - 2 * DNP
            wo, _ = COLS["dn_w"]
            blocks = [(DN_CTX0, 256, 0)] + [(DN_LAT0 + 512 * k, 512, CTX + 512 * k) for k in range(8)]
            bi = 0
            for ci in range(6):
                b = ci % 2
                kind = ci // 2
                half = ci % 2
                fw.dma(raw[b][:], S["dnqkv"][0][ci * 128:(ci + 1) * 128, :], reads=[S["dnqkv"][1]], writes=[RAW[b]])
                for (a, e_) in ((0, DNP), (DN_CTX0 + CTX, DN_LAT0), (DN_LAT0 + SEQ, TPD)):
                    fw.op("pool", lambda: nc.gpsimd.memset(raw[b][:, a:e_], 0.0), writes=[RAW[b]])
                for j in range(5):
                    wcol = self.colp[:, wo + ci * 5 + j:wo + ci * 5 + j + 1]
                    if j == 0:
                        fw.op("dve", lambda: nc.vector.tensor_scalar(out=acc[b][:, DNP:DNP + L], in0=raw[b][:, j:j + L], scalar1=wcol, scalar2=None, op0=ALU.mult),
                              reads=[RAW[b], self.COLP], writes=[ACC[b]])
                    else:
                        fw.op("dve", lambda: nc.vector.scalar_tensor_tensor(out=acc[b][:, DNP:DNP + L], in0=raw[b][:, j:j + L], scalar=wcol, in1=acc[b][:, DNP:DNP + L],
                                                                            op0=ALU.mult, op1=ALU.add), reads=[RAW[b], self.COLP, ACC[b]], writes=[ACC[b]])
                fw.op("act", lambda: nc.scalar.activation(out=acc[b][:, DNP:DNP + L], in_=acc[b][:, DNP:DNP + L], func=AF.Silu), reads=[ACC[b]], writes=[ACC[b]])
                for (p0, n, tok0) in blocks:
                    nb = bi % 2
                    bi += 1
                    if kind < 2:
                        fw.op("act", lambda: nc.scalar.activation(out=sq[:, 0:n], in_=acc[b][:, p0:p0 + n], func=AF.Square), reads=[ACC[b]], writes=[SQ])
                        fw.op("pe", lambda: nc.tensor.matmul(ssp[:, 0:n], lhsT=self.C("blk64"), rhs=sq[:, 0:n], start=True, stop=True), reads=[self.CST, SQ], writes=[SSP])
                        fw.op("act", lambda: nc.scalar.activation(out=rs[:, 0:n], in_=ssp[:, 0:n], func=AF.Sqrt, bias=self.epsc[:, 0:1], scale=1.0), reads=[SSP, self.EPSC], writes=[RS])
                        fw.op("dve", lambda: nc.vector.reciprocal(out=rs[:, 0:n], in_=rs[:, 0:n]), reads=[RS], writes=[RS])
                        fw.op("dve", lambda: nc.vector.scalar_tensor_tensor(out=nrm[nb][:, 0:n], in0=acc[b][:, p0:p0 + n], scalar=(0.125 if kind == 0 else 1.0), in1=rs[:, 0:n],
                                                                            op0=ALU.mult, op1=ALU.mult), reads=[ACC[b], RS], writes=[NRM[nb]])
                        dst = S["dn_qT"] if kind == 0 else S["dn_kT"]
                        fw.dma(dst[0][half * 128:(half + 1) * 128, tok0:tok0 + n], nrm[nb][:, 0:n], reads=[NRM[nb]], writes=[dst[1]])
                        src, SRC, soff = nrm[nb], NRM[nb], 0
                    else:
                        src, SRC, soff = acc[b], ACC[b], p0
                    if kind >= 1:
                        for sub in range(n // 128):
                            fw.op("pe", lambda: nc.tensor.transpose(trp[nb][:, sub * 128:(sub + 1) * 128], src[:, soff + sub * 128:soff + (sub + 1) * 128], self.C("ident")),
                                  reads=[SRC, self.CST], writes=[TRP[nb]])
                        fw.op("act", lambda: nc.scalar.copy(out=tk_[nb][:, 0:n], in_=trp[nb][:, 0:n]), reads=[TRP[nb]], writes=[TK[nb]])
                        dst = S["dn_ktok"] if kind == 1 else S["dn_vtok"]
                        fw.dma(dst[0][tok0:tok0 + n, half * 128:(half + 1) * 128].rearrange("(a p) c -> p a c", p=128),
                               tk_[nb][:, 0:n].rearrange("p (a c) -> p a c", c=128), reads=[TK[nb]], writes=[dst[1]])
            fw.barrier()
        if getattr(self, "d_level", 9) < 2:
            return
        with ExitStack() as st:
            T_ = lambda n, s, d=F32: st.enter_context(nc.sbuf_tensor(self.uniq(n), s, d))
            P_ = lambda n, s, d=F32: st.enter_context(nc.psum_tensor(self.uniq(n), s, d))
            ident16, ID16 = self.common_tiles(st)
            gt = self.gated_tiles(st, ident16, ID16)
            identf = self.C("ident")
            cio, _ = COFF["chunkind"]
            negA = T_("negA", [128, 8]); NEGA = Buf()
            ao, _ = ROWS["a_log"]
            dto, _ = ROWS["dt_b"]
            fw.op("act", lambda: nc.scalar.activation(out=negA[:], in_=self.rowp[:, ao:ao + 8], func=AF.Exp), reads=[self.ROWP], writes=[NEGA])
            fw.op("dve", lambda: nc.vector.tensor_scalar(out=negA[:], in0=negA[:], scalar1=-1.0, scalar2=None, op0=ALU.mult), reads=[NEGA], writes=[NEGA])
            hm64 = self.C("chunkind")
            Sm = [T_("Sm%d" % i, [128, 128]) for i in range(2)]; SM = fw.bufs(2)
            qTp = [T_("qTp%d" % i, [128, 128]) for i in range(2)]; QTP = fw.bufs(2)
            kTp = [T_("kTp%d" % i, [128, 128]) for i in range(2)]; KTP = fw.bufs(2)
            kTm = [T_("kTm%d" % i, [128, 128]) for i in range(4)]; KTM = fw.bufs(4)
            ktok = T_("ktok", [128, 256]); KTOK = Buf()
            vtok = T_("vtok", [128, 256]); VTOK = Buf()
            bag = T_("bag", [128, 272]); BAG = Buf()
            sm = T_("sm", [128, 24]); SMB = Buf()
            G = T_("G", [128, 128]); GB_ = Buf()
            gch = T_("gch", [128, 4]); GCH = Buf()
            dm = T_("dm", [128, 128]); DM = Buf()
            dmT = T_("dmT", [128, 128]); DMT = Buf()
            dcs = T_("dcs", [128, 128]); DCS = Buf()
            e1b = T_("e1b", [128, 128]); E1B = Buf()
            gcs = T_("gcs", [128, 12]); GCS = Buf()
            Pm = [T_("Pm%d" % i, [128, 128]) for i in range(2)]; PM = fw.bufs(2)
            Qm = [T_("Qm%d" % i, [128, 128]) for i in range(2)]; QM = fw.bufs(2)
            Rm = [T_("Rm%d" % i, [128, 128]) for i in range(2)]; RM = fw.bufs(2)
            aqkc = T_("aqkc", [128, 2, 4, 128]); AQKC = Buf()
            vb = T_("vb", [128, 64]); VBB = Buf()
            kbgm = [T_("kbgm%d" % i, [128, 128]) for i in range(4)]; KBGM = fw.bufs(4)
            kend = T_("kend", [128, 2, 256]); KEND = Buf()
            qin = [T_("qin%d" % i, [128, 128]) for i in range(2)]; QIN = fw.bufs(2)
            gendp = [T_("gendp%d" % i, [128, 2]) for i in range(2)]; GENDP = fw.bufs(2)
            usb = T_("usb", [128, 256]); USB = Buf()
            wT = [T_("wT%d" % i, [128, 128]) for i in range(2)]; WT = fw.bufs(2)
            vnew = T_("vnew", [128, 256]); VNEW = Buf()
            tmp = T_("tmp", [128, 128]); TMP = Buf()
            osb = T_("osb", [128, 256]); OSB = Buf()
            of = T_("of", [128, 256]); OF = Buf()
            dd = P_("dd", [128, 512]); DD = Buf()
            kk = P_("kk", [128, 512]); KK = Buf()
            ch = P_("ch", [128, 512]); CH = Buf()
            rr = P_("rr", [128, 512]); RR = Buf()
            uw = P_("uw", [128, 512]); UW = Buf()
            wsp = P_("wsp", [128, 512]); WSP = Buf()
            oo = P_("oo", [128, 512]); OO = Buf()
            fw.psum(DD, KK, CH, RR, UW, WSP, OO)
            for h in range(4):
                fw.op("pool", lambda: nc.gpsimd.memset(kbgm[h][:], 0.0), writes=[KBGM[h]])
            onescol = self.C("ones")[:, 0:1]
            for d in range(2):
                nm = "f" if d == 0 else "r"
                order = list(range(NT)) if d == 0 else [1, 0] + list(range(NT - 1, 1, -1))
                if getattr(self, "nt_dbg", None):
                    order = [t for t in order if t < self.nt_dbg]
                for hp in range(2):
                    fw.op("pool", lambda: nc.gpsimd.memset(Sm[hp][:], 0.0), writes=[SM[hp]])
                for t in order:
                    tk = slice(t * 128, (t + 1) * 128)
                    for hp in range(2):
                        fw.dma(qTp[hp][:], S["dn_qT"][0][hp * 128:(hp + 1) * 128, tk], reads=[S["dn_qT"][1]], writes=[QTP[hp]])
                        fw.dma(kTp[hp][:], S["dn_kT"][0][hp * 128:(hp + 1) * 128, tk], reads=[S["dn_kT"][1]], writes=[KTP[hp]])
                    fw.dma(ktok[:], S["dn_ktok"][0][tk, :], reads=[S["dn_ktok"][1]], writes=[KTOK])
                    fw.dma(vtok[:], S["dn_vtok"][0][tk, :], reads=[S["dn_vtok"][1]], writes=[VTOK])
                    fw.dma(bag[:], S["tokmaj"][0][tk, 256:528], reads=[S["tokmaj"][1]], writes=[BAG])
                    fw.op("act", lambda: nc.scalar.activation(out=sm[:, 0:4], in_=bag[:, d * 4:d * 4 + 4], func=AF.Sigmoid), reads=[BAG], writes=[SMB])
                    fw.op("dve", lambda: nc.vector.tensor_scalar(out=sm[:, 4:8], in0=sm[:, 0:4], scalar1=-1.0, scalar2=None, op0=ALU.mult), reads=[SMB], writes=[SMB])
                    fw.op("dve", lambda: nc.vector.tensor_tensor(out=sm[:, 8:12], in0=bag[:, 8 + d * 4:12 + d * 4], in1=self.rowp[:, dto + d * 4:dto + d * 4 + 4], op=ALU.add),
                          reads=[BAG, self.ROWP], writes=[SMB])
                    fw.op("act", lambda: nc.scalar.activation(out=sm[:, 8:12], in_=sm[:, 8:12], func=AF.Exp), reads=[SMB], writes=[SMB])
                    fw.op("act", lambda: nc.scalar.activation(out=sm[:, 8:12], in_=sm[:, 8:12], func=AF.Ln, bias=onescol, scale=1.0), reads=[SMB, self.CST], writes=[SMB])
                    fw.op("dve", lambda: nc.vector.tensor_tensor(out=sm[:, 12:16], in0=sm[:, 8:12], in1=negA[:, d * 4:d * 4 + 4], op=ALU.mult), reads=[SMB, NEGA], writes=[SMB])
                    for h in range(4):
                        hp, hh = h // 2, h % 2
                        hs = slice(h * 64, (h + 1) * 64)
                        rows = slice(hh * 64, (hh + 1) * 64)
                        gcol = sm[:, 12 + h:13 + h]
                        fw.op("dve", lambda: nc.vector.tensor_scalar(out=kTm[h][:], in0=kTp[hp][:], scalar1=hm64[:, hh:hh + 1], scalar2=None, op0=ALU.mult),
                              reads=[KTP[hp], self.CST], writes=[KTM[h]])
                        fw.op("dve", lambda: nc.vector.tensor_scalar(out=G[:], in0=self.C("tri_" + nm), scalar1=gcol, scalar2=None, op0=ALU.mult), reads=[self.CST, SMB], writes=[GB_])
                        fw.op("dve", lambda: nc.vector.tensor_scalar(out=gch[:, 0:2], in0=self.C("chunkind"), scalar1=gcol, scalar2=None, op0=ALU.mult), reads=[self.CST, SMB], writes=[GCH])
                        fw.op("dve", lambda: nc.vector.tensor_scalar(out=gch[:, 2:4], in0=self.C("ones")[:, 0:2], scalar1=gcol, scalar2=None, op0=ALU.mult), reads=[self.CST, SMB], writes=[GCH])
                        mm = lambda out, lhsT, rhs, st_, sp_, rd: fw.op("pe", lambda: nc.tensor.matmul(out, lhsT=lhsT, rhs=rhs, start=st_, stop=sp_), reads=rd, writes=[DD])
                        mm(dd[:, 0:128], G[:], self.C("ones"), True, False, [GB_, self.CST])
                        mm(dd[:, 0:128], self.C("negones"), G[:], False, True, [GB_, self.CST])
                        mm(dd[:, 128:256], self.C("ones"), G[:], True, False, [GB_, self.CST])
                        mm(dd[:, 128:256], G[:], self.C("negones"), False, True, [GB_, self.CST])
                        mm(dd[:, 256:384], self.C("ones"), G[:], True, True, [GB_, self.CST])
                        mm(dd[:, 384:386], G[:], self.C("ones")[:, 0:2], True, True, [GB_, self.CST])
                        mm(dd[:, 386:388], self.C("blk"), gch[:, 2:4], True, True, [GCH, self.CST])
                        mm(dd[:, 388:390], self.C("ones"), gch[:, 0:2], True, True, [GCH, self.CST])
                        fw.op("dve", lambda: nc.vector.tensor_scalar(out=dm[:], in0=dd[:, 0:128], scalar1=0.0, scalar2=-40.0, op0=ALU.min, op1=ALU.max), reads=[DD], writes=[DM])
                        fw.op("pool", lambda: nc.gpsimd.tensor_tensor(out=dm[:], in0=dm[:], in1=self.C("negc_" + nm), op=ALU.add), reads=[DM, self.CST], writes=[DM])
                        fw.op("act", lambda: nc.scalar.activation(out=dm[:], in_=dm[:], func=AF.Exp), reads=[DM], writes=[DM])
                        fw.op("pool", lambda: nc.gpsimd.tensor_tensor(out=dcs[:], in0=dm[:], in1=self.C("strict_" + nm), op=ALU.mult), reads=[DM, self.CST], writes=[DCS])
                        fw.op("dve", lambda: nc.vector.tensor_scalar(out=dmT[:], in0=dd[:, 128:256], scalar1=0.0, scalar2=-40.0, op0=ALU.min, op1=ALU.max), reads=[DD], writes=[DMT])
                        fw.op("pool", lambda: nc.gpsimd.tensor_tensor(out=dmT[:], in0=dmT[:], in1=self.C("negcT_" + nm), op=ALU.add), reads=[DMT, self.CST], writes=[DMT])
                        fw.op("act", lambda: nc.scalar.activation(out=dmT[:], in_=dmT[:], func=AF.Exp), reads=[DMT], writes=[DMT])
                        fw.op("dve", lambda: nc.vector.tensor_scalar(out=e1b[:], in0=dd[:, 256:384], scalar1=-40.0, scalar2=None, op0=ALU.max), reads=[DD], writes=[E1B])
                        fw.op("act", lambda: nc.scalar.activation(out=e1b[:], in_=e1b[:], func=AF.Exp), reads=[E1B], writes=[E1B])
                        fw.op("act", lambda: nc.scalar.copy(out=gcs[:, 0:6], in_=dd[:, 384:390]), reads=[DD], writes=[GCS])
                        fw.op("dve", lambda: nc.vector.tensor_tensor(out=gcs[:, 6:7], in0=gcs[:, 2:3], in1=gcs[:, 0:1], op=ALU.subtract), reads=[GCS], writes=[GCS])
                        fw.op("dve", lambda: nc.vector.tensor_scalar(out=gcs[:, 0:7], in0=gcs[:, 0:7], scalar1=-40.0, scalar2=None, op0=ALU.max), reads=[GCS], writes=[GCS])
                        fw.op("act", lambda: nc.scalar.activation(out=gcs[:, 7:8], in_=gcs[:, 0:1], func=AF.Exp), reads=[GCS], writes=[GCS])
                        fw.op("act", lambda: nc.scalar.activation(out=gcs[:, 6:7], in_=gcs[:, 6:7], func=AF.Exp), reads=[GCS], writes=[GCS])
                        fw.op("act", lambda: nc.scalar.activation(out=gcs[:, 8:10], in_=gcs[:, 4:6], func=AF.Exp), reads=[GCS], writes=[GCS])
                        fw.op("pool", lambda: nc.gpsimd.tensor_copy(out=gendp[hp][rows, :], in_=gcs[rows, 8:10]), reads=[GCS], writes=[GENDP[hp]])
                        if getattr(self, "d_level", 9) < 3:
                            continue
                        fw.op("pe", lambda: nc.tensor.matmul(kk[:, 0:128], lhsT=kTm[h][:], rhs=kTp[hp][:], start=True, stop=True), reads=[KTM[h], KTP[hp]], writes=[KK])
                        fw.op("pe", lambda: nc.tensor.matmul(kk[:, 128:256], lhsT=kTm[h][:], rhs=qTp[hp][:], start=True, stop=True), reads=[KTM[h], QTP[hp]], writes=[KK])
                        fw.op("dve", lambda: nc.vector.scalar_tensor_tensor(out=Pm[0][:], in0=kk[:, 0:128], scalar=sm[:, 4 + h:5 + h], in1=dcs[:], op0=ALU.mult, op1=ALU.mult),
                              reads=[KK, SMB, DCS], writes=[PM[0]])
                        for c in range(2):
                            fw.op("dve", lambda: nc.vector.scalar_tensor_tensor(out=aqkc[:, c, h, :], in0=kk[:, 128:256], scalar=self.cst[:, cio + c:cio + c + 1], in1=dmT[:],
                                                                                op0=ALU.mult, op1=ALU.mult), reads=[KK, self.CST, DMT], writes=[AQKC])
                        if getattr(self, "d_level", 9) < 3.2:
                            continue
                        fw.op("pe", lambda: nc.tensor.transpose(rr[:, 0:128], Pm[0][:], identf), reads=[PM[0], self.CST], writes=[RR])
                        fw.op("act", lambda: nc.scalar.copy(out=Qm[0][:], in_=rr[:, 0:128]), reads=[RR], writes=[QM[0]])
                        fw.op("pool", lambda: nc.gpsimd.tensor_tensor(out=Rm[0][:], in0=Qm[0][:], in1=identf, op=ALU.add), reads=[QM[0], self.CST], writes=[RM[0]])
                        cur = 0
                        rc = 0
                        for s_ in range(1, 6 if getattr(self, "d_level", 9) >= 3.3 else 1):
                            nx = 1 - cur
                            fw.op("pe", lambda: nc.tensor.matmul(ch[:, 0:128], lhsT=Qm[cur][:], rhs=Pm[cur][:], start=True, stop=True), reads=[QM[cur], PM[cur]], writes=[CH])
                            if s_ < 5:
                                fw.op("pe", lambda: nc.tensor.matmul(ch[:, 128:256], lhsT=Pm[cur][:], rhs=Qm[cur][:], start=True, stop=True), reads=[QM[cur], PM[cur]], writes=[CH])
                            fw.op("act", lambda: nc.scalar.copy(out=Pm[nx][:], in_=ch[:, 0:128]), reads=[CH], writes=[PM[nx]])
                            if s_ < 5:
                                fw.op("dve", lambda: nc.vector.tensor_copy(out=Qm[nx][:], in_=ch[:, 128:256]), reads=[CH], writes=[QM[nx]])
                            fw.op("pe", lambda: nc.tensor.matmul(rr[:, 128:256], lhsT=Pm[nx][:], rhs=Rm[rc][:], start=True, stop=True), reads=[PM[nx], RM[rc]], writes=[RR])
                            fw.op("dve", lambda: nc.vector.tensor_tensor(out=Rm[1 - rc][:], in0=rr[:, 128:256], in1=Rm[rc][:], op=ALU.add), reads=[RR, RM[rc]], writes=[RM[1 - rc]])
                            rc = 1 - rc
                            cur = nx
                        R = Rm[rc]; RB = RM[rc]
                        if getattr(self, "d_level", 9) < 3.4:
                            continue
                        fw.op("pool", lambda: nc.gpsimd.tensor_scalar(out=vb[:], in0=vtok[:, hs], scalar1=sm[:, h:h + 1], scalar2=None, op0=ALU.mult), reads=[VTOK, SMB], writes=[VBB])
                        fw.op("dve", lambda: nc.vector.tensor_scalar(out=kbgm[h][:, rows], in0=ktok[:, hs], scalar1=sm[:, h:h + 1], scalar2=gcs[:, 7:8], op0=ALU.mult, op1=ALU.mult),
                              reads=[KTOK, SMB, GCS], writes=[KBGM[h]])
                        for c in range(2):
                            fw.op("dve", lambda: nc.vector.tensor_scalar(out=kend[:, c, hs], in0=ktok[:, hs], scalar1=self.cst[:, cio + c:cio + c + 1], scalar2=gcs[:, 6:7], op0=ALU.mult, op1=ALU.mult),
                                  reads=[KTOK, self.CST, GCS], writes=[KEND])
                        fw.op("pool", lambda: nc.gpsimd.tensor_tensor(out=qin[hp][rows, :], in0=qTp[hp][rows, :], in1=e1b[rows, :], op=ALU.mult), reads=[QTP[hp], E1B], writes=[QIN[hp]])
                        if getattr(self, "d_level", 9) < 3.5:
                            continue
                        fw.op("pe", lambda: nc.tensor.matmul(kk[:, 256 + h * 64:256 + (h + 1) * 64], lhsT=R[:], rhs=vb[:], start=True, stop=True), reads=[RB, VBB], writes=[KK])
                        fw.op("pe", lambda: nc.tensor.matmul(uw[:, 256 + hp * 128:256 + (hp + 1) * 128], lhsT=kbgm[h][:], rhs=R[:], start=(hh == 0), stop=(hh == 1)),
                              reads=[KBGM[h], RB], writes=[UW])
                    if getattr(self, "d_level", 9) < 4:
                        continue
                    fw.op("act", lambda: nc.scalar.copy(out=usb[:], in_=kk[:, 256:512]), reads=[KK], writes=[USB])
                    for hp in range(2):
                        fw.op("act", lambda: nc.scalar.copy(out=wT[hp][:], in_=uw[:, 256 + hp * 128:256 + (hp + 1) * 128]), reads=[UW], writes=[WT[hp]])
                    for c in ((0, 1) if d == 0 else (1, 0)):
                        r0 = 64 * c
                        for hp in range(2):
                            fw.op("pe", lambda: nc.tensor.matmul(wsp[:, hp * 128:(hp + 1) * 128], lhsT=wT[hp][:], rhs=Sm[hp][:], start=True, stop=True), reads=[WT[hp], SM[hp]], writes=[WSP])
                        fw.op("dve", lambda: nc.vector.tensor_tensor(out=vnew[:], in0=usb[:], in1=wsp[:, 0:256], op=ALU.subtract), reads=[USB, WSP], writes=[VNEW])
                        for h in range(4):
                            hp, hh = h // 2, h % 2
                            hs = slice(h * 64, (h + 1) * 64)
                            fw.op("pe", lambda: nc.tensor.matmul(oo[:, hs], lhsT=qin[hp][:], rhs=Sm[hp][:, hh * 64:(hh + 1) * 64], start=True, stop=False), reads=[QIN[hp], SM[hp]], writes=[OO])
                            fw.op("pe", lambda: nc.tensor.matmul(oo[:, hs], lhsT=aqkc[:, c, h, :], rhs=vnew[:, hs], start=False, stop=True), reads=[AQKC, VNEW], writes=[OO])
                        for hp in range(2):
                            fw.op("pe", lambda: nc.tensor.matmul(wsp[:, 256 + hp * 128:256 + (hp + 1) * 128], lhsT=kend[:, c, hp * 128:(hp + 1) * 128], rhs=vnew[:, hp * 128:(hp + 1) * 128],
                                                                 start=True, stop=True), reads=[KEND, VNEW], writes=[WSP])
                        for hp in range(2):
                            fw.op("dve", lambda: nc.vector.tensor_tensor(out=tmp[:], in0=wsp[:, 256 + hp * 128:256 + (hp + 1) * 128], in1=self.C("bmask2"), op=ALU.mult), reads=[WSP, self.CST], writes=[TMP])
                            fw.op("dve", lambda: nc.vector.scalar_tensor_tensor(out=Sm[hp][:], in0=Sm[hp][:], scalar=gendp[hp][:, c:c + 1], in1=tmp[:], op0=ALU.mult, op1=ALU.add),
                                  reads=[SM[hp], GENDP[hp], TMP], writes=[SM[hp]])
                        fw.op("act", lambda: nc.scalar.copy(out=osb[r0:r0 + 64, :], in_=oo[r0:r0 + 64, 0:256]), reads=[OO], writes=[OSB])
                    if d == 0:
                        fw.dma(S["ofwd"][0][tk, :], osb[:], reads=[OSB], writes=[S["ofwd"][1]])
                    else:
                        fw.dma(of[:], S["ofwd"][0][tk, :], reads=[S["ofwd"][1]], writes=[OF])
                        fw.op("pool", lambda: nc.gpsimd.tensor_tensor(out=osb[:], in0=osb[:], in1=of[:], op=ALU.add), reads=[OSB, OF], writes=[OSB])
                        self.gated_out(st, gt, osb, OSB, bag[:, 16:272], BAG, "dn_on", 512, t)
            fw.barrier()

    def load_cast(self, st, name, src_ap_fn, nk, ncols, blk, eng="pool"):
        fw, nc = self.fw, self.nc
        w = st.enter_context(nc.sbuf_tensor(self.uniq(name), [128, nk, ncols], BF16))
        W = Buf()
        with ExitStack() as st2:
            stg = [st2.enter_context(nc.sbuf_tensor(self.uniq(name + "s"), [128, nk, blk], F32)) for i in range(2)]
            STG = self.fw.bufs(2)
            nb = (ncols + blk - 1) // blk
            for cb in range(nb):
                b = cb % 2
                c0 = cb * blk
                n = min(blk, ncols - c0)
                fw.dma(stg[b][:, :, 0:n], src_ap_fn(c0, n), writes=[STG[b]])
                e = ("pool", "dve")[cb % 2] if eng == "both" else eng
                if e == "pool":
                    fw.op("pool", lambda: nc.gpsimd.tensor_copy(out=w[:, :, c0:c0 + n], in_=stg[b][:, :, 0:n]), reads=[STG[b]], writes=[W])
                else:
                    fw.op("dve", lambda: nc.vector.tensor_copy(out=w[:, :, c0:c0 + n], in_=stg[b][:, :, 0:n]), reads=[STG[b]], writes=[W])
            fw.barrier()
        return w, W

    def phaseF(self, l, xsrc):
        fw, nc, I = self.fw, self.nc, self.I
        S = self.scr
        last = (l == DEPTH - 1)
        with ExitStack() as st:
            T_ = lambda n, s, d=F32: st.enter_context(nc.sbuf_tensor(self.uniq(n), s, d))
            P_ = lambda n, s, d=F32: st.enter_context(nc.psum_tensor(self.uniq(n), s, d))
            ident16, ID16 = self.common_tiles(st)
            wout, WOUT = self.load_cast(st, "wout", lambda c0, n: I["w_out"][l, :, c0:c0 + n].rearrange("(k p) c -> p k c", p=128), 8, D, 256)
            zt = T_("zt", [128, 8, 128], BF16)
            ZT = Buf()
            fw.op("pool", lambda: nc.gpsimd.memset(zt[:], 0.0), writes=[ZT])
            h2 = S["h2T"][0].rearrange("(k p) t -> p k t", p=128)
            H2 = S["h2T"][1]
            fw.dma(h2[:, :, 0:1], zt[:, :, 0:1], reads=[ZT], writes=[H2], allow_slow_non_contiguous=True)
            fw.dma(h2[:, :, 257:259], zt[:, :, 0:2], reads=[ZT], writes=[H2], allow_slow_non_contiguous=True)
            fw.dma(h2[:, :, 4355:4355 + 128], zt[:, :, :], reads=[ZT], writes=[H2])
            fw.dma(h2[:, :, 4483:4483 + 109], zt[:, :, 0:109], reads=[ZT], writes=[H2])
            mx = [T_("mx%d" % i, [128, 8, 128], BF16) for i in range(2)]
            MX = fw.bufs(2)
            xt = [T_("xt%d" % i, [128, 1024]) for i in range(2)]
            XT = fw.bufs(2)
            x1 = [T_("x1t%d" % i, [128, 1024]) for i in range(2)]
            X1 = fw.bufs(2)
            tmp = [T_("tmp%d" % i, [128, 512]) for i in range(2)]
            TMP = fw.bufs(2)
            hT = [T_("h2t%d" % i, [128, 8, 128], BF16) for i in range(2)]
            HT = fw.bufs(2)
            pst = [P_("pst%d" % i, [128, 1024], BF16) for i in range(2)]
            PST = fw.bufs(2)
            mp = [P_("mp%d" % i, [128, 512]) for i in range(4)]
            MP = fw.bufs(4)
            fw.psum(PST, MP)
            mixv = S["mixT"][0].rearrange("(k p) t -> p k t", p=128)
            for t in range(NT):
                b = t % 2
                s = 1 if t < 2 else 0
                fw.dma(mx[b][:], mixv[:, :, t * 128:(t + 1) * 128], reads=[S["mixT"][1]], writes=[MX[b]])
                fw.dma(xt[b][:], xsrc[t * 128:(t + 1) * 128, :], writes=[XT[b]])
                for hc in range(2):
                    pi = (2 * t + hc) % 4
                    for k in range(8):
                        fw.op("pe", lambda: nc.tensor.matmul(mp[pi][:, :], lhsT=mx[b][:, k, :], rhs=wout[:, k, hc * 512:(hc + 1) * 512],
                                                             start=(k == 0), stop=(k == 7)), reads=[MX[b], WOUT], writes=[MP[pi]])
                    fw.op("dve", lambda: nc.vector.tensor_tensor(out=tmp[hc][:], in0=mp[pi][:, :], in1=self.gb[:, s, hc * 512:(hc + 1) * 512], op=ALU.mult),
                          reads=[MP[pi], self.GB], writes=[TMP[hc]])
                    fw.op("pool", lambda: nc.gpsimd.tensor_tensor(out=x1[b][:, hc * 512:(hc + 1) * 512], in0=xt[b][:, hc * 512:(hc + 1) * 512], in1=tmp[hc][:], op=ALU.add),
                          reads=[TMP[hc], XT[b]], writes=[X1[b]])
                fw.dma(S["x1"][0][t * 128:(t + 1) * 128, :], x1[b][:], reads=[X1[b]], writes=[S["x1"][1]])
                self.norm_mod_T(st, 1, x1[b][:], X1[b], s, hT[b], HT[b], 0, "F", pst[b], PST[b], ident16, ID16)
                pos = (F_CTX0 + t * 128) if t < 2 else (F_LAT0 + (t - 2) * 128)
                fw.dma(h2[:, :, pos:pos + 128], hT[b][:], reads=[HT[b]], writes=[H2])
            fw.barrier()
        for half in range(2):
            with ExitStack() as st:
                T_ = lambda n, s, d=F32: st.enter_context(nc.sbuf_tensor(self.uniq(n), s, d))
                P_ = lambda n, s, d=F32: st.enter_context(nc.psum_tensor(self.uniq(n), s, d))
                NJ = 11
                a0 = half * NJ * 128
                g0 = DFF + half * NJ * 128
                wua, WUA = self.load_cast(st, "wua", lambda c0, n: I["w_up"][l, :, a0 + c0:a0 + c0 + n].rearrange("(k p) c -> p k c", p=128), 8, NJ * 128, 352, eng="both")
                wug, WUG = self.load_cast(st, "wug", lambda c0, n: I["w_up"][l, :, g0 + c0:g0 + c0 + n].rearrange("(k p) c -> p k c", p=128), 8, NJ * 128, 352, eng="both")
                wdn, WDN = self.load_cast(st, "wdn", lambda c0, n: I["w_down"][l, half * NJ * 128:(half + 1) * NJ * 128, c0:c0 + n].rearrange("(k p) c -> p k c", p=128), NJ, D, 256, eng="both")
                xin_ap = S["x1"][0] if half == 0 else S["xa"][0]
                XIN = S["x1"][1] if half == 0 else S["xa"][1]
                hw = [T_("hw%d" % i, [128, 8, 512], BF16) for i in range(2)]
                HW = fw.bufs(2)
                zT = T_("zT", [128, NJ, 512], BF16)
                ZT = Buf()
                ya = [T_("ya%d" % i, [128, 512]) for i in range(2)]
                YA = fw.bufs(2)
                yg = [T_("yg%d" % i, [128, 512]) for i in range(2)]
                YG = fw.bufs(2)
                sgt = [T_("sgt%d" % i, [128, 512]) for i in range(2)]
                SGT = fw.bufs(2)
                xt = [T_("fxt%d" % i, [128, 1024]) for i in range(2)]
                XT = fw.bufs(2)
                tmp = [T_("ftmp%d" % i, [128, 512]) for i in range(2)]
                TMP = fw.bufs(2)
                up = [P_("up%d" % i, [128, 512]) for i in range(4)]
                UP = fw.bufs(4)
                dp = [P_("dp%d" % i, [128, 512]) for i in range(2)]
                DP = fw.bufs(2)
                fw.psum(UP, DP)
                h2 = S["h2T"][0].rearrange("(k p) t -> p k t", p=128)
                fo, _ = COLS["ffn_w"]
                it = 0
                for fb in range(NFB):
                    b = fb % 2
                    c0 = fb * FB
                    fw.dma(hw[b][:], h2[:, :, c0:c0 + 512], reads=[S["h2T"][1]], writes=[HW[b]])
                    for j in range(NJ):
                        jb = j % 2
                        ja = half * NJ + j
                        for (which, wt, WT, y, Y, cj) in ((0, wua, WUA, ya[jb], YA[jb], ja), (1, wug, WUG, yg[jb], YG[jb], 22 + ja)):
                            pi = it % 4
                            it += 1
                            for k in range(8):
                                fw.op("pe", lambda: nc.tensor.matmul(up[pi][:, :], lhsT=wt[:, k, j * 128:(j + 1) * 128], rhs=hw[b][:, k, :],
                                                                     start=(k == 0), stop=(k == 7)), reads=[WT, HW[b]], writes=[UP[pi]])
                            wc = lambda tap: self.colp[:, fo + cj * 3 + tap:fo + cj * 3 + tap + 1]
                            fw.op("act", lambda: nc.scalar.activation(out=y[:, 0:FB], in_=up[pi][:, 0:FB], func=AF.Identity, bias=0.0, scale=wc(0)),
                                  reads=[UP[pi], self.COLP], writes=[Y])
                            fw.op("dve", lambda: nc.vector.scalar_tensor_tensor(out=y[:, 0:FB], in0=up[pi][:, 1:FB + 1], scalar=wc(1), in1=y[:, 0:FB], op0=ALU.mult, op1=ALU.add),
                                  reads=[UP[pi], self.COLP, Y], writes=[Y])
                            fw.op("dve", lambda: nc.vector.scalar_tensor_tensor(out=y[:, 0:FB], in0=up[pi][:, 2:FB + 2], scalar=wc(2), in1=y[:, 0:FB], op0=ALU.mult, op1=ALU.add),
                                  reads=[UP[pi], self.COLP, Y], writes=[Y])
                        fw.op("act", lambda: nc.scalar.activation(out=sgt[jb][:, 0:FB], in_=yg[jb][:, 0:FB], func=AF.Silu), reads=[YG[jb]], writes=[SGT[jb]])
                        fw.op("pool", lambda: nc.gpsimd.tensor_tensor(out=zT[:, j, 0:FB], in0=sgt[jb][:, 0:FB], in1=ya[jb][:, 0:FB], op=ALU.mult),
                              reads=[SGT[jb], YA[jb]], writes=[ZT])
                    for sub in range(4):
                        q0 = fb * FB + 1 + sub * 128
                        n = min(128, FB - sub * 128)
                        segs = []
                        for (p0, p1, t0) in ((F_CTX0, F_CTX0 + CTX, 0), (F_LAT0, F_LAT0 + SEQ, CTX)):
                            lo = max(q0, p0)
                            hi = min(q0 + n, p1)
                            if hi > lo:
                                segs.append((lo - q0, hi - lo, t0 + lo - p0))
                        if not segs:
                            continue
                        s = 1 if q0 < F_CTX0 + CTX else 0
                        xb = (fb * 4 + sub) % 2
                        for (r0, nr, tok0) in segs:
                            fw.dma(xt[xb][r0:r0 + nr, :], xin_ap[tok0:tok0 + nr, :], reads=[XIN], writes=[XT[xb]])
                        for hc in range(2):
                            for j in range(NJ):
                                fw.op("pe", lambda: nc.tensor.matmul(dp[hc][0:n, :], lhsT=zT[:, j, sub * 128:sub * 128 + n], rhs=wdn[:, j, hc * 512:(hc + 1) * 512],
                                                                     start=(j == 0), stop=(j == NJ - 1)), reads=[ZT, WDN], writes=[DP[hc]])
                            fw.op("dve", lambda: nc.vector.tensor_tensor(out=tmp[hc][0:n, :], in0=dp[hc][0:n, :], in1=self.gb[0:n, 2 + s, hc * 512:(hc + 1) * 512], op=ALU.mult),
                                  reads=[DP[hc], self.GB], writes=[TMP[hc]])
                            fw.op("pool", lambda: nc.gpsimd.tensor_tensor(out=xt[xb][0:n, hc * 512:(hc + 1) * 512], in0=xt[xb][0:n, hc * 512:(hc + 1) * 512], in1=tmp[hc][0:n, :], op=ALU.add),
                                  reads=[TMP[hc], XT[xb]], writes=[XT[xb]])
                        for (r0, nr, tok0) in segs:
                            if half == 0:
                                fw.dma(S["xa"][0][tok0:tok0 + nr, :], xt[xb][r0:r0 + nr, :], reads=[XT[xb]], writes=[S["xa"][1]])
                            elif not last:
                                fw.dma(S["xres"][0][tok0:tok0 + nr, :], xt[xb][r0:r0 + nr, :], reads=[XT[xb]], writes=[S["xres"][1]])
                            elif tok0 >= CTX:
                                fw.dma(self.out[tok0 - CTX:tok0 - CTX + nr, :], xt[xb][r0:r0 + nr, :], reads=[XT[xb]], writes=[self.OUTB])
                            if self.dbg and half == 1 and last:
                                fw.dma(S["xres"][0][tok0:tok0 + nr, :], xt[xb][r0:r0 + nr, :], reads=[XT[xb]], writes=[S["xres"][1]])
                fw.barrier()


def _host_inputs(inp):
    rope = _rope_tables()
    per_core = []
    packed = []
    for b in range(8):
        cvec = np.stack([inp["c"][b], inp["c_ctx"]], axis=0).astype(np.float32)
        cols, rows, w2p = [], [], []
        for l in range(DEPTH):
            c_, r_, w_ = _pack_params(inp, l, cvec)
            cols.append(c_); rows.append(r_); w2p.append(w_)
        xin = np.concatenate([inp["ctx"][b], inp["x"][b]], axis=0).astype(np.float32)
        per_core.append({
            "xin": np.ascontiguousarray(xin), "consts": CONSTS, "rope": rope,
            "cols": np.stack(cols), "rows": np.stack(rows), "w2p": np.stack(w2p),
            "w_mod": inp["w_mod"], "w_in": inp["w_in"], "w_out": inp["w_out"],
            "w_up": inp["ffn_w_up"], "w_down": inp["ffn_w_down"],
        })
    return per_core


def kernel(**inputs):
    inp = {k: np.asarray(v) for k, v in inputs.items()}
    bld = Builder(phases=PHASES)
    nc = bld.build()
    in_maps = _host_inputs(inp)
    res = run_bass_kernel_spmd(nc, in_maps, core_ids=list(range(8)))
    out = np.stack([np.asarray(r["out"]) for r in res.results], axis=0)
    return out.astype(np.float32)
```

```python
import math
import numpy as np
from contextlib import ExitStack
import concourse.bass as bass
import concourse.mybir as mybir
from concourse.bass_utils import run_bass_kernel_spmd

F32 = mybir.dt.float32
BF16 = mybir.dt.bfloat16
ALU = mybir.AluOpType
AF = mybir.ActivationFunctionType
AX = mybir.AxisListType

D = 1024
SEQ = 4096
CTX = 256
T = SEQ + CTX
NT = T // 128
DEPTH = 2
INC = 3120
DFF = 2816
EPS = 1e-6
CMP = 15
TP = CMP + CTX + 2 * CMP + SEQ + CMP
CM_CTX0 = CMP
CM_LAT0 = CMP + CTX + 2 * CMP
DNP = 2
TPD = DNP + CTX + 2 * DNP + SEQ + DNP
DN_CTX0 = DNP
DN_LAT0 = DNP + CTX + 2 * DNP
NTM = 1072
FB = 510
NFB = 9
FPAD = 1 + NFB * FB + 1
F_CTX0 = 1
F_LAT0 = 1 + CTX + 2


class Buf:
    __slots__ = ("name", "lw", "rd", "ps")

    def __init__(self, name=""):
        self.name = name
        self.lw = None
        self.rd = []
        self.ps = False


class FW:
    NDMA = 40

    def __init__(self, nc, stack):
        self.nc = nc
        self.eng = {"pe": nc.tensor, "act": nc.scalar, "dve": nc.vector,
                    "pool": nc.gpsimd, "sp": nc.sync}
        self.sem = {}
        self.cnt = {}
        for e in self.eng:
            self.sem[e] = stack.enter_context(nc.semaphore("s_" + e))
            self.cnt[e] = 0
        self.dsem = [stack.enter_context(nc.semaphore("d%d" % i)) for i in range(self.NDMA)]
        self.dcnt = [0] * self.NDMA
        self.dnext = 0
        self.seen = {e: {} for e in self.eng}

    def buf(self, name=""):
        return Buf(name)

    def bufs(self, n, name=""):
        return [Buf(name + str(i)) for i in range(n)]

    def _semobj(self, key):
        return self.sem[key] if isinstance(key, str) else self.dsem[key]

    def _wait(self, e, ev):
        if ev is None:
            return
        key, val = ev
        if key == "pe" and e == "pe":
            return
        if key == e and val <= self.cnt[e] - 6:
            return
        if self.seen[e].get(key, 0) >= val:
            return
        self.seen[e][key] = val
        self.eng[e].wait_ge(self._semobj(key), val)

    def psum(self, *bufs):
        for b in bufs:
            if isinstance(b, (list, tuple)):
                self.psum(*b)
            else:
                b.ps = True

    def _deps(self, e, reads, writes):
        for b in reads:
            self._wait(e, b.lw)
            if b.ps:
                for ev in b.rd:
                    if ev[0] != e:
                        self._wait(e, ev)
        for b in writes:
            self._wait(e, b.lw)
            for ev in b.rd:
                self._wait(e, ev)

    def _commit(self, ev, reads, writes):
        for b in reads:
            b.rd.append(ev)
            if len(b.rd) > 48:
                best = {}
                for k, v in b.rd:
                    if best.get(k, 0) < v:
                        best[k] = v
                b.rd = list(best.items())
        for b in writes:
            b.lw = ev
            b.rd = []

    def op(self, e, fn, reads=(), writes=()):
        self._deps(e, reads, writes)
        ins = fn()
        self.cnt[e] += 1
        ins.then_inc(self.sem[e], 1)
        self._commit((e, self.cnt[e]), reads, writes)
        return ins

    def dma(self, out, in_, reads=(), writes=(), q="sp", **kw):
        k = self.dnext
        self.dnext = (self.dnext + 1) % self.NDMA
        if self.dcnt[k] > 0:
            self._wait(q, (k, self.dcnt[k]))
        self._deps(q, reads, writes)
        ins = self.eng[q].dma_start(out=out, in_=in_, **kw)
        self.dcnt[k] += 16
        ins.then_inc(self.dsem[k], 16)
        self._commit((k, self.dcnt[k]), reads, writes)
        return ins

    def barrier(self):
        for e in self.eng:
            for f in self.eng:
                if f != e and self.cnt[f] > 0:
                    self._wait(e, (f, self.cnt[f]))
            if self.cnt[e] > 0 and e != "pe" and self.seen[e].get(e, 0) < self.cnt[e]:
                self.seen[e][e] = self.cnt[e]
                self.eng[e].wait_ge(self.sem[e], self.cnt[e])
            for k in range(self.NDMA):
                if self.dcnt[k] > 0:
                    self._wait(e, (k, self.dcnt[k]))


def _consts():
    c = {}
    idx = np.arange(128)
    same = (idx[:, None] // 64) == (idx[None, :] // 64)
    le = idx[:, None] <= idx[None, :]
    ge = idx[:, None] >= idx[None, :]
    c["ident"] = np.eye(128)
    c["ones"] = np.ones((128, 128))
    c["negones"] = -np.ones((128, 128))
    c["blk"] = same.astype(np.float64)
    for d, (a_le, name) in enumerate(((le, "f"), (ge, "r"))):
        tri = (same & a_le).astype(np.float64)
        c["tri_" + name] = tri
        c["tris_" + name] = -tri / 16.0
        causal = tri.T
        c["negc_" + name] = np.where(causal > 0, 0.0, -30000.0)
        c["negcT_" + name] = np.where(tri > 0, 0.0, -30000.0)
        c["strict_" + name] = causal * (1 - np.eye(128))
        c["cT4_" + name] = np.tile(tri, (1, 4))
    c["blks"] = -same.astype(np.float64) / 16.0
    c["blk32"] = ((idx[:, None] // 32) == (idx[None, :] // 32)).astype(np.float64) / 32.0
    c["blk64"] = same.astype(np.float64)
    c["div256"] = np.ones((128, 128)) / 256.0
    perm = np.zeros((128, 128))
    for m in range(128):
        k = m + 16 if (m % 32) < 16 else m - 16
        perm[k, m] = 1.0
    c["perm"] = perm
    c["bmask4"] = ((idx[:, None] // 32) == (np.arange(256)[None, :] // 64)).astype(np.float64)
    c["bmask2"] = same.astype(np.float64)
    c["hm32"] = ((idx[:, None] // 32) == np.arange(4)[None, :]).astype(np.float64)
    ci = np.zeros((128, 2)); ci[:64, 0] = 1; ci[64:, 1] = 1
    c["chunkind"] = ci
    sel = np.zeros((128, 256)); sel[0, :128] = 1; sel[1, 128:] = 1
    c["sel"] = sel
    names = list(c.keys())
    offs = {}
    o = 0
    for n in names:
        offs[n] = (o, c[n].shape[1])
        o += c[n].shape[1]
    arr = np.concatenate([c[n] for n in names], axis=1).astype(np.float32)
    return arr, offs


def _rope_tables():
    rows = SEQ // 64
    row = np.repeat(np.arange(rows, dtype=np.float32), 64)
    col = np.tile(np.arange(64, dtype=np.float32), rows)
    nf = 8
    inv = (np.float32(10000.0) ** (-np.arange(nf, dtype=np.float32) / nf)).astype(np.float32)
    ang = np.concatenate([row[:, None] * inv, col[:, None] * inv], axis=-1).astype(np.float32)
    cos = np.cos(ang).astype(np.float32)
    sin = np.sin(ang).astype(np.float32)
    p = np.arange(128)
    ct = cos[:, p % 16].T
    st = sin[:, p % 16].T * np.where((p % 32) < 16, -1.0, 1.0)[:, None]
    return np.ascontiguousarray(np.concatenate([ct, st], axis=1).astype(np.float32))


CONSTS, COFF = _consts()
NCONST = CONSTS.shape[1]

COLS = {}
_o = 0
for _n, _w in (("b_mod", 48), ("n1g", 8), ("n2g", 8), ("cm_w", 62), ("cm_b", 2), ("cm_lg", 2), ("cm_lb", 2),
               ("qg", 1), ("kg", 1), ("dn_w", 30), ("ffn_w", 132), ("gla_b2", 2), ("ccol", 16)):
    COLS[_n] = (_o, _w)
    _o += _w
NCOL = _o
ROWS = {}
_o = 0
for _n, _w in (("subln", 64), ("dn_on", 64), ("gla_on", 64), ("a_log", 8), ("dt_b", 8), ("lam", 128),
               ("b_g1", 1024), ("b_g2", 1024), ("gla_b2r", 256)):
    ROWS[_n] = (_o, _w)
    _o += _w
NROW = _o


def _pack_params(inp, l, cvec):
    cols = np.zeros((128, NCOL), np.float32)

    def put(name, a):
        o, w = COLS[name]
        assert a.shape == (128, w), (name, a.shape)
        cols[:, o:o + w] = a

    put("b_mod", inp["b_mod"][l].reshape(48, 128).T)
    put("n1g", inp["norm1_g"][l].reshape(8, 128).T)
    put("n2g", inp["norm2_g"][l].reshape(8, 128).T)
    put("cm_w", inp["cm_conv_w"][l].reshape(31, 2, 128).transpose(2, 1, 0).reshape(128, 62))
    put("cm_b", inp["cm_conv_b"][l].reshape(2, 128).T)
    put("cm_lg", inp["cm_ln_g"][l].reshape(2, 128).T)
    put("cm_lb", inp["cm_ln_b"][l].reshape(2, 128).T)
    put("qg", np.tile(inp["da_qnorm_g"][l], 4)[:, None])
    put("kg", np.tile(inp["da_knorm_g"][l], 4)[:, None])
    put("dn_w", inp["dn_conv_w"][l].reshape(5, 6, 128).transpose(2, 1, 0).reshape(128, 30))
    put("ffn_w", inp["ffn_conv_w"][l].reshape(3, 44, 128).transpose(2, 1, 0).reshape(128, 132))
    put("gla_b2", inp["gla_b2"][l].T)
    put("ccol", cvec.reshape(2, 8, 128).transpose(2, 1, 0).reshape(128, 16))
    rows = np.zeros((1, NROW), np.float32)

    def putr(name, a):
        o, w = ROWS[name]
        rows[0, o:o + w] = a.reshape(-1)

    putr("subln", inp["da_subln_g"][l])
    putr("dn_on", inp["dn_onorm_g"][l])
    putr("gla_on", inp["gla_onorm_g"][l])
    putr("a_log", inp["dn_a_log"][l])
    putr("dt_b", inp["dn_dt_bias"][l])
    putr("lam", inp["da_lambda"][l])
    putr("b_g1", inp["b_mod"][l][2048:3072])
    putr("b_g2", inp["b_mod"][l][5120:6144])
    putr("gla_b2r", inp["gla_b2"][l])
    w2p = np.zeros((2, 32, 128), np.float32)
    w2p[0, 0:16] = inp["gla_w2"][l][0]
    w2p[1, 16:32] = inp["gla_w2"][l][1]
    return cols, rows, w2p


PHASES = None


class Builder:
    def __init__(self, dbg=False, phases=None):
        self.dbg = dbg
        self.phases = phases
        self.nc = bass.Bass("TRN2", target_bir_lowering=False)
        self.scr = {}

    def din(self, name, shape, dt=F32):
        return self.nc.dram_tensor(name, list(shape), dt, kind="ExternalInput").ap()

    def dscr(self, name, shape, dt=F32):
        kind = "ExternalOutput" if self.dbg else "Internal"
        if name in getattr(self, "dbg_inputs", ()):
            kind = "ExternalInput"
        t = self.nc.dram_tensor(name, list(shape), dt, kind=kind).ap()
        self.scr[name] = (t, Buf(name))
        return t

    def uniq(self, n):
        self._u = getattr(self, "_u", 0) + 1
        return "%s_%d" % (n, self._u)

    def want(self, ph):
        return self.phases is None or ph in self.phases

    def build(self):
        nc = self.nc
        I = {}
        I["xin"] = self.din("xin", [T, D])
        I["consts"] = self.din("consts", [128, NCONST])
        I["rope"] = self.din("rope", [128, 2 * SEQ])
        I["cols"] = self.din("cols", [DEPTH, 128, NCOL])
        I["rows"] = self.din("rows", [DEPTH, 1, NROW])
        I["w2p"] = self.din("w2p", [DEPTH, 2, 32, 128])
        I["w_mod"] = self.din("w_mod", [DEPTH, D, 6 * D])
        I["w_in"] = self.din("w_in", [DEPTH, D, INC])
        I["w_out"] = self.din("w_out", [DEPTH, D, D])
        I["w_up"] = self.din("w_up", [DEPTH, D, 2 * DFF])
        I["w_down"] = self.din("w_down", [DEPTH, DFF, D])
        self.I = I
        self.out = nc.dram_tensor("out", [SEQ, D], F32, kind="ExternalOutput").ap()
        self.OUTB = Buf("out")
        self.dscr("xres", [T, D])
        self.dscr("x1", [T, D])
        self.dscr("xa", [T, D])
        self.dscr("cmY", [256, TP])
        self.dscr("qT", [256, T])
        self.dscr("kT", [256, T])
        self.dscr("dnqkv", [768, TPD])
        self.dscr("glaqT", [128, T])
        self.dscr("glakT", [128, T])
        self.dscr("glalrT", [32, T])
        self.dscr("tokmaj", [T, NTM])
        self.dscr("mixT", [D, T], BF16)
        self.dscr("dn_qT", [256, T])
        self.dscr("dn_kT", [256, T])
        self.dscr("dn_ktok", [T, 256])
        self.dscr("dn_vtok", [T, 256])
        self.dscr("ofwd", [T, 256])
        self.dscr("h2T", [8 * 128, FPAD], BF16)
        with ExitStack() as top:
            self.fw = FW(nc, top)
            fw = self.fw
            self.cst = top.enter_context(nc.sbuf_tensor("cst", [128, NCONST], F32))
            self.CST = Buf("cst")
            fw.dma(self.cst[:], I["consts"][:, :], writes=[self.CST])
            self.colp = top.enter_context(nc.sbuf_tensor("colp", [128, NCOL], F32))
            self.COLP = Buf("colp")
            self.rowp = top.enter_context(nc.sbuf_tensor("rowp", [128, NROW], F32))
            self.ROWP = Buf("rowp")
            self.modc = top.enter_context(nc.sbuf_tensor("modc", [128, 96], F32))
            self.MODC = Buf("modc")
            self.a1 = top.enter_context(nc.sbuf_tensor("a1", [128, 32], F32))
            self.A1 = Buf("a1")
            self.gb = top.enter_context(nc.sbuf_tensor("gb", [128, 4, D], F32))
            self.GB = Buf("gb")
            for l in range(getattr(self, 'depth_run', DEPTH)):
                self.layer(l)
            fw.barrier()
        return nc

    def C(self, name, rows=128):
        o, w = COFF[name]
        return self.cst[0:rows, o:o + w]

    def col(self, name, j=0, n=1):
        o, w = COLS[name]
        return self.colp[:, o + j:o + j + n]

    def row(self, name, j=0, n=None):
        o, w = ROWS[name]
        if n is None:
            n = w
        return self.rowp[:, o + j:o + j + n]

    def layer(self, l):
        fw, nc, I = self.fw, self.nc, self.I
        fw.barrier()
        fw.dma(self.colp[:], I["cols"][l, :, :], writes=[self.COLP])
        fw.dma(self.rowp[:], I["rows"][l, :, :].partition_broadcast(128), writes=[self.ROWP])
        self.phase0(l)
        xsrc = I["xin"] if l == 0 else self.scr["xres"][0]
        if self.want("A"):
            self.phaseA(l, xsrc)
        if self.want("B"):
            self.phaseB(l)
        if self.want("C"):
            self.phaseC(l)
        if not (self.want("D") and self.want("E")):
            self.zero_mix(l)
        if self.want("D"):
            self.phaseD(l)
        if self.want("E"):
            self.phaseE(l)
        if self.want("F"):
            self.phaseF(l, xsrc)

    def phase0(self, l):
        fw, nc, I = self.fw, self.nc, self.I
        with ExitStack() as st:
            T_ = lambda n, s, d=F32: st.enter_context(nc.sbuf_tensor(self.uniq(n), s, d))
            P_ = lambda n, s, d=F32: st.enter_context(nc.psum_tensor(self.uniq(n), s, d))
            cact = T_("cact", [128, 16])
            CACT = Buf()
            wm = [T_("wm%d" % i, [128, 8, 1024]) for i in range(2)]
            WM = fw.bufs(2)
            mps = P_("mps", [128, 96])
            MPS = Buf()
            rps = [P_("rps%d" % i, [128, 512]) for i in range(2)]
            RPS = fw.bufs(2)
            grow = T_("grow", [2, 2, 1024])
            GROW = Buf()
            gps = [P_("gps%d" % i, [128, 512]) for i in range(2)]
            GPS = fw.bufs(2)
            fw.psum(MPS, RPS, GPS)
            o, w = COLS["ccol"]
            fw.op("act", lambda: nc.scalar.activation(out=cact[:], in_=self.colp[:, o:o + 16], func=AF.Silu),
                  reads=[self.COLP], writes=[CACT])
            ob, _ = COLS["b_mod"]
            for comp in range(6):
                b = comp % 2
                fw.dma(wm[b][:], I["w_mod"][l, :, comp * 1024:(comp + 1) * 1024].rearrange("(k p) c -> p k c", p=128),
                       writes=[WM[b]])
                for jj in range(8):
                    j = comp * 8 + jj
                    for k in range(8):
                        fw.op("pe", lambda: nc.tensor.matmul(mps[:, 2 * j:2 * j + 2], lhsT=wm[b][:, k, jj * 128:(jj + 1) * 128],
                                                             rhs=cact[:, 2 * k:2 * k + 2], start=(k == 0), stop=(k == 7)),
                              reads=[WM[b], CACT], writes=[MPS])
                if comp in (2, 5):
                    which = 0 if comp == 2 else 1
                    bname = "b_g1" if comp == 2 else "b_g2"
                    for hc in range(2):
                        for k in range(8):
                            fw.op("pe", lambda: nc.tensor.matmul(rps[hc][0:2, :], lhsT=cact[:, 2 * k:2 * k + 2],
                                                                 rhs=wm[b][:, k, hc * 512:(hc + 1) * 512],
                                                                 start=(k == 0), stop=False),
                                  reads=[WM[b], CACT], writes=[RPS[hc]])
                        ro, _ = ROWS[bname]
                        fw.op("pe", lambda: nc.tensor.matmul(rps[hc][0:2, :], lhsT=self.C("ones")[0:1, 0:2],
                                                             rhs=self.rowp[0:1, ro + hc * 512:ro + (hc + 1) * 512],
                                                             start=False, stop=True),
                              reads=[self.ROWP, self.CST], writes=[RPS[hc]])
                        fw.op("dve", lambda: nc.vector.tensor_copy(out=grow[:, which, hc * 512:(hc + 1) * 512], in_=rps[hc][0:2, :]),
                              reads=[RPS[hc]], writes=[GROW])
            for s in range(2):
                fw.op("dve", lambda: nc.vector.tensor_tensor(out=self.modc[:].rearrange("p (j s) -> p j s", s=2)[:, :, s],
                                                             in0=mps[:].rearrange("p (j s) -> p j s", s=2)[:, :, s],
                                                             in1=self.colp[:, ob:ob + 48], op=ALU.add),
                      reads=[MPS, self.COLP], writes=[self.MODC])
            for which, (gname, scbase) in enumerate((("n1g", 8), ("n2g", 32))):
                go, _ = COLS[gname]
                for s in range(2):
                    fw.op("dve", lambda: nc.vector.scalar_tensor_tensor(
                        out=self.a1[:, which * 16:(which + 1) * 16].rearrange("p (k s) -> p k s", s=2)[:, :, s],
                        in0=self.modc[:].rearrange("p (j s) -> p j s", s=2)[:, scbase:scbase + 8, s],
                        scalar=1.0, in1=self.colp[:, go:go + 8], op0=ALU.add, op1=ALU.mult),
                        reads=[self.MODC, self.COLP], writes=[self.A1])
            so, _ = COFF["sel"]
            for which in range(2):
                for s in range(2):
                    for hc in range(2):
                        pi = hc
                        fw.op("pe", lambda: nc.tensor.matmul(gps[pi][:, :], lhsT=self.cst[0:2, so + s * 128:so + (s + 1) * 128],
                                                             rhs=grow[:, which, hc * 512:(hc + 1) * 512], start=True, stop=True),
                              reads=[GROW, self.CST], writes=[GPS[pi]])
                        fw.op("act", lambda: nc.scalar.copy(out=self.gb[:, which * 2 + s, hc * 512:(hc + 1) * 512], in_=gps[pi][:, :]),
                              reads=[GPS[pi]], writes=[self.GB])
            fw.barrier()

    def shift(self, k, s):
        raise NotImplementedError

    def norm_mod_T(self, st, which, xt, XT, s, hT, HT, hoff, tag, pst, PST, ident16, ID16):
        fw, nc = self.fw, self.nc
        sq, SQ, ssq, SSQ, xs, XS = self._nm_tmp
        fw.op("act", lambda: nc.scalar.activation(out=sq[:], in_=xt, func=AF.Square, accum_out=ssq[:, 0:1]),
              reads=[XT], writes=[SQ, SSQ])
        fw.op("act", lambda: nc.scalar.activation(out=ssq[:, 1:2], in_=ssq[:, 0:1], func=AF.Sqrt, bias=self.epsc[:, 0:1], scale=1.0 / D),
              reads=[SSQ], writes=[SSQ])
        fw.op("dve", lambda: nc.vector.reciprocal(out=ssq[:, 2:3], in_=ssq[:, 1:2]), reads=[SSQ], writes=[SSQ])
        fw.op("dve", lambda: nc.vector.tensor_scalar(out=xs[:], in0=xt, scalar1=ssq[:, 2:3], scalar2=None, op0=ALU.mult),
              reads=[XT, SSQ], writes=[XS])
        for k in range(8):
            fw.op("pe", lambda: nc.tensor.transpose(pst[:, k * 128:(k + 1) * 128], xs[:, k * 128:(k + 1) * 128], ident16[:]),
                  reads=[XS, ID16], writes=[PST])
        shbase = 0 if which == 0 else 24
        for k in range(8):
            eng = "act" if k % 2 == 0 else "dve"
            acol = self.a1[:, which * 16 + 2 * k + s:which * 16 + 2 * k + s + 1]
            shcol = self.modc[:, 2 * (shbase + k) + s:2 * (shbase + k) + s + 1]
            if eng == "act":
                fw.op("act", lambda: nc.scalar.activation(out=hT[:, k, hoff:hoff + 128], in_=pst[:, k * 128:(k + 1) * 128],
                                                          func=AF.Identity, bias=shcol, scale=acol),
                      reads=[PST, self.A1, self.MODC], writes=[HT])
            else:
                fw.op("dve", lambda: nc.vector.tensor_scalar(out=hT[:, k, hoff:hoff + 128], in0=pst[:, k * 128:(k + 1) * 128],
                                                             scalar1=acol, scalar2=shcol, op0=ALU.mult, op1=ALU.add),
                      reads=[PST, self.A1, self.MODC], writes=[HT])

    def common_tiles(self, st):
        fw, nc = self.fw, self.nc
        T_ = lambda n, s, d=F32: st.enter_context(nc.sbuf_tensor(self.uniq(n), s, d))
        self.epsc = T_("epsc", [128, 1])
        self.EPSC = Buf()
        fw.op("pool", lambda: nc.gpsimd.memset(self.epsc[:], EPS), writes=[self.EPSC])
        ident16 = T_("ident16", [128, 128], BF16)
        ID16 = Buf()
        fw.op("dve", lambda: nc.vector.tensor_copy(out=ident16[:], in_=self.C("ident")), reads=[self.CST], writes=[ID16])
        sq = T_("nm_sq", [128, 1024], BF16)
        ssq = T_("nm_ssq", [128, 4])
        xs = T_("nm_xs", [128, 1024], BF16)
        self._nm_tmp = (sq, Buf(), ssq, Buf(), xs, Buf())
        return ident16, ID16

    def phaseA(self, l, xsrc):
        fw, nc, I = self.fw, self.nc, self.I
        S = self.scr
        with ExitStack() as st:
            T_ = lambda n, s, d=F32: st.enter_context(nc.sbuf_tensor(self.uniq(n), s, d))
            P_ = lambda n, s, d=F32: st.enter_context(nc.psum_tensor(self.uniq(n), s, d))
            ident16, ID16 = self.common_tiles(st)
            win = T_("win", [128, 8, INC], BF16)
            WIN = Buf()
            stg = [T_("wstg%d" % i, [128, 8, 390]) for i in range(2)]
            STG = fw.bufs(2)
            for cb in range(8):
                b = cb % 2
                fw.dma(stg[b][:], I["w_in"][l, :, cb * 390:(cb + 1) * 390].rearrange("(k p) c -> p k c", p=128), writes=[STG[b]])
                fw.op("pool", lambda: nc.gpsimd.tensor_copy(out=win[:, :, cb * 390:(cb + 1) * 390], in_=stg[b][:]),
                      reads=[STG[b]], writes=[WIN])
            xt = [T_("xt%d" % i, [128, 2, 1024]) for i in range(2)]
            XT = fw.bufs(2)
            hT = [T_("hT%d" % i, [128, 8, 256], BF16) for i in range(2)]
            HT = fw.bufs(2)
            pst = [P_("pst%d" % i, [128, 1024], BF16) for i in range(2)]
            PST = fw.bufs(2)
            mp = [P_("mp%d" % i, [128, 512]) for i in range(4)]
            MP = fw.bufs(4)
            fw.psum(PST, MP)
            fo = [T_("fo%d" % i, [128, 17, 256]) for i in range(2)]
            FO = fw.bufs(2)
            to = [T_("to%d" % i, [128, 2, NTM]) for i in range(2)]
            TO = fw.bufs(2)
            sg = [T_("sg%d" % i, [128, 256]) for i in range(2)]
            SG = fw.bufs(2)
            fchunks = [(0, 128), (128, 128), (256, 128), (384, 128),
                       (512, 128), (640, 128), (768, 128), (896, 128)]
            fchunks += [(1280 + 128 * i, 128) for i in range(6)]
            fchunks += [(2320, 128), (2448, 128), (2832, 32)]
            tpieces = [(1024, 256, 0), (2048, 272, 256), (2576, 272, 528), (2848, 272, 800)]
            mpi = 0
            ngroups = T // 256
            for g in range(ngroups):
                b = g % 2
                s = 1 if g == 0 else 0
                fw.dma(xt[b][:], xsrc[g * 256:(g + 1) * 256, :].rearrange("(a p) c -> p a c", p=128), writes=[XT[b]])
                for a in range(2):
                    self.norm_mod_T(st, 0, xt[b][:, a, :], XT[b], s, hT[b], HT[b], a * 128, "A", pst[a], PST[a], ident16, ID16)
                if g == 0:
                    tok0 = 0
                    cmpos = CM_CTX0
                    dnpos = DN_CTX0
                else:
                    tok0 = g * 256
                    cmpos = CM_LAT0 + (g - 1) * 256
                    dnpos = DN_LAT0 + (g - 1) * 256
                for ci, (c0, ncol) in enumerate(fchunks):
                    pi = mpi % 4
                    mpi += 1
                    for k in range(8):
                        fw.op("pe", lambda: nc.tensor.matmul(mp[pi][0:ncol, 0:256], lhsT=win[:, k, c0:c0 + ncol], rhs=hT[b][:, k, :],
                                                             start=(k == 0), stop=(k == 7)),
                              reads=[WIN, HT[b]], writes=[MP[pi]])
                    if ci in (2, 3):
                        fw.op("act", lambda: nc.scalar.activation(out=sg[ci - 2][:], in_=mp[pi][:, 0:256], func=AF.Sigmoid),
                              reads=[MP[pi]], writes=[SG[ci - 2]])
                        fw.op("dve", lambda: nc.vector.tensor_tensor(out=fo[b][:, ci - 2, :], in0=fo[b][:, ci - 2, :], in1=sg[ci - 2][:], op=ALU.mult),
                              reads=[SG[ci - 2], FO[b]], writes=[FO[b]])
                    else:
                        eng = "act" if ci % 2 == 0 else "dve"
                        if eng == "act":
                            fw.op("act", lambda: nc.scalar.copy(out=fo[b][0:ncol, ci, :], in_=mp[pi][0:ncol, 0:256]),
                                  reads=[MP[pi]], writes=[FO[b]])
                        else:
                            fw.op("dve", lambda: nc.vector.tensor_copy(out=fo[b][0:ncol, ci, :], in_=mp[pi][0:ncol, 0:256]),
                                  reads=[MP[pi]], writes=[FO[b]])
                fw.dma(S["cmY"][0][:, cmpos:cmpos + 256].rearrange("(c p) t -> p c t", p=128), fo[b][:, 0:2, :], reads=[FO[b]], writes=[S["cmY"][1]])
                fw.dma(S["qT"][0][:, tok0:tok0 + 256].rearrange("(c p) t -> p c t", p=128), fo[b][:, 4:6, :], reads=[FO[b]], writes=[S["qT"][1]])
                fw.dma(S["kT"][0][:, tok0:tok0 + 256].rearrange("(c p) t -> p c t", p=128), fo[b][:, 6:8, :], reads=[FO[b]], writes=[S["kT"][1]])
                fw.dma(S["dnqkv"][0][:, dnpos:dnpos + 256].rearrange("(c p) t -> p c t", p=128), fo[b][:, 8:14, :], reads=[FO[b]], writes=[S["dnqkv"][1]])
                fw.dma(S["glaqT"][0][:, tok0:tok0 + 256], fo[b][:, 14, :], reads=[FO[b]], writes=[S["glaqT"][1]])
                fw.dma(S["glakT"][0][:, tok0:tok0 + 256], fo[b][:, 15, :], reads=[FO[b]], writes=[S["glakT"][1]])
                fw.dma(S["glalrT"][0][:, tok0:tok0 + 256], fo[b][0:32, 16, :], reads=[FO[b]], writes=[S["glalrT"][1]])
                for a in range(2):
                    for (c0, ncol, o0) in tpieces:
                        pi = mpi % 4
                        mpi += 1
                        for k in range(8):
                            fw.op("pe", lambda: nc.tensor.matmul(mp[pi][:, 0:ncol], lhsT=hT[b][:, k, a * 128:(a + 1) * 128], rhs=win[:, k, c0:c0 + ncol],
                                                                 start=(k == 0), stop=(k == 7)),
                                  reads=[WIN, HT[b]], writes=[MP[pi]])
                        eng = "act" if (pi % 2 == 0) else "dve"
                        if eng == "act":
                            fw.op("act", lambda: nc.scalar.copy(out=to[b][:, a, o0:o0 + ncol], in_=mp[pi][:, 0:ncol]), reads=[MP[pi]], writes=[TO[b]])
                        else:
                            fw.op("dve", lambda: nc.vector.tensor_copy(out=to[b][:, a, o0:o0 + ncol], in_=mp[pi][:, 0:ncol]), reads=[MP[pi]], writes=[TO[b]])
                fw.dma(S["tokmaj"][0][tok0:tok0 + 256, :].rearrange("(a p) c -> p a c", p=128), to[b][:], reads=[TO[b]], writes=[S["tokmaj"][1]])
            fw.barrier()

    def zero_mix(self, l):
        fw, nc = self.fw, self.nc
        S = self.scr
        with ExitStack() as st:
            z = st.enter_context(nc.sbuf_tensor(self.uniq("zmix"), [128, T], BF16))
            Z = Buf()
            fw.op("pool", lambda: nc.gpsimd.memset(z[:], 0.0), writes=[Z])
            for c in range(4, 8):
                if (c < 6 and not self.want("E")) or (c >= 6 and not self.want("D")):
                    fw.dma(S["mixT"][0][c * 128:(c + 1) * 128, :], z[:], reads=[Z], writes=[S["mixT"][1]])
            fw.barrier()

    def phaseB(self, l):
        fw, nc = self.fw, self.nc
        S = self.scr
        with ExitStack() as st:
            T_ = lambda n, s, d=F32: st.enter_context(nc.sbuf_tensor(self.uniq(n), s, d))
            P_ = lambda n, s, d=F32: st.enter_context(nc.psum_tensor(self.uniq(n), s, d))
            self.common_tiles(st)
            Y = T_("cmy", [128, 2, TP])
            YB = fw.bufs(2)
            accA = T_("accA", [128, 2, TP])
            AA = fw.bufs(2)
            L = TP - 2 * CMP
            wo, _ = COLS["cm_w"]
            bo, _ = COLS["cm_b"]
            for c in range(2):
                fw.dma(Y[:, c, :], S["cmY"][0][c * 128:(c + 1) * 128, :], reads=[S["cmY"][1]], writes=[YB[c]])
                for (a, b) in ((0, CMP), (CM_CTX0 + CTX, CM_LAT0), (CM_LAT0 + SEQ, TP)):
                    fw.op("pool", lambda: nc.gpsimd.memset(Y[:, c, a:b], 0.0), writes=[YB[c]])
            for c in range(2):
                for j in range(31):
                    wcol = self.colp[:, wo + c * 31 + j:wo + c * 31 + j + 1]
                    e, eng, acc, AC, first = "dve", nc.vector, accA, AA[c], (j == 0)
                    if first:
                        fw.op(e, lambda: eng.tensor_scalar(out=acc[:, c, CMP:CMP + L], in0=Y[:, c, j:j + L], scalar1=wcol, scalar2=None, op0=ALU.mult),
                              reads=[YB[c], self.COLP], writes=[AC])
                    else:
                        fw.op(e, lambda: eng.scalar_tensor_tensor(out=acc[:, c, CMP:CMP + L], in0=Y[:, c, j:j + L], scalar=wcol, in1=acc[:, c, CMP:CMP + L],
                                                                  op0=ALU.mult, op1=ALU.add), reads=[YB[c], self.COLP, AC], writes=[AC])
                fw.op("pool", lambda: nc.gpsimd.tensor_scalar(out=accA[:, c, CMP:CMP + L], in0=accA[:, c, CMP:CMP + L], scalar1=self.colp[:, bo + c:bo + c + 1],
                                                              scalar2=None, op0=ALU.add),
                      reads=[AA[c], self.COLP], writes=[AA[c]])
            sq = T_("lnsq", [128, 2, 512])
            SQ = Buf()
            msq = T_("msq", [128, 512])
            MSQ = Buf()
            var = T_("var", [128, 512])
            VAR = Buf()
            tt = [T_("lnt%d" % i, [128, 512]) for i in range(2)]
            TT = fw.bufs(2)
            ob = [T_("lno%d" % i, [128, 512], BF16) for i in range(2)]
            OB = fw.bufs(2)
            mps = P_("lnm", [128, 512])
            MPS = Buf()
            eps_ = P_("lne", [128, 512])
            EPS_ = Buf()
            fw.psum(MPS, EPS_)
            lg, _ = COLS["cm_lg"]
            lb, _ = COLS["cm_lb"]
            blocks = [(CM_CTX0, 256, 0)] + [(CM_LAT0 + 512 * k, 512, CTX + 512 * k) for k in range(8)]
            for (p0, n, tok0) in blocks:
                for c in range(2):
                    fw.op("act", lambda: nc.scalar.activation(out=sq[:, c, 0:n], in_=accA[:, c, p0:p0 + n], func=AF.Square), reads=[AA[c]], writes=[SQ])
                for c in range(2):
                    fw.op("pe", lambda: nc.tensor.matmul(mps[:, 0:n], lhsT=self.C("div256"), rhs=accA[:, c, p0:p0 + n], start=(c == 0), stop=(c == 1)),
                          reads=[self.CST, AA[c]], writes=[MPS])
                for c in range(2):
                    fw.op("pe", lambda: nc.tensor.matmul(eps_[:, 0:n], lhsT=self.C("div256"), rhs=sq[:, c, 0:n], start=(c == 0), stop=(c == 1)),
                          reads=[self.CST, SQ], writes=[EPS_])
                fw.op("act", lambda: nc.scalar.activation(out=msq[:, 0:n], in_=mps[:, 0:n], func=AF.Square), reads=[MPS], writes=[MSQ])
                fw.op("dve", lambda: nc.vector.tensor_tensor(out=var[:, 0:n], in0=eps_[:, 0:n], in1=msq[:, 0:n], op=ALU.subtract), reads=[EPS_, MSQ], writes=[VAR])
                fw.op("act", lambda: nc.scalar.activation(out=var[:, 0:n], in_=var[:, 0:n], func=AF.Sqrt, bias=self.epsc[:, 0:1], scale=1.0), reads=[VAR, self.EPSC], writes=[VAR])
                fw.op("dve", lambda: nc.vector.reciprocal(out=var[:, 0:n], in_=var[:, 0:n]), reads=[VAR], writes=[VAR])
                for c in range(2):
                    fw.op("dve", lambda: nc.vector.tensor_tensor(out=tt[c][:, 0:n], in0=accA[:, c, p0:p0 + n], in1=mps[:, 0:n], op=ALU.subtract),
                          reads=[AA[c], MPS], writes=[TT[c]])
                    fw.op("pool", lambda: nc.gpsimd.tensor_tensor(out=tt[c][:, 0:n], in0=tt[c][:, 0:n], in1=var[:, 0:n], op=ALU.mult), reads=[TT[c], VAR], writes=[TT[c]])
                    fw.op("act", lambda: nc.scalar.activation(out=ob[c][:, 0:n], in_=tt[c][:, 0:n], func=AF.Silu, bias=self.colp[:, lb + c:lb + c + 1],
                                                              scale=self.colp[:, lg + c:lg + c + 1]), reads=[TT[c], self.COLP], writes=[OB[c]])
                    fw.dma(S["mixT"][0][c * 128:(c + 1) * 128, tok0:tok0 + n], ob[c][:, 0:n], reads=[OB[c]], writes=[S["mixT"][1]])
            fw.barrier()

    def phaseC(self, l):
        fw, nc, I = self.fw, self.nc, self.I
        S = self.scr
        lam_init = 0.8 - 0.6 * math.exp(-0.3 * l)
        scale = 32.0 ** -0.5
        with ExitStack() as st:
            T_ = lambda n, s, d=F32: st.enter_context(nc.sbuf_tensor(self.uniq(n), s, d))
            P_ = lambda n, s, d=F32: st.enter_context(nc.psum_tensor(self.uniq(n), s, d))
            ident16, ID16 = self.common_tiles(st)
            QT = T_("QT", [128, 2, T], BF16)
            KT = T_("KT", [128, 2, T], BF16)
            QTB, KTB = Buf(), Buf()
            V = T_("V", [128, NT, 4, 65], BF16)
            VB = Buf()
            fw.op("pool", lambda: nc.gpsimd.memset(V[:].rearrange("p a b c -> p (a b c)"), 1.0), writes=[VB])
            blocks = [(0, 256)] + [(CTX + 512 * k, 512) for k in range(8)]
            with ExitStack() as st2:
                T2 = lambda n, s, d=F32: st2.enter_context(nc.sbuf_tensor(self.uniq(n), s, d))
                P2 = lambda n, s, d=F32: st2.enter_context(nc.psum_tensor(self.uniq(n), s, d))
                raw = [T2("raw%d" % i, [128, 512]) for i in range(2)]
                RAW = fw.bufs(2)
                cs = [T2("cs%d" % i, [128, 2, 512]) for i in range(2)]
                CS = fw.bufs(2)
                sq = T2("sq", [128, 512]); SQ = Buf()
                rs = T2("rs", [128, 512]); RS = Buf()
                qn = T2("qn", [128, 512]); QN = Buf()
                r1 = T2("r1", [128, 512]); R1 = Buf()
                r2 = T2("r2", [128, 512]); R2 = Buf()
                ssp = P2("ssp", [128, 512]); SSP = Buf()
                swp = P2("swp", [128, 512]); SWP = Buf()
                fw.psum(SSP, SWP)
                vst = [T2("vst%d" % i, [128, 256]) for i in range(2)]
                VST = fw.bufs(2)
                it = 0
                for (name, dst, DST, gname) in (("qT", QT, QTB, "qg"), ("kT", KT, KTB, "kg")):
                    go, _ = COLS[gname]
                    for c in range(2):
                        for (t0, n) in blocks:
                            b = it % 2
                            it += 1
                            fw.dma(raw[b][:, 0:n], S[name][0][c * 128:(c + 1) * 128, t0:t0 + n], reads=[S[name][1]], writes=[RAW[b]])
                            fw.op("act", lambda: nc.scalar.activation(out=sq[:, 0:n], in_=raw[b][:, 0:n], func=AF.Square), reads=[RAW[b]], writes=[SQ])
                            fw.op("pe", lambda: nc.tensor.matmul(ssp[:, 0:n], lhsT=self.C("blk32"), rhs=sq[:, 0:n], start=True, stop=True), reads=[self.CST, SQ], writes=[SSP])
                            fw.op("act", lambda: nc.scalar.activation(out=rs[:, 0:n], in_=ssp[:, 0:n], func=AF.Sqrt, bias=self.epsc[:, 0:1], scale=1.0), reads=[SSP, self.EPSC], writes=[RS])
                            fw.op("dve", lambda: nc.vector.reciprocal(out=rs[:, 0:n], in_=rs[:, 0:n]), reads=[RS], writes=[RS])
                            if t0 < CTX:
                                fw.op("dve", lambda: nc.vector.scalar_tensor_tensor(out=dst[:, c, t0:t0 + n], in0=raw[b][:, 0:n], scalar=self.colp[:, go:go + 1], in1=rs[:, 0:n],
                                                                                    op0=ALU.mult, op1=ALU.mult), reads=[RAW[b], RS, self.COLP], writes=[DST])
                                continue
                            fw.op("dve", lambda: nc.vector.scalar_tensor_tensor(out=qn[:, 0:n], in0=raw[b][:, 0:n], scalar=self.colp[:, go:go + 1], in1=rs[:, 0:n],
                                                                                op0=ALU.mult, op1=ALU.mult), reads=[RAW[b], RS, self.COLP], writes=[QN])
                            lt = t0 - CTX
                            fw.dma(cs[b][:, 0, 0:n], I["rope"][:, lt:lt + n], writes=[CS[b]])
                            fw.dma(cs[b][:, 1, 0:n], I["rope"][:, SEQ + lt:SEQ + lt + n], writes=[CS[b]])
                            fw.op("pe", lambda: nc.tensor.matmul(swp[:, 0:n], lhsT=self.C("perm"), rhs=qn[:, 0:n], start=True, stop=True), reads=[self.CST, QN], writes=[SWP])
                            fw.op("pool", lambda: nc.gpsimd.tensor_tensor(out=r1[:, 0:n], in0=qn[:, 0:n], in1=cs[b][:, 0, 0:n], op=ALU.mult), reads=[QN, CS[b]], writes=[R1])
                            fw.op("dve", lambda: nc.vector.tensor_tensor(out=r2[:, 0:n], in0=swp[:, 0:n], in1=cs[b][:, 1, 0:n], op=ALU.mult), reads=[SWP, CS[b]], writes=[R2])
                            fw.op("pool", lambda: nc.gpsimd.tensor_tensor(out=dst[:, c, t0:t0 + n], in0=r1[:, 0:n], in1=r2[:, 0:n], op=ALU.add), reads=[R1, R2], writes=[DST])
                for t in range(NT):
                    b = t % 2
                    fw.dma(vst[b][:], S["tokmaj"][0][t * 128:(t + 1) * 128, 0:256], reads=[S["tokmaj"][1]], writes=[VST[b]])
                    fw.op("pool", lambda: nc.gpsimd.tensor_copy(out=V[:, t, :, 0:64], in_=vst[b][:].rearrange("p (h d) -> p h d", h=4)), reads=[VST[b]], writes=[VB])
                fw.barrier()
            lt_ = T_("lamt", [128, 8]); LT = Buf()
            lp = T_("lamp", [128, 64]); LP = Buf()
            lo, _ = ROWS["lam"]
            for i in range(2):
                fw.op("dve", lambda: nc.vector.tensor_tensor(out=lp[:, i * 32:(i + 1) * 32], in0=self.rowp[:, lo + 64 * i:lo + 64 * i + 32], in1=self.rowp[:, lo + 64 * i + 32:lo + 64 * i + 64], op=ALU.mult),
                      reads=[self.ROWP], writes=[LP])
                fw.op("dve", lambda: nc.vector.reduce_sum(out=lt_[:, i:i + 1], in_=lp[:, i * 32:(i + 1) * 32], axis=AX.X), reads=[LP], writes=[LT])
            fw.op("act", lambda: nc.scalar.activation(out=lt_[:, 2:4], in_=lt_[:, 0:2], func=AF.Exp), reads=[LT], writes=[LT])
            fw.op("dve", lambda: nc.vector.scalar_tensor_tensor(out=lt_[:, 4:5], in0=lt_[:, 3:4], scalar=-lam_init, in1=lt_[:, 2:3], op0=ALU.add, op1=ALU.subtract), reads=[LT], writes=[LT])
            subg = T_("subg", [128, 64]); SUBG = Buf()
            so, _ = ROWS["subln"]
            fw.op("dve", lambda: nc.vector.tensor_scalar(out=subg[:], in0=self.rowp[:, so:so + 64], scalar1=(1.0 - lam_init), scalar2=None, op0=ALU.mult), reads=[self.ROWP], writes=[SUBG])
            identf = self.C("ident")
            scps = [P_("scps%d" % i, [128, 512]) for i in range(3)]
            SCPS = fw.bufs(3)
            otps = [P_("otps%d" % i, [128, 512]) for i in range(2)]
            OTPS = fw.bufs(2)
            trps = [P_("trps%d" % i, [128, 4, 65]) for i in range(2)]
            TRPS = fw.bufs(2)
            ytps = P_("ytps", [128, 1024], BF16)
            YTPS = Buf()
            fw.psum(SCPS, OTPS, TRPS, YTPS)
            pb = [T_("pb%d" % i, [128, 512], BF16) for i in range(4)]
            PB = fw.bufs(4)
            otsb = [T_("otsb%d" % i, [128, 512]) for i in range(2)]
            OTSB = fw.bufs(2)
            rcp = T_("rcp", [128, 2, 4]); RCP = Buf()
            o0 = T_("o0", [128, 64]); O0 = Buf()
            o1 = T_("o1", [128, 64]); O1 = Buf()
            avs = [T_("avs%d" % i, [128, 4, 4, 64]) for i in range(2)]; AVS = fw.bufs(2)
            junk = T_("junk", [128, 64]); JUNK = Buf()
            ssa = [T_("ssa%d" % i, [128, 48]) for i in range(2)]; SSA = fw.bufs(2)
            yb = [T_("yb%d" % i, [128, 4, 256], BF16) for i in range(2)]
            YB = fw.bufs(2)
            ybT = [T_("ybT%d" % i, [128, 2, 512], BF16) for i in range(2)]
            YBT = fw.bufs(2)
            qblocks = [(0, 256, 2)] + [(CTX + 512 * k, 512, NT) for k in range(8)]
            items = []
            for qi, (q0, nq, nkt) in enumerate(qblocks):
                for h in range(4):
                    for m in range(2):
                        for kt in range(nkt):
                            items.append((qi, q0, nq, nkt, h, m, kt))

            def emit_sc(idx):
                qi, q0, nq, nkt, h, m, kt = items[idx]
                r = (h % 2) * 2 + m
                ch = h // 2
                si = idx % 3
                fw.op("pe", lambda: nc.tensor.matmul(scps[si][:, 0:nq], lhsT=KT[32 * r:32 * r + 32, ch, kt * 128:(kt + 1) * 128],
                                                     rhs=QT[32 * r:32 * r + 32, ch, q0:q0 + nq], start=True, stop=True, tile_position=(32 * r, 0)),
                      reads=[KTB, QTB], writes=[SCPS[si]])

            def emit_exp_av(idx):
                qi, q0, nq, nkt, h, m, kt = items[idx]
                si = idx % 3
                pi = idx % 4
                fw.op("act", lambda: nc.scalar.activation(out=pb[pi][:, 0:nq], in_=scps[si][:, 0:nq], func=AF.Exp, scale=scale), reads=[SCPS[si]], writes=[PB[pi]])
                fw.op("pe", lambda: nc.tensor.matmul(otps[m][0:65, 0:nq], lhsT=V[:, kt, h, :], rhs=pb[pi][:, 0:nq], start=(kt == 0), stop=(kt == nkt - 1)),
                      reads=[VB, PB[pi]], writes=[OTPS[m]])

            def epilogue(qi, q0, nq, h, m):
                nsub = nq // 128
                qb = qi % 2
                fw.op("dve", lambda: nc.vector.tensor_copy(out=otsb[m][0:65, 0:nq], in_=otps[m][0:65, 0:nq]), reads=[OTPS[m]], writes=[OTSB[m]])
                for sub in range(nsub):
                    fw.op("pe", lambda: nc.tensor.transpose(trps[m][:, sub, :], otsb[m][0:65, sub * 128:(sub + 1) * 128], identf[0:65, 0:65]),
                          reads=[OTSB[m], self.CST], writes=[TRPS[m]])
                fw.op("dve", lambda: nc.vector.reciprocal(out=rcp[:, m, 0:nsub], in_=trps[m][:, 0:nsub, 64]), reads=[TRPS[m]], writes=[RCP])
                if m == 0:
                    return
                for sub in range(nsub):
                    col = h * 4 + sub
                    fw.op("dve", lambda: nc.vector.tensor_scalar(out=o0[:], in0=trps[0][:, sub, 0:64], scalar1=rcp[:, 0, sub:sub + 1], scalar2=None, op0=ALU.mult),
                          reads=[TRPS[0], RCP], writes=[O0])
                    fw.op("dve", lambda: nc.vector.tensor_scalar(out=o1[:], in0=trps[1][:, sub, 0:64], scalar1=rcp[:, 1, sub:sub + 1], scalar2=None, op0=ALU.mult),
                          reads=[TRPS[1], RCP], writes=[O1])
                    fw.op("dve", lambda: nc.vector.scalar_tensor_tensor(out=avs[qb][:, h, sub, :], in0=o1[:], scalar=lt_[:, 4:5], in1=o0[:], op0=ALU.mult, op1=ALU.add),
                          reads=[O0, O1, LT], writes=[AVS[qb]])
                    fw.op("pool", lambda: nc.gpsimd.tensor_tensor(out=junk[:], in0=avs[qb][:, h, sub, :], in1=avs[qb][:, h, sub, :], op=ALU.mult), reads=[AVS[qb]], writes=[JUNK])
                    fw.op("dve", lambda: nc.vector.reduce_sum(out=ssa[qb][:, col:col + 1], in_=junk[:], axis=AX.X), reads=[JUNK], writes=[SSA[qb]])
                if h < 3:
                    return
                fw.op("act", lambda: nc.scalar.activation(out=ssa[qb][:, 16:32], in_=ssa[qb][:, 0:16], func=AF.Sqrt, bias=self.epsc[:, 0:1], scale=1.0 / 64), reads=[SSA[qb], self.EPSC], writes=[SSA[qb]])
                fw.op("dve", lambda: nc.vector.reciprocal(out=ssa[qb][:, 32:48], in_=ssa[qb][:, 16:32]), reads=[SSA[qb]], writes=[SSA[qb]])
                for hh in range(4):
                    for sub in range(nsub):
                        col = 32 + hh * 4 + sub
                        fw.op("dve", lambda: nc.vector.scalar_tensor_tensor(out=yb[qb][:, sub, hh * 64:(hh + 1) * 64], in0=avs[qb][:, hh, sub, :], scalar=ssa[qb][:, col:col + 1], in1=subg[:],
                                                                            op0=ALU.mult, op1=ALU.mult), reads=[AVS[qb], SSA[qb], SUBG], writes=[YB[qb]])
                for sub in range(nsub):
                    for c2 in range(2):
                        fw.op("pe", lambda: nc.tensor.transpose(ytps[:, c2 * 512 + sub * 128:c2 * 512 + (sub + 1) * 128], yb[qb][:, sub, c2 * 128:(c2 + 1) * 128], ident16[:]),
                              reads=[YB[qb], ID16], writes=[YTPS])
                for c2 in range(2):
                    fw.op("dve", lambda: nc.vector.tensor_copy(out=ybT[qb][:, c2, 0:nq], in_=ytps[:, c2 * 512:c2 * 512 + nq]), reads=[YTPS], writes=[YBT[qb]])
                    fw.dma(S["mixT"][0][256 + c2 * 128:256 + (c2 + 1) * 128, q0:q0 + nq], ybT[qb][:, c2, 0:nq], reads=[YBT[qb]], writes=[S["mixT"][1]])

            pending = []
            nit = len(items)
            emit_sc(0)
            emit_sc(1)
            for idx in range(nit):
                qi, q0, nq, nkt, h, m, kt = items[idx]
                if idx + 2 < nit:
                    emit_sc(idx + 2)
                emit_exp_av(idx)
                if pending and kt == min(2, nkt - 1):
                    for args in pending:
                        epilogue(*args)
                    pending = []
                if kt == nkt - 1:
                    pending.append((qi, q0, nq, h, m))
            for args in pending:
                epilogue(*args)
            fw.barrier()

    def gated_out(self, st, tiles, osb, OSB, gate_ap, GATE, on_name, row0, t):
        fw, nc = self.fw, self.nc
        (sq, SQ, ssq, SSQ, sg, SG, y, Y, yf, YF, ytps, YTPS, yT, YT, ident16, ID16) = tiles
        S = self.scr
        go, _ = ROWS[on_name]
        fw.op("pool", lambda: nc.gpsimd.tensor_tensor(out=sq[:], in0=osb[:], in1=osb[:], op=ALU.mult), reads=[OSB], writes=[SQ])
        fw.op("dve", lambda: nc.vector.reduce_sum(out=ssq[:, 0:4], in_=sq[:].rearrange("p (h d) -> p h d", h=4), axis=AX.X), reads=[SQ], writes=[SSQ])
        fw.op("act", lambda: nc.scalar.activation(out=ssq[:, 4:8], in_=ssq[:, 0:4], func=AF.Sqrt, bias=self.epsc[:, 0:1], scale=1.0 / 64), reads=[SSQ, self.EPSC], writes=[SSQ])
        fw.op("dve", lambda: nc.vector.reciprocal(out=ssq[:, 8:12], in_=ssq[:, 4:8]), reads=[SSQ], writes=[SSQ])
        fw.op("act", lambda: nc.scalar.activation(out=sg[:], in_=gate_ap, func=AF.Silu), reads=[GATE], writes=[SG])
        for h in range(4):
            fw.op("dve", lambda: nc.vector.scalar_tensor_tensor(out=y[:, h * 64:(h + 1) * 64], in0=osb[:, h * 64:(h + 1) * 64], scalar=ssq[:, 8 + h:9 + h],
                                                                in1=self.rowp[:, go:go + 64], op0=ALU.mult, op1=ALU.mult), reads=[OSB, SSQ, self.ROWP], writes=[Y])
        fw.op("pool", lambda: nc.gpsimd.tensor_tensor(out=yf[:], in0=y[:], in1=sg[:], op=ALU.mult), reads=[Y, SG], writes=[YF])
        for c2 in range(2):
            fw.op("pe", lambda: nc.tensor.transpose(ytps[:, c2 * 128:(c2 + 1) * 128], yf[:, c2 * 128:(c2 + 1) * 128], ident16[:]), reads=[YF, ID16], writes=[YTPS])
        fw.op("act", lambda: nc.scalar.copy(out=yT[:], in_=ytps[:, 0:256]), reads=[YTPS], writes=[YT])
        fw.dma(S["mixT"][0][row0:row0 + 256, t * 128:(t + 1) * 128].rearrange("(c p) t -> p c t", p=128), yT[:].rearrange("p (c t) -> p c t", c=2), reads=[YT], writes=[S["mixT"][1]])

    def gated_tiles(self, st, ident16, ID16):
        nc = self.nc
        T_ = lambda n, s, d=F32: st.enter_context(nc.sbuf_tensor(self.uniq(n), s, d))
        P_ = lambda n, s, d=F32: st.enter_context(nc.psum_tensor(self.uniq(n), s, d))
        ytb = Buf()
        ytb.ps = True
        return (T_("g_sq", [128, 256]), Buf(), T_("g_ssq", [128, 12]), Buf(), T_("g_sg", [128, 256]), Buf(), T_("g_y", [128, 256]), Buf(),
                T_("g_yf", [128, 256], BF16), Buf(), P_("g_ytps", [128, 1024], BF16), ytb, T_("g_yT", [128, 256], BF16), Buf(), ident16, ID16)

    def phaseD(self, l):
        fw, nc, I = self.fw, self.nc, self.I
        S = self.scr
        scale = 32.0 ** -0.5
        with ExitStack() as st:
            T_ = lambda n, s, d=F32: st.enter_context(nc.sbuf_tensor(self.uniq(n), s, d))
            P_ = lambda n, s, d=F32: st.enter_context(nc.psum_tensor(self.uniq(n), s, d))
            ident16, ID16 = self.common_tiles(st)
            gt = self.gated_tiles(st, ident16, ID16)
            w2 = T_("w2", [32, 2, 128]); W2 = Buf()
            fw.dma(w2[:], I["w2p"][l].rearrange("d r c -> r d c"), writes=[W2])
            Sst = T_("Sst", [128, 256]); SST = Buf()
            qT = T_("qT", [128, 128]); QTB = Buf()
            kT = T_("kT", [128, 128]); KTB = Buf()
            lrT = T_("lrT", [32, 128]); LRT = Buf()
            tm = T_("tm", [128, 544]); TM = Buf()
            z = T_("z", [128, 128]); Z = Buf()
            ebT = T_("ebT", [128, 128]); EBT = Buf()
            enbT = T_("enbT", [128, 128]); ENBT = Buf()
            qin = T_("qin", [128, 128]); QIN = Buf()
            kn = T_("kn", [128, 4, 128]); KN = Buf()
            ktok = T_("ktok", [128, 128]); KTOK = Buf()
            bcs = T_("bcs", [128, 128]); BCS = Buf()
            kend = T_("kend", [128, 2, 128]); KEND = Buf()
            aqk = T_("aqk", [128, 2, 512]); AQK = Buf()
            tmp = T_("tmp", [128, 256]); TMP = Buf()
            osb = T_("osb", [128, 256]); OSB = Buf()
            of = T_("of", [128, 256]); OF = Buf()
            zk = P_("zk", [128, 512]); ZP = Buf(); KP = ZP
            b3 = P_("b3", [128, 512]); B3 = [Buf()] * 3
            aq = P_("aq", [128, 512]); AQ = Buf()
            ops = P_("ops", [128, 512]); OPS = [Buf()] * 2
            sp_ = P_("sp", [128, 512]); sp = sp_[:, 0:256]; SP = Buf()
            fw.psum(ZP, B3, AQ, OPS, SP)
            onescol = self.C("ones")[:, 0:1]
            b2o, _ = ROWS["gla_b2r"]
            for d in range(2):
                nm = "f" if d == 0 else "r"
                order = list(range(NT)) if d == 0 else [1, 0] + list(range(NT - 1, 1, -1))
                if getattr(self, "nt_dbg", None):
                    order = [t for t in order if t < self.nt_dbg]
                fw.op("pool", lambda: nc.gpsimd.memset(Sst[:], 0.0), writes=[SST])
                for t in order:
                    tk = slice(t * 128, (t + 1) * 128)
                    fw.dma(qT[:], S["glaqT"][0][:, tk], reads=[S["glaqT"][1]], writes=[QTB])
                    fw.dma(kT[:], S["glakT"][0][:, tk], reads=[S["glakT"][1]], writes=[KTB])
                    fw.dma(lrT[:], S["glalrT"][0][:, tk], reads=[S["glalrT"][1]], writes=[LRT])
                    fw.dma(tm[:], S["tokmaj"][0][tk, 528:1072], reads=[S["tokmaj"][1]], writes=[TM])
                    fw.op("pe", lambda: nc.tensor.matmul(zk[:, 0:128], lhsT=lrT[:, :], rhs=w2[:, d, :], start=True, stop=True), reads=[LRT, W2], writes=[ZP])
                    fw.op("dve", lambda: nc.vector.tensor_tensor(out=z[:], in0=zk[:, 0:128], in1=self.rowp[:, b2o + d * 128:b2o + (d + 1) * 128], op=ALU.add), reads=[ZP, self.ROWP], writes=[Z])
                    fw.op("act", lambda: nc.scalar.activation(out=z[:], in_=z[:], func=AF.Exp, scale=-1.0), reads=[Z], writes=[Z])
                    fw.op("act", lambda: nc.scalar.activation(out=z[:], in_=z[:], func=AF.Ln, bias=onescol, scale=1.0), reads=[Z, self.CST], writes=[Z])
                    fw.op("pe", lambda: nc.tensor.matmul(b3[:, 0:128], lhsT=self.C("tris_" + nm), rhs=z[:], start=True, stop=True), reads=[self.CST, Z], writes=[B3[0]])
                    fw.op("pe", lambda: nc.tensor.matmul(b3[:, 128:256], lhsT=z[:], rhs=self.C("tris_" + nm), start=True, stop=True), reads=[self.CST, Z], writes=[B3[1]])
                    fw.op("pe", lambda: nc.tensor.matmul(b3[:, 256:384], lhsT=self.C("blks"), rhs=z[:], start=True, stop=True), reads=[self.CST, Z], writes=[B3[2]])
                    fw.op("act", lambda: nc.scalar.activation(out=ebT[:], in_=b3[:, 128:256], func=AF.Exp), reads=[B3[1]], writes=[EBT])
                    fw.op("act", lambda: nc.scalar.activation(out=enbT[:], in_=b3[:, 128:256], func=AF.Exp, scale=-1.0), reads=[B3[1]], writes=[ENBT])
                    fw.op("dve", lambda: nc.vector.scalar_tensor_tensor(out=qin[:], in0=qT[:], scalar=scale, in1=ebT[:], op0=ALU.mult, op1=ALU.mult), reads=[QTB, EBT], writes=[QIN])
                    hmo, _ = COFF["hm32"]
                    for h in range(4):
                        fw.op("dve", lambda: nc.vector.scalar_tensor_tensor(out=kn[:, h, :], in0=kT[:], scalar=self.cst[:, hmo + h:hmo + h + 1], in1=enbT[:], op0=ALU.mult, op1=ALU.mult),
                              reads=[KTB, ENBT, self.CST], writes=[KN])
                    fw.op("pe", lambda: nc.tensor.transpose(zk[:, 128:256], kT[:], self.C("ident")), reads=[KTB, self.CST], writes=[KP])
                    fw.op("act", lambda: nc.scalar.copy(out=ktok[:], in_=zk[:, 128:256]), reads=[KP], writes=[KTOK])
                    fw.op("act", lambda: nc.scalar.copy(out=bcs[:], in_=b3[:, 0:128]), reads=[B3[0]], writes=[BCS])
                    fw.op("dve", lambda: nc.vector.tensor_tensor(out=bcs[:], in0=b3[:, 256:384], in1=bcs[:], op=ALU.subtract), reads=[B3[2], BCS], writes=[BCS])
                    fw.op("act", lambda: nc.scalar.activation(out=bcs[:], in_=bcs[:], func=AF.Exp), reads=[BCS], writes=[BCS])
                    cio, _ = COFF["chunkind"]
                    for c in range(2):
                        fw.op("dve", lambda: nc.vector.scalar_tensor_tensor(out=kend[:, c, :], in0=ktok[:], scalar=self.cst[:, cio + c:cio + c + 1], in1=bcs[:], op0=ALU.mult, op1=ALU.mult),
                              reads=[KTOK, BCS, self.CST], writes=[KEND])
                    if getattr(self, "d_level", 9) < 2:
                        continue
                    for h in range(4):
                        fw.op("pe", lambda: nc.tensor.matmul(aq[:, h * 128:(h + 1) * 128], lhsT=kn[:, h, :], rhs=qin[:], start=True, stop=True), reads=[KN, QIN], writes=[AQ])
                    for c in range(2):
                        fw.op("dve", lambda: nc.vector.scalar_tensor_tensor(out=aqk[:, c, :], in0=aq[:], scalar=self.cst[:, cio + c:cio + c + 1], in1=self.C("cT4_" + nm), op0=ALU.mult, op1=ALU.mult),
                              reads=[AQ, self.CST], writes=[AQK])
                    if getattr(self, "d_level", 9) < 3:
                        continue
                    for c in ((0, 1) if d == 0 else (1, 0)):
                        r0 = 64 * c
                        dcol = (63 + 64 * c) if d == 0 else (64 * c)
                        for h in range(4):
                            fw.op("pe", lambda: nc.tensor.matmul(ops[:, c * 256 + h * 64:c * 256 + (h + 1) * 64], lhsT=qin[:], rhs=Sst[:, h * 64:(h + 1) * 64], start=True, stop=False),
                                  reads=[QIN, SST], writes=[OPS[c]])
                            fw.op("pe", lambda: nc.tensor.matmul(ops[:, c * 256 + h * 64:c * 256 + (h + 1) * 64], lhsT=aqk[:, c, h * 128:(h + 1) * 128],
                                                                 rhs=tm[:, h * 64:(h + 1) * 64], start=False, stop=True), reads=[AQK, TM], writes=[OPS[c]])
                        fw.op("pe", lambda: nc.tensor.matmul(sp, lhsT=kend[:, c, :], rhs=tm[:, 0:256], start=True, stop=True), reads=[KEND, TM], writes=[SP])
                        fw.op("dve", lambda: nc.vector.tensor_tensor(out=tmp[:], in0=sp, in1=self.C("bmask4"), op=ALU.mult), reads=[SP, self.CST], writes=[TMP])
                        fw.op("dve", lambda: nc.vector.scalar_tensor_tensor(out=Sst[:], in0=Sst[:], scalar=ebT[:, dcol:dcol + 1], in1=tmp[:], op0=ALU.mult, op1=ALU.add),
                              reads=[SST, EBT, TMP], writes=[SST])
                        fw.op("act", lambda: nc.scalar.copy(out=osb[r0:r0 + 64, :], in_=ops[r0:r0 + 64, c * 256:(c + 1) * 256]), reads=[OPS[c]], writes=[OSB])
                    if getattr(self, "d_level", 9) < 4:
                        continue
                    if d == 0:
                        fw.dma(S["ofwd"][0][tk, :], osb[:], reads=[OSB], writes=[S["ofwd"][1]])
                    else:
                        fw.dma(of[:], S["ofwd"][0][tk, :], reads=[S["ofwd"][1]], writes=[OF])
                        fw.op("pool", lambda: nc.gpsimd.tensor_tensor(out=osb[:], in0=osb[:], in1=of[:], op=ALU.add), reads=[OSB, OF], writes=[OSB])
                        self.gated_out(st, gt, osb, OSB, tm[:, 288:544], TM, "gla_on", 768, t)
            fw.barrier()

    def phaseE(self, l):
        fw, nc, I = self.fw, self.nc, self.I
        S = self.scr
        with ExitStack() as st:
            T_ = lambda n, s, d=F32: st.enter_context(nc.sbuf_tensor(self.uniq(n), s, d))
            P_ = lambda n, s, d=F32: st.enter_context(nc.psum_tensor(self.uniq(n), s, d))
            self.common_tiles(st)
            raw = [T_("dnraw%d" % i, [128, TPD]) for i in range(2)]
            RAW = fw.bufs(2)
            acc = [T_("dnacc%d" % i, [128, TPD]) for i in range(2)]
            ACC = fw.bufs(2)
            sq = T_("dnsq", [128, 512]); SQ = Buf()
            rs = T_("dnrs", [128, 512]); RS = Buf()
            nrm = [T_("dnnrm%d" % i, [128, 512]) for i in range(2)]
            NRM = fw.bufs(2)
            tk_ = [T_("dntk%d" % i, [128, 512]) for i in range(2)]
            TK = fw.bufs(2)
            ssp = P_("dnssp", [128, 512]); SSP = Buf()
            trp = [P_("dntrp%d" % i, [128, 512]) for i in range(2)]
            TRP = fw.bufs(2)
            fw.psum(SSP, TRP)
            L = TPD - 2 * DNP
            wo, _ = COLS["dn_w"]
            blocks = [(DN_CTX0, 256, 0)] + [(DN_LAT0 + 512 * k, 512, CTX + 512 * k) for k in range(8)]
            bi = 0
            for ci in range(6):
                b = ci % 2
                kind = ci // 2
                half = ci % 2
                fw.dma(raw[b][:], S["dnqkv"][0][ci * 128:(ci + 1) * 128, :], reads=[S["dnqkv"][1]], writes=[RAW[b]])
                for (a, e_) in ((0, DNP), (DN_CTX0 + CTX, DN_LAT0), (DN_LAT0 + SEQ, TPD)):
                    fw.op("pool", lambda: nc.gpsimd.memset(raw[b][:, a:e_], 0.0), writes=[RAW[b]])
                for j in range(5):
                    wcol = self.colp[:, wo + ci * 5 + j:wo + ci * 5 + j + 1]
                    if j == 0:
                        fw.op("dve", lambda: nc.vector.tensor_scalar(out=acc[b][:, DNP:DNP + L], in0=raw[b][:, j:j + L], scalar1=wcol, scalar2=None, op0=ALU.mult),
                              reads=[RAW[b], self.COLP], writes=[ACC[b]])
                    else:
                        fw.op("dve", lambda: nc.vector.scalar_tensor_tensor(out=acc[b][:, DNP:DNP + L], in0=raw[b][:, j:j + L], scalar=wcol, in1=acc[b][:, DNP:DNP + L],
                                                                            op0=ALU.mult, op1=ALU.add), reads=[RAW[b], self.COLP, ACC[b]], writes=[ACC[b]])
                fw.op("act", lambda: nc.scalar.activation(out=acc[b][:, DNP:DNP + L], in_=acc[b][:, DNP:DNP + L], func=AF.Silu), reads=[ACC[b]], writes=[ACC[b]])
                for (p0, n, tok0) in blocks:
                    nb = bi % 2
                    bi += 1
                    if kind < 2:
                        fw.op("act", lambda: nc.scalar.activation(out=sq[:, 0:n], in_=acc[b][:, p0:p0 + n], func=AF.Square), reads=[ACC[b]], writes=[SQ])
                        fw.op("pe", lambda: nc.tensor.matmul(ssp[:, 0:n], lhsT=self.C("blk64"), rhs=sq[:, 0:n], start=True, stop=True), reads=[self.CST, SQ], writes=[SSP])
                        fw.op("act", lambda: nc.scalar.activation(out=rs[:, 0:n], in_=ssp[:, 0:n], func=AF.Sqrt, bias=self.epsc[:, 0:1], scale=1.0), reads=[SSP, self.EPSC], writes=[RS])
                        fw.op("dve", lambda: nc.vector.reciprocal(out=rs[:, 0:n], in_=rs[:, 0:n]), reads=[RS], writes=[RS])
                        fw.op("dve", lambda: nc.vector.scalar_tensor_tensor(out=nrm[nb][:, 0:n], in0=acc[b][:, p0:p0 + n], scalar=(0.125 if kind == 0 else 1.0), in1=rs[:, 0:n],
                                                                            op0=ALU.mult, op1=ALU.mult), reads=[ACC[b], RS], writes=[NRM[nb]])
                        dst = S["dn_qT"] if kind == 0 else S["dn_kT"]
                        fw.dma(dst[0][half * 128:(half + 1) * 128, tok0:tok0 + n], nrm[nb][:, 0:n], reads=[NRM[nb]], writes=[dst[1]])
                        src, SRC, soff = nrm[nb], NRM[nb], 0
                    else:
                        src, SRC, soff = acc[b], ACC[b], p0
                    if kind >= 1:
                        for sub in range(n // 128):
                            fw.op("pe", lambda: nc.tensor.transpose(trp[nb][:, sub * 128:(sub + 1) * 128], src[:, soff + sub * 128:soff + (sub + 1) * 128], self.C("ident")),
                                  reads=[SRC, self.CST], writes=[TRP[nb]])
                        fw.op("act", lambda: nc.scalar.copy(out=tk_[nb][:, 0:n], in_=trp[nb][:, 0:n]), reads=[TRP[nb]], writes=[TK[nb]])
                        dst = S["dn_ktok"] if kind == 1 else S["dn_vtok"]
                        fw.dma(dst[0][tok0:tok0 + n, half * 128:(half + 1) * 128].rearrange("(a p) c -> p a c", p=128),
                               tk_[nb][:, 0:n].rearrange("p (a c) -> p a c", c=128), reads=[TK[nb]], writes=[dst[1]])
            fw.barrier()
        if getattr(self, "d_level", 9) < 2:
            return
        with ExitStack() as st:
            T_ = lambda n, s, d=F32: st.enter_context(nc.sbuf_tensor(self.uniq(n), s, d))
            P_ = lambda n, s, d=F32: st.enter_context(nc.psum_tensor(self.uniq(n), s, d))
            ident16, ID16 = self.common_tiles(st)
            gt = self.gated_tiles(st, ident16, ID16)
            identf = self.C("ident")
            cio, _ = COFF["chunkind"]
            negA = T_("negA", [128, 8]); NEGA = Buf()
            ao, _ = ROWS["a_log"]
            dto, _ = ROWS["dt_b"]
            fw.op("act", lambda: nc.scalar.activation(out=negA[:], in_=self.rowp[:, ao:ao + 8], func=AF.Exp), reads=[self.ROWP], writes=[NEGA])
            fw.op("dve", lambda: nc.vector.tensor_scalar(out=negA[:], in0=negA[:], scalar1=-1.0, scalar2=None, op0=ALU.mult), reads=[NEGA], writes=[NEGA])
            hm64 = self.C("chunkind")
            Sm = [T_("Sm%d" % i, [128, 128]) for i in range(2)]; SM = fw.bufs(2)
            qTp = [T_("qTp%d" % i, [128, 128]) for i in range(2)]; QTP = fw.bufs(2)
            kTp = [T_("kTp%d" % i, [128, 128]) for i in range(2)]; KTP = fw.bufs(2)
            kTm = [T_("kTm%d" % i, [128, 128]) for i in range(4)]; KTM = fw.bufs(4)
            ktok = T_("ktok", [128, 256]); KTOK = Buf()
            vtok = T_("vtok", [128, 256]); VTOK = Buf()
            bag = T_("bag", [128, 272]); BAG = Buf()
            sm = T_("sm", [128, 24]); SMB = Buf()
            G = T_("G", [128, 128]); GB_ = Buf()
            gch = T_("gch", [128, 4]); GCH = Buf()
            dm = T_("dm", [128, 128]); DM = Buf()
            dmT = T_("dmT", [128, 128]); DMT = Buf()
            dcs = T_("dcs", [128, 128]); DCS = Buf()
            e1b = T_("e1b", [128, 128]); E1B = Buf()
            gcs = T_("gcs", [128, 12]); GCS = Buf()
            Pm = [T_("Pm%d" % i, [128, 128]) for i in range(2)]; PM = fw.bufs(2)
            Qm = [T_("Qm%d" % i, [128, 128]) for i in range(2)]; QM = fw.bufs(2)
            Rm = [T_("Rm%d" % i, [128, 128]) for i in range(2)]; RM = fw.bufs(2)
            aqkc = T_("aqkc", [128, 2, 4, 128]); AQKC = Buf()
            vb = T_("vb", [128, 64]); VBB = Buf()
            kbgm = [T_("kbgm%d" % i, [128, 128]) for i in range(4)]; KBGM = fw.bufs(4)
            kend = T_("kend", [128, 2, 256]); KEND = Buf()
            qin = [T_("qin%d" % i, [128, 128]) for i in range(2)]; QIN = fw.bufs(2)
            gendp = [T_("gendp%d" % i, [128, 2]) for i in range(2)]; GENDP = fw.bufs(2)
            usb = T_("usb", [128, 256]); USB = Buf()
            wT = [T_("wT%d" % i, [128, 128]) for i in range(2)]; WT = fw.bufs(2)
            vnew = T_("vnew", [128, 256]); VNEW = Buf()
            tmp = T_("tmp", [128, 128]); TMP = Buf()
            osb = T_("osb", [128, 256]); OSB = Buf()
            of = T_("of", [128, 256]); OF = Buf()
            dd = P_("dd", [128, 512]); DD = Buf()
            kk = P_("kk", [128, 512]); KK = Buf()
            ch = P_("ch", [128, 512]); CH = Buf()
            rr = P_("rr", [128, 512]); RR = Buf()
            uw = P_("uw", [128, 512]); UW = Buf()
            wsp = P_("wsp", [128, 512]); WSP = Buf()
            oo = P_("oo", [128, 512]); OO = Buf()
            fw.psum(DD, KK, CH, RR, UW, WSP, OO)
            for h in range(4):
                fw.op("pool", lambda: nc.gpsimd.memset(kbgm[h][:], 0.0), writes=[KBGM[h]])
            onescol = self.C("ones")[:, 0:1]
            for d in range(2):
                nm = "f" if d == 0 else "r"
                order = list(range(NT)) if d == 0 else [1, 0] + list(range(NT - 1, 1, -1))
                if getattr(self, "nt_dbg", None):
                    order = [t for t in order if t < self.nt_dbg]
                for hp in range(2):
                    fw.op("pool", lambda: nc.gpsimd.memset(Sm[hp][:], 0.0), writes=[SM[hp]])
                for t in order:
                    tk = slice(t * 128, (t + 1) * 128)
                    for hp in range(2):
                        fw.dma(qTp[hp][:], S["dn_qT"][0][hp * 128:(hp + 1) * 128, tk], reads=[S["dn_qT"][1]], writes=[QTP[hp]])
                        fw.dma(kTp[hp][:], S["dn_kT"][0][hp * 128:(hp + 1) * 128, tk], reads=[S["dn_kT"][1]], writes=[KTP[hp]])
                    fw.dma(ktok[:], S["dn_ktok"][0][tk, :], reads=[S["dn_ktok"][1]], writes=[KTOK])
                    fw.dma(vtok[:], S["dn_vtok"][0][tk, :], reads=[S["dn_vtok"][1]], writes=[VTOK])
                    fw.dma(bag[:], S["tokmaj"][0][tk, 256:528], reads=[S["tokmaj"][1]], writes=[BAG])
                    fw.op("act", lambda: nc.scalar.activation(out=sm[:, 0:4], in_=bag[:, d * 4:d * 4 + 4], func=AF.Sigmoid), reads=[BAG], writes=[SMB])
                    fw.op("dve", lambda: nc.vector.tensor_scalar(out=sm[:, 4:8], in0=sm[:, 0:4], scalar1=-1.0, scalar2=None, op0=ALU.mult), reads=[SMB], writes=[SMB])
                    fw.op("dve", lambda: nc.vector.tensor_tensor(out=sm[:, 8:12], in0=bag[:, 8 + d * 4:12 + d * 4], in1=self.rowp[:, dto + d * 4:dto + d * 4 + 4], op=ALU.add),
                          reads=[BAG, self.ROWP], writes=[SMB])
                    fw.op("act", lambda: nc.scalar.activation(out=sm[:, 8:12], in_=sm[:, 8:12], func=AF.Exp), reads=[SMB], writes=[SMB])
                    fw.op("act", lambda: nc.scalar.activation(out=sm[:, 8:12], in_=sm[:, 8:12], func=AF.Ln, bias=onescol, scale=1.0), reads=[SMB, self.CST], writes=[SMB])
                    fw.op("dve", lambda: nc.vector.tensor_tensor(out=sm[:, 12:16], in0=sm[:, 8:12], in1=negA[:, d * 4:d * 4 + 4], op=ALU.mult), reads=[SMB, NEGA], writes=[SMB])
                    for h in range(4):
                        hp, hh = h // 2, h % 2
                        hs = slice(h * 64, (h + 1) * 64)
                        rows = slice(hh * 64, (hh + 1) * 64)
                        gcol = sm[:, 12 + h:13 + h]
                        fw.op("dve", lambda: nc.vector.tensor_scalar(out=kTm[h][:], in0=kTp[hp][:], scalar1=hm64[:, hh:hh + 1], scalar2=None, op0=ALU.mult),
                              reads=[KTP[hp], self.CST], writes=[KTM[h]])
                        fw.op("dve", lambda: nc.vector.tensor_scalar(out=G[:], in0=self.C("tri_" + nm), scalar1=gcol, scalar2=None, op0=ALU.mult), reads=[self.CST, SMB], writes=[GB_])
                        fw.op("dve", lambda: nc.vector.tensor_scalar(out=gch[:, 0:2], in0=self.C("chunkind"), scalar1=gcol, scalar2=None, op0=ALU.mult), reads=[self.CST, SMB], writes=[GCH])
                        fw.op("dve", lambda: nc.vector.tensor_scalar(out=gch[:, 2:4], in0=self.C("ones")[:, 0:2], scalar1=gcol, scalar2=None, op0=ALU.mult), reads=[self.CST, SMB], writes=[GCH])
                        mm = lambda out, lhsT, rhs, st_, sp_, rd: fw.op("pe", lambda: nc.tensor.matmul(out, lhsT=lhsT, rhs=rhs, start=st_, stop=sp_), reads=rd, writes=[DD])
                        mm(dd[:, 0:128], G[:], self.C("ones"), True, False, [GB_, self.CST])
                        mm(dd[:, 0:128], self.C("negones"), G[:], False, True, [GB_, self.CST])
                        mm(dd[:, 128:256], self.C("ones"), G[:], True, False, [GB_, self.CST])
                        mm(dd[:, 128:256], G[:], self.C("negones"), False, True, [GB_, self.CST])
                        mm(dd[:, 256:384], self.C("ones"), G[:], True, True, [GB_, self.CST])
                        mm(dd[:, 384:386], G[:], self.C("ones")[:, 0:2], True, True, [GB_, self.CST])
                        mm(dd[:, 386:388], self.C("blk"), gch[:, 2:4], True, True, [GCH, self.CST])
                        mm(dd[:, 388:390], self.C("ones"), gch[:, 0:2], True, True, [GCH, self.CST])
                        fw.op("dve", lambda: nc.vector.tensor_scalar(out=dm[:], in0=dd[:, 0:128], scalar1=0.0, scalar2=-40.0, op0=ALU.min, op1=ALU.max), reads=[DD], writes=[DM])
                        fw.op("pool", lambda: nc.gpsimd.tensor_tensor(out=dm[:], in0=dm[:], in1=self.C("negc_" + nm), op=ALU.add), reads=[DM, self.CST], writes=[DM])
                        fw.op("act", lambda: nc.scalar.activation(out=dm[:], in_=dm[:], func=AF.Exp), reads=[DM], writes=[DM])
                        fw.op("pool", lambda: nc.gpsimd.tensor_tensor(out=dcs[:], in0=dm[:], in1=self.C("strict_" + nm), op=ALU.mult), reads=[DM, self.CST], writes=[DCS])
                        fw.op("dve", lambda: nc.vector.tensor_scalar(out=dmT[:], in0=dd[:, 128:256], scalar1=0.0, scalar2=-40.0, op0=ALU.min, op1=ALU.max), reads=[DD], writes=[DMT])
                        fw.op("pool", lambda: nc.gpsimd.tensor_tensor(out=dmT[:], in0=dmT[:], in1=self.C("negcT_" + nm), op=ALU.add), reads=[DMT, self.CST], writes=[DMT])
                        fw.op("act", lambda: nc.scalar.activation(out=dmT[:], in_=dmT[:], func=AF.Exp), reads=[DMT], writes=[DMT])
                        fw.op("dve", lambda: nc.vector.tensor_scalar(out=e1b[:], in0=dd[:, 256:384], scalar1=-40.0, scalar2=None, op0=ALU.max), reads=[DD], writes=[E1B])
                        fw.op("act", lambda: nc.scalar.activation(out=e1b[:], in_=e1b[:], func=AF.Exp), reads=[E1B], writes=[E1B])
                        fw.op("act", lambda: nc.scalar.copy(out=gcs[:, 0:6], in_=dd[:, 384:390]), reads=[DD], writes=[GCS])
                        fw.op("dve", lambda: nc.vector.tensor_tensor(out=gcs[:, 6:7], in0=gcs[:, 2:3], in1=gcs[:, 0:1], op=ALU.subtract), reads=[GCS], writes=[GCS])
                        fw.op("dve", lambda: nc.vector.tensor_scalar(out=gcs[:, 0:7], in0=gcs[:, 0:7], scalar1=-40.0, scalar2=None, op0=ALU.max), reads=[GCS], writes=[GCS])
                        fw.op("act", lambda: nc.scalar.activation(out=gcs[:, 7:8], in_=gcs[:, 0:1], func=AF.Exp), reads=[GCS], writes=[GCS])
                        fw.op("act", lambda: nc.scalar.activation(out=gcs[:, 6:7], in_=gcs[:, 6:7], func=AF.Exp), reads=[GCS], writes=[GCS])
                        fw.op("act", lambda: nc.scalar.activation(out=gcs[:, 8:10], in_=gcs[:, 4:6], func=AF.Exp), reads=[GCS], writes=[GCS])
                        fw.op("pool", lambda: nc.gpsimd.tensor_copy(out=gendp[hp][rows, :], in_=gcs[rows, 8:10]), reads=[GCS], writes=[GENDP[hp]])
                        if getattr(self, "d_level", 9) < 3:
                            continue
                        fw.op("pe", lambda: nc.tensor.matmul(kk[:, 0:128], lhsT=kTm[h][:], rhs=kTp[hp][:], start=True, stop=True), reads=[KTM[h], KTP[hp]], writes=[KK])
                        fw.op("pe", lambda: nc.tensor.matmul(kk[:, 128:256], lhsT=kTm[h][:], rhs=qTp[hp][:], start=True, stop=True), reads=[KTM[h], QTP[hp]], writes=[KK])
                        fw.op("dve", lambda: nc.vector.scalar_tensor_tensor(out=Pm[0][:], in0=kk[:, 0:128], scalar=sm[:, 4 + h:5 + h], in1=dcs[:], op0=ALU.mult, op1=ALU.mult),
                              reads=[KK, SMB, DCS], writes=[PM[0]])
                        for c in range(2):
                            fw.op("dve", lambda: nc.vector.scalar_tensor_tensor(out=aqkc[:, c, h, :], in0=kk[:, 128:256], scalar=self.cst[:, cio + c:cio + c + 1], in1=dmT[:],
                                                                                op0=ALU.mult, op1=ALU.mult), reads=[KK, self.CST, DMT], writes=[AQKC])
                        if getattr(self, "d_level", 9) < 3.2:
                            continue
                        fw.op("pe", lambda: nc.tensor.transpose(rr[:, 0:128], Pm[0][:], identf), reads=[PM[0], self.CST], writes=[RR])
                        fw.op("act", lambda: nc.scalar.copy(out=Qm[0][:], in_=rr[:, 0:128]), reads=[RR], writes=[QM[0]])
                        fw.op("pool", lambda: nc.gpsimd.tensor_tensor(out=Rm[0][:], in0=Qm[0][:], in1=identf, op=ALU.add), reads=[QM[0], self.CST], writes=[RM[0]])
                        cur = 0
                        rc = 0
                        for s_ in range(1, 6 if getattr(self, "d_level", 9) >= 3.3 else 1):
                            nx = 1 - cur
                            fw.op("pe", lambda: nc.tensor.matmul(ch[:, 0:128], lhsT=Qm[cur][:], rhs=Pm[cur][:], start=True, stop=True), reads=[QM[cur], PM[cur]], writes=[CH])
                            if s_ < 5:
                                fw.op("pe", lambda: nc.tensor.matmul(ch[:, 128:256], lhsT=Pm[cur][:], rhs=Qm[cur][:], start=True, stop=True), reads=[QM[cur], PM[cur]], writes=[CH])
                            fw.op("act", lambda: nc.scalar.copy(out=Pm[nx][:], in_=ch[:, 0:128]), reads=[CH], writes=[PM[nx]])
                            if s_ < 5:
                                fw.op("dve", lambda: nc.vector.tensor_copy(out=Qm[nx][:], in_=ch[:, 128:256]), reads=[CH], writes=[QM[nx]])
                            fw.op("pe", lambda: nc.tensor.matmul(rr[:, 128:256], lhsT=Pm[nx][:], rhs=Rm[rc][:], start=True, stop=True), reads=[PM[nx], RM[rc]], writes=[RR])
                            fw.op("dve", lambda: nc.vector.tensor_tensor(out=Rm[1 - rc][:], in0=rr[:, 128:256], in1=Rm[rc][:], op=ALU.add), reads=[RR, RM[rc]], writes=[RM[1 - rc]])
                            rc = 1 - rc
                            cur = nx
                        R = Rm[rc]; RB = RM[rc]
                        if getattr(self, "d_level", 9) < 3.4:
                            continue
                        fw.op("pool", lambda: nc.gpsimd.tensor_scalar(out=vb[:], in0=vtok[:, hs], scalar1=sm[:, h:h + 1], scalar2=None, op0=ALU.mult), reads=[VTOK, SMB], writes=[VBB])
                        fw.op("dve", lambda: nc.vector.tensor_scalar(out=kbgm[h][:, rows], in0=ktok[:, hs], scalar1=sm[:, h:h + 1], scalar2=gcs[:, 7:8], op0=ALU.mult, op1=ALU.mult),
                              reads=[KTOK, SMB, GCS], writes=[KBGM[h]])
                        for c in range(2):
                            fw.op("dve", lambda: nc.vector.tensor_scalar(out=kend[:, c, hs], in0=ktok[:, hs], scalar1=self.cst[:, cio + c:cio + c + 1], scalar2=gcs[:, 6:7], op0=ALU.mult, op1=ALU.mult),
                                  reads=[KTOK, self.CST, GCS], writes=[KEND])
                        fw.op("pool", lambda: nc.gpsimd.tensor_tensor(out=qin[hp][rows, :], in0=qTp[hp][rows, :], in1=e1b[rows, :], op=ALU.mult), reads=[QTP[hp], E1B], writes=[QIN[hp]])
                        if getattr(self, "d_level", 9) < 3.5:
                            continue
                        fw.op("pe", lambda: nc.tensor.matmul(kk[:, 256 + h * 64:256 + (h + 1) * 64], lhsT=R[:], rhs=vb[:], start=True, stop=True), reads=[RB, VBB], writes=[KK])
                        fw.op("pe", lambda: nc.tensor.matmul(uw[:, 256 + hp * 128:256 + (hp + 1) * 128], lhsT=kbgm[h][:], rhs=R[:], start=(hh == 0), stop=(hh == 1)),
                              reads=[KBGM[h], RB], writes=[UW])
                    if getattr(self, "d_level", 9) < 4:
                        continue
                    fw.op("act", lambda: nc.scalar.copy(out=usb[:], in_=kk[:, 256:512]), reads=[KK], writes=[USB])
                    for hp in range(2):
                        fw.op("act", lambda: nc.scalar.copy(out=wT[hp][:], in_=uw[:, 256 + hp * 128:256 + (hp + 1) * 128]), reads=[UW], writes=[WT[hp]])
                    for c in ((0, 1) if d == 0 else (1, 0)):
                        r0 = 64 * c
                        for hp in range(2):
                            fw.op("pe", lambda: nc.tensor.matmul(wsp[:, hp * 128:(hp + 1) * 128], lhsT=wT[hp][:], rhs=Sm[hp][:], start=True, stop=True), reads=[WT[hp], SM[hp]], writes=[WSP])
                        fw.op("dve", lambda: nc.vector.tensor_tensor(out=vnew[:], in0=usb[:], in1=wsp[:, 0:256], op=ALU.subtract), reads=[USB, WSP], writes=[VNEW])
                        for h in range(4):
                            hp, hh = h // 2, h % 2
                            hs = slice(h * 64, (h + 1) * 64)
                            fw.op("pe", lambda: nc.tensor.matmul(oo[:, hs], lhsT=qin[hp][:], rhs=Sm[hp][:, hh * 64:(hh + 1) * 64], start=True, stop=False), reads=[QIN[hp], SM[hp]], writes=[OO])
                            fw.op("pe", lambda: nc.tensor.matmul(oo[:, hs], lhsT=aqkc[:, c, h, :], rhs=vnew[:, hs], start=False, stop=True), reads=[AQKC, VNEW], writes=[OO])
                        for hp in range(2):
                            fw.op("pe", lambda: nc.tensor.matmul(wsp[:, 256 + hp * 128:256 + (hp + 1) * 128], lhsT=kend[:, c, hp * 128:(hp + 1) * 128], rhs=vnew[:, hp * 128:(hp + 1) * 128],
                                                                 start=True, stop=True), reads=[KEND, VNEW], writes=[WSP])
                        for hp in range(2):
                            fw.op("dve", lambda: nc.vector.tensor_tensor(out=tmp[:], in0=wsp[:, 256 + hp * 128:256 + (hp + 1) * 128], in1=self.C("bmask2"), op=ALU.mult), reads=[WSP, self.CST], writes=[TMP])
                            fw.op("dve", lambda: nc.vector.scalar_tensor_tensor(out=Sm[hp][:], in0=Sm[hp][:], scalar=gendp[hp][:, c:c + 1], in1=tmp[:], op0=ALU.mult, op1=ALU.add),
                                  reads=[SM[hp], GENDP[hp], TMP], writes=[SM[hp]])
                        fw.op("act", lambda: nc.scalar.copy(out=osb[r0:r0 + 64, :], in_=oo[r0:r0 + 64, 0:256]), reads=[OO], writes=[OSB])
                    if d == 0:
                        fw.dma(S["ofwd"][0][tk, :], osb[:], reads=[OSB], writes=[S["ofwd"][1]])
                    else:
                        fw.dma(of[:], S["ofwd"][0][tk, :], reads=[S["ofwd"][1]], writes=[OF])
                        fw.op("pool", lambda: nc.gpsimd.tensor_tensor(out=osb[:], in0=osb[:], in1=of[:], op=ALU.add), reads=[OSB, OF], writes=[OSB])
                        self.gated_out(st, gt, osb, OSB, bag[:, 16:272], BAG, "dn_on", 512, t)
            fw.barrier()

    def load_cast(self, st, name, src_ap_fn, nk, ncols, blk, eng="pool"):
        fw, nc = self.fw, self.nc
        w = st.enter_context(nc.sbuf_tensor(self.uniq(name), [128, nk, ncols], BF16))
        W = Buf()
        with ExitStack() as st2:
            stg = [st2.enter_context(nc.sbuf_tensor(self.uniq(name + "s"), [128, nk, blk], F32)) for i in range(2)]
            STG = self.fw.bufs(2)
            nb = (ncols + blk - 1) // blk
            for cb in range(nb):
                b = cb % 2
                c0 = cb * blk
                n = min(blk, ncols - c0)
                fw.dma(stg[b][:, :, 0:n], src_ap_fn(c0, n), writes=[STG[b]])
                e = ("pool", "dve")[cb % 2] if eng == "both" else eng
                if e == "pool":
                    fw.op("pool", lambda: nc.gpsimd.tensor_copy(out=w[:, :, c0:c0 + n], in_=stg[b][:, :, 0:n]), reads=[STG[b]], writes=[W])
                else:
                    fw.op("dve", lambda: nc.vector.tensor_copy(out=w[:, :, c0:c0 + n], in_=stg[b][:, :, 0:n]), reads=[STG[b]], writes=[W])
            fw.barrier()
        return w, W

    def phaseF(self, l, xsrc):
        fw, nc, I = self.fw, self.nc, self.I
        S = self.scr
        last = (l == DEPTH - 1)
        with ExitStack() as st:
            T_ = lambda n, s, d=F32: st.enter_context(nc.sbuf_tensor(self.uniq(n), s, d))
            P_ = lambda n, s, d=F32: st.enter_context(nc.psum_tensor(self.uniq(n), s, d))
            ident16, ID16 = self.common_tiles(st)
            wout, WOUT = self.load_cast(st, "wout", lambda c0, n: I["w_out"][l, :, c0:c0 + n].rearrange("(k p) c -> p k c", p=128), 8, D, 256)
            zt = T_("zt", [128, 8, 128], BF16)
            ZT = Buf()
            fw.op("pool", lambda: nc.gpsimd.memset(zt[:], 0.0), writes=[ZT])
            h2 = S["h2T"][0].rearrange("(k p) t -> p k t", p=128)
            H2 = S["h2T"][1]
            fw.dma(h2[:, :, 0:1], zt[:, :, 0:1], reads=[ZT], writes=[H2], allow_slow_non_contiguous=True)
            fw.dma(h2[:, :, 257:259], zt[:, :, 0:2], reads=[ZT], writes=[H2], allow_slow_non_contiguous=True)
            fw.dma(h2[:, :, 4355:4355 + 128], zt[:, :, :], reads=[ZT], writes=[H2])
            fw.dma(h2[:, :, 4483:4483 + 109], zt[:, :, 0:109], reads=[ZT], writes=[H2])
            mx = [T_("mx%d" % i, [128, 8, 128], BF16) for i in range(2)]
            MX = fw.bufs(2)
            xt = [T_("xt%d" % i, [128, 1024]) for i in range(2)]
            XT = fw.bufs(2)
            x1 = [T_("x1t%d" % i, [128, 1024]) for i in range(2)]
            X1 = fw.bufs(2)
            tmp = [T_("tmp%d" % i, [128, 512]) for i in range(2)]
            TMP = fw.bufs(2)
            hT = [T_("h2t%d" % i, [128, 8, 128], BF16) for i in range(2)]
            HT = fw.bufs(2)
            pst = [P_("pst%d" % i, [128, 1024], BF16) for i in range(2)]
            PST = fw.bufs(2)
            mp = [P_("mp%d" % i, [128, 512]) for i in range(4)]
            MP = fw.bufs(4)
            fw.psum(PST, MP)
            mixv = S["mixT"][0].rearrange("(k p) t -> p k t", p=128)
            for t in range(NT):
                b = t % 2
                s = 1 if t < 2 else 0
                fw.dma(mx[b][:], mixv[:, :, t * 128:(t + 1) * 128], reads=[S["mixT"][1]], writes=[MX[b]])
                fw.dma(xt[b][:], xsrc[t * 128:(t + 1) * 128, :], writes=[XT[b]])
                for hc in range(2):
                    pi = (2 * t + hc) % 4
                    for k in range(8):
                        fw.op("pe", lambda: nc.tensor.matmul(mp[pi][:, :], lhsT=mx[b][:, k, :], rhs=wout[:, k, hc * 512:(hc + 1) * 512],
                                                             start=(k == 0), stop=(k == 7)), reads=[MX[b], WOUT], writes=[MP[pi]])
                    fw.op("dve", lambda: nc.vector.tensor_tensor(out=tmp[hc][:], in0=mp[pi][:, :], in1=self.gb[:, s, hc * 512:(hc + 1) * 512], op=ALU.mult),
                          reads=[MP[pi], self.GB], writes=[TMP[hc]])
                    fw.op("pool", lambda: nc.gpsimd.tensor_tensor(out=x1[b][:, hc * 512:(hc + 1) * 512], in0=xt[b][:, hc * 512:(hc + 1) * 512], in1=tmp[hc][:], op=ALU.add),
                          reads=[TMP[hc], XT[b]], writes=[X1[b]])
                fw.dma(S["x1"][0][t * 128:(t + 1) * 128, :], x1[b][:], reads=[X1[b]], writes=[S["x1"][1]])
                self.norm_mod_T(st, 1, x1[b][:], X1[b], s, hT[b], HT[b], 0, "F", pst[b], PST[b], ident16, ID16)
                pos = (F_CTX0 + t * 128) if t < 2 else (F_LAT0 + (t - 2) * 128)
                fw.dma(h2[:, :, pos:pos + 128], hT[b][:], reads=[HT[b]], writes=[H2])
            fw.barrier()
        for half in range(2):
            with ExitStack() as st:
                T_ = lambda n, s, d=F32: st.enter_context(nc.sbuf_tensor(self.uniq(n), s, d))
                P_ = lambda n, s, d=F32: st.enter_context(nc.psum_tensor(self.uniq(n), s, d))
                NJ = 11
                a0 = half * NJ * 128
                g0 = DFF + half * NJ * 128
                wua, WUA = self.load_cast(st, "wua", lambda c0, n: I["w_up"][l, :, a0 + c0:a0 + c0 + n].rearrange("(k p) c -> p k c", p=128), 8, NJ * 128, 352, eng="both")
                wug, WUG = self.load_cast(st, "wug", lambda c0, n: I["w_up"][l, :, g0 + c0:g0 + c0 + n].rearrange("(k p) c -> p k c", p=128), 8, NJ * 128, 352, eng="both")
                wdn, WDN = self.load_cast(st, "wdn", lambda c0, n: I["w_down"][l, half * NJ * 128:(half + 1) * NJ * 128, c0:c0 + n].rearrange("(k p) c -> p k c", p=128), NJ, D, 256, eng="both")
                xin_ap = S["x1"][0] if half == 0 else S["xa"][0]
                XIN = S["x1"][1] if half == 0 else S["xa"][1]
                hw = [T_("hw%d" % i, [128, 8, 512], BF16) for i in range(2)]
                HW = fw.bufs(2)
                zT = T_("zT", [128, NJ, 512], BF16)
                ZT = Buf()
                ya = [T_("ya%d" % i, [128, 512]) for i in range(2)]
                YA = fw.bufs(2)
                yg = [T_("yg%d" % i, [128, 512]) for i in range(2)]
                YG = fw.bufs(2)
                sgt = [T_("sgt%d" % i, [128, 512]) for i in range(2)]
                SGT = fw.bufs(2)
                xt = [T_("fxt%d" % i, [128, 1024]) for i in range(2)]
                XT = fw.bufs(2)
                tmp = [T_("ftmp%d" % i, [128, 512]) for i in range(2)]
                TMP = fw.bufs(2)
                up = [P_("up%d" % i, [128, 512]) for i in range(4)]
                UP = fw.bufs(4)
                dp = [P_("dp%d" % i, [128, 512]) for i in range(2)]
                DP = fw.bufs(2)
                fw.psum(UP, DP)
                h2 = S["h2T"][0].rearrange("(k p) t -> p k t", p=128)
                fo, _ = COLS["ffn_w"]
                it = 0
                for fb in range(NFB):
                    b = fb % 2
                    c0 = fb * FB
                    fw.dma(hw[b][:], h2[:, :, c0:c0 + 512], reads=[S["h2T"][1]], writes=[HW[b]])
                    for j in range(NJ):
                        jb = j % 2
                        ja = half * NJ + j
                        for (which, wt, WT, y, Y, cj) in ((0, wua, WUA, ya[jb], YA[jb], ja), (1, wug, WUG, yg[jb], YG[jb], 22 + ja)):
                            pi = it % 4
                            it += 1
                            for k in range(8):
                                fw.op("pe", lambda: nc.tensor.matmul(up[pi][:, :], lhsT=wt[:, k, j * 128:(j + 1) * 128], rhs=hw[b][:, k, :],
                                                                     start=(k == 0), stop=(k == 7)), reads=[WT, HW[b]], writes=[UP[pi]])
                            wc = lambda tap: self.colp[:, fo + cj * 3 + tap:fo + cj * 3 + tap + 1]
                            fw.op("act", lambda: nc.scalar.activation(out=y[:, 0:FB], in_=up[pi][:, 0:FB], func=AF.Identity, bias=0.0, scale=wc(0)),
                                  reads=[UP[pi], self.COLP], writes=[Y])
                            fw.op("dve", lambda: nc.vector.scalar_tensor_tensor(out=y[:, 0:FB], in0=up[pi][:, 1:FB + 1], scalar=wc(1), in1=y[:, 0:FB], op0=ALU.mult, op1=ALU.add),
                                  reads=[UP[pi], self.COLP, Y], writes=[Y])
                            fw.op("dve", lambda: nc.vector.scalar_tensor_tensor(out=y[:, 0:FB], in0=up[pi][:, 2:FB + 2], scalar=wc(2), in1=y[:, 0:FB], op0=ALU.mult, op1=ALU.add),
                                  reads=[UP[pi], self.COLP, Y], writes=[Y])
                        fw.op("act", lambda: nc.scalar.activation(out=sgt[jb][:, 0:FB], in_=yg[jb][:, 0:FB], func=AF.Silu), reads=[YG[jb]], writes=[SGT[jb]])
                        fw.op("pool", lambda: nc.gpsimd.tensor_tensor(out=zT[:, j, 0:FB], in0=sgt[jb][:, 0:FB], in1=ya[jb][:, 0:FB], op=ALU.mult),
                              reads=[SGT[jb], YA[jb]], writes=[ZT])
                    for sub in range(4):
                        q0 = fb * FB + 1 + sub * 128
                        n = min(128, FB - sub * 128)
                        segs = []
                        for (p0, p1, t0) in ((F_CTX0, F_CTX0 + CTX, 0), (F_LAT0, F_LAT0 + SEQ, CTX)):
                            lo = max(q0, p0)
                            hi = min(q0 + n, p1)
                            if hi > lo:
                                segs.append((lo - q0, hi - lo, t0 + lo - p0))
                        if not segs:
                            continue
                        s = 1 if q0 < F_CTX0 + CTX else 0
                        xb = (fb * 4 + sub) % 2
                        for (r0, nr, tok0) in segs:
                            fw.dma(xt[xb][r0:r0 + nr, :], xin_ap[tok0:tok0 + nr, :], reads=[XIN], writes=[XT[xb]])
                        for hc in range(2):
                            for j in range(NJ):
                                fw.op("pe", lambda: nc.tensor.matmul(dp[hc][0:n, :], lhsT=zT[:, j, sub * 128:sub * 128 + n], rhs=wdn[:, j, hc * 512:(hc + 1) * 512],
                                                                     start=(j == 0), stop=(j == NJ - 1)), reads=[ZT, WDN], writes=[DP[hc]])
                            fw.op("dve", lambda: nc.vector.tensor_tensor(out=tmp[hc][0:n, :], in0=dp[hc][0:n, :], in1=self.gb[0:n, 2 + s, hc * 512:(hc + 1) * 512], op=ALU.mult),
                                  reads=[DP[hc], self.GB], writes=[TMP[hc]])
                            fw.op("pool", lambda: nc.gpsimd.tensor_tensor(out=xt[xb][0:n, hc * 512:(hc + 1) * 512], in0=xt[xb][0:n, hc * 512:(hc + 1) * 512], in1=tmp[hc][0:n, :], op=ALU.add),
                                  reads=[TMP[hc], XT[xb]], writes=[XT[xb]])
                        for (r0, nr, tok0) in segs:
                            if half == 0:
                                fw.dma(S["xa"][0][tok0:tok0 + nr, :], xt[xb][r0:r0 + nr, :], reads=[XT[xb]], writes=[S["xa"][1]])
                            elif not last:
                                fw.dma(S["xres"][0][tok0:tok0 + nr, :], xt[xb][r0:r0 + nr, :], reads=[XT[xb]], writes=[S["xres"][1]])
                            elif tok0 >= CTX:
                                fw.dma(self.out[tok0 - CTX:tok0 - CTX + nr, :], xt[xb][r0:r0 + nr, :], reads=[XT[xb]], writes=[self.OUTB])
                            if self.dbg and half == 1 and last:
                                fw.dma(S["xres"][0][tok0:tok0 + nr, :], xt[xb][r0:r0 + nr, :], reads=[XT[xb]], writes=[S["xres"][1]])
                fw.barrier()


def _host_inputs(inp):
    rope = _rope_tables()
    per_core = []
    packed = []
    for b in range(8):
        cvec = np.stack([inp["c"][b], inp["c_ctx"]], axis=0).astype(np.float32)
        cols, rows, w2p = [], [], []
        for l in range(DEPTH):
            c_, r_, w_ = _pack_params(inp, l, cvec)
            cols.append(c_); rows.append(r_); w2p.append(w_)
        xin = np.concatenate([inp["ctx"][b], inp["x"][b]], axis=0).astype(np.float32)
        per_core.append({
            "xin": np.ascontiguousarray(xin), "consts": CONSTS, "rope": rope,
            "cols": np.stack(cols), "rows": np.stack(rows), "w2p": np.stack(w2p),
            "w_mod": inp["w_mod"], "w_in": inp["w_in"], "w_out": inp["w_out"],
            "w_up": inp["ffn_w_up"], "w_down": inp["ffn_w_down"],
        })
    return per_core


def kernel(**inputs):
    inp = {k: np.asarray(v) for k, v in inputs.items()}
    bld = Builder(phases=PHASES)
    nc = bld.build()
    in_maps = _host_inputs(inp)
    res = run_bass_kernel_spmd(nc, in_maps, core_ids=list(range(8)))
    out = np.stack([np.asarray(r["out"]) for r in res.results], axis=0)
    return out.astype(np.float32)
```

```python
import math
import numpy as np
from contextlib import ExitStack
import concourse.bass as bass
import concourse.mybir as mybir
from concourse.bass_utils import run_bass_kernel_spmd

F32 = mybir.dt.float32
BF16 = mybir.dt.bfloat16
ALU = mybir.AluOpType
AF = mybir.ActivationFunctionType
AX = mybir.AxisListType

D = 1024
SEQ = 4096
CTX = 256
T = SEQ + CTX
NT = T // 128
DEPTH = 2
INC = 3120
DFF = 2816
EPS = 1e-6
CMP = 15
TP = CMP + CTX + 2 * CMP + SEQ + CMP
CM_CTX0 = CMP
CM_LAT0 = CMP + CTX + 2 * CMP
DNP = 2
TPD = DNP + CTX + 2 * DNP + SEQ + DNP
DN_CTX0 = DNP
DN_LAT0 = DNP + CTX + 2 * DNP
NTM = 1072
FB = 510
NFB = 9
FPAD = 1 + NFB * FB + 1
F_CTX0 = 1
F_LAT0 = 1 + CTX + 2


class Buf:
    __slots__ = ("name", "lw", "rd", "ps")

    def __init__(self, name=""):
        self.name = name
        self.lw = None
        self.rd = []
        self.ps = False


class FW:
    NDMA = 40

    def __init__(self, nc, stack):
        self.nc = nc
        self.eng = {"pe": nc.tensor, "act": nc.scalar, "dve": nc.vector,
                    "pool": nc.gpsimd, "sp": nc.sync}
        self.sem = {}
        self.cnt = {}
        for e in self.eng:
            self.sem[e] = stack.enter_context(nc.semaphore("s_" + e))
            self.cnt[e] = 0
        self.dsem = [stack.enter_context(nc.semaphore("d%d" % i)) for i in range(self.NDMA)]
        self.dcnt = [0] * self.NDMA
        self.dnext = 0
        self.seen = {e: {} for e in self.eng}

    def buf(self, name=""):
        return Buf(name)

    def bufs(self, n, name=""):
        return [Buf(name + str(i)) for i in range(n)]

    def _semobj(self, key):
        return self.sem[key] if isinstance(key, str) else self.dsem[key]

    def _wait(self, e, ev):
        if ev is None:
            return
        key, val = ev
        if key == "pe" and e == "pe":
            return
        if key == e and val <= self.cnt[e] - 6:
            return
        if self.seen[e].get(key, 0) >= val:
            return
        self.seen[e][key] = val
        self.eng[e].wait_ge(self._semobj(key), val)

    def psum(self, *bufs):
        for b in bufs:
            if isinstance(b, (list, tuple)):
                self.psum(*b)
            else:
                b.ps = True

    def _deps(self, e, reads, writes):
        for b in reads:
            self._wait(e, b.lw)
            if b.ps:
                for ev in b.rd:
                    if ev[0] != e:
                        self._wait(e, ev)
        for b in writes:
            self._wait(e, b.lw)
            for ev in b.rd:
                self._wait(e, ev)

    def _commit(self, ev, reads, writes):
        for b in reads:
            b.rd.append(ev)
            if len(b.rd) > 48:
                best = {}
                for k, v in b.rd:
                    if best.get(k, 0) < v:
                        best[k] = v
                b.rd = list(best.items())
        for b in writes:
            b.lw = ev
            b.rd = []

    def op(self, e, fn, reads=(), writes=()):
        self._deps(e, reads, writes)
        ins = fn()
        self.cnt[e] += 1
        ins.then_inc(self.sem[e], 1)
        self._commit((e, self.cnt[e]), reads, writes)
        return ins

    def dma(self, out, in_, reads=(), writes=(), q="sp", **kw):
        k = self.dnext
        self.dnext = (self.dnext + 1) % self.NDMA
        if self.dcnt[k] > 0:
            self._wait(q, (k, self.dcnt[k]))
        self._deps(q, reads, writes)
        ins = self.eng[q].dma_start(out=out, in_=in_, **kw)
        self.dcnt[k] += 16
        ins.then_inc(self.dsem[k], 16)
        self._commit((k, self.dcnt[k]), reads, writes)
        return ins

    def barrier(self):
        for e in self.eng:
            for f in self.eng:
                if f != e and self.cnt[f] > 0:
                    self._wait(e, (f, self.cnt[f]))
            if self.cnt[e] > 0 and e != "pe" and self.seen[e].get(e, 0) < self.cnt[e]:
                self.seen[e][e] = self.cnt[e]
                self.eng[e].wait_ge(self.sem[e], self.cnt[e])
            for k in range(self.NDMA):
                if self.dcnt[k] > 0:
                    self._wait(e, (k, self.dcnt[k]))


def _consts():
    c = {}
    idx = np.arange(128)
    same = (idx[:, None] // 64) == (idx[None, :] // 64)
    le = idx[:, None] <= idx[None, :]
    ge = idx[:, None] >= idx[None, :]
    c["ident"] = np.eye(128)
    c["ones"] = np.ones((128, 128))
    c["negones"] = -np.ones((128, 128))
    c["blk"] = same.astype(np.float64)
    for d, (a_le, name) in enumerate(((le, "f"), (ge, "r"))):
        tri = (same & a_le).astype(np.float64)
        c["tri_" + name] = tri
        c["tris_" + name] = -tri / 16.0
        causal = tri.T
        c["negc_" + name] = np.where(causal > 0, 0.0, -30000.0)
        c["negcT_" + name] = np.where(tri > 0, 0.0, -30000.0)
        c["strict_" + name] = causal * (1 - np.eye(128))
        c["cT4_" + name] = np.tile(tri, (1, 4))
    c["blks"] = -same.astype(np.float64) / 16.0
    c["blk32"] = ((idx[:, None] // 32) == (idx[None, :] // 32)).astype(np.float64) / 32.0
    c["blk64"] = same.astype(np.float64)
    c["div256"] = np.ones((128, 128)) / 256.0
    perm = np.zeros((128, 128))
    for m in range(128):
        k = m + 16 if (m % 32) < 16 else m - 16
        perm[k, m] = 1.0
    c["perm"] = perm
    c["bmask4"] = ((idx[:, None] // 32) == (np.arange(256)[None, :] // 64)).astype(np.float64)
    c["bmask2"] = same.astype(np.float64)
    c["hm32"] = ((idx[:, None] // 32) == np.arange(4)[None, :]).astype(np.float64)
    ci = np.zeros((128, 2)); ci[:64, 0] = 1; ci[64:, 1] = 1
    c["chunkind"] = ci
    c["cc4"] = np.concatenate([ci, np.ones((128, 2))], axis=1)
    sel = np.zeros((128, 256)); sel[0, :128] = 1; sel[1, 128:] = 1
    c["sel"] = sel
    names = list(c.keys())
    offs = {}
    o = 0
    for n in names:
        offs[n] = (o, c[n].shape[1])
        o += c[n].shape[1]
    arr = np.concatenate([c[n] for n in names], axis=1).astype(np.float32)
    return arr, offs


def _rope_tables():
    rows = SEQ // 64
    row = np.repeat(np.arange(rows, dtype=np.float32), 64)
    col = np.tile(np.arange(64, dtype=np.float32), rows)
    nf = 8
    inv = (np.float32(10000.0) ** (-np.arange(nf, dtype=np.float32) / nf)).astype(np.float32)
    ang = np.concatenate([row[:, None] * inv, col[:, None] * inv], axis=-1).astype(np.float32)
    cos = np.cos(ang).astype(np.float32)
    sin = np.sin(ang).astype(np.float32)
    p = np.arange(128)
    ct = cos[:, p % 16].T
    st = sin[:, p % 16].T * np.where((p % 32) < 16, -1.0, 1.0)[:, None]
    return np.ascontiguousarray(np.concatenate([ct, st], axis=1).astype(np.float32))


CONSTS, COFF = _consts()
NCONST = CONSTS.shape[1]

COLS = {}
_o = 0
for _n, _w in (("b_mod", 48), ("n1g", 8), ("n2g", 8), ("cm_w", 62), ("cm_b", 2), ("cm_lg", 2), ("cm_lb", 2),
               ("qg", 1), ("kg", 1), ("dn_w", 30), ("ffn_w", 132), ("gla_b2", 2), ("ccol", 16)):
    COLS[_n] = (_o, _w)
    _o += _w
NCOL = _o
ROWS = {}
_o = 0
for _n, _w in (("subln", 64), ("dn_on", 64), ("gla_on", 64), ("a_log", 8), ("dt_b", 8), ("lam", 128),
               ("b_g1", 1024), ("b_g2", 1024), ("gla_b2r", 256)):
    ROWS[_n] = (_o, _w)
    _o += _w
NROW = _o


def _pack_params(inp, l, cvec):
    cols = np.zeros((128, NCOL), np.float32)

    def put(name, a):
        o, w = COLS[name]
        assert a.shape == (128, w), (name, a.shape)
        cols[:, o:o + w] = a

    put("b_mod", inp["b_mod"][l].reshape(48, 128).T)
    put("n1g", inp["norm1_g"][l].reshape(8, 128).T)
    put("n2g", inp["norm2_g"][l].reshape(8, 128).T)
    put("cm_w", inp["cm_conv_w"][l].reshape(31, 2, 128).transpose(2, 1, 0).reshape(128, 62))
    put("cm_b", inp["cm_conv_b"][l].reshape(2, 128).T)
    put("cm_lg", inp["cm_ln_g"][l].reshape(2, 128).T)
    put("cm_lb", inp["cm_ln_b"][l].reshape(2, 128).T)
    put("qg", np.tile(inp["da_qnorm_g"][l], 4)[:, None])
    put("kg", np.tile(inp["da_knorm_g"][l], 4)[:, None])
    put("dn_w", inp["dn_conv_w"][l].reshape(5, 6, 128).transpose(2, 1, 0).reshape(128, 30))
    put("ffn_w", inp["ffn_conv_w"][l].reshape(3, 44, 128).transpose(2, 1, 0).reshape(128, 132))
    put("gla_b2", inp["gla_b2"][l].T)
    put("ccol", cvec.reshape(2, 8, 128).transpose(2, 1, 0).reshape(128, 16))
    rows = np.zeros((1, NROW), np.float32)

    def putr(name, a):
        o, w = ROWS[name]
        rows[0, o:o + w] = a.reshape(-1)

    putr("subln", inp["da_subln_g"][l])
    putr("dn_on", inp["dn_onorm_g"][l])
    putr("gla_on", inp["gla_onorm_g"][l])
    putr("a_log", inp["dn_a_log"][l])
    putr("dt_b", inp["dn_dt_bias"][l])
    putr("lam", inp["da_lambda"][l])
    putr("b_g1", inp["b_mod"][l][2048:3072])
    putr("b_g2", inp["b_mod"][l][5120:6144])
    putr("gla_b2r", inp["gla_b2"][l])
    w2p = np.zeros((2, 32, 128), np.float32)
    w2p[0, 0:16] = inp["gla_w2"][l][0]
    w2p[1, 16:32] = inp["gla_w2"][l][1]
    return cols, rows, w2p


PHASES = None


class Builder:
    def __init__(self, dbg=False, phases=None):
        self.dbg = dbg
        self.phases = phases
        self.nc = bass.Bass("TRN2", target_bir_lowering=False)
        self.scr = {}

    def din(self, name, shape, dt=F32):
        return self.nc.dram_tensor(name, list(shape), dt, kind="ExternalInput").ap()

    def dscr(self, name, shape, dt=F32):
        kind = "ExternalOutput" if self.dbg else "Internal"
        if name in getattr(self, "dbg_inputs", ()):
            kind = "ExternalInput"
        t = self.nc.dram_tensor(name, list(shape), dt, kind=kind).ap()
        self.scr[name] = (t, Buf(name))
        return t

    def uniq(self, n):
        self._u = getattr(self, "_u", 0) + 1
        return "%s_%d" % (n, self._u)

    def want(self, ph):
        return self.phases is None or ph in self.phases

    def build(self):
        nc = self.nc
        I = {}
        I["xin"] = self.din("xin", [T, D])
        I["consts"] = self.din("consts", [128, NCONST])
        I["rope"] = self.din("rope", [128, 2 * SEQ])
        I["cols"] = self.din("cols", [DEPTH, 128, NCOL])
        I["rows"] = self.din("rows", [DEPTH, 1, NROW])
        I["w2p"] = self.din("w2p", [DEPTH, 2, 32, 128])
        I["w_mod"] = self.din("w_mod", [DEPTH, D, 6 * D])
        I["w_in"] = self.din("w_in", [DEPTH, D, INC])
        I["w_out"] = self.din("w_out", [DEPTH, D, D])
        I["w_up"] = self.din("w_up", [DEPTH, D, 2 * DFF])
        I["w_down"] = self.din("w_down", [DEPTH, DFF, D])
        self.I = I
        self.out = nc.dram_tensor("out", [SEQ, D], F32, kind="ExternalOutput").ap()
        self.OUTB = Buf("out")
        self.dscr("xres", [T, D])
        self.dscr("x1", [T, D])
        self.dscr("xa", [T, D])
        self.dscr("cmY", [256, TP])
        self.dscr("qT", [256, T])
        self.dscr("kT", [256, T])
        self.dscr("dnqkv", [768, TPD])
        self.dscr("glaqT", [128, T])
        self.dscr("glakT", [128, T])
        self.dscr("glalrT", [32, T])
        self.dscr("tokmaj", [T, NTM])
        self.dscr("mixT", [D, T], BF16)
        self.dscr("dn_qT", [256, T])
        self.dscr("dn_kT", [256, T])
        self.dscr("dn_ktok", [T, 256])
        self.dscr("dn_vtok", [T, 256])
        self.dscr("ofwd", [T, 256])
        self.dscr("orev", [T, 256])
        self.dscr("h2T", [8 * 128, FPAD], BF16)
        with ExitStack() as top:
            self.fw = FW(nc, top)
            fw = self.fw
            self.cst = top.enter_context(nc.sbuf_tensor("cst", [128, NCONST], F32))
            self.CST = Buf("cst")
            fw.dma(self.cst[:], I["consts"][:, :], writes=[self.CST])
            self.colp = top.enter_context(nc.sbuf_tensor("colp", [128, NCOL], F32))
            self.COLP = Buf("colp")
            self.rowp = top.enter_context(nc.sbuf_tensor("rowp", [128, NROW], F32))
            self.ROWP = Buf("rowp")
            self.modc = top.enter_context(nc.sbuf_tensor("modc", [128, 96], F32))
            self.MODC = Buf("modc")
            self.a1 = top.enter_context(nc.sbuf_tensor("a1", [128, 32], F32))
            self.A1 = Buf("a1")
            self.gb = top.enter_context(nc.sbuf_tensor("gb", [128, 4, D], F32))
            self.GB = Buf("gb")
            for l in range(getattr(self, 'depth_run', DEPTH)):
                self.layer(l)
            fw.barrier()
        return nc

    def C(self, name, rows=128):
        o, w = COFF[name]
        return self.cst[0:rows, o:o + w]

    def col(self, name, j=0, n=1):
        o, w = COLS[name]
        return self.colp[:, o + j:o + j + n]

    def row(self, name, j=0, n=None):
        o, w = ROWS[name]
        if n is None:
            n = w
        return self.rowp[:, o + j:o + j + n]

    def layer(self, l):
        fw, nc, I = self.fw, self.nc, self.I
        fw.barrier()
        fw.dma(self.colp[:], I["cols"][l, :, :], writes=[self.COLP])
        fw.dma(self.rowp[:], I["rows"][l, :, :].partition_broadcast(128), writes=[self.ROWP])
        self.phase0(l)
        xsrc = I["xin"] if l == 0 else self.scr["xres"][0]
        if self.want("A"):
            self.phaseA(l, xsrc)
        if self.want("B"):
            self.phaseB(l)
        if self.want("C"):
            self.phaseC(l)
        if not (self.want("D") and self.want("E")):
            self.zero_mix(l)
        if self.want("D"):
            self.phaseD(l)
        if self.want("E"):
            self.phaseE(l)
        if self.want("F"):
            self.phaseF(l, xsrc)

    def phase0(self, l):
        fw, nc, I = self.fw, self.nc, self.I
        with ExitStack() as st:
            T_ = lambda n, s, d=F32: st.enter_context(nc.sbuf_tensor(self.uniq(n), s, d))
            P_ = lambda n, s, d=F32: st.enter_context(nc.psum_tensor(self.uniq(n), s, d))
            cact = T_("cact", [128, 16])
            CACT = Buf()
            wm = [T_("wm%d" % i, [128, 8, 1024]) for i in range(2)]
            WM = fw.bufs(2)
            mps = P_("mps", [128, 96])
            MPS = Buf()
            rps = [P_("rps%d" % i, [128, 512]) for i in range(2)]
            RPS = fw.bufs(2)
            grow = T_("grow", [2, 2, 1024])
            GROW = Buf()
            gps = [P_("gps%d" % i, [128, 512]) for i in range(2)]
            GPS = fw.bufs(2)
            fw.psum(MPS, RPS, GPS)
            o, w = COLS["ccol"]
            fw.op("act", lambda: nc.scalar.activation(out=cact[:], in_=self.colp[:, o:o + 16], func=AF.Silu),
                  reads=[self.COLP], writes=[CACT])
            ob, _ = COLS["b_mod"]
            for comp in range(6):
                b = comp % 2
                fw.dma(wm[b][:], I["w_mod"][l, :, comp * 1024:(comp + 1) * 1024].rearrange("(k p) c -> p k c", p=128),
                       writes=[WM[b]])
                for jj in range(8):
                    j = comp * 8 + jj
                    for k in range(8):
                        fw.op("pe", lambda: nc.tensor.matmul(mps[:, 2 * j:2 * j + 2], lhsT=wm[b][:, k, jj * 128:(jj + 1) * 128],
                                                             rhs=cact[:, 2 * k:2 * k + 2], start=(k == 0), stop=(k == 7)),
                              reads=[WM[b], CACT], writes=[MPS])
                if comp in (2, 5):
                    which = 0 if comp == 2 else 1
                    bname = "b_g1" if comp == 2 else "b_g2"
                    for hc in range(2):
                        for k in range(8):
                            fw.op("pe", lambda: nc.tensor.matmul(rps[hc][0:2, :], lhsT=cact[:, 2 * k:2 * k + 2],
                                                                 rhs=wm[b][:, k, hc * 512:(hc + 1) * 512],
                                                                 start=(k == 0), stop=False),
                                  reads=[WM[b], CACT], writes=[RPS[hc]])
                        ro, _ = ROWS[bname]
                        fw.op("pe", lambda: nc.tensor.matmul(rps[hc][0:2, :], lhsT=self.C("ones")[0:1, 0:2],
                                                             rhs=self.rowp[0:1, ro + hc * 512:ro + (hc + 1) * 512],
                                                             start=False, stop=True),
                              reads=[self.ROWP, self.CST], writes=[RPS[hc]])
                        fw.op("dve", lambda: nc.vector.tensor_copy(out=grow[:, which, hc * 512:(hc + 1) * 512], in_=rps[hc][0:2, :]),
                              reads=[RPS[hc]], writes=[GROW])
            for s in range(2):
                fw.op("dve", lambda: nc.vector.tensor_tensor(out=self.modc[:].rearrange("p (j s) -> p j s", s=2)[:, :, s],
                                                             in0=mps[:].rearrange("p (j s) -> p j s", s=2)[:, :, s],
                                                             in1=self.colp[:, ob:ob + 48], op=ALU.add),
                      reads=[MPS, self.COLP], writes=[self.MODC])
            for which, (gname, scbase) in enumerate((("n1g", 8), ("n2g", 32))):
                go, _ = COLS[gname]
                for s in range(2):
                    fw.op("dve", lambda: nc.vector.scalar_tensor_tensor(
                        out=self.a1[:, which * 16:(which + 1) * 16].rearrange("p (k s) -> p k s", s=2)[:, :, s],
                        in0=self.modc[:].rearrange("p (j s) -> p j s", s=2)[:, scbase:scbase + 8, s],
                        scalar=1.0, in1=self.colp[:, go:go + 8], op0=ALU.add, op1=ALU.mult),
                        reads=[self.MODC, self.COLP], writes=[self.A1])
            so, _ = COFF["sel"]
            for which in range(2):
                for s in range(2):
                    for hc in range(2):
                        pi = hc
                        fw.op("pe", lambda: nc.tensor.matmul(gps[pi][:, :], lhsT=self.cst[0:2, so + s * 128:so + (s + 1) * 128],
                                                             rhs=grow[:, which, hc * 512:(hc + 1) * 512], start=True, stop=True),
                              reads=[GROW, self.CST], writes=[GPS[pi]])
                        fw.op("act", lambda: nc.scalar.copy(out=self.gb[:, which * 2 + s, hc * 512:(hc + 1) * 512], in_=gps[pi][:, :]),
                              reads=[GPS[pi]], writes=[self.GB])
            fw.barrier()

    def shift(self, k, s):
        raise NotImplementedError

    def norm_mod_T(self, st, which, xt, XT, s, hT, HT, hoff, tag, pst, PST, ident16, ID16):
        fw, nc = self.fw, self.nc
        sq, SQ, ssq, SSQ, xs, XS = self._nm_tmp
        fw.op("act", lambda: nc.scalar.activation(out=sq[:], in_=xt, func=AF.Square, accum_out=ssq[:, 0:1]),
              reads=[XT], writes=[SQ, SSQ])
        fw.op("act", lambda: nc.scalar.activation(out=ssq[:, 1:2], in_=ssq[:, 0:1], func=AF.Sqrt, bias=self.epsc[:, 0:1], scale=1.0 / D),
              reads=[SSQ], writes=[SSQ])
        fw.op("dve", lambda: nc.vector.reciprocal(out=ssq[:, 2:3], in_=ssq[:, 1:2]), reads=[SSQ], writes=[SSQ])
        fw.op("dve", lambda: nc.vector.tensor_scalar(out=xs[:], in0=xt, scalar1=ssq[:, 2:3], scalar2=None, op0=ALU.mult),
              reads=[XT, SSQ], writes=[XS])
        for k in range(8):
            fw.op("pe", lambda: nc.tensor.transpose(pst[:, k * 128:(k + 1) * 128], xs[:, k * 128:(k + 1) * 128], ident16[:]),
                  reads=[XS, ID16], writes=[PST])
        shbase = 0 if which == 0 else 24
        for k in range(8):
            eng = "act" if k % 2 == 0 else "dve"
            acol = self.a1[:, which * 16 + 2 * k + s:which * 16 + 2 * k + s + 1]
            shcol = self.modc[:, 2 * (shbase + k) + s:2 * (shbase + k) + s + 1]
            if eng == "act":
                fw.op("act", lambda: nc.scalar.activation(out=hT[:, k, hoff:hoff + 128], in_=pst[:, k * 128:(k + 1) * 128],
                                                          func=AF.Identity, bias=shcol, scale=acol),
                      reads=[PST, self.A1, self.MODC], writes=[HT])
            else:
                fw.op("dve", lambda: nc.vector.tensor_scalar(out=hT[:, k, hoff:hoff + 128], in0=pst[:, k * 128:(k + 1) * 128],
                                                             scalar1=acol, scalar2=shcol, op0=ALU.mult, op1=ALU.add),
                      reads=[PST, self.A1, self.MODC], writes=[HT])

    def common_tiles(self, st):
        fw, nc = self.fw, self.nc
        T_ = lambda n, s, d=F32: st.enter_context(nc.sbuf_tensor(self.uniq(n), s, d))
        self.epsc = T_("epsc", [128, 1])
        self.EPSC = Buf()
        fw.op("pool", lambda: nc.gpsimd.memset(self.epsc[:], EPS), writes=[self.EPSC])
        ident16 = T_("ident16", [128, 128], BF16)
        ID16 = Buf()
        fw.op("dve", lambda: nc.vector.tensor_copy(out=ident16[:], in_=self.C("ident")), reads=[self.CST], writes=[ID16])
        sq = T_("nm_sq", [128, 1024], BF16)
        ssq = T_("nm_ssq", [128, 4])
        xs = T_("nm_xs", [128, 1024], BF16)
        self._nm_tmp = (sq, Buf(), ssq, Buf(), xs, Buf())
        return ident16, ID16

    def phaseA(self, l, xsrc):
        fw, nc, I = self.fw, self.nc, self.I
        S = self.scr
        with ExitStack() as st:
            T_ = lambda n, s, d=F32: st.enter_context(nc.sbuf_tensor(self.uniq(n), s, d))
            P_ = lambda n, s, d=F32: st.enter_context(nc.psum_tensor(self.uniq(n), s, d))
            ident16, ID16 = self.common_tiles(st)
            win = T_("win", [128, 8, INC], BF16)
            WIN = Buf()
            stg = [T_("wstg%d" % i, [128, 8, 390]) for i in range(2)]
            STG = fw.bufs(2)
            for cb in range(8):
                b = cb % 2
                fw.dma(stg[b][:], I["w_in"][l, :, cb * 390:(cb + 1) * 390].rearrange("(k p) c -> p k c", p=128), writes=[STG[b]])
                fw.op("pool", lambda: nc.gpsimd.tensor_copy(out=win[:, :, cb * 390:(cb + 1) * 390], in_=stg[b][:]),
                      reads=[STG[b]], writes=[WIN])
            xt = [T_("xt%d" % i, [128, 2, 1024]) for i in range(2)]
            XT = fw.bufs(2)
            hT = [T_("hT%d" % i, [128, 8, 256], BF16) for i in range(2)]
            HT = fw.bufs(2)
            pst = [P_("pst%d" % i, [128, 1024], BF16) for i in range(2)]
            PST = fw.bufs(2)
            mp = [P_("mp%d" % i, [128, 512]) for i in range(4)]
            MP = fw.bufs(4)
            fw.psum(PST, MP)
            fo = [T_("fo%d" % i, [128, 17, 256]) for i in range(2)]
            FO = fw.bufs(2)
            to = [T_("to%d" % i, [128, 2, NTM]) for i in range(2)]
            TO = fw.bufs(2)
            sg = [T_("sg%d" % i, [128, 256]) for i in range(2)]
            SG = fw.bufs(2)
            fchunks = [(0, 128), (128, 128), (256, 128), (384, 128),
                       (512, 128), (640, 128), (768, 128), (896, 128)]
            fchunks += [(1280 + 128 * i, 128) for i in range(6)]
            fchunks += [(2320, 128), (2448, 128), (2832, 32)]
            tpieces = [(1024, 256, 0), (2048, 272, 256), (2576, 272, 528), (2848, 272, 800)]
            mpi = 0
            ngroups = T // 256
            for g in range(ngroups):
                b = g % 2
                s = 1 if g == 0 else 0
                fw.dma(xt[b][:], xsrc[g * 256:(g + 1) * 256, :].rearrange("(a p) c -> p a c", p=128), writes=[XT[b]])
                for a in range(2):
                    self.norm_mod_T(st, 0, xt[b][:, a, :], XT[b], s, hT[b], HT[b], a * 128, "A", pst[a], PST[a], ident16, ID16)
                if g == 0:
                    tok0 = 0
                    cmpos = CM_CTX0
                    dnpos = DN_CTX0
                else:
                    tok0 = g * 256
                    cmpos = CM_LAT0 + (g - 1) * 256
                    dnpos = DN_LAT0 + (g - 1) * 256
                for ci, (c0, ncol) in enumerate(fchunks):
                    pi = mpi % 4
                    mpi += 1
                    for k in range(8):
                        fw.op("pe", lambda: nc.tensor.matmul(mp[pi][0:ncol, 0:256], lhsT=win[:, k, c0:c0 + ncol], rhs=hT[b][:, k, :],
                                                             start=(k == 0), stop=(k == 7)),
                              reads=[WIN, HT[b]], writes=[MP[pi]])
                    if ci in (2, 3):
                        fw.op("act", lambda: nc.scalar.activation(out=sg[ci - 2][:], in_=mp[pi][:, 0:256], func=AF.Sigmoid),
                              reads=[MP[pi]], writes=[SG[ci - 2]])
                        fw.op("dve", lambda: nc.vector.tensor_tensor(out=fo[b][:, ci - 2, :], in0=fo[b][:, ci - 2, :], in1=sg[ci - 2][:], op=ALU.mult),
                              reads=[SG[ci - 2], FO[b]], writes=[FO[b]])
                    else:
                        eng = "act" if ci % 2 == 0 else "dve"
                        if eng == "act":
                            fw.op("act", lambda: nc.scalar.copy(out=fo[b][0:ncol, ci, :], in_=mp[pi][0:ncol, 0:256]),
                                  reads=[MP[pi]], writes=[FO[b]])
                        else:
                            fw.op("dve", lambda: nc.vector.tensor_copy(out=fo[b][0:ncol, ci, :], in_=mp[pi][0:ncol, 0:256]),
                                  reads=[MP[pi]], writes=[FO[b]])
                fw.dma(S["cmY"][0][:, cmpos:cmpos + 256].rearrange("(c p) t -> p c t", p=128), fo[b][:, 0:2, :], reads=[FO[b]], writes=[S["cmY"][1]])
                fw.dma(S["qT"][0][:, tok0:tok0 + 256].rearrange("(c p) t -> p c t", p=128), fo[b][:, 4:6, :], reads=[FO[b]], writes=[S["qT"][1]])
                fw.dma(S["kT"][0][:, tok0:tok0 + 256].rearrange("(c p) t -> p c t", p=128), fo[b][:, 6:8, :], reads=[FO[b]], writes=[S["kT"][1]])
                fw.dma(S["dnqkv"][0][:, dnpos:dnpos + 256].rearrange("(c p) t -> p c t", p=128), fo[b][:, 8:14, :], reads=[FO[b]], writes=[S["dnqkv"][1]])
                fw.dma(S["glaqT"][0][:, tok0:tok0 + 256], fo[b][:, 14, :], reads=[FO[b]], writes=[S["glaqT"][1]])
                fw.dma(S["glakT"][0][:, tok0:tok0 + 256], fo[b][:, 15, :], reads=[FO[b]], writes=[S["glakT"][1]])
                fw.dma(S["glalrT"][0][:, tok0:tok0 + 256], fo[b][0:32, 16, :], reads=[FO[b]], writes=[S["glalrT"][1]])
                for a in range(2):
                    for (c0, ncol, o0) in tpieces:
                        pi = mpi % 4
                        mpi += 1
                        for k in range(8):
                            fw.op("pe", lambda: nc.tensor.matmul(mp[pi][:, 0:ncol], lhsT=hT[b][:, k, a * 128:(a + 1) * 128], rhs=win[:, k, c0:c0 + ncol],
                                                                 start=(k == 0), stop=(k == 7)),
                                  reads=[WIN, HT[b]], writes=[MP[pi]])
                        eng = "act" if (pi % 2 == 0) else "dve"
                        if eng == "act":
                            fw.op("act", lambda: nc.scalar.copy(out=to[b][:, a, o0:o0 + ncol], in_=mp[pi][:, 0:ncol]), reads=[MP[pi]], writes=[TO[b]])
                        else:
                            fw.op("dve", lambda: nc.vector.tensor_copy(out=to[b][:, a, o0:o0 + ncol], in_=mp[pi][:, 0:ncol]), reads=[MP[pi]], writes=[TO[b]])
                fw.dma(S["tokmaj"][0][tok0:tok0 + 256, :].rearrange("(a p) c -> p a c", p=128), to[b][:], reads=[TO[b]], writes=[S["tokmaj"][1]])
            fw.barrier()

    def zero_mix(self, l):
        fw, nc = self.fw, self.nc
        S = self.scr
        with ExitStack() as st:
            z = st.enter_context(nc.sbuf_tensor(self.uniq("zmix"), [128, T], BF16))
            Z = Buf()
            fw.op("pool", lambda: nc.gpsimd.memset(z[:], 0.0), writes=[Z])
            for c in range(4, 8):
                if (c < 6 and not self.want("E")) or (c >= 6 and not self.want("D")):
                    fw.dma(S["mixT"][0][c * 128:(c + 1) * 128, :], z[:], reads=[Z], writes=[S["mixT"][1]])
            fw.barrier()

    def phaseB(self, l):
        fw, nc = self.fw, self.nc
        S = self.scr
        with ExitStack() as st:
            T_ = lambda n, s, d=F32: st.enter_context(nc.sbuf_tensor(self.uniq(n), s, d))
            P_ = lambda n, s, d=F32: st.enter_context(nc.psum_tensor(self.uniq(n), s, d))
            self.common_tiles(st)
            Y = T_("cmy", [128, 2, TP])
            YB = fw.bufs(2)
            accA = T_("accA", [128, 2, TP])
            AA = fw.bufs(2)
            L = TP - 2 * CMP
            wo, _ = COLS["cm_w"]
            bo, _ = COLS["cm_b"]
            for c in range(2):
                fw.dma(Y[:, c, :], S["cmY"][0][c * 128:(c + 1) * 128, :], reads=[S["cmY"][1]], writes=[YB[c]])
                for (a, b) in ((0, CMP), (CM_CTX0 + CTX, CM_LAT0), (CM_LAT0 + SEQ, TP)):
                    fw.op("pool", lambda: nc.gpsimd.memset(Y[:, c, a:b], 0.0), writes=[YB[c]])
            for c in range(2):
                for j in range(31):
                    wcol = self.colp[:, wo + c * 31 + j:wo + c * 31 + j + 1]
                    e, eng, acc, AC, first = "dve", nc.vector, accA, AA[c], (j == 0)
                    if first:
                        fw.op(e, lambda: eng.tensor_scalar(out=acc[:, c, CMP:CMP + L], in0=Y[:, c, j:j + L], scalar1=wcol, scalar2=None, op0=ALU.mult),
                              reads=[YB[c], self.COLP], writes=[AC])
                    else:
                        fw.op(e, lambda: eng.scalar_tensor_tensor(out=acc[:, c, CMP:CMP + L], in0=Y[:, c, j:j + L], scalar=wcol, in1=acc[:, c, CMP:CMP + L],
                                                                  op0=ALU.mult, op1=ALU.add), reads=[YB[c], self.COLP, AC], writes=[AC])
                fw.op("pool", lambda: nc.gpsimd.tensor_scalar(out=accA[:, c, CMP:CMP + L], in0=accA[:, c, CMP:CMP + L], scalar1=self.colp[:, bo + c:bo + c + 1],
                                                              scalar2=None, op0=ALU.add),
                      reads=[AA[c], self.COLP], writes=[AA[c]])
            sq = T_("lnsq", [128, 2, 512])
            SQ = Buf()
            msq = T_("msq", [128, 512])
            MSQ = Buf()
            var = T_("var", [128, 512])
            VAR = Buf()
            tt = [T_("lnt%d" % i, [128, 512]) for i in range(2)]
            TT = fw.bufs(2)
            ob = [T_("lno%d" % i, [128, 512], BF16) for i in range(2)]
            OB = fw.bufs(2)
            mps = P_("lnm", [128, 512])
            MPS = Buf()
            eps_ = P_("lne", [128, 512])
            EPS_ = Buf()
            fw.psum(MPS, EPS_)
            lg, _ = COLS["cm_lg"]
            lb, _ = COLS["cm_lb"]
            blocks = [(CM_CTX0, 256, 0)] + [(CM_LAT0 + 512 * k, 512, CTX + 512 * k) for k in range(8)]
            for (p0, n, tok0) in blocks:
                for c in range(2):
                    fw.op("act", lambda: nc.scalar.activation(out=sq[:, c, 0:n], in_=accA[:, c, p0:p0 + n], func=AF.Square), reads=[AA[c]], writes=[SQ])
                for c in range(2):
                    fw.op("pe", lambda: nc.tensor.matmul(mps[:, 0:n], lhsT=self.C("div256"), rhs=accA[:, c, p0:p0 + n], start=(c == 0), stop=(c == 1)),
                          reads=[self.CST, AA[c]], writes=[MPS])
                for c in range(2):
                    fw.op("pe", lambda: nc.tensor.matmul(eps_[:, 0:n], lhsT=self.C("div256"), rhs=sq[:, c, 0:n], start=(c == 0), stop=(c == 1)),
                          reads=[self.CST, SQ], writes=[EPS_])
                fw.op("act", lambda: nc.scalar.activation(out=msq[:, 0:n], in_=mps[:, 0:n], func=AF.Square), reads=[MPS], writes=[MSQ])
                fw.op("dve", lambda: nc.vector.tensor_tensor(out=var[:, 0:n], in0=eps_[:, 0:n], in1=msq[:, 0:n], op=ALU.subtract), reads=[EPS_, MSQ], writes=[VAR])
                fw.op("act", lambda: nc.scalar.activation(out=var[:, 0:n], in_=var[:, 0:n], func=AF.Sqrt, bias=self.epsc[:, 0:1], scale=1.0), reads=[VAR, self.EPSC], writes=[VAR])
                fw.op("dve", lambda: nc.vector.reciprocal(out=var[:, 0:n], in_=var[:, 0:n]), reads=[VAR], writes=[VAR])
                for c in range(2):
                    fw.op("dve", lambda: nc.vector.tensor_tensor(out=tt[c][:, 0:n], in0=accA[:, c, p0:p0 + n], in1=mps[:, 0:n], op=ALU.subtract),
                          reads=[AA[c], MPS], writes=[TT[c]])
                    fw.op("pool", lambda: nc.gpsimd.tensor_tensor(out=tt[c][:, 0:n], in0=tt[c][:, 0:n], in1=var[:, 0:n], op=ALU.mult), reads=[TT[c], VAR], writes=[TT[c]])
                    fw.op("act", lambda: nc.scalar.activation(out=ob[c][:, 0:n], in_=tt[c][:, 0:n], func=AF.Silu, bias=self.colp[:, lb + c:lb + c + 1],
                                                              scale=self.colp[:, lg + c:lg + c + 1]), reads=[TT[c], self.COLP], writes=[OB[c]])
                    fw.dma(S["mixT"][0][c * 128:(c + 1) * 128, tok0:tok0 + n], ob[c][:, 0:n], reads=[OB[c]], writes=[S["mixT"][1]])
            fw.barrier()

    def phaseC(self, l):
        fw, nc, I = self.fw, self.nc, self.I
        S = self.scr
        lam_init = 0.8 - 0.6 * math.exp(-0.3 * l)
        scale = 32.0 ** -0.5
        with ExitStack() as st:
            T_ = lambda n, s, d=F32: st.enter_context(nc.sbuf_tensor(self.uniq(n), s, d))
            P_ = lambda n, s, d=F32: st.enter_context(nc.psum_tensor(self.uniq(n), s, d))
            ident16, ID16 = self.common_tiles(st)
            QT = T_("QT", [128, 2, T], BF16)
            KT = T_("KT", [128, 2, T], BF16)
            QTB, KTB = Buf(), Buf()
            V = T_("V", [128, NT, 4, 65], BF16)
            VB = Buf()
            fw.op("pool", lambda: nc.gpsimd.memset(V[:].rearrange("p a b c -> p (a b c)"), 1.0), writes=[VB])
            blocks = [(0, 256)] + [(CTX + 512 * k, 512) for k in range(8)]
            with ExitStack() as st2:
                T2 = lambda n, s, d=F32: st2.enter_context(nc.sbuf_tensor(self.uniq(n), s, d))
                P2 = lambda n, s, d=F32: st2.enter_context(nc.psum_tensor(self.uniq(n), s, d))
                raw = [T2("raw%d" % i, [128, 512]) for i in range(2)]
                RAW = fw.bufs(2)
                cs = [T2("cs%d" % i, [128, 2, 512]) for i in range(2)]
                CS = fw.bufs(2)
                sq = T2("sq", [128, 512]); SQ = Buf()
                rs = T2("rs", [128, 512]); RS = Buf()
                qn = T2("qn", [128, 512]); QN = Buf()
                r1 = T2("r1", [128, 512]); R1 = Buf()
                r2 = T2("r2", [128, 512]); R2 = Buf()
                ssp = P2("ssp", [128, 512]); SSP = Buf()
                swp = P2("swp", [128, 512]); SWP = Buf()
                fw.psum(SSP, SWP)
                vst = [T2("vst%d" % i, [128, 256]) for i in range(2)]
                VST = fw.bufs(2)
                it = 0
                for (name, dst, DST, gname) in (("qT", QT, QTB, "qg"), ("kT", KT, KTB, "kg")):
                    go, _ = COLS[gname]
                    for c in range(2):
                        for (t0, n) in blocks:
                            b = it % 2
                            it += 1
                            fw.dma(raw[b][:, 0:n], S[name][0][c * 128:(c + 1) * 128, t0:t0 + n], reads=[S[name][1]], writes=[RAW[b]])
                            fw.op("act", lambda: nc.scalar.activation(out=sq[:, 0:n], in_=raw[b][:, 0:n], func=AF.Square), reads=[RAW[b]], writes=[SQ])
                            fw.op("pe", lambda: nc.tensor.matmul(ssp[:, 0:n], lhsT=self.C("blk32"), rhs=sq[:, 0:n], start=True, stop=True), reads=[self.CST, SQ], writes=[SSP])
                            fw.op("act", lambda: nc.scalar.activation(out=rs[:, 0:n], in_=ssp[:, 0:n], func=AF.Sqrt, bias=self.epsc[:, 0:1], scale=1.0), reads=[SSP, self.EPSC], writes=[RS])
                            fw.op("dve", lambda: nc.vector.reciprocal(out=rs[:, 0:n], in_=rs[:, 0:n]), reads=[RS], writes=[RS])
                            if t0 < CTX:
                                fw.op("dve", lambda: nc.vector.scalar_tensor_tensor(out=dst[:, c, t0:t0 + n], in0=raw[b][:, 0:n], scalar=self.colp[:, go:go + 1], in1=rs[:, 0:n],
                                                                                    op0=ALU.mult, op1=ALU.mult), reads=[RAW[b], RS, self.COLP], writes=[DST])
                                continue
                            fw.op("dve", lambda: nc.vector.scalar_tensor_tensor(out=qn[:, 0:n], in0=raw[b][:, 0:n], scalar=self.colp[:, go:go + 1], in1=rs[:, 0:n],
                                                                                op0=ALU.mult, op1=ALU.mult), reads=[RAW[b], RS, self.COLP], writes=[QN])
                            lt = t0 - CTX
                            fw.dma(cs[b][:, 0, 0:n], I["rope"][:, lt:lt + n], writes=[CS[b]])
                            fw.dma(cs[b][:, 1, 0:n], I["rope"][:, SEQ + lt:SEQ + lt + n], writes=[CS[b]])
                            fw.op("pe", lambda: nc.tensor.matmul(swp[:, 0:n], lhsT=self.C("perm"), rhs=qn[:, 0:n], start=True, stop=True), reads=[self.CST, QN], writes=[SWP])
                            fw.op("pool", lambda: nc.gpsimd.tensor_tensor(out=r1[:, 0:n], in0=qn[:, 0:n], in1=cs[b][:, 0, 0:n], op=ALU.mult), reads=[QN, CS[b]], writes=[R1])
                            fw.op("dve", lambda: nc.vector.tensor_tensor(out=r2[:, 0:n], in0=swp[:, 0:n], in1=cs[b][:, 1, 0:n], op=ALU.mult), reads=[SWP, CS[b]], writes=[R2])
                            fw.op("pool", lambda: nc.gpsimd.tensor_tensor(out=dst[:, c, t0:t0 + n], in0=r1[:, 0:n], in1=r2[:, 0:n], op=ALU.add), reads=[R1, R2], writes=[DST])
                for t in range(NT):
                    b = t % 2
                    fw.dma(vst[b][:], S["tokmaj"][0][t * 128:(t + 1) * 128, 0:256], reads=[S["tokmaj"][1]], writes=[VST[b]])
                    fw.op("pool", lambda: nc.gpsimd.tensor_copy(out=V[:, t, :, 0:64], in_=vst[b][:].rearrange("p (h d) -> p h d", h=4)), reads=[VST[b]], writes=[VB])
                fw.barrier()
            lt_ = T_("lamt", [128, 8]); LT = Buf()
            lp = T_("lamp", [128, 64]); LP = Buf()
            lo, _ = ROWS["lam"]
            for i in range(2):
                fw.op("dve", lambda: nc.vector.tensor_tensor(out=lp[:, i * 32:(i + 1) * 32], in0=self.rowp[:, lo + 64 * i:lo + 64 * i + 32], in1=self.rowp[:, lo + 64 * i + 32:lo + 64 * i + 64], op=ALU.mult),
                      reads=[self.ROWP], writes=[LP])
                fw.op("dve", lambda: nc.vector.reduce_sum(out=lt_[:, i:i + 1], in_=lp[:, i * 32:(i + 1) * 32], axis=AX.X), reads=[LP], writes=[LT])
            fw.op("act", lambda: nc.scalar.activation(out=lt_[:, 2:4], in_=lt_[:, 0:2], func=AF.Exp), reads=[LT], writes=[LT])
            fw.op("dve", lambda: nc.vector.scalar_tensor_tensor(out=lt_[:, 4:5], in0=lt_[:, 3:4], scalar=-lam_init, in1=lt_[:, 2:3], op0=ALU.add, op1=ALU.subtract), reads=[LT], writes=[LT])
            subg = T_("subg", [128, 64]); SUBG = Buf()
            so, _ = ROWS["subln"]
            fw.op("dve", lambda: nc.vector.tensor_scalar(out=subg[:], in0=self.rowp[:, so:so + 64], scalar1=(1.0 - lam_init), scalar2=None, op0=ALU.mult), reads=[self.ROWP], writes=[SUBG])
            identf = self.C("ident")
            scps = [P_("scps%d" % i, [128, 512]) for i in range(3)]
            SCPS = fw.bufs(3)
            otps = [P_("otps%d" % i, [128, 512]) for i in range(2)]
            OTPS = fw.bufs(2)
            trps = [P_("trps%d" % i, [128, 4, 65]) for i in range(2)]
            TRPS = fw.bufs(2)
            ytps = P_("ytps", [128, 1024], BF16)
            YTPS = Buf()
            fw.psum(SCPS, OTPS, TRPS, YTPS)
            pb = [T_("pb%d" % i, [128, 512], BF16) for i in range(4)]
            PB = fw.bufs(4)
            otsb = [T_("otsb%d" % i, [128, 512]) for i in range(2)]
            OTSB = fw.bufs(2)
            rcp = T_("rcp", [128, 2, 4]); RCP = Buf()
            o0 = T_("o0", [128, 64]); O0 = Buf()
            o1 = T_("o1", [128, 64]); O1 = Buf()
            avs = [T_("avs%d" % i, [128, 4, 4, 64]) for i in range(2)]; AVS = fw.bufs(2)
            junk = T_("junk", [128, 64]); JUNK = Buf()
            ssa = [T_("ssa%d" % i, [128, 48]) for i in range(2)]; SSA = fw.bufs(2)
            yb = [T_("yb%d" % i, [128, 4, 256], BF16) for i in range(2)]
            YB = fw.bufs(2)
            ybT = [T_("ybT%d" % i, [128, 2, 512], BF16) for i in range(2)]
            YBT = fw.bufs(2)
            qblocks = [(0, 256, 2)] + [(CTX + 512 * k, 512, NT) for k in range(8)]
            items = []
            for qi, (q0, nq, nkt) in enumerate(qblocks):
                for h in range(4):
                    for m in range(2):
                        for kt in range(nkt):
                            items.append((qi, q0, nq, nkt, h, m, kt))

            def emit_sc(idx):
                qi, q0, nq, nkt, h, m, kt = items[idx]
                r = (h % 2) * 2 + m
                ch = h // 2
                si = idx % 3
                fw.op("pe", lambda: nc.tensor.matmul(scps[si][:, 0:nq], lhsT=KT[32 * r:32 * r + 32, ch, kt * 128:(kt + 1) * 128],
                                                     rhs=QT[32 * r:32 * r + 32, ch, q0:q0 + nq], start=True, stop=True, tile_position=(32 * r, 0)),
                      reads=[KTB, QTB], writes=[SCPS[si]])

            def emit_exp_av(idx):
                qi, q0, nq, nkt, h, m, kt = items[idx]
                si = idx % 3
                pi = idx % 4
                fw.op("act", lambda: nc.scalar.activation(out=pb[pi][:, 0:nq], in_=scps[si][:, 0:nq], func=AF.Exp, scale=scale), reads=[SCPS[si]], writes=[PB[pi]])
                fw.op("pe", lambda: nc.tensor.matmul(otps[m][0:65, 0:nq], lhsT=V[:, kt, h, :], rhs=pb[pi][:, 0:nq], start=(kt == 0), stop=(kt == nkt - 1)),
                      reads=[VB, PB[pi]], writes=[OTPS[m]])

            def epilogue(qi, q0, nq, h, m):
                nsub = nq // 128
                qb = qi % 2
                fw.op("dve", lambda: nc.vector.tensor_copy(out=otsb[m][0:65, 0:nq], in_=otps[m][0:65, 0:nq]), reads=[OTPS[m]], writes=[OTSB[m]])
                for sub in range(nsub):
                    fw.op("pe", lambda: nc.tensor.transpose(trps[m][:, sub, :], otsb[m][0:65, sub * 128:(sub + 1) * 128], identf[0:65, 0:65]),
                          reads=[OTSB[m], self.CST], writes=[TRPS[m]])
                fw.op("dve", lambda: nc.vector.reciprocal(out=rcp[:, m, 0:nsub], in_=trps[m][:, 0:nsub, 64]), reads=[TRPS[m]], writes=[RCP])
                if m == 0:
                    return
                for sub in range(nsub):
                    col = h * 4 + sub
                    fw.op("dve", lambda: nc.vector.tensor_scalar(out=o0[:], in0=trps[0][:, sub, 0:64], scalar1=rcp[:, 0, sub:sub + 1], scalar2=None, op0=ALU.mult),
                          reads=[TRPS[0], RCP], writes=[O0])
                    fw.op("dve", lambda: nc.vector.tensor_scalar(out=o1[:], in0=trps[1][:, sub, 0:64], scalar1=rcp[:, 1, sub:sub + 1], scalar2=None, op0=ALU.mult),
                          reads=[TRPS[1], RCP], writes=[O1])
                    fw.op("dve", lambda: nc.vector.scalar_tensor_tensor(out=avs[qb][:, h, sub, :], in0=o1[:], scalar=lt_[:, 4:5], in1=o0[:], op0=ALU.mult, op1=ALU.add),
                          reads=[O0, O1, LT], writes=[AVS[qb]])
                    fw.op("pool", lambda: nc.gpsimd.tensor_tensor(out=junk[:], in0=avs[qb][:, h, sub, :], in1=avs[qb][:, h, sub, :], op=ALU.mult), reads=[AVS[qb]], writes=[JUNK])
                    fw.op("dve", lambda: nc.vector.reduce_sum(out=ssa[qb][:, col:col + 1], in_=junk[:], axis=AX.X), reads=[JUNK], writes=[SSA[qb]])
                if h < 3:
                    return
                fw.op("act", lambda: nc.scalar.activation(out=ssa[qb][:, 16:32], in_=ssa[qb][:, 0:16], func=AF.Sqrt, bias=self.epsc[:, 0:1], scale=1.0 / 64), reads=[SSA[qb], self.EPSC], writes=[SSA[qb]])
                fw.op("dve", lambda: nc.vector.reciprocal(out=ssa[qb][:, 32:48], in_=ssa[qb][:, 16:32]), reads=[SSA[qb]], writes=[SSA[qb]])
                for hh in range(4):
                    for sub in range(nsub):
                        col = 32 + hh * 4 + sub
                        fw.op("dve", lambda: nc.vector.scalar_tensor_tensor(out=yb[qb][:, sub, hh * 64:(hh + 1) * 64], in0=avs[qb][:, hh, sub, :], scalar=ssa[qb][:, col:col + 1], in1=subg[:],
                                                                            op0=ALU.mult, op1=ALU.mult), reads=[AVS[qb], SSA[qb], SUBG], writes=[YB[qb]])
                for sub in range(nsub):
                    for c2 in range(2):
                        fw.op("pe", lambda: nc.tensor.transpose(ytps[:, c2 * 512 + sub * 128:c2 * 512 + (sub + 1) * 128], yb[qb][:, sub, c2 * 128:(c2 + 1) * 128], ident16[:]),
                              reads=[YB[qb], ID16], writes=[YTPS])
                for c2 in range(2):
                    fw.op("dve", lambda: nc.vector.tensor_copy(out=ybT[qb][:, c2, 0:nq], in_=ytps[:, c2 * 512:c2 * 512 + nq]), reads=[YTPS], writes=[YBT[qb]])
                    fw.dma(S["mixT"][0][256 + c2 * 128:256 + (c2 + 1) * 128, q0:q0 + nq], ybT[qb][:, c2, 0:nq], reads=[YBT[qb]], writes=[S["mixT"][1]])

            pending = []
            nit = len(items)
            emit_sc(0)
            emit_sc(1)
            for idx in range(nit):
                qi, q0, nq, nkt, h, m, kt = items[idx]
                if idx + 2 < nit:
                    emit_sc(idx + 2)
                emit_exp_av(idx)
                if pending and kt == min(2, nkt - 1):
                    for args in pending:
                        epilogue(*args)
                    pending = []
                if kt == nkt - 1:
                    pending.append((qi, q0, nq, h, m))
            for args in pending:
                epilogue(*args)
            fw.barrier()

    def gated_out(self, st, tiles, osb, OSB, gate_ap, GATE, on_name, row0, t):
        fw, nc = self.fw, self.nc
        (sq, SQ, ssq, SSQ, sg, SG, y, Y, yf, YF, ytps, YTPS, yT, YT, ident16, ID16) = tiles
        S = self.scr
        go, _ = ROWS[on_name]
        fw.op("pool", lambda: nc.gpsimd.tensor_tensor(out=sq[:], in0=osb[:], in1=osb[:], op=ALU.mult), reads=[OSB], writes=[SQ])
        fw.op("dve", lambda: nc.vector.reduce_sum(out=ssq[:, 0:4], in_=sq[:].rearrange("p (h d) -> p h d", h=4), axis=AX.X), reads=[SQ], writes=[SSQ])
        fw.op("act", lambda: nc.scalar.activation(out=ssq[:, 4:8], in_=ssq[:, 0:4], func=AF.Sqrt, bias=self.epsc[:, 0:1], scale=1.0 / 64), reads=[SSQ, self.EPSC], writes=[SSQ])
        fw.op("dve", lambda: nc.vector.reciprocal(out=ssq[:, 8:12], in_=ssq[:, 4:8]), reads=[SSQ], writes=[SSQ])
        fw.op("act", lambda: nc.scalar.activation(out=sg[:], in_=gate_ap, func=AF.Silu), reads=[GATE], writes=[SG])
        for h in range(4):
            fw.op("dve", lambda: nc.vector.scalar_tensor_tensor(out=y[:, h * 64:(h + 1) * 64], in0=osb[:, h * 64:(h + 1) * 64], scalar=ssq[:, 8 + h:9 + h],
                                                                in1=self.rowp[:, go:go + 64], op0=ALU.mult, op1=ALU.mult), reads=[OSB, SSQ, self.ROWP], writes=[Y])
        fw.op("pool", lambda: nc.gpsimd.tensor_tensor(out=yf[:], in0=y[:], in1=sg[:], op=ALU.mult), reads=[Y, SG], writes=[YF])
        for c2 in range(2):
            fw.op("pe", lambda: nc.tensor.transpose(ytps[:, c2 * 128:(c2 + 1) * 128], yf[:, c2 * 128:(c2 + 1) * 128], ident16[:]), reads=[YF, ID16], writes=[YTPS])
        fw.op("act", lambda: nc.scalar.copy(out=yT[:], in_=ytps[:, 0:256]), reads=[YTPS], writes=[YT])
        fw.dma(S["mixT"][0][row0:row0 + 256, t * 128:(t + 1) * 128].rearrange("(c p) t -> p c t", p=128), yT[:].rearrange("p (c t) -> p c t", c=2), reads=[YT], writes=[S["mixT"][1]])

    def gated_tiles(self, st, ident16, ID16):
        nc = self.nc
        T_ = lambda n, s, d=F32: st.enter_context(nc.sbuf_tensor(self.uniq(n), s, d))
        P_ = lambda n, s, d=F32: st.enter_context(nc.psum_tensor(self.uniq(n), s, d))
        ytb = Buf()
        ytb.ps = True
        return (T_("g_sq", [128, 256]), Buf(), T_("g_ssq", [128, 12]), Buf(), T_("g_sg", [128, 256]), Buf(), T_("g_y", [128, 256]), Buf(),
                T_("g_yf", [128, 256], BF16), Buf(), P_("g_ytps", [128, 1024], BF16), ytb, T_("g_yT", [128, 256], BF16), Buf(), ident16, ID16)

    def phaseD(self, l):
        fw, nc, I = self.fw, self.nc, self.I
        S = self.scr
        scale = 32.0 ** -0.5
        with ExitStack() as st:
            T_ = lambda n, s, d=F32: st.enter_context(nc.sbuf_tensor(self.uniq(n), s, d))
            P_ = lambda n, s, d=F32: st.enter_context(nc.psum_tensor(self.uniq(n), s, d))
            ident16, ID16 = self.common_tiles(st)
            gt = self.gated_tiles(st, ident16, ID16)
            w2 = T_("w2", [32, 2, 128]); W2 = Buf()
            fw.dma(w2[:], I["w2p"][l].rearrange("d r c -> r d c"), writes=[W2])
            Sst = T_("Sst", [128, 256]); SST = Buf()
            qT = T_("qT", [128, 128]); QTB = Buf()
            kT = T_("kT", [128, 128]); KTB = Buf()
            lrT = T_("lrT", [32, 128]); LRT = Buf()
            tm = T_("tm", [128, 544]); TM = Buf()
            z = T_("z", [128, 128]); Z = Buf()
            ebT = T_("ebT", [128, 128]); EBT = Buf()
            enbT = T_("enbT", [128, 128]); ENBT = Buf()
            qin = T_("qin", [128, 128]); QIN = Buf()
            kn = T_("kn", [128, 4, 128]); KN = Buf()
            ktok = T_("ktok", [128, 128]); KTOK = Buf()
            bcs = T_("bcs", [128, 128]); BCS = Buf()
            kend = T_("kend", [128, 2, 128]); KEND = Buf()
            aqk = T_("aqk", [128, 2, 512]); AQK = Buf()
            tmp = T_("tmp", [128, 256]); TMP = Buf()
            osb = T_("osb", [128, 256]); OSB = Buf()
            of = T_("of", [128, 256]); OF = Buf()
            zk = P_("zk", [128, 512]); ZP = Buf(); KP = ZP
            b3 = P_("b3", [128, 512]); B3 = [Buf()] * 3
            aq = P_("aq", [128, 512]); AQ = Buf()
            ops = P_("ops", [128, 512]); OPS = [Buf()] * 2
            sp_ = P_("sp", [128, 512]); sp = sp_[:, 0:256]; SP = Buf()
            fw.psum(ZP, B3, AQ, OPS, SP)
            onescol = self.C("ones")[:, 0:1]
            b2o, _ = ROWS["gla_b2r"]
            for d in range(2):
                nm = "f" if d == 0 else "r"
                order = list(range(NT)) if d == 0 else [1, 0] + list(range(NT - 1, 1, -1))
                if getattr(self, "nt_dbg", None):
                    order = [t for t in order if t < self.nt_dbg]
                fw.op("pool", lambda: nc.gpsimd.memset(Sst[:], 0.0), writes=[SST])
                for t in order:
                    tk = slice(t * 128, (t + 1) * 128)
                    fw.dma(qT[:], S["glaqT"][0][:, tk], reads=[S["glaqT"][1]], writes=[QTB])
                    fw.dma(kT[:], S["glakT"][0][:, tk], reads=[S["glakT"][1]], writes=[KTB])
                    fw.dma(lrT[:], S["glalrT"][0][:, tk], reads=[S["glalrT"][1]], writes=[LRT])
                    fw.dma(tm[:], S["tokmaj"][0][tk, 528:1072], reads=[S["tokmaj"][1]], writes=[TM])
                    fw.op("pe", lambda: nc.tensor.matmul(zk[:, 0:128], lhsT=lrT[:, :], rhs=w2[:, d, :], start=True, stop=True), reads=[LRT, W2], writes=[ZP])
                    fw.op("dve", lambda: nc.vector.tensor_tensor(out=z[:], in0=zk[:, 0:128], in1=self.rowp[:, b2o + d * 128:b2o + (d + 1) * 128], op=ALU.add), reads=[ZP, self.ROWP], writes=[Z])
                    fw.op("act", lambda: nc.scalar.activation(out=z[:], in_=z[:], func=AF.Exp, scale=-1.0), reads=[Z], writes=[Z])
                    fw.op("act", lambda: nc.scalar.activation(out=z[:], in_=z[:], func=AF.Ln, bias=onescol, scale=1.0), reads=[Z, self.CST], writes=[Z])
                    fw.op("pe", lambda: nc.tensor.matmul(b3[:, 0:128], lhsT=self.C("tris_" + nm), rhs=z[:], start=True, stop=True), reads=[self.CST, Z], writes=[B3[0]])
                    fw.op("pe", lambda: nc.tensor.matmul(b3[:, 128:256], lhsT=z[:], rhs=self.C("tris_" + nm), start=True, stop=True), reads=[self.CST, Z], writes=[B3[1]])
                    fw.op("pe", lambda: nc.tensor.matmul(b3[:, 256:384], lhsT=self.C("blks"), rhs=z[:], start=True, stop=True), reads=[self.CST, Z], writes=[B3[2]])
                    fw.op("act", lambda: nc.scalar.activation(out=ebT[:], in_=b3[:, 128:256], func=AF.Exp), reads=[B3[1]], writes=[EBT])
                    fw.op("act", lambda: nc.scalar.activation(out=enbT[:], in_=b3[:, 128:256], func=AF.Exp, scale=-1.0), reads=[B3[1]], writes=[ENBT])
                    fw.op("dve", lambda: nc.vector.scalar_tensor_tensor(out=qin[:], in0=qT[:], scalar=scale, in1=ebT[:], op0=ALU.mult, op1=ALU.mult), reads=[QTB, EBT], writes=[QIN])
                    hmo, _ = COFF["hm32"]
                    for h in range(4):
                        fw.op("dve", lambda: nc.vector.scalar_tensor_tensor(out=kn[:, h, :], in0=kT[:], scalar=self.cst[:, hmo + h:hmo + h + 1], in1=enbT[:], op0=ALU.mult, op1=ALU.mult),
                              reads=[KTB, ENBT, self.CST], writes=[KN])
                    fw.op("pe", lambda: nc.tensor.transpose(zk[:, 128:256], kT[:], self.C("ident")), reads=[KTB, self.CST], writes=[KP])
                    fw.op("act", lambda: nc.scalar.copy(out=ktok[:], in_=zk[:, 128:256]), reads=[KP], writes=[KTOK])
                    fw.op("act", lambda: nc.scalar.copy(out=bcs[:], in_=b3[:, 0:128]), reads=[B3[0]], writes=[BCS])
                    fw.op("dve", lambda: nc.vector.tensor_tensor(out=bcs[:], in0=b3[:, 256:384], in1=bcs[:], op=ALU.subtract), reads=[B3[2], BCS], writes=[BCS])
                    fw.op("act", lambda: nc.scalar.activation(out=bcs[:], in_=bcs[:], func=AF.Exp), reads=[BCS], writes=[BCS])
                    cio, _ = COFF["chunkind"]
                    for c in range(2):
                        fw.op("dve", lambda: nc.vector.scalar_tensor_tensor(out=kend[:, c, :], in0=ktok[:], scalar=self.cst[:, cio + c:cio + c + 1], in1=bcs[:], op0=ALU.mult, op1=ALU.mult),
                              reads=[KTOK, BCS, self.CST], writes=[KEND])
                    if getattr(self, "d_level", 9) < 2:
                        continue
                    for h in range(4):
                        fw.op("pe", lambda: nc.tensor.matmul(aq[:, h * 128:(h + 1) * 128], lhsT=kn[:, h, :], rhs=qin[:], start=True, stop=True), reads=[KN, QIN], writes=[AQ])
                    for c in range(2):
                        fw.op("dve", lambda: nc.vector.scalar_tensor_tensor(out=aqk[:, c, :], in0=aq[:], scalar=self.cst[:, cio + c:cio + c + 1], in1=self.C("cT4_" + nm), op0=ALU.mult, op1=ALU.mult),
                              reads=[AQ, self.CST], writes=[AQK])
                    if getattr(self, "d_level", 9) < 3:
                        continue
                    for c in ((0, 1) if d == 0 else (1, 0)):
                        r0 = 64 * c
                        dcol = (63 + 64 * c) if d == 0 else (64 * c)
                        for h in range(4):
                            fw.op("pe", lambda: nc.tensor.matmul(ops[:, c * 256 + h * 64:c * 256 + (h + 1) * 64], lhsT=qin[:], rhs=Sst[:, h * 64:(h + 1) * 64], start=True, stop=False),
                                  reads=[QIN, SST], writes=[OPS[c]])
                            fw.op("pe", lambda: nc.tensor.matmul(ops[:, c * 256 + h * 64:c * 256 + (h + 1) * 64], lhsT=aqk[:, c, h * 128:(h + 1) * 128],
                                                                 rhs=tm[:, h * 64:(h + 1) * 64], start=False, stop=True), reads=[AQK, TM], writes=[OPS[c]])
                        fw.op("pe", lambda: nc.tensor.matmul(sp, lhsT=kend[:, c, :], rhs=tm[:, 0:256], start=True, stop=True), reads=[KEND, TM], writes=[SP])
                        fw.op("dve", lambda: nc.vector.tensor_tensor(out=tmp[:], in0=sp, in1=self.C("bmask4"), op=ALU.mult), reads=[SP, self.CST], writes=[TMP])
                        fw.op("dve", lambda: nc.vector.scalar_tensor_tensor(out=Sst[:], in0=Sst[:], scalar=ebT[:, dcol:dcol + 1], in1=tmp[:], op0=ALU.mult, op1=ALU.add),
                              reads=[SST, EBT, TMP], writes=[SST])
                        fw.op("act", lambda: nc.scalar.copy(out=osb[r0:r0 + 64, :], in_=ops[r0:r0 + 64, c * 256:(c + 1) * 256]), reads=[OPS[c]], writes=[OSB])
                    if getattr(self, "d_level", 9) < 4:
                        continue
                    if d == 0:
                        fw.dma(S["ofwd"][0][tk, :], osb[:], reads=[OSB], writes=[S["ofwd"][1]])
                    else:
                        fw.dma(of[:], S["ofwd"][0][tk, :], reads=[S["ofwd"][1]], writes=[OF])
                        fw.op("pool", lambda: nc.gpsimd.tensor_tensor(out=osb[:], in0=osb[:], in1=of[:], op=ALU.add), reads=[OSB, OF], writes=[OSB])
                        self.gated_out(st, gt, osb, OSB, tm[:, 288:544], TM, "gla_on", 768, t)
            fw.barrier()

    def phaseE(self, l):
        fw, nc, I = self.fw, self.nc, self.I
        S = self.scr
        with ExitStack() as st:
            T_ = lambda n, s, d=F32: st.enter_context(nc.sbuf_tensor(self.uniq(n), s, d))
            P_ = lambda n, s, d=F32: st.enter_context(nc.psum_tensor(self.uniq(n), s, d))
            self.common_tiles(st)
            raw = [T_("dnraw%d" % i, [128, TPD]) for i in range(2)]
            RAW = fw.bufs(2)
            acc = [T_("dnacc%d" % i, [128, TPD]) for i in range(2)]
            ACC = fw.bufs(2)
            sq = T_("dnsq", [128, 512]); SQ = Buf()
            rs = T_("dnrs", [128, 512]); RS = Buf()
            nrm = [T_("dnnrm%d" % i, [128, 512]) for i in range(2)]
            NRM = fw.bufs(2)
            tk_ = [T_("dntk%d" % i, [128, 512]) for i in range(2)]
            TK = fw.bufs(2)
            ssp = P_("dnssp", [128, 512]); SSP = Buf()
            trp = [P_("dntrp%d" % i, [128, 512]) for i in range(2)]
            TRP = fw.bufs(2)
            fw.psum(SSP, TRP)
            L = TPD - 2 * DNP
            wo, _ = COLS["dn_w"]
            blocks = [(DN_CTX0, 256, 0)] + [(DN_LAT0 + 512 * k, 512, CTX + 512 * k) for k in range(8)]
            bi = 0
            for ci in range(6):
                b = ci % 2
                kind = ci // 2
                half = ci % 2
                fw.dma(raw[b][:], S["dnqkv"][0][ci * 128:(ci + 1) * 128, :], reads=[S["dnqkv"][1]], writes=[RAW[b]])
                for (a, e_) in ((0, DNP), (DN_CTX0 + CTX, DN_LAT0), (DN_LAT0 + SEQ, TPD)):
                    fw.op("pool", lambda: nc.gpsimd.memset(raw[b][:, a:e_], 0.0), writes=[RAW[b]])
                for j in range(5):
                    wcol = self.colp[:, wo + ci * 5 + j:wo + ci * 5 + j + 1]
                    if j == 0:
                        fw.op("dve", lambda: nc.vector.tensor_scalar(out=acc[b][:, DNP:DNP + L], in0=raw[b][:, j:j + L], scalar1=wcol, scalar2=None, op0=ALU.mult),
                              reads=[RAW[b], self.COLP], writes=[ACC[b]])
                    else:
                        fw.op("dve", lambda: nc.vector.scalar_tensor_tensor(out=acc[b][:, DNP:DNP + L], in0=raw[b][:, j:j + L], scalar=wcol, in1=acc[b][:, DNP:DNP + L],
                                                                            op0=ALU.mult, op1=ALU.add), reads=[RAW[b], self.COLP, ACC[b]], writes=[ACC[b]])
                fw.op("act", lambda: nc.scalar.activation(out=acc[b][:, DNP:DNP + L], in_=acc[b][:, DNP:DNP + L], func=AF.Silu), reads=[ACC[b]], writes=[ACC[b]])
                for (p0, n, tok0) in blocks:
                    nb = bi % 2
                    bi += 1
                    if kind < 2:
                        fw.op("act", lambda: nc.scalar.activation(out=sq[:, 0:n], in_=acc[b][:, p0:p0 + n], func=AF.Square), reads=[ACC[b]], writes=[SQ])
                        fw.op("pe", lambda: nc.tensor.matmul(ssp[:, 0:n], lhsT=self.C("blk64"), rhs=sq[:, 0:n], start=True, stop=True), reads=[self.CST, SQ], writes=[SSP])
                        fw.op("act", lambda: nc.scalar.activation(out=rs[:, 0:n], in_=ssp[:, 0:n], func=AF.Sqrt, bias=self.epsc[:, 0:1], scale=1.0), reads=[SSP, self.EPSC], writes=[RS])
                        fw.op("dve", lambda: nc.vector.reciprocal(out=rs[:, 0:n], in_=rs[:, 0:n]), reads=[RS], writes=[RS])
                        fw.op("dve", lambda: nc.vector.scalar_tensor_tensor(out=nrm[nb][:, 0:n], in0=acc[b][:, p0:p0 + n], scalar=(0.125 if kind == 0 else 1.0), in1=rs[:, 0:n],
                                                                            op0=ALU.mult, op1=ALU.mult), reads=[ACC[b], RS], writes=[NRM[nb]])
                        dst = S["dn_qT"] if kind == 0 else S["dn_kT"]
                        fw.dma(dst[0][half * 128:(half + 1) * 128, tok0:tok0 + n], nrm[nb][:, 0:n], reads=[NRM[nb]], writes=[dst[1]])
                        src, SRC, soff = nrm[nb], NRM[nb], 0
                    else:
                        src, SRC, soff = acc[b], ACC[b], p0
                    if kind >= 1:
                        for sub in range(n // 128):
                            fw.op("pe", lambda: nc.tensor.transpose(trp[nb][:, sub * 128:(sub + 1) * 128], src[:, soff + sub * 128:soff + (sub + 1) * 128], self.C("ident")),
                                  reads=[SRC, self.CST], writes=[TRP[nb]])
                        fw.op("act", lambda: nc.scalar.copy(out=tk_[nb][:, 0:n], in_=trp[nb][:, 0:n]), reads=[TRP[nb]], writes=[TK[nb]])
                        dst = S["dn_ktok"] if kind == 1 else S["dn_vtok"]
                        fw.dma(dst[0][tok0:tok0 + n, half * 128:(half + 1) * 128].rearrange("(a p) c -> p a c", p=128),
                               tk_[nb][:, 0:n].rearrange("p (a c) -> p a c", c=128), reads=[TK[nb]], writes=[dst[1]])
            fw.barrier()
        if getattr(self, "d_level", 9) < 2:
            return
        with ExitStack() as st:
            T_ = lambda n, s, d=F32: st.enter_context(nc.sbuf_tensor(self.uniq(n), s, d))
            P_ = lambda n, s, d=F32: st.enter_context(nc.psum_tensor(self.uniq(n), s, d))
            ident16, ID16 = self.common_tiles(st)
            identf = self.C("ident")
            cio, _ = COFF["chunkind"]
            bc = lambda ap, shape, ax: ap.unsqueeze(ax).to_broadcast(shape)
            S4 = [128, 4, 128]
            negA = T_("negA", [128, 8]); NEGA = Buf()
            ao, _ = ROWS["a_log"]
            dto, _ = ROWS["dt_b"]
            fw.op("act", lambda: nc.scalar.activation(out=negA[:], in_=self.rowp[:, ao:ao + 8], func=AF.Exp), reads=[self.ROWP], writes=[NEGA])
            fw.op("dve", lambda: nc.vector.tensor_scalar(out=negA[:], in0=negA[:], scalar1=-1.0, scalar2=None, op0=ALU.mult), reads=[NEGA], writes=[NEGA])
            hm64 = self.C("chunkind")
            def stream(d):
                Sm = [T_("Sm%d" % i, [128, 128]) for i in range(2)]; SM = fw.bufs(2)
                qTp = T_("qTp", [128, 2, 128]); QTP = Buf()
                kTp = T_("kTp", [128, 2, 128]); KTP = Buf()
                kTm = T_("kTm", S4); KTM = Buf()
                ktok = T_("ktok", [128, 256]); KTOK = Buf()
                vtok = T_("vtok", [128, 256]); VTOK = Buf()
                bag = T_("bag", [128, 272]); BAG = Buf()
                sm = T_("sm", [128, 24]); SMB = Buf()
                G4 = T_("G4", S4); GB_ = Buf()
                gch = T_("gch", [128, 4, 4]); GCH = Buf()
                dm = T_("dm", S4); DM = Buf()
                dmT = T_("dmT", S4); DMT = Buf()
                dcs = T_("dcs", S4); DCS = Buf()
                e1b = T_("e1b", S4); E1B = Buf()
                gcs = T_("gcs", [128, 4, 12]); GCS = Buf()
                P0f = T_("P0f", S4); P0F = Buf()
                Q0f = T_("Q0f", S4); Q0F = Buf()
                Pm = [T_("Pm%d" % i, S4) for i in range(2)]; PM = fw.bufs(2)
                Qm = [T_("Qm%d" % i, S4) for i in range(2)]; QM = fw.bufs(2)
                Rm = [T_("Rm%d" % i, S4) for i in range(2)]; RM = fw.bufs(2)
                aqkc = T_("aqkc", [128, 2, 4, 128]); AQKC = Buf()
                vb = T_("vb", [128, 4, 64]); VBB = Buf()
                kbgm = T_("kbgm", S4); KBGM = Buf()
                kend = T_("kend", [128, 2, 256]); KEND = Buf()
                qin = T_("qin", [128, 2, 128]); QIN = Buf()
                gendp = T_("gendp", [128, 2, 2]); GENDP = Buf()
                usb = T_("usb", [128, 256]); USB = Buf()
                wT = T_("wT", [128, 2, 128]); WT = Buf()
                vnew = T_("vnew", [128, 256]); VNEW = Buf()
                tmp = T_("tmp", [128, 2, 128]); TMP = Buf()
                osb = T_("osb", [128, 256]); OSB = Buf()
                of = T_("of", [128, 256]); OF = Buf()
                p1 = P_("p1", S4); P1 = Buf()
                p2 = P_("p2", S4); P2 = Buf()
                p3f = P_("p3", [128, 512]); P3 = Buf()
                p3 = p3f[:].rearrange("p (h c) -> p h c", h=4)
                p4 = P_("p4", [128, 512]); P4 = Buf()
                fw.psum(P1, P2, P3, P4)
                fw.op("pool", lambda: nc.gpsimd.memset(kbgm[:], 0.0), writes=[KBGM])
                ones = self.C("ones")

                nm = "f" if d == 0 else "r"
                order = list(range(NT)) if d == 0 else [1, 0] + list(range(NT - 1, 1, -1))
                if getattr(self, "nt_dbg", None):
                    order = [t for t in order if t < self.nt_dbg]
                for hp in range(2):
                    fw.op("pool", lambda: nc.gpsimd.memset(Sm[hp][:], 0.0), writes=[SM[hp]])
                for t in order:
                    tk = slice(t * 128, (t + 1) * 128)
                    fw.dma(qTp[:], S["dn_qT"][0][:, tk].rearrange("(a p) t -> p a t", p=128), reads=[S["dn_qT"][1]], writes=[QTP])
                    fw.dma(kTp[:], S["dn_kT"][0][:, tk].rearrange("(a p) t -> p a t", p=128), reads=[S["dn_kT"][1]], writes=[KTP])
                    fw.dma(ktok[:], S["dn_ktok"][0][tk, :], reads=[S["dn_ktok"][1]], writes=[KTOK])
                    fw.dma(vtok[:], S["dn_vtok"][0][tk, :], reads=[S["dn_vtok"][1]], writes=[VTOK])
                    fw.dma(bag[:], S["tokmaj"][0][tk, 256:528], reads=[S["tokmaj"][1]], writes=[BAG])
                    fw.op("act", lambda: nc.scalar.activation(out=sm[:, 0:4], in_=bag[:, d * 4:d * 4 + 4], func=AF.Sigmoid), reads=[BAG], writes=[SMB])
                    fw.op("dve", lambda: nc.vector.tensor_scalar(out=sm[:, 4:8], in0=sm[:, 0:4], scalar1=-1.0, scalar2=None, op0=ALU.mult), reads=[SMB], writes=[SMB])
                    fw.op("dve", lambda: nc.vector.tensor_tensor(out=sm[:, 8:12], in0=bag[:, 8 + d * 4:12 + d * 4], in1=self.rowp[:, dto + d * 4:dto + d * 4 + 4], op=ALU.add),
                          reads=[BAG, self.ROWP], writes=[SMB])
                    fw.op("act", lambda: nc.scalar.activation(out=sm[:, 8:12], in_=sm[:, 8:12], func=AF.Exp), reads=[SMB], writes=[SMB])
                    fw.op("act", lambda: nc.scalar.activation(out=sm[:, 8:12], in_=sm[:, 8:12], func=AF.Ln, bias=ones[:, 0:1], scale=1.0), reads=[SMB, self.CST], writes=[SMB])
                    fw.op("dve", lambda: nc.vector.tensor_tensor(out=sm[:, 12:16], in0=sm[:, 8:12], in1=negA[:, d * 4:d * 4 + 4], op=ALU.mult), reads=[SMB, NEGA], writes=[SMB])
                    g4 = sm[:, 12:16]
                    yield
                    fw.op("dve", lambda: nc.vector.tensor_tensor(out=G4[:], in0=bc(self.C("tri_" + nm), S4, 1), in1=bc(g4, S4, 2), op=ALU.mult), reads=[self.CST, SMB], writes=[GB_])
                    cco, _ = COFF["cc4"]
                    fw.op("dve", lambda: nc.vector.tensor_tensor(out=gch[:], in0=bc(self.cst[:, cco:cco + 4], [128, 4, 4], 1), in1=bc(g4, [128, 4, 4], 2), op=ALU.mult),
                          reads=[self.CST, SMB], writes=[GCH])
                    for hp in range(2):
                        fw.op("pool", lambda: nc.gpsimd.tensor_tensor(out=kTm[:, 2 * hp:2 * hp + 2, :], in0=bc(kTp[:, hp, :], [128, 2, 128], 1), in1=bc(hm64, [128, 2, 128], 2), op=ALU.mult),
                              reads=[KTP, self.CST], writes=[KTM])
                    mm = lambda out, lhsT, rhs, st_, sp_, rd, W: fw.op("pe", lambda: nc.tensor.matmul(out, lhsT=lhsT, rhs=rhs, start=st_, stop=sp_), reads=rd, writes=[W])
                    for h in range(4):
                        mm(p1[:, h, :], ones, G4[:, h, :], True, True, [GB_, self.CST], P1)
                    for h in range(4):
                        mm(p3f[:, 256 + h * 2:256 + h * 2 + 2], G4[:, h, :], ones[:, 0:2], True, True, [GB_, self.CST], P3)
                    yield
                    for h in range(4):
                        mm(p2[:, h, :], kTm[:, h, :], kTp[:, h // 2, :], True, True, [KTM, KTP], P2)
                    yield
                    lastc = (63, 127) if d == 0 else (0, 64)
                    fw.op("dve", lambda: nc.vector.tensor_copy(out=gcs[:, :, 0], in_=p3f[:, 256:264].rearrange("p (h c) -> p h c", h=4)[:, :, 0]), reads=[P3], writes=[GCS])
                    for c in range(2):
                        fw.op("dve", lambda: nc.vector.tensor_copy(out=gcs[:, :, 2 + c], in_=p1[:, :, lastc[c]]), reads=[P1], writes=[GCS])
                        fw.op("dve", lambda: nc.vector.tensor_copy(out=gcs[64 * c:64 * c + 64, :, 1], in_=p1[64 * c:64 * c + 64, :, lastc[c]]), reads=[P1], writes=[GCS])
                    fw.op("dve", lambda: nc.vector.tensor_tensor(out=dm[:], in0=bc(gcs[:, :, 0], S4, 2), in1=p1[:], op=ALU.subtract), reads=[GCS, P1], writes=[DM])
                    fw.op("dve", lambda: nc.vector.tensor_scalar(out=dm[:], in0=dm[:], scalar1=0.0, scalar2=-40.0, op0=ALU.min, op1=ALU.max), reads=[DM], writes=[DM])
                    fw.op("pool", lambda: nc.gpsimd.tensor_tensor(out=dm[:], in0=dm[:], in1=bc(self.C("negc_" + nm), S4, 1), op=ALU.add), reads=[DM, self.CST], writes=[DM])
                    fw.op("act", lambda: nc.scalar.activation(out=dm[:], in_=dm[:], func=AF.Exp), reads=[DM], writes=[DM])
                    fw.op("pool", lambda: nc.gpsimd.tensor_tensor(out=dcs[:], in0=dm[:], in1=bc(self.C("strict_" + nm), S4, 1), op=ALU.mult), reads=[DM, self.CST], writes=[DCS])
                    fw.op("dve", lambda: nc.vector.tensor_tensor(out=dmT[:], in0=p1[:], in1=bc(gcs[:, :, 0], S4, 2), op=ALU.subtract), reads=[GCS, P1], writes=[DMT])
                    fw.op("dve", lambda: nc.vector.tensor_scalar(out=dmT[:], in0=dmT[:], scalar1=0.0, scalar2=-40.0, op0=ALU.min, op1=ALU.max), reads=[DMT], writes=[DMT])
                    fw.op("pool", lambda: nc.gpsimd.tensor_tensor(out=dmT[:], in0=dmT[:], in1=bc(self.C("negcT_" + nm), S4, 1), op=ALU.add), reads=[DMT, self.CST], writes=[DMT])
                    fw.op("act", lambda: nc.scalar.activation(out=dmT[:], in_=dmT[:], func=AF.Exp), reads=[DMT], writes=[DMT])
                    fw.op("dve", lambda: nc.vector.tensor_scalar(out=e1b[:], in0=p1[:], scalar1=-40.0, scalar2=None, op0=ALU.max), reads=[P1], writes=[E1B])
                    fw.op("act", lambda: nc.scalar.activation(out=e1b[:], in_=e1b[:], func=AF.Exp), reads=[E1B], writes=[E1B])
                    fw.op("dve", lambda: nc.vector.tensor_tensor(out=gcs[:, :, 6], in0=gcs[:, :, 1], in1=gcs[:, :, 0], op=ALU.subtract), reads=[GCS], writes=[GCS])
                    fw.op("dve", lambda: nc.vector.tensor_scalar(out=gcs[:, :, 6], in0=gcs[:, :, 6], scalar1=-40.0, scalar2=None, op0=ALU.max), reads=[GCS], writes=[GCS])
                    fw.op("dve", lambda: nc.vector.tensor_scalar(out=gcs[:, :, 0:4], in0=gcs[:, :, 0:4], scalar1=-40.0, scalar2=None, op0=ALU.max), reads=[GCS], writes=[GCS])
                    fw.op("act", lambda: nc.scalar.activation(out=gcs[:, :, 7], in_=gcs[:, :, 0], func=AF.Exp), reads=[GCS], writes=[GCS])
                    fw.op("act", lambda: nc.scalar.activation(out=gcs[:, :, 6], in_=gcs[:, :, 6], func=AF.Exp), reads=[GCS], writes=[GCS])
                    fw.op("act", lambda: nc.scalar.activation(out=gcs[:, :, 8:10], in_=gcs[:, :, 2:4], func=AF.Exp), reads=[GCS], writes=[GCS])
                    for h in range(4):
                        hp, hh = h // 2, h % 2
                        rows = slice(hh * 64, (hh + 1) * 64)
                        fw.op("pool", lambda: nc.gpsimd.tensor_copy(out=gendp[rows, hp, :], in_=gcs[rows, h, 8:10]), reads=[GCS], writes=[GENDP])
                    yield
                    fw.op("dve", lambda: nc.vector.tensor_tensor(out=P0f[:], in0=p2[:], in1=bc(sm[:, 4:8], S4, 2), op=ALU.mult), reads=[P2, SMB], writes=[P0F])
                    fw.op("pool", lambda: nc.gpsimd.tensor_tensor(out=P0f[:], in0=P0f[:], in1=dcs[:], op=ALU.mult), reads=[P0F, DCS], writes=[P0F])
                    for h in range(4):
                        mm(p2[:, h, :], kTm[:, h, :], qTp[:, h // 2, :], True, True, [KTM, QTP], P2)
                    for c in range(2):
                        fw.op("dve", lambda: nc.vector.scalar_tensor_tensor(out=aqkc[:, c, :, :], in0=p2[:], scalar=self.cst[:, cio + c:cio + c + 1], in1=dmT[:], op0=ALU.mult, op1=ALU.mult),
                              reads=[P2, self.CST, DMT], writes=[AQKC])
                    yield
                    for h in range(4):
                        fw.op("pe", lambda: nc.tensor.transpose(p2[:, h, :], P0f[:, h, :], identf), reads=[P0F, self.CST], writes=[P2])
                    fw.op("pool", lambda: nc.gpsimd.tensor_copy(out=Pm[0][:], in_=P0f[:]), reads=[P0F], writes=[PM[0]])
                    fw.op("act", lambda: nc.scalar.copy(out=Q0f[:], in_=p2[:]), reads=[P2], writes=[Q0F])
                    fw.op("dve", lambda: nc.vector.tensor_copy(out=Qm[0][:], in_=Q0f[:]), reads=[Q0F], writes=[QM[0]])
                    fw.op("pool", lambda: nc.gpsimd.tensor_tensor(out=Rm[0][:], in0=Q0f[:], in1=bc(identf, S4, 1), op=ALU.add), reads=[Q0F, self.CST], writes=[RM[0]])
                    cur = 0
                    rc = 0
                    for s_ in range(1, 6):
                        nx = 1 - cur
                        for h in range(4):
                            mm(p1[:, h, :], Qm[cur][:, h, :], Pm[cur][:, h, :], True, True, [QM[cur], PM[cur]], P1)
                        if s_ < 5:
                            for h in range(4):
                                mm(p2[:, h, :], Pm[cur][:, h, :], Qm[cur][:, h, :], True, True, [QM[cur], PM[cur]], P2)
                        fw.op("act", lambda: nc.scalar.copy(out=Pm[nx][:], in_=p1[:]), reads=[P1], writes=[PM[nx]])
                        if s_ < 5:
                            fw.op("dve", lambda: nc.vector.tensor_copy(out=Qm[nx][:], in_=p2[:]), reads=[P2], writes=[QM[nx]])
                        for h in range(4):
                            mm(p3[:, h, :], Pm[nx][:, h, :], Rm[rc][:, h, :], True, True, [PM[nx], RM[rc]], P3)
                        fw.op("dve", lambda: nc.vector.tensor_tensor(out=Rm[1 - rc][:], in0=p3[:], in1=Rm[rc][:], op=ALU.add), reads=[P3, RM[rc]], writes=[RM[1 - rc]])
                        yield
                        rc = 1 - rc
                        cur = nx
                    R = Rm[rc]; RB = RM[rc]
                    yield
                    v4 = vtok[:].rearrange("p (h d) -> p h d", h=4)
                    k4 = ktok[:].rearrange("p (h d) -> p h d", h=4)
                    fw.op("pool", lambda: nc.gpsimd.tensor_tensor(out=vb[:], in0=v4, in1=bc(sm[:, 0:4], [128, 4, 64], 2), op=ALU.mult), reads=[VTOK, SMB], writes=[VBB])
                    fw.op("dve", lambda: nc.vector.tensor_tensor(out=sm[:, 16:20], in0=sm[:, 0:4], in1=gcs[:, :, 7], op=ALU.mult), reads=[SMB, GCS], writes=[SMB])
                    for hh in range(2):
                        fw.op("dve", lambda: nc.vector.tensor_tensor(
                            out=kbgm[:].rearrange("p (a b) c -> p a b c", a=2)[:, :, hh, hh * 64:(hh + 1) * 64],
                            in0=ktok[:].rearrange("p (a b d) -> p a b d", a=2, b=2)[:, :, hh, :],
                            in1=bc(sm[:, 16:20].rearrange("p (a b) -> p a b", a=2)[:, :, hh], [128, 2, 64], 2), op=ALU.mult),
                            reads=[KTOK, SMB], writes=[KBGM])
                    for c in range(2):
                        fw.op("dve", lambda: nc.vector.scalar_tensor_tensor(out=kend[:, c, :].rearrange("p (h d) -> p h d", h=4), in0=k4, scalar=self.cst[:, cio + c:cio + c + 1],
                                                                            in1=bc(gcs[:, :, 6], [128, 4, 64], 2), op0=ALU.mult, op1=ALU.mult), reads=[KTOK, self.CST, GCS], writes=[KEND])
                    for h in range(4):
                        hp, hh = h // 2, h % 2
                        rows = slice(hh * 64, (hh + 1) * 64)
                        fw.op("pool", lambda: nc.gpsimd.tensor_tensor(out=qin[rows, hp, :], in0=qTp[rows, hp, :], in1=e1b[rows, h, :], op=ALU.mult), reads=[QTP, E1B], writes=[QIN])
                    for h in range(4):
                        mm(p3f[:, h * 64:(h + 1) * 64], R[:, h, :], vb[:, h, :], True, True, [RB, VBB], P3)
                    for h in range(4):
                        hp, hh = h // 2, h % 2
                        mm(p3f[:, 256 + hp * 128:256 + (hp + 1) * 128], kbgm[:, h, :], R[:, h, :], hh == 0, hh == 1, [KBGM, RB], P3)
                    fw.op("act", lambda: nc.scalar.copy(out=usb[:], in_=p3f[:, 0:256]), reads=[P3], writes=[USB])
                    fw.op("act", lambda: nc.scalar.copy(out=wT[:].rearrange("p a b -> p (a b)"), in_=p3f[:, 256:512]), reads=[P3], writes=[WT])
                    yield
                    for c in ((0, 1) if d == 0 else (1, 0)):
                        r0 = 64 * c
                        for hp in range(2):
                            mm(p3f[:, hp * 128:(hp + 1) * 128], wT[:, hp, :], Sm[hp][:], True, True, [WT, SM[hp]], P3)
                        fw.op("dve", lambda: nc.vector.tensor_tensor(out=vnew[:], in0=usb[:], in1=p3f[:, 0:256], op=ALU.subtract), reads=[USB, P3], writes=[VNEW])
                        for h in range(4):
                            hp, hh = h // 2, h % 2
                            hs = slice(h * 64, (h + 1) * 64)
                            mm(p4[:, hs], qin[:, hp, :], Sm[hp][:, hh * 64:(hh + 1) * 64], True, False, [QIN, SM[hp]], P4)
                            mm(p4[:, hs], aqkc[:, c, h, :], vnew[:, hs], False, True, [AQKC, VNEW], P4)
                        for hp in range(2):
                            mm(p4[:, 256 + hp * 128:256 + (hp + 1) * 128], kend[:, c, hp * 128:(hp + 1) * 128], vnew[:, hp * 128:(hp + 1) * 128], True, True, [KEND, VNEW], P4)
                        fw.op("dve", lambda: nc.vector.tensor_tensor(out=tmp[:], in0=p4[:, 256:512].rearrange("p (a b) -> p a b", a=2), in1=bc(self.C("bmask2"), [128, 2, 128], 1), op=ALU.mult),
                              reads=[P4, self.CST], writes=[TMP])
                        for hp in range(2):
                            fw.op("dve", lambda: nc.vector.scalar_tensor_tensor(out=Sm[hp][:], in0=Sm[hp][:], scalar=gendp[:, hp, c:c + 1], in1=tmp[:, hp, :], op0=ALU.mult, op1=ALU.add),
                                  reads=[SM[hp], GENDP, TMP], writes=[SM[hp]])
                        fw.op("act", lambda: nc.scalar.copy(out=osb[r0:r0 + 64, :], in_=p4[r0:r0 + 64, 0:256]), reads=[P4], writes=[OSB])
                        yield
                    dst = S["ofwd"] if d == 0 else S["orev"]
                    fw.dma(dst[0][tk, :], osb[:], reads=[OSB], writes=[dst[1]])
                    yield

            gens = [stream(0), stream(1)]
            while gens:
                for g in list(gens):
                    try:
                        next(g)
                    except StopIteration:
                        gens.remove(g)
            fw.barrier()
        with ExitStack() as st:
            T_ = lambda n, s, d=F32: st.enter_context(nc.sbuf_tensor(self.uniq(n), s, d))
            ident16, ID16 = self.common_tiles(st)
            gts = [self.gated_tiles(st, ident16, ID16) for i in range(2)]
            oa = [T_("oa%d" % i, [128, 256]) for i in range(2)]; OA = fw.bufs(2)
            ob_ = [T_("ob%d" % i, [128, 256]) for i in range(2)]; OB = fw.bufs(2)
            gg = [T_("gg%d" % i, [128, 256]) for i in range(2)]; GG = fw.bufs(2)
            tiles = list(range(NT))
            if getattr(self, "nt_dbg", None):
                tiles = [t for t in tiles if t < self.nt_dbg]
            for t in tiles:
                b = t % 2
                tk = slice(t * 128, (t + 1) * 128)
                fw.dma(oa[b][:], S["ofwd"][0][tk, :], reads=[S["ofwd"][1]], writes=[OA[b]])
                fw.dma(ob_[b][:], S["orev"][0][tk, :], reads=[S["orev"][1]], writes=[OB[b]])
                fw.dma(gg[b][:], S["tokmaj"][0][tk, 272:528], reads=[S["tokmaj"][1]], writes=[GG[b]])
                fw.op("pool", lambda: nc.gpsimd.tensor_tensor(out=oa[b][:], in0=oa[b][:], in1=ob_[b][:], op=ALU.add), reads=[OA[b], OB[b]], writes=[OA[b]])
                self.gated_out(st, gts[b], oa[b], OA[b], gg[b][:], GG[b], "dn_on", 512, t)
            fw.barrier()

    def load_cast(self, st, name, src_ap_fn, nk, ncols, blk, eng="pool"):
        fw, nc = self.fw, self.nc
        w = st.enter_context(nc.sbuf_tensor(self.uniq(name), [128, nk, ncols], BF16))
        W = Buf()
        with ExitStack() as st2:
            stg = [st2.enter_context(nc.sbuf_tensor(self.uniq(name + "s"), [128, nk, blk], F32)) for i in range(2)]
            STG = self.fw.bufs(2)
            nb = (ncols + blk - 1) // blk
            for cb in range(nb):
                b = cb % 2
                c0 = cb * blk
                n = min(blk, ncols - c0)
                fw.dma(stg[b][:, :, 0:n], src_ap_fn(c0, n), writes=[STG[b]])
                e = ("pool", "dve")[cb % 2] if eng == "both" else eng
                if e == "pool":
                    fw.op("pool", lambda: nc.gpsimd.tensor_copy(out=w[:, :, c0:c0 + n], in_=stg[b][:, :, 0:n]), reads=[STG[b]], writes=[W])
                else:
                    fw.op("dve", lambda: nc.vector.tensor_copy(out=w[:, :, c0:c0 + n], in_=stg[b][:, :, 0:n]), reads=[STG[b]], writes=[W])
            fw.barrier()
        return w, W

    def phaseF(self, l, xsrc):
        fw, nc, I = self.fw, self.nc, self.I
        S = self.scr
        last = (l == DEPTH - 1)
        with ExitStack() as st:
            T_ = lambda n, s, d=F32: st.enter_context(nc.sbuf_tensor(self.uniq(n), s, d))
            P_ = lambda n, s, d=F32: st.enter_context(nc.psum_tensor(self.uniq(n), s, d))
            ident16, ID16 = self.common_tiles(st)
            wout, WOUT = self.load_cast(st, "wout", lambda c0, n: I["w_out"][l, :, c0:c0 + n].rearrange("(k p) c -> p k c", p=128), 8, D, 256)
            zt = T_("zt", [128, 8, 128], BF16)
            ZT = Buf()
            fw.op("pool", lambda: nc.gpsimd.memset(zt[:], 0.0), writes=[ZT])
            h2 = S["h2T"][0].rearrange("(k p) t -> p k t", p=128)
            H2 = S["h2T"][1]
            fw.dma(h2[:, :, 0:1], zt[:, :, 0:1], reads=[ZT], writes=[H2], allow_slow_non_contiguous=True)
            fw.dma(h2[:, :, 257:259], zt[:, :, 0:2], reads=[ZT], writes=[H2], allow_slow_non_contiguous=True)
            fw.dma(h2[:, :, 4355:4355 + 128], zt[:, :, :], reads=[ZT], writes=[H2])
            fw.dma(h2[:, :, 4483:4483 + 109], zt[:, :, 0:109], reads=[ZT], writes=[H2])
            mx = [T_("mx%d" % i, [128, 8, 128], BF16) for i in range(2)]
            MX = fw.bufs(2)
            xt = [T_("xt%d" % i, [128, 1024]) for i in range(2)]
            XT = fw.bufs(2)
            x1 = [T_("x1t%d" % i, [128, 1024]) for i in range(2)]
            X1 = fw.bufs(2)
            tmp = [T_("tmp%d" % i, [128, 512]) for i in range(2)]
            TMP = fw.bufs(2)
            hT = [T_("h2t%d" % i, [128, 8, 128], BF16) for i in range(2)]
            HT = fw.bufs(2)
            pst = [P_("pst%d" % i, [128, 1024], BF16) for i in range(2)]
            PST = fw.bufs(2)
            mp = [P_("mp%d" % i, [128, 512]) for i in range(4)]
            MP = fw.bufs(4)
            fw.psum(PST, MP)
            mixv = S["mixT"][0].rearrange("(k p) t -> p k t", p=128)
            for t in range(NT):
                b = t % 2
                s = 1 if t < 2 else 0
                fw.dma(mx[b][:], mixv[:, :, t * 128:(t + 1) * 128], reads=[S["mixT"][1]], writes=[MX[b]])
                fw.dma(xt[b][:], xsrc[t * 128:(t + 1) * 128, :], writes=[XT[b]])
                for hc in range(2):
                    pi = (2 * t + hc) % 4
                    for k in range(8):
                        fw.op("pe", lambda: nc.tensor.matmul(mp[pi][:, :], lhsT=mx[b][:, k, :], rhs=wout[:, k, hc * 512:(hc + 1) * 512],
                                                             start=(k == 0), stop=(k == 7)), reads=[MX[b], WOUT], writes=[MP[pi]])
                    fw.op("dve", lambda: nc.vector.tensor_tensor(out=tmp[hc][:], in0=mp[pi][:, :], in1=self.gb[:, s, hc * 512:(hc + 1) * 512], op=ALU.mult),
                          reads=[MP[pi], self.GB], writes=[TMP[hc]])
                    fw.op("pool", lambda: nc.gpsimd.tensor_tensor(out=x1[b][:, hc * 512:(hc + 1) * 512], in0=xt[b][:, hc * 512:(hc + 1) * 512], in1=tmp[hc][:], op=ALU.add),
                          reads=[TMP[hc], XT[b]], writes=[X1[b]])
                fw.dma(S["x1"][0][t * 128:(t + 1) * 128, :], x1[b][:], reads=[X1[b]], writes=[S["x1"][1]])
                self.norm_mod_T(st, 1, x1[b][:], X1[b], s, hT[b], HT[b], 0, "F", pst[b], PST[b], ident16, ID16)
                pos = (F_CTX0 + t * 128) if t < 2 else (F_LAT0 + (t - 2) * 128)
                fw.dma(h2[:, :, pos:pos + 128], hT[b][:], reads=[HT[b]], writes=[H2])
            fw.barrier()
        for half in range(2):
            with ExitStack() as st:
                T_ = lambda n, s, d=F32: st.enter_context(nc.sbuf_tensor(self.uniq(n), s, d))
                P_ = lambda n, s, d=F32: st.enter_context(nc.psum_tensor(self.uniq(n), s, d))
                NJ = 11
                a0 = half * NJ * 128
                g0 = DFF + half * NJ * 128
                wua, WUA = self.load_cast(st, "wua", lambda c0, n: I["w_up"][l, :, a0 + c0:a0 + c0 + n].rearrange("(k p) c -> p k c", p=128), 8, NJ * 128, 352, eng="both")
                wug, WUG = self.load_cast(st, "wug", lambda c0, n: I["w_up"][l, :, g0 + c0:g0 + c0 + n].rearrange("(k p) c -> p k c", p=128), 8, NJ * 128, 352, eng="both")
                wdn, WDN = self.load_cast(st, "wdn", lambda c0, n: I["w_down"][l, half * NJ * 128:(half + 1) * NJ * 128, c0:c0 + n].rearrange("(k p) c -> p k c", p=128), NJ, D, 256, eng="both")
                xin_ap = S["x1"][0] if half == 0 else S["xa"][0]
                XIN = S["x1"][1] if half == 0 else S["xa"][1]
                hw = [T_("hw%d" % i, [128, 8, 512], BF16) for i in range(2)]
                HW = fw.bufs(2)
                zT = T_("zT", [128, NJ, 512], BF16)
                ZT = Buf()
                ya = [T_("ya%d" % i, [128, 512]) for i in range(2)]
                YA = fw.bufs(2)
                yg = [T_("yg%d" % i, [128, 512]) for i in range(2)]
                YG = fw.bufs(2)
                sgt = [T_("sgt%d" % i, [128, 512]) for i in range(2)]
                SGT = fw.bufs(2)
                xt = [T_("fxt%d" % i, [128, 1024]) for i in range(2)]
                XT = fw.bufs(2)
                tmp = [T_("ftmp%d" % i, [128, 512]) for i in range(2)]
                TMP = fw.bufs(2)
                up = [P_("up%d" % i, [128, 512]) for i in range(4)]
                UP = fw.bufs(4)
                dp = [P_("dp%d" % i, [128, 512]) for i in range(2)]
                DP = fw.bufs(2)
                fw.psum(UP, DP)
                h2 = S["h2T"][0].rearrange("(k p) t -> p k t", p=128)
                fo, _ = COLS["ffn_w"]
                it = 0
                for fb in range(NFB):
                    b = fb % 2
                    c0 = fb * FB
                    fw.dma(hw[b][:], h2[:, :, c0:c0 + 512], reads=[S["h2T"][1]], writes=[HW[b]])
                    for j in range(NJ):
                        jb = j % 2
                        ja = half * NJ + j
                        for (which, wt, WT, y, Y, cj) in ((0, wua, WUA, ya[jb], YA[jb], ja), (1, wug, WUG, yg[jb], YG[jb], 22 + ja)):
                            pi = it % 4
                            it += 1
                            for k in range(8):
                                fw.op("pe", lambda: nc.tensor.matmul(up[pi][:, :], lhsT=wt[:, k, j * 128:(j + 1) * 128], rhs=hw[b][:, k, :],
                                                                     start=(k == 0), stop=(k == 7)), reads=[WT, HW[b]], writes=[UP[pi]])
                            wc = lambda tap: self.colp[:, fo + cj * 3 + tap:fo + cj * 3 + tap + 1]
                            fw.op("act", lambda: nc.scalar.activation(out=y[:, 0:FB], in_=up[pi][:, 0:FB], func=AF.Identity, bias=0.0, scale=wc(0)),
                                  reads=[UP[pi], self.COLP], writes=[Y])
                            fw.op("dve", lambda: nc.vector.scalar_tensor_tensor(out=y[:, 0:FB], in0=up[pi][:, 1:FB + 1], scalar=wc(1), in1=y[:, 0:FB], op0=ALU.mult, op1=ALU.add),
                                  reads=[UP[pi], self.COLP, Y], writes=[Y])
                            fw.op("dve", lambda: nc.vector.scalar_tensor_tensor(out=y[:, 0:FB], in0=up[pi][:, 2:FB + 2], scalar=wc(2), in1=y[:, 0:FB], op0=ALU.mult, op1=ALU.add),
                                  reads=[UP[pi], self.COLP, Y], writes=[Y])
                        fw.op("act", lambda: nc.scalar.activation(out=sgt[jb][:, 0:FB], in_=yg[jb][:, 0:FB], func=AF.Silu), reads=[YG[jb]], writes=[SGT[jb]])
                        fw.op("pool", lambda: nc.gpsimd.tensor_tensor(out=zT[:, j, 0:FB], in0=sgt[jb][:, 0:FB], in1=ya[jb][:, 0:FB], op=ALU.mult),
                              reads=[SGT[jb], YA[jb]], writes=[ZT])
                    for sub in range(4):
                        q0 = fb * FB + 1 + sub * 128
                        n = min(128, FB - sub * 128)
                        segs = []
                        for (p0, p1, t0) in ((F_CTX0, F_CTX0 + CTX, 0), (F_LAT0, F_LAT0 + SEQ, CTX)):
                            lo = max(q0, p0)
                            hi = min(q0 + n, p1)
                            if hi > lo:
                                segs.append((lo - q0, hi - lo, t0 + lo - p0))
                        if not segs:
                            continue
                        s = 1 if q0 < F_CTX0 + CTX else 0
                        xb = (fb * 4 + sub) % 2
                        for (r0, nr, tok0) in segs:
                            fw.dma(xt[xb][r0:r0 + nr, :], xin_ap[tok0:tok0 + nr, :], reads=[XIN], writes=[XT[xb]])
                        for hc in range(2):
                            for j in range(NJ):
                                fw.op("pe", lambda: nc.tensor.matmul(dp[hc][0:n, :], lhsT=zT[:, j, sub * 128:sub * 128 + n], rhs=wdn[:, j, hc * 512:(hc + 1) * 512],
                                                                     start=(j == 0), stop=(j == NJ - 1)), reads=[ZT, WDN], writes=[DP[hc]])
                            fw.op("dve", lambda: nc.vector.tensor_tensor(out=tmp[hc][0:n, :], in0=dp[hc][0:n, :], in1=self.gb[0:n, 2 + s, hc * 512:(hc + 1) * 512], op=ALU.mult),
                                  reads=[DP[hc], self.GB], writes=[TMP[hc]])
                            fw.op("pool", lambda: nc.gpsimd.tensor_tensor(out=xt[xb][0:n, hc * 512:(hc + 1) * 512], in0=xt[xb][0:n, hc * 512:(hc + 1) * 512], in1=tmp[hc][0:n, :], op=ALU.add),
                                  reads=[TMP[hc], XT[xb]], writes=[XT[xb]])
                        for (r0, nr, tok0) in segs:
                            if half == 0:
                                fw.dma(S["xa"][0][tok0:tok0 + nr, :], xt[xb][r0:r0 + nr, :], reads=[XT[xb]], writes=[S["xa"][1]])
                            elif not last:
                                fw.dma(S["xres"][0][tok0:tok0 + nr, :], xt[xb][r0:r0 + nr, :], reads=[XT[xb]], writes=[S["xres"][1]])
                            elif tok0 >= CTX:
                                fw.dma(self.out[tok0 - CTX:tok0 - CTX + nr, :], xt[xb][r0:r0 + nr, :], reads=[XT[xb]], writes=[self.OUTB])
                            if self.dbg and half == 1 and last:
                                fw.dma(S["xres"][0][tok0:tok0 + nr, :], xt[xb][r0:r0 + nr, :], reads=[XT[xb]], writes=[S["xres"][1]])
                fw.barrier()


def _host_inputs(inp):
    rope = _rope_tables()
    per_core = []
    packed = []
    for b in range(8):
        cvec = np.stack([inp["c"][b], inp["c_ctx"]], axis=0).astype(np.float32)
        cols, rows, w2p = [], [], []
        for l in range(DEPTH):
            c_, r_, w_ = _pack_params(inp, l, cvec)
            cols.append(c_); rows.append(r_); w2p.append(w_)
        xin = np.concatenate([inp["ctx"][b], inp["x"][b]], axis=0).astype(np.float32)
        per_core.append({
            "xin": np.ascontiguousarray(xin), "consts": CONSTS, "rope": rope,
            "cols": np.stack(cols), "rows": np.stack(rows), "w2p": np.stack(w2p),
            "w_mod": inp["w_mod"], "w_in": inp["w_in"], "w_out": inp["w_out"],
            "w_up": inp["ffn_w_up"], "w_down": inp["ffn_w_down"],
        })
    return per_core


def kernel(**inputs):
    inp = {k: np.asarray(v) for k, v in inputs.items()}
    bld = Builder(phases=PHASES)
    nc = bld.build()
    in_maps = _host_inputs(inp)
    res = run_bass_kernel_spmd(nc, in_maps, core_ids=list(range(8)))
    out = np.stack([np.asarray(r["out"]) for r in res.results], axis=0)
    return out.astype(np.float32)
```

```python
import math
import numpy as np
from contextlib import ExitStack
import concourse.bass as bass
import concourse.mybir as mybir
from concourse.bass_utils import run_bass_kernel_spmd

F32 = mybir.dt.float32
BF16 = mybir.dt.bfloat16
ALU = mybir.AluOpType
AF = mybir.ActivationFunctionType
AX = mybir.AxisListType

D = 1024
SEQ = 4096
CTX = 256
T = SEQ + CTX
NT = T // 128
DEPTH = 2
INC = 3120
DFF = 2816
EPS = 1e-6
CMP = 15
TP = CMP + CTX + 2 * CMP + SEQ + CMP
CM_CTX0 = CMP
CM_LAT0 = CMP + CTX + 2 * CMP
DNP = 2
TPD = DNP + CTX + 2 * DNP + SEQ + DNP
DN_CTX0 = DNP
DN_LAT0 = DNP + CTX + 2 * DNP
NTM = 1072
FB = 510
NFB = 9
FPAD = 1 + NFB * FB + 1
F_CTX0 = 1
F_LAT0 = 1 + CTX + 2


class Buf:
    __slots__ = ("name", "lw", "rd", "ps")

    def __init__(self, name=""):
        self.name = name
        self.lw = None
        self.rd = []
        self.ps = False


class FW:
    NDMA = 40

    def __init__(self, nc, stack):
        self.nc = nc
        self.eng = {"pe": nc.tensor, "act": nc.scalar, "dve": nc.vector,
                    "pool": nc.gpsimd, "sp": nc.sync}
        self.sem = {}
        self.cnt = {}
        for e in self.eng:
            self.sem[e] = stack.enter_context(nc.semaphore("s_" + e))
            self.cnt[e] = 0
        self.dsem = [stack.enter_context(nc.semaphore("d%d" % i)) for i in range(self.NDMA)]
        self.dcnt = [0] * self.NDMA
        self.dnext = 0
        self.seen = {e: {} for e in self.eng}

    def buf(self, name=""):
        return Buf(name)

    def bufs(self, n, name=""):
        return [Buf(name + str(i)) for i in range(n)]

    def _semobj(self, key):
        return self.sem[key] if isinstance(key, str) else self.dsem[key]

    def _wait(self, e, ev):
        if ev is None:
            return
        key, val = ev
        if key == "pe" and e == "pe":
            return
        if key == e and val <= self.cnt[e] - 6:
            return
        if self.seen[e].get(key, 0) >= val:
            return
        self.seen[e][key] = val
        self.eng[e].wait_ge(self._semobj(key), val)

    def psum(self, *bufs):
        for b in bufs:
            if isinstance(b, (list, tuple)):
                self.psum(*b)
            else:
                b.ps = True

    def _deps(self, e, reads, writes):
        for b in reads:
            self._wait(e, b.lw)
            if b.ps:
                for ev in b.rd:
                    if ev[0] != e:
                        self._wait(e, ev)
        for b in writes:
            self._wait(e, b.lw)
            for ev in b.rd:
                self._wait(e, ev)

    def _commit(self, ev, reads, writes):
        for b in reads:
            b.rd.append(ev)
            if len(b.rd) > 48:
                best = {}
                for k, v in b.rd:
                    if best.get(k, 0) < v:
                        best[k] = v
                b.rd = list(best.items())
        for b in writes:
            b.lw = ev
            b.rd = []

    def op(self, e, fn, reads=(), writes=()):
        self._deps(e, reads, writes)
        ins = fn()
        self.cnt[e] += 1
        ins.then_inc(self.sem[e], 1)
        self._commit((e, self.cnt[e]), reads, writes)
        return ins

    def dma(self, out, in_, reads=(), writes=(), q="sp", **kw):
        k = self.dnext
        self.dnext = (self.dnext + 1) % self.NDMA
        if self.dcnt[k] > 0:
            self._wait(q, (k, self.dcnt[k]))
        self._deps(q, reads, writes)
        ins = self.eng[q].dma_start(out=out, in_=in_, **kw)
        self.dcnt[k] += 16
        ins.then_inc(self.dsem[k], 16)
        self._commit((k, self.dcnt[k]), reads, writes)
        return ins

    def barrier(self):
        for e in self.eng:
            for f in self.eng:
                if f != e and self.cnt[f] > 0:
                    self._wait(e, (f, self.cnt[f]))
            if self.cnt[e] > 0 and e != "pe" and self.seen[e].get(e, 0) < self.cnt[e]:
                self.seen[e][e] = self.cnt[e]
                self.eng[e].wait_ge(self.sem[e], self.cnt[e])
            for k in range(self.NDMA):
                if self.dcnt[k] > 0:
                    self._wait(e, (k, self.dcnt[k]))


def _consts():
    c = {}
    idx = np.arange(128)
    same = (idx[:, None] // 64) == (idx[None, :] // 64)
    le = idx[:, None] <= idx[None, :]
    ge = idx[:, None] >= idx[None, :]
    c["ident"] = np.eye(128)
    c["ones"] = np.ones((128, 128))
    c["negones"] = -np.ones((128, 128))
    c["blk"] = same.astype(np.float64)
    for d, (a_le, name) in enumerate(((le, "f"), (ge, "r"))):
        tri = (same & a_le).astype(np.float64)
        c["tri_" + name] = tri
        c["tris_" + name] = -tri / 16.0
        causal = tri.T
        c["negc_" + name] = np.where(causal > 0, 0.0, -30000.0)
        c["negcT_" + name] = np.where(tri > 0, 0.0, -30000.0)
        c["strict_" + name] = causal * (1 - np.eye(128))
        c["cT4_" + name] = np.tile(tri, (1, 4))
    c["blks"] = -same.astype(np.float64) / 16.0
    c["blk32"] = ((idx[:, None] // 32) == (idx[None, :] // 32)).astype(np.float64) / 32.0
    c["blk64"] = same.astype(np.float64)
    c["div256"] = np.ones((128, 128)) / 256.0
    perm = np.zeros((128, 128))
    for m in range(128):
        k = m + 16 if (m % 32) < 16 else m - 16
        perm[k, m] = 1.0
    c["perm"] = perm
    c["bmask4"] = ((idx[:, None] // 32) == (np.arange(256)[None, :] // 64)).astype(np.float64)
    c["bmask2"] = same.astype(np.float64)
    c["hm32"] = ((idx[:, None] // 32) == np.arange(4)[None, :]).astype(np.float64)
    ci = np.zeros((128, 2)); ci[:64, 0] = 1; ci[64:, 1] = 1
    c["chunkind"] = ci
    c["cc4"] = np.concatenate([ci, np.ones((128, 2))], axis=1)
    sel = np.zeros((128, 256)); sel[0, :128] = 1; sel[1, 128:] = 1
    c["sel"] = sel
    names = list(c.keys())
    offs = {}
    o = 0
    for n in names:
        offs[n] = (o, c[n].shape[1])
        o += c[n].shape[1]
    arr = np.concatenate([c[n] for n in names], axis=1).astype(np.float32)
    return arr, offs


def _rope_tables():
    rows = SEQ // 64
    row = np.repeat(np.arange(rows, dtype=np.float32), 64)
    col = np.tile(np.arange(64, dtype=np.float32), rows)
    nf = 8
    inv = (np.float32(10000.0) ** (-np.arange(nf, dtype=np.float32) / nf)).astype(np.float32)
    ang = np.concatenate([row[:, None] * inv, col[:, None] * inv], axis=-1).astype(np.float32)
    cos = np.cos(ang).astype(np.float32)
    sin = np.sin(ang).astype(np.float32)
    p = np.arange(128)
    ct = cos[:, p % 16].T
    st = sin[:, p % 16].T * np.where((p % 32) < 16, -1.0, 1.0)[:, None]
    return np.ascontiguousarray(np.concatenate([ct, st], axis=1).astype(np.float32))


CONSTS, COFF = _consts()
NCONST = CONSTS.shape[1]

COLS = {}
_o = 0
for _n, _w in (("b_mod", 48), ("n1g", 8), ("n2g", 8), ("cm_w", 62), ("cm_b", 2), ("cm_lg", 2), ("cm_lb", 2),
               ("qg", 1), ("kg", 1), ("dn_w", 30), ("ffn_w", 132), ("gla_b2", 2), ("ccol", 16)):
    COLS[_n] = (_o, _w)
    _o += _w
NCOL = _o
ROWS = {}
_o = 0
for _n, _w in (("subln", 64), ("dn_on", 64), ("gla_on", 64), ("a_log", 8), ("dt_b", 8), ("lam", 128),
               ("b_g1", 1024), ("b_g2", 1024), ("gla_b2r", 256)):
    ROWS[_n] = (_o, _w)
    _o += _w
NROW = _o


def _pack_params(inp, l, cvec):
    cols = np.zeros((128, NCOL), np.float32)

    def put(name, a):
        o, w = COLS[name]
        assert a.shape == (128, w), (name, a.shape)
        cols[:, o:o + w] = a

    put("b_mod", inp["b_mod"][l].reshape(48, 128).T)
    put("n1g", inp["norm1_g"][l].reshape(8, 128).T)
    put("n2g", inp["norm2_g"][l].reshape(8, 128).T)
    put("cm_w", inp["cm_conv_w"][l].reshape(31, 2, 128).transpose(2, 1, 0).reshape(128, 62))
    put("cm_b", inp["cm_conv_b"][l].reshape(2, 128).T)
    put("cm_lg", inp["cm_ln_g"][l].reshape(2, 128).T)
    put("cm_lb", inp["cm_ln_b"][l].reshape(2, 128).T)
    put("qg", np.tile(inp["da_qnorm_g"][l], 4)[:, None])
    put("kg", np.tile(inp["da_knorm_g"][l], 4)[:, None])
    put("dn_w", inp["dn_conv_w"][l].reshape(5, 6, 128).transpose(2, 1, 0).reshape(128, 30))
    put("ffn_w", inp["ffn_conv_w"][l].reshape(3, 44, 128).transpose(2, 1, 0).reshape(128, 132))
    put("gla_b2", inp["gla_b2"][l].T)
    put("ccol", cvec.reshape(2, 8, 128).transpose(2, 1, 0).reshape(128, 16))
    rows = np.zeros((1, NROW), np.float32)

    def putr(name, a):
        o, w = ROWS[name]
        rows[0, o:o + w] = a.reshape(-1)

    putr("subln", inp["da_subln_g"][l])
    putr("dn_on", inp["dn_onorm_g"][l])
    putr("gla_on", inp["gla_onorm_g"][l])
    putr("a_log", inp["dn_a_log"][l])
    putr("dt_b", inp["dn_dt_bias"][l])
    putr("lam", inp["da_lambda"][l])
    putr("b_g1", inp["b_mod"][l][2048:3072])
    putr("b_g2", inp["b_mod"][l][5120:6144])
    putr("gla_b2r", inp["gla_b2"][l])
    w2p = np.zeros((2, 32, 128), np.float32)
    w2p[0, 0:16] = inp["gla_w2"][l][0]
    w2p[1, 16:32] = inp["gla_w2"][l][1]
    return cols, rows, w2p


PHASES = None


class Builder:
    def __init__(self, dbg=False, phases=None):
        self.dbg = dbg
        self.phases = phases
        self.nc = bass.Bass("TRN2", target_bir_lowering=False)
        self.scr = {}

    def din(self, name, shape, dt=F32):
        return self.nc.dram_tensor(name, list(shape), dt, kind="ExternalInput").ap()

    def dscr(self, name, shape, dt=F32):
        kind = "ExternalOutput" if self.dbg else "Internal"
        if name in getattr(self, "dbg_inputs", ()):
            kind = "ExternalInput"
        t = self.nc.dram_tensor(name, list(shape), dt, kind=kind).ap()
        self.scr[name] = (t, Buf(name))
        return t

    def uniq(self, n):
        self._u = getattr(self, "_u", 0) + 1
        return "%s_%d" % (n, self._u)

    def want(self, ph):
        return self.phases is None or ph in self.phases

    def build(self):
        nc = self.nc
        I = {}
        I["xin"] = self.din("xin", [T, D])
        I["consts"] = self.din("consts", [128, NCONST])
        I["rope"] = self.din("rope", [128, 2 * SEQ])
        I["cols"] = self.din("cols", [DEPTH, 128, NCOL])
        I["rows"] = self.din("rows", [DEPTH, 1, NROW])
        I["w2p"] = self.din("w2p", [DEPTH, 2, 32, 128])
        I["w_mod"] = self.din("w_mod", [DEPTH, D, 6 * D])
        I["w_in"] = self.din("w_in", [DEPTH, D, INC])
        I["w_out"] = self.din("w_out", [DEPTH, D, D])
        I["w_up"] = self.din("w_up", [DEPTH, D, 2 * DFF])
        I["w_down"] = self.din("w_down", [DEPTH, DFF, D])
        self.I = I
        self.out = nc.dram_tensor("out", [SEQ, D], F32, kind="ExternalOutput").ap()
        self.OUTB = Buf("out")
        self.dscr("xres", [T, D])
        self.dscr("x1", [T, D])
        self.dscr("xa", [T, D])
        self.dscr("cmY", [256, TP])
        self.dscr("qT", [256, T])
        self.dscr("kT", [256, T])
        self.dscr("dnqkv", [768, TPD])
        self.dscr("glaqT", [128, T])
        self.dscr("glakT", [128, T])
        self.dscr("glalrT", [32, T])
        self.dscr("tokmaj", [T, NTM])
        self.dscr("mixT", [D, T], BF16)
        self.dscr("dn_qT", [256, T])
        self.dscr("dn_kT", [256, T])
        self.dscr("dn_ktok", [T, 256])
        self.dscr("dn_vtok", [T, 256])
        self.dscr("ofwd", [T, 256])
        self.dscr("orev", [T, 256])
        self.dscr("h2T", [8 * 128, FPAD], BF16)
        with ExitStack() as top:
            self.fw = FW(nc, top)
            fw = self.fw
            self.cst = top.enter_context(nc.sbuf_tensor("cst", [128, NCONST], F32))
            self.CST = Buf("cst")
            fw.dma(self.cst[:], I["consts"][:, :], writes=[self.CST])
            self.colp = top.enter_context(nc.sbuf_tensor("colp", [128, NCOL], F32))
            self.COLP = Buf("colp")
            self.rowp = top.enter_context(nc.sbuf_tensor("rowp", [128, NROW], F32))
            self.ROWP = Buf("rowp")
            self.modc = top.enter_context(nc.sbuf_tensor("modc", [128, 96], F32))
            self.MODC = Buf("modc")
            self.a1 = top.enter_context(nc.sbuf_tensor("a1", [128, 32], F32))
            self.A1 = Buf("a1")
            self.gb = top.enter_context(nc.sbuf_tensor("gb", [128, 4, D], F32))
            self.GB = Buf("gb")
            for l in range(getattr(self, 'depth_run', DEPTH)):
                self.layer(l)
            fw.barrier()
        return nc

    def C(self, name, rows=128):
        o, w = COFF[name]
        return self.cst[0:rows, o:o + w]

    def col(self, name, j=0, n=1):
        o, w = COLS[name]
        return self.colp[:, o + j:o + j + n]

    def row(self, name, j=0, n=None):
        o, w = ROWS[name]
        if n is None:
            n = w
        return self.rowp[:, o + j:o + j + n]

    def layer(self, l):
        fw, nc, I = self.fw, self.nc, self.I
        fw.barrier()
        fw.dma(self.colp[:], I["cols"][l, :, :], writes=[self.COLP])
        fw.dma(self.rowp[:], I["rows"][l, :, :].partition_broadcast(128), writes=[self.ROWP])
        self.phase0(l)
        xsrc = I["xin"] if l == 0 else self.scr["xres"][0]
        if self.want("A"):
            self.phaseA(l, xsrc)
        if self.want("B"):
            self.phaseB(l)
        if self.want("C"):
            self.phaseC(l)
        if not (self.want("D") and self.want("E")):
            self.zero_mix(l)
        if self.want("D"):
            self.phaseD(l)
        if self.want("E"):
            self.phaseE(l)
        if self.want("F"):
            self.phaseF(l, xsrc)

    def phase0(self, l):
        fw, nc, I = self.fw, self.nc, self.I
        with ExitStack() as st:
            T_ = lambda n, s, d=F32: st.enter_context(nc.sbuf_tensor(self.uniq(n), s, d))
            P_ = lambda n, s, d=F32: st.enter_context(nc.psum_tensor(self.uniq(n), s, d))
            cact = T_("cact", [128, 16])
            CACT = Buf()
            wm = [T_("wm%d" % i, [128, 8, 1024]) for i in range(2)]
            WM = fw.bufs(2)
            mps = P_("mps", [128, 96])
            MPS = Buf()
            rps = [P_("rps%d" % i, [128, 512]) for i in range(2)]
            RPS = fw.bufs(2)
            grow = T_("grow", [2, 2, 1024])
            GROW = Buf()
            gps = [P_("gps%d" % i, [128, 512]) for i in range(2)]
            GPS = fw.bufs(2)
            fw.psum(MPS, RPS, GPS)
            o, w = COLS["ccol"]
            fw.op("act", lambda: nc.scalar.activation(out=cact[:], in_=self.colp[:, o:o + 16], func=AF.Silu),
                  reads=[self.COLP], writes=[CACT])
            ob, _ = COLS["b_mod"]
            for comp in range(6):
                b = comp % 2
                fw.dma(wm[b][:], I["w_mod"][l, :, comp * 1024:(comp + 1) * 1024].rearrange("(k p) c -> p k c", p=128),
                       writes=[WM[b]])
                for jj in range(8):
                    j = comp * 8 + jj
                    for k in range(8):
                        fw.op("pe", lambda: nc.tensor.matmul(mps[:, 2 * j:2 * j + 2], lhsT=wm[b][:, k, jj * 128:(jj + 1) * 128],
                                                             rhs=cact[:, 2 * k:2 * k + 2], start=(k == 0), stop=(k == 7)),
                              reads=[WM[b], CACT], writes=[MPS])
                if comp in (2, 5):
                    which = 0 if comp == 2 else 1
                    bname = "b_g1" if comp == 2 else "b_g2"
                    for hc in range(2):
                        for k in range(8):
                            fw.op("pe", lambda: nc.tensor.matmul(rps[hc][0:2, :], lhsT=cact[:, 2 * k:2 * k + 2],
                                                                 rhs=wm[b][:, k, hc * 512:(hc + 1) * 512],
                                                                 start=(k == 0), stop=False),
                                  reads=[WM[b], CACT], writes=[RPS[hc]])
                        ro, _ = ROWS[bname]
                        fw.op("pe", lambda: nc.tensor.matmul(rps[hc][0:2, :], lhsT=self.C("ones")[0:1, 0:2],
                                                             rhs=self.rowp[0:1, ro + hc * 512:ro + (hc + 1) * 512],
                                                             start=False, stop=True),
                              reads=[self.ROWP, self.CST], writes=[RPS[hc]])
                        fw.op("dve", lambda: nc.vector.tensor_copy(out=grow[:, which, hc * 512:(hc + 1) * 512], in_=rps[hc][0:2, :]),
                              reads=[RPS[hc]], writes=[GROW])
            for s in range(2):
                fw.op("dve", lambda: nc.vector.tensor_tensor(out=self.modc[:].rearrange("p (j s) -> p j s", s=2)[:, :, s],
                                                             in0=mps[:].rearrange("p (j s) -> p j s", s=2)[:, :, s],
                                                             in1=self.colp[:, ob:ob + 48], op=ALU.add),
                      reads=[MPS, self.COLP], writes=[self.MODC])
            for which, (gname, scbase) in enumerate((("n1g", 8), ("n2g", 32))):
                go, _ = COLS[gname]
                for s in range(2):
                    fw.op("dve", lambda: nc.vector.scalar_tensor_tensor(
                        out=self.a1[:, which * 16:(which + 1) * 16].rearrange("p (k s) -> p k s", s=2)[:, :, s],
                        in0=self.modc[:].rearrange("p (j s) -> p j s", s=2)[:, scbase:scbase + 8, s],
                        scalar=1.0, in1=self.colp[:, go:go + 8], op0=ALU.add, op1=ALU.mult),
                        reads=[self.MODC, self.COLP], writes=[self.A1])
            so, _ = COFF["sel"]
            for which in range(2):
                for s in range(2):
                    for hc in range(2):
                        pi = hc
                        fw.op("pe", lambda: nc.tensor.matmul(gps[pi][:, :], lhsT=self.cst[0:2, so + s * 128:so + (s + 1) * 128],
                                                             rhs=grow[:, which, hc * 512:(hc + 1) * 512], start=True, stop=True),
                              reads=[GROW, self.CST], writes=[GPS[pi]])
                        fw.op("act", lambda: nc.scalar.copy(out=self.gb[:, which * 2 + s, hc * 512:(hc + 1) * 512], in_=gps[pi][:, :]),
                              reads=[GPS[pi]], writes=[self.GB])
            fw.barrier()

    def shift(self, k, s):
        raise NotImplementedError

    def norm_mod_T(self, st, which, xt, XT, s, hT, HT, hoff, tag, pst, PST, ident16, ID16):
        fw, nc = self.fw, self.nc
        sq, SQ, ssq, SSQ, xs, XS = self._nm_tmp
        fw.op("act", lambda: nc.scalar.activation(out=sq[:], in_=xt, func=AF.Square, accum_out=ssq[:, 0:1]),
              reads=[XT], writes=[SQ, SSQ])
        fw.op("act", lambda: nc.scalar.activation(out=ssq[:, 1:2], in_=ssq[:, 0:1], func=AF.Sqrt, bias=self.epsc[:, 0:1], scale=1.0 / D),
              reads=[SSQ], writes=[SSQ])
        fw.op("dve", lambda: nc.vector.reciprocal(out=ssq[:, 2:3], in_=ssq[:, 1:2]), reads=[SSQ], writes=[SSQ])
        fw.op("dve", lambda: nc.vector.tensor_scalar(out=xs[:], in0=xt, scalar1=ssq[:, 2:3], scalar2=None, op0=ALU.mult),
              reads=[XT, SSQ], writes=[XS])
        for k in range(8):
            fw.op("pe", lambda: nc.tensor.transpose(pst[:, k * 128:(k + 1) * 128], xs[:, k * 128:(k + 1) * 128], ident16[:]),
                  reads=[XS, ID16], writes=[PST])
        shbase = 0 if which == 0 else 24
        for k in range(8):
            eng = "act" if k % 2 == 0 else "dve"
            acol = self.a1[:, which * 16 + 2 * k + s:which * 16 + 2 * k + s + 1]
            shcol = self.modc[:, 2 * (shbase + k) + s:2 * (shbase + k) + s + 1]
            if eng == "act":
                fw.op("act", lambda: nc.scalar.activation(out=hT[:, k, hoff:hoff + 128], in_=pst[:, k * 128:(k + 1) * 128],
                                                          func=AF.Identity, bias=shcol, scale=acol),
                      reads=[PST, self.A1, self.MODC], writes=[HT])
            else:
                fw.op("dve", lambda: nc.vector.tensor_scalar(out=hT[:, k, hoff:hoff + 128], in0=pst[:, k * 128:(k + 1) * 128],
                                                             scalar1=acol, scalar2=shcol, op0=ALU.mult, op1=ALU.add),
                      reads=[PST, self.A1, self.MODC], writes=[HT])

    def common_tiles(self, st):
        fw, nc = self.fw, self.nc
        T_ = lambda n, s, d=F32: st.enter_context(nc.sbuf_tensor(self.uniq(n), s, d))
        self.epsc = T_("epsc", [128, 1])
        self.EPSC = Buf()
        fw.op("pool", lambda: nc.gpsimd.memset(self.epsc[:], EPS), writes=[self.EPSC])
        ident16 = T_("ident16", [128, 128], BF16)
        ID16 = Buf()
        fw.op("dve", lambda: nc.vector.tensor_copy(out=ident16[:], in_=self.C("ident")), reads=[self.CST], writes=[ID16])
        sq = T_("nm_sq", [128, 1024], BF16)
        ssq = T_("nm_ssq", [128, 4])
        xs = T_("nm_xs", [128, 1024], BF16)
        self._nm_tmp = (sq, Buf(), ssq, Buf(), xs, Buf())
        return ident16, ID16

    def phaseA(self, l, xsrc):
        fw, nc, I = self.fw, self.nc, self.I
        S = self.scr
        with ExitStack() as st:
            T_ = lambda n, s, d=F32: st.enter_context(nc.sbuf_tensor(self.uniq(n), s, d))
            P_ = lambda n, s, d=F32: st.enter_context(nc.psum_tensor(self.uniq(n), s, d))
            ident16, ID16 = self.common_tiles(st)
            win = T_("win", [128, 8, INC], BF16)
            WIN = Buf()
            stg = [T_("wstg%d" % i, [128, 8, 390]) for i in range(2)]
            STG = fw.bufs(2)
            for cb in range(8):
                b = cb % 2
                fw.dma(stg[b][:], I["w_in"][l, :, cb * 390:(cb + 1) * 390].rearrange("(k p) c -> p k c", p=128), writes=[STG[b]])
                fw.op("pool", lambda: nc.gpsimd.tensor_copy(out=win[:, :, cb * 390:(cb + 1) * 390], in_=stg[b][:]),
                      reads=[STG[b]], writes=[WIN])
            xt = [T_("xt%d" % i, [128, 2, 1024]) for i in range(2)]
            XT = fw.bufs(2)
            hT = [T_("hT%d" % i, [128, 8, 256], BF16) for i in range(2)]
            HT = fw.bufs(2)
            pst = [P_("pst%d" % i, [128, 1024], BF16) for i in range(2)]
            PST = fw.bufs(2)
            mp = [P_("mp%d" % i, [128, 512]) for i in range(4)]
            MP = fw.bufs(4)
            fw.psum(PST, MP)
            fo = [T_("fo%d" % i, [128, 17, 256]) for i in range(2)]
            FO = fw.bufs(2)
            to = [T_("to%d" % i, [128, 2, NTM]) for i in range(2)]
            TO = fw.bufs(2)
            sg = [T_("sg%d" % i, [128, 256]) for i in range(2)]
            SG = fw.bufs(2)
            fchunks = [(0, 128), (128, 128), (256, 128), (384, 128),
                       (512, 128), (640, 128), (768, 128), (896, 128)]
            fchunks += [(1280 + 128 * i, 128) for i in range(6)]
            fchunks += [(2320, 128), (2448, 128), (2832, 32)]
            tpieces = [(1024, 256, 0), (2048, 272, 256), (2576, 272, 528), (2848, 272, 800)]
            mpi = 0
            ngroups = T // 256
            for g in range(ngroups):
                b = g % 2
                s = 1 if g == 0 else 0
                fw.dma(xt[b][:], xsrc[g * 256:(g + 1) * 256, :].rearrange("(a p) c -> p a c", p=128), writes=[XT[b]])
                for a in range(2):
                    self.norm_mod_T(st, 0, xt[b][:, a, :], XT[b], s, hT[b], HT[b], a * 128, "A", pst[a], PST[a], ident16, ID16)
                if g == 0:
                    tok0 = 0
                    cmpos = CM_CTX0
                    dnpos = DN_CTX0
                else:
                    tok0 = g * 256
                    cmpos = CM_LAT0 + (g - 1) * 256
                    dnpos = DN_LAT0 + (g - 1) * 256
                for ci, (c0, ncol) in enumerate(fchunks):
                    pi = mpi % 4
                    mpi += 1
                    for k in range(8):
                        fw.op("pe", lambda: nc.tensor.matmul(mp[pi][0:ncol, 0:256], lhsT=win[:, k, c0:c0 + ncol], rhs=hT[b][:, k, :],
                                                             start=(k == 0), stop=(k == 7)),
                              reads=[WIN, HT[b]], writes=[MP[pi]])
                    if ci in (2, 3):
                        fw.op("act", lambda: nc.scalar.activation(out=sg[ci - 2][:], in_=mp[pi][:, 0:256], func=AF.Sigmoid),
                              reads=[MP[pi]], writes=[SG[ci - 2]])
                        fw.op("dve", lambda: nc.vector.tensor_tensor(out=fo[b][:, ci - 2, :], in0=fo[b][:, ci - 2, :], in1=sg[ci - 2][:], op=ALU.mult),
                              reads=[SG[ci - 2], FO[b]], writes=[FO[b]])
                    else:
                        eng = "act" if ci % 2 == 0 else "dve"
                        if eng == "act":
                            fw.op("act", lambda: nc.scalar.copy(out=fo[b][0:ncol, ci, :], in_=mp[pi][0:ncol, 0:256]),
                                  reads=[MP[pi]], writes=[FO[b]])
                        else:
                            fw.op("dve", lambda: nc.vector.tensor_copy(out=fo[b][0:ncol, ci, :], in_=mp[pi][0:ncol, 0:256]),
                                  reads=[MP[pi]], writes=[FO[b]])
                fw.dma(S["cmY"][0][:, cmpos:cmpos + 256].rearrange("(c p) t -> p c t", p=128), fo[b][:, 0:2, :], reads=[FO[b]], writes=[S["cmY"][1]])
                fw.dma(S["qT"][0][:, tok0:tok0 + 256].rearrange("(c p) t -> p c t", p=128), fo[b][:, 4:6, :], reads=[FO[b]], writes=[S["qT"][1]])
                fw.dma(S["kT"][0][:, tok0:tok0 + 256].rearrange("(c p) t -> p c t", p=128), fo[b][:, 6:8, :], reads=[FO[b]], writes=[S["kT"][1]])
                fw.dma(S["dnqkv"][0][:, dnpos:dnpos + 256].rearrange("(c p) t -> p c t", p=128), fo[b][:, 8:14, :], reads=[FO[b]], writes=[S["dnqkv"][1]])
                fw.dma(S["glaqT"][0][:, tok0:tok0 + 256], fo[b][:, 14, :], reads=[FO[b]], writes=[S["glaqT"][1]])
                fw.dma(S["glakT"][0][:, tok0:tok0 + 256], fo[b][:, 15, :], reads=[FO[b]], writes=[S["glakT"][1]])
                fw.dma(S["glalrT"][0][:, tok0:tok0 + 256], fo[b][0:32, 16, :], reads=[FO[b]], writes=[S["glalrT"][1]])
                for a in range(2):
                    for (c0, ncol, o0) in tpieces:
                        pi = mpi % 4
                        mpi += 1
                        for k in range(8):
                            fw.op("pe", lambda: nc.tensor.matmul(mp[pi][:, 0:ncol], lhsT=hT[b][:, k, a * 128:(a + 1) * 128], rhs=win[:, k, c0:c0 + ncol],
                                                                 start=(k == 0), stop=(k == 7)),
                                  reads=[WIN, HT[b]], writes=[MP[pi]])
                        eng = "act" if (pi % 2 == 0) else "dve"
                        if eng == "act":
                            fw.op("act", lambda: nc.scalar.copy(out=to[b][:, a, o0:o0 + ncol], in_=mp[pi][:, 0:ncol]), reads=[MP[pi]], writes=[TO[b]])
                        else:
                            fw.op("dve", lambda: nc.vector.tensor_copy(out=to[b][:, a, o0:o0 + ncol], in_=mp[pi][:, 0:ncol]), reads=[MP[pi]], writes=[TO[b]])
                fw.dma(S["tokmaj"][0][tok0:tok0 + 256, :].rearrange("(a p) c -> p a c", p=128), to[b][:], reads=[TO[b]], writes=[S["tokmaj"][1]])
            fw.barrier()

    def zero_mix(self, l):
        fw, nc = self.fw, self.nc
        S = self.scr
        with ExitStack() as st:
            z = st.enter_context(nc.sbuf_tensor(self.uniq("zmix"), [128, T], BF16))
            Z = Buf()
            fw.op("pool", lambda: nc.gpsimd.memset(z[:], 0.0), writes=[Z])
            for c in range(4, 8):
                if (c < 6 and not self.want("E")) or (c >= 6 and not self.want("D")):
                    fw.dma(S["mixT"][0][c * 128:(c + 1) * 128, :], z[:], reads=[Z], writes=[S["mixT"][1]])
            fw.barrier()

    def phaseB(self, l):
        fw, nc = self.fw, self.nc
        S = self.scr
        with ExitStack() as st:
            T_ = lambda n, s, d=F32: st.enter_context(nc.sbuf_tensor(self.uniq(n), s, d))
            P_ = lambda n, s, d=F32: st.enter_context(nc.psum_tensor(self.uniq(n), s, d))
            self.common_tiles(st)
            Y = T_("cmy", [128, 2, TP])
            YB = fw.bufs(2)
            accA = T_("accA", [128, 2, TP])
            AA = fw.bufs(2)
            L = TP - 2 * CMP
            wo, _ = COLS["cm_w"]
            bo, _ = COLS["cm_b"]
            for c in range(2):
                fw.dma(Y[:, c, :], S["cmY"][0][c * 128:(c + 1) * 128, :], reads=[S["cmY"][1]], writes=[YB[c]])
                for (a, b) in ((0, CMP), (CM_CTX0 + CTX, CM_LAT0), (CM_LAT0 + SEQ, TP)):
                    fw.op("pool", lambda: nc.gpsimd.memset(Y[:, c, a:b], 0.0), writes=[YB[c]])
            for c in range(2):
                for j in range(31):
                    wcol = self.colp[:, wo + c * 31 + j:wo + c * 31 + j + 1]
                    e, eng, acc, AC, first = "dve", nc.vector, accA, AA[c], (j == 0)
                    if first:
                        fw.op(e, lambda: eng.tensor_scalar(out=acc[:, c, CMP:CMP + L], in0=Y[:, c, j:j + L], scalar1=wcol, scalar2=None, op0=ALU.mult),
                              reads=[YB[c], self.COLP], writes=[AC])
                    else:
                        fw.op(e, lambda: eng.scalar_tensor_tensor(out=acc[:, c, CMP:CMP + L], in0=Y[:, c, j:j + L], scalar=wcol, in1=acc[:, c, CMP:CMP + L],
                                                                  op0=ALU.mult, op1=ALU.add), reads=[YB[c], self.COLP, AC], writes=[AC])
                fw.op("pool", lambda: nc.gpsimd.tensor_scalar(out=accA[:, c, CMP:CMP + L], in0=accA[:, c, CMP:CMP + L], scalar1=self.colp[:, bo + c:bo + c + 1],
                                                              scalar2=None, op0=ALU.add),
                      reads=[AA[c], self.COLP], writes=[AA[c]])
            sq = T_("lnsq", [128, 2, 512])
            SQ = Buf()
            msq = T_("msq", [128, 512])
            MSQ = Buf()
            var = T_("var", [128, 512])
            VAR = Buf()
            tt = [T_("lnt%d" % i, [128, 512]) for i in range(2)]
            TT = fw.bufs(2)
            ob = [T_("lno%d" % i, [128, 512], BF16) for i in range(2)]
            OB = fw.bufs(2)
            mps = P_("lnm", [128, 512])
            MPS = Buf()
            eps_ = P_("lne", [128, 512])
            EPS_ = Buf()
            fw.psum(MPS, EPS_)
            lg, _ = COLS["cm_lg"]
            lb, _ = COLS["cm_lb"]
            blocks = [(CM_CTX0, 256, 0)] + [(CM_LAT0 + 512 * k, 512, CTX + 512 * k) for k in range(8)]
            for (p0, n, tok0) in blocks:
                for c in range(2):
                    fw.op("act", lambda: nc.scalar.activation(out=sq[:, c, 0:n], in_=accA[:, c, p0:p0 + n], func=AF.Square), reads=[AA[c]], writes=[SQ])
                for c in range(2):
                    fw.op("pe", lambda: nc.tensor.matmul(mps[:, 0:n], lhsT=self.C("div256"), rhs=accA[:, c, p0:p0 + n], start=(c == 0), stop=(c == 1)),
                          reads=[self.CST, AA[c]], writes=[MPS])
                for c in range(2):
                    fw.op("pe", lambda: nc.tensor.matmul(eps_[:, 0:n], lhsT=self.C("div256"), rhs=sq[:, c, 0:n], start=(c == 0), stop=(c == 1)),
                          reads=[self.CST, SQ], writes=[EPS_])
                fw.op("act", lambda: nc.scalar.activation(out=msq[:, 0:n], in_=mps[:, 0:n], func=AF.Square), reads=[MPS], writes=[MSQ])
                fw.op("dve", lambda: nc.vector.tensor_tensor(out=var[:, 0:n], in0=eps_[:, 0:n], in1=msq[:, 0:n], op=ALU.subtract), reads=[EPS_, MSQ], writes=[VAR])
                fw.op("act", lambda: nc.scalar.activation(out=var[:, 0:n], in_=var[:, 0:n], func=AF.Sqrt, bias=self.epsc[:, 0:1], scale=1.0), reads=[VAR, self.EPSC], writes=[VAR])
                fw.op("dve", lambda: nc.vector.reciprocal(out=var[:, 0:n], in_=var[:, 0:n]), reads=[VAR], writes=[VAR])
                for c in range(2):
                    fw.op("dve", lambda: nc.vector.tensor_tensor(out=tt[c][:, 0:n], in0=accA[:, c, p0:p0 + n], in1=mps[:, 0:n], op=ALU.subtract),
                          reads=[AA[c], MPS], writes=[TT[c]])
                    fw.op("pool", lambda: nc.gpsimd.tensor_tensor(out=tt[c][:, 0:n], in0=tt[c][:, 0:n], in1=var[:, 0:n], op=ALU.mult), reads=[TT[c], VAR], writes=[TT[c]])
                    fw.op("act", lambda: nc.scalar.activation(out=ob[c][:, 0:n], in_=tt[c][:, 0:n], func=AF.Silu, bias=self.colp[:, lb + c:lb + c + 1],
                                                              scale=self.colp[:, lg + c:lg + c + 1]), reads=[TT[c], self.COLP], writes=[OB[c]])
                    fw.dma(S["mixT"][0][c * 128:(c + 1) * 128, tok0:tok0 + n], ob[c][:, 0:n], reads=[OB[c]], writes=[S["mixT"][1]])
            fw.barrier()

    def phaseC(self, l):
        fw, nc, I = self.fw, self.nc, self.I
        S = self.scr
        lam_init = 0.8 - 0.6 * math.exp(-0.3 * l)
        scale = 32.0 ** -0.5
        with ExitStack() as st:
            T_ = lambda n, s, d=F32: st.enter_context(nc.sbuf_tensor(self.uniq(n), s, d))
            P_ = lambda n, s, d=F32: st.enter_context(nc.psum_tensor(self.uniq(n), s, d))
            ident16, ID16 = self.common_tiles(st)
            QT = T_("QT", [128, 2, T], BF16)
            KT = T_("KT", [128, 2, T], BF16)
            QTB, KTB = Buf(), Buf()
            V = T_("V", [128, NT, 4, 65], BF16)
            VB = Buf()
            fw.op("pool", lambda: nc.gpsimd.memset(V[:].rearrange("p a b c -> p (a b c)"), 1.0), writes=[VB])
            blocks = [(0, 256)] + [(CTX + 512 * k, 512) for k in range(8)]
            with ExitStack() as st2:
                T2 = lambda n, s, d=F32: st2.enter_context(nc.sbuf_tensor(self.uniq(n), s, d))
                P2 = lambda n, s, d=F32: st2.enter_context(nc.psum_tensor(self.uniq(n), s, d))
                raw = [T2("raw%d" % i, [128, 512]) for i in range(2)]
                RAW = fw.bufs(2)
                cs = [T2("cs%d" % i, [128, 2, 512]) for i in range(2)]
                CS = fw.bufs(2)
                sq = T2("sq", [128, 512]); SQ = Buf()
                rs = T2("rs", [128, 512]); RS = Buf()
                qn = T2("qn", [128, 512]); QN = Buf()
                r1 = T2("r1", [128, 512]); R1 = Buf()
                r2 = T2("r2", [128, 512]); R2 = Buf()
                ssp = P2("ssp", [128, 512]); SSP = Buf()
                swp = P2("swp", [128, 512]); SWP = Buf()
                fw.psum(SSP, SWP)
                vst = [T2("vst%d" % i, [128, 256]) for i in range(2)]
                VST = fw.bufs(2)
                it = 0
                for (name, dst, DST, gname) in (("qT", QT, QTB, "qg"), ("kT", KT, KTB, "kg")):
                    go, _ = COLS[gname]
                    for c in range(2):
                        for (t0, n) in blocks:
                            b = it % 2
                            it += 1
                            fw.dma(raw[b][:, 0:n], S[name][0][c * 128:(c + 1) * 128, t0:t0 + n], reads=[S[name][1]], writes=[RAW[b]])
                            fw.op("act", lambda: nc.scalar.activation(out=sq[:, 0:n], in_=raw[b][:, 0:n], func=AF.Square), reads=[RAW[b]], writes=[SQ])
                            fw.op("pe", lambda: nc.tensor.matmul(ssp[:, 0:n], lhsT=self.C("blk32"), rhs=sq[:, 0:n], start=True, stop=True), reads=[self.CST, SQ], writes=[SSP])
                            fw.op("act", lambda: nc.scalar.activation(out=rs[:, 0:n], in_=ssp[:, 0:n], func=AF.Sqrt, bias=self.epsc[:, 0:1], scale=1.0), reads=[SSP, self.EPSC], writes=[RS])
                            fw.op("dve", lambda: nc.vector.reciprocal(out=rs[:, 0:n], in_=rs[:, 0:n]), reads=[RS], writes=[RS])
                            if t0 < CTX:
                                fw.op("dve", lambda: nc.vector.scalar_tensor_tensor(out=dst[:, c, t0:t0 + n], in0=raw[b][:, 0:n], scalar=self.colp[:, go:go + 1], in1=rs[:, 0:n],
                                                                                    op0=ALU.mult, op1=ALU.mult), reads=[RAW[b], RS, self.COLP], writes=[DST])
                                continue
                            fw.op("dve", lambda: nc.vector.scalar_tensor_tensor(out=qn[:, 0:n], in0=raw[b][:, 0:n], scalar=self.colp[:, go:go + 1], in1=rs[:, 0:n],
                                                                                op0=ALU.mult, op1=ALU.mult), reads=[RAW[b], RS, self.COLP], writes=[QN])
                            lt = t0 - CTX
                            fw.dma(cs[b][:, 0, 0:n], I["rope"][:, lt:lt + n], writes=[CS[b]])
                            fw.dma(cs[b][:, 1, 0:n], I["rope"][:, SEQ + lt:SEQ + lt + n], writes=[CS[b]])
                            fw.op("pe", lambda: nc.tensor.matmul(swp[:, 0:n], lhsT=self.C("perm"), rhs=qn[:, 0:n], start=True, stop=True), reads=[self.CST, QN], writes=[SWP])
                            fw.op("pool", lambda: nc.gpsimd.tensor_tensor(out=r1[:, 0:n], in0=qn[:, 0:n], in1=cs[b][:, 0, 0:n], op=ALU.mult), reads=[QN, CS[b]], writes=[R1])
                            fw.op("dve", lambda: nc.vector.tensor_tensor(out=r2[:, 0:n], in0=swp[:, 0:n], in1=cs[b][:, 1, 0:n], op=ALU.mult), reads=[SWP, CS[b]], writes=[R2])
                            fw.op("pool", lambda: nc.gpsimd.tensor_tensor(out=dst[:, c, t0:t0 + n], in0=r1[:, 0:n], in1=r2[:, 0:n], op=ALU.add), reads=[R1, R2], writes=[DST])
                for t in range(NT):
                    b = t % 2
                    fw.dma(vst[b][:], S["tokmaj"][0][t * 128:(t + 1) * 128, 0:256], reads=[S["tokmaj"][1]], writes=[VST[b]])
                    fw.op("pool", lambda: nc.gpsimd.tensor_copy(out=V[:, t, :, 0:64], in_=vst[b][:].rearrange("p (h d) -> p h d", h=4)), reads=[VST[b]], writes=[VB])
                fw.barrier()
            lt_ = T_("lamt", [128, 8]); LT = Buf()
            lp = T_("lamp", [128, 64]); LP = Buf()
            lo, _ = ROWS["lam"]
            for i in range(2):
                fw.op("dve", lambda: nc.vector.tensor_tensor(out=lp[:, i * 32:(i + 1) * 32], in0=self.rowp[:, lo + 64 * i:lo + 64 * i + 32], in1=self.rowp[:, lo + 64 * i + 32:lo + 64 * i + 64], op=ALU.mult),
                      reads=[self.ROWP], writes=[LP])
                fw.op("dve", lambda: nc.vector.reduce_sum(out=lt_[:, i:i + 1], in_=lp[:, i * 32:(i + 1) * 32], axis=AX.X), reads=[LP], writes=[LT])
            fw.op("act", lambda: nc.scalar.activation(out=lt_[:, 2:4], in_=lt_[:, 0:2], func=AF.Exp), reads=[LT], writes=[LT])
            fw.op("dve", lambda: nc.vector.scalar_tensor_tensor(out=lt_[:, 4:5], in0=lt_[:, 3:4], scalar=-lam_init, in1=lt_[:, 2:3], op0=ALU.add, op1=ALU.subtract), reads=[LT], writes=[LT])
            subg = T_("subg", [128, 64]); SUBG = Buf()
            so, _ = ROWS["subln"]
            fw.op("dve", lambda: nc.vector.tensor_scalar(out=subg[:], in0=self.rowp[:, so:so + 64], scalar1=(1.0 - lam_init), scalar2=None, op0=ALU.mult), reads=[self.ROWP], writes=[SUBG])
            identf = self.C("ident")
            scps = [P_("scps%d" % i, [128, 512]) for i in range(3)]
            SCPS = fw.bufs(3)
            otps = [P_("otps%d" % i, [128, 512]) for i in range(2)]
            OTPS = fw.bufs(2)
            trps = [P_("trps%d" % i, [128, 4, 65]) for i in range(2)]
            TRPS = fw.bufs(2)
            ytps = P_("ytps", [128, 1024], BF16)
            YTPS = Buf()
            fw.psum(SCPS, OTPS, TRPS, YTPS)
            pb = [T_("pb%d" % i, [128, 512], BF16) for i in range(4)]
            PB = fw.bufs(4)
            otsb = [T_("otsb%d" % i, [128, 512]) for i in range(2)]
            OTSB = fw.bufs(2)
            rcp = T_("rcp", [128, 2, 4]); RCP = Buf()
            o0 = T_("o0", [128, 64]); O0 = Buf()
            o1 = T_("o1", [128, 64]); O1 = Buf()
            avs = [T_("avs%d" % i, [128, 4, 4, 64]) for i in range(2)]; AVS = fw.bufs(2)
            junk = T_("junk", [128, 64]); JUNK = Buf()
            ssa = [T_("ssa%d" % i, [128, 48]) for i in range(2)]; SSA = fw.bufs(2)
            yb = [T_("yb%d" % i, [128, 4, 256], BF16) for i in range(2)]
            YB = fw.bufs(2)
            ybT = [T_("ybT%d" % i, [128, 2, 512], BF16) for i in range(2)]
            YBT = fw.bufs(2)
            qblocks = [(0, 256, 2)] + [(CTX + 512 * k, 512, NT) for k in range(8)]
            items = []
            for qi, (q0, nq, nkt) in enumerate(qblocks):
                for h in range(4):
                    for m in range(2):
                        for kt in range(nkt):
                            items.append((qi, q0, nq, nkt, h, m, kt))

            def emit_sc(idx):
                qi, q0, nq, nkt, h, m, kt = items[idx]
                r = (h % 2) * 2 + m
                ch = h // 2
                si = idx % 3
                fw.op("pe", lambda: nc.tensor.matmul(scps[si][:, 0:nq], lhsT=KT[32 * r:32 * r + 32, ch, kt * 128:(kt + 1) * 128],
                                                     rhs=QT[32 * r:32 * r + 32, ch, q0:q0 + nq], start=True, stop=True, tile_position=(32 * r, 0)),
                      reads=[KTB, QTB], writes=[SCPS[si]])

            def emit_exp_av(idx):
                qi, q0, nq, nkt, h, m, kt = items[idx]
                si = idx % 3
                pi = idx % 4
                fw.op("act", lambda: nc.scalar.activation(out=pb[pi][:, 0:nq], in_=scps[si][:, 0:nq], func=AF.Exp, scale=scale), reads=[SCPS[si]], writes=[PB[pi]])
                fw.op("pe", lambda: nc.tensor.matmul(otps[m][0:65, 0:nq], lhsT=V[:, kt, h, :], rhs=pb[pi][:, 0:nq], start=(kt == 0), stop=(kt == nkt - 1)),
                      reads=[VB, PB[pi]], writes=[OTPS[m]])

            def epilogue(qi, q0, nq, h, m):
                nsub = nq // 128
                qb = qi % 2
                fw.op("dve", lambda: nc.vector.tensor_copy(out=otsb[m][0:65, 0:nq], in_=otps[m][0:65, 0:nq]), reads=[OTPS[m]], writes=[OTSB[m]])
                for sub in range(nsub):
                    fw.op("pe", lambda: nc.tensor.transpose(trps[m][:, sub, :], otsb[m][0:65, sub * 128:(sub + 1) * 128], identf[0:65, 0:65]),
                          reads=[OTSB[m], self.CST], writes=[TRPS[m]])
                fw.op("dve", lambda: nc.vector.reciprocal(out=rcp[:, m, 0:nsub], in_=trps[m][:, 0:nsub, 64]), reads=[TRPS[m]], writes=[RCP])
                if m == 0:
                    return
                for sub in range(nsub):
                    col = h * 4 + sub
                    fw.op("dve", lambda: nc.vector.tensor_scalar(out=o0[:], in0=trps[0][:, sub, 0:64], scalar1=rcp[:, 0, sub:sub + 1], scalar2=None, op0=ALU.mult),
                          reads=[TRPS[0], RCP], writes=[O0])
                    fw.op("dve", lambda: nc.vector.tensor_scalar(out=o1[:], in0=trps[1][:, sub, 0:64], scalar1=rcp[:, 1, sub:sub + 1], scalar2=None, op0=ALU.mult),
                          reads=[TRPS[1], RCP], writes=[O1])
                    fw.op("dve", lambda: nc.vector.scalar_tensor_tensor(out=avs[qb][:, h, sub, :], in0=o1[:], scalar=lt_[:, 4:5], in1=o0[:], op0=ALU.mult, op1=ALU.add),
                          reads=[O0, O1, LT], writes=[AVS[qb]])
                    fw.op("pool", lambda: nc.gpsimd.tensor_tensor(out=junk[:], in0=avs[qb][:, h, sub, :], in1=avs[qb][:, h, sub, :], op=ALU.mult), reads=[AVS[qb]], writes=[JUNK])
                    fw.op("dve", lambda: nc.vector.reduce_sum(out=ssa[qb][:, col:col + 1], in_=junk[:], axis=AX.X), reads=[JUNK], writes=[SSA[qb]])
                if h < 3:
                    return
                fw.op("act", lambda: nc.scalar.activation(out=ssa[qb][:, 16:32], in_=ssa[qb][:, 0:16], func=AF.Sqrt, bias=self.epsc[:, 0:1], scale=1.0 / 64), reads=[SSA[qb], self.EPSC], writes=[SSA[qb]])
                fw.op("dve", lambda: nc.vector.reciprocal(out=ssa[qb][:, 32:48], in_=ssa[qb][:, 16:32]), reads=[SSA[qb]], writes=[SSA[qb]])
                for hh in range(4):
                    for sub in range(nsub):
                        col = 32 + hh * 4 + sub
                        fw.op("dve", lambda: nc.vector.scalar_tensor_tensor(out=yb[qb][:, sub, hh * 64:(hh + 1) * 64], in0=avs[qb][:, hh, sub, :], scalar=ssa[qb][:, col:col + 1], in1=subg[:],
                                                                            op0=ALU.mult, op1=ALU.mult), reads=[AVS[qb], SSA[qb], SUBG], writes=[YB[qb]])
                for sub in range(nsub):
                    for c2 in range(2):
                        fw.op("pe", lambda: nc.tensor.transpose(ytps[:, c2 * 512 + sub * 128:c2 * 512 + (sub + 1) * 128], yb[qb][:, sub, c2 * 128:(c2 + 1) * 128], ident16[:]),
                              reads=[YB[qb], ID16], writes=[YTPS])
                for c2 in range(2):
                    fw.op("dve", lambda: nc.vector.tensor_copy(out=ybT[qb][:, c2, 0:nq], in_=ytps[:, c2 * 512:c2 * 512 + nq]), reads=[YTPS], writes=[YBT[qb]])
                    fw.dma(S["mixT"][0][256 + c2 * 128:256 + (c2 + 1) * 128, q0:q0 + nq], ybT[qb][:, c2, 0:nq], reads=[YBT[qb]], writes=[S["mixT"][1]])

            pending = []
            nit = len(items)
            emit_sc(0)
            emit_sc(1)
            for idx in range(nit):
                qi, q0, nq, nkt, h, m, kt = items[idx]
                if idx + 2 < nit:
                    emit_sc(idx + 2)
                emit_exp_av(idx)
                if pending and kt == min(2, nkt - 1):
                    for args in pending:
                        epilogue(*args)
                    pending = []
                if kt == nkt - 1:
                    pending.append((qi, q0, nq, h, m))
            for args in pending:
                epilogue(*args)
            fw.barrier()

    def gated_out(self, st, tiles, osb, OSB, gate_ap, GATE, on_name, row0, t):
        fw, nc = self.fw, self.nc
        (sq, SQ, ssq, SSQ, sg, SG, y, Y, yf, YF, ytps, YTPS, yT, YT, ident16, ID16) = tiles
        S = self.scr
        go, _ = ROWS[on_name]
        fw.op("pool", lambda: nc.gpsimd.tensor_tensor(out=sq[:], in0=osb[:], in1=osb[:], op=ALU.mult), reads=[OSB], writes=[SQ])
        fw.op("dve", lambda: nc.vector.reduce_sum(out=ssq[:, 0:4], in_=sq[:].rearrange("p (h d) -> p h d", h=4), axis=AX.X), reads=[SQ], writes=[SSQ])
        fw.op("act", lambda: nc.scalar.activation(out=ssq[:, 4:8], in_=ssq[:, 0:4], func=AF.Sqrt, bias=self.epsc[:, 0:1], scale=1.0 / 64), reads=[SSQ, self.EPSC], writes=[SSQ])
        fw.op("dve", lambda: nc.vector.reciprocal(out=ssq[:, 8:12], in_=ssq[:, 4:8]), reads=[SSQ], writes=[SSQ])
        fw.op("act", lambda: nc.scalar.activation(out=sg[:], in_=gate_ap, func=AF.Silu), reads=[GATE], writes=[SG])
        for h in range(4):
            fw.op("dve", lambda: nc.vector.scalar_tensor_tensor(out=y[:, h * 64:(h + 1) * 64], in0=osb[:, h * 64:(h + 1) * 64], scalar=ssq[:, 8 + h:9 + h],
                                                                in1=self.rowp[:, go:go + 64], op0=ALU.mult, op1=ALU.mult), reads=[OSB, SSQ, self.ROWP], writes=[Y])
        fw.op("pool", lambda: nc.gpsimd.tensor_tensor(out=yf[:], in0=y[:], in1=sg[:], op=ALU.mult), reads=[Y, SG], writes=[YF])
        for c2 in range(2):
            fw.op("pe", lambda: nc.tensor.transpose(ytps[:, c2 * 128:(c2 + 1) * 128], yf[:, c2 * 128:(c2 + 1) * 128], ident16[:]), reads=[YF, ID16], writes=[YTPS])
        fw.op("act", lambda: nc.scalar.copy(out=yT[:], in_=ytps[:, 0:256]), reads=[YTPS], writes=[YT])
        fw.dma(S["mixT"][0][row0:row0 + 256, t * 128:(t + 1) * 128].rearrange("(c p) t -> p c t", p=128), yT[:].rearrange("p (c t) -> p c t", c=2), reads=[YT], writes=[S["mixT"][1]])

    def gated_tiles(self, st, ident16, ID16):
        nc = self.nc
        T_ = lambda n, s, d=F32: st.enter_context(nc.sbuf_tensor(self.uniq(n), s, d))
        P_ = lambda n, s, d=F32: st.enter_context(nc.psum_tensor(self.uniq(n), s, d))
        ytb = Buf()
        ytb.ps = True
        return (T_("g_sq", [128, 256]), Buf(), T_("g_ssq", [128, 12]), Buf(), T_("g_sg", [128, 256]), Buf(), T_("g_y", [128, 256]), Buf(),
                T_("g_yf", [128, 256], BF16), Buf(), P_("g_ytps", [128, 1024], BF16), ytb, T_("g_yT", [128, 256], BF16), Buf(), ident16, ID16)

    def phaseD(self, l):
        fw, nc, I = self.fw, self.nc, self.I
        S = self.scr
        scale = 32.0 ** -0.5
        with ExitStack() as st:
            T_ = lambda n, s, d=F32: st.enter_context(nc.sbuf_tensor(self.uniq(n), s, d))
            P_ = lambda n, s, d=F32: st.enter_context(nc.psum_tensor(self.uniq(n), s, d))
            ident16, ID16 = self.common_tiles(st)
            w2 = T_("w2", [32, 2, 128]); W2 = Buf()
            fw.dma(w2[:], I["w2p"][l].rearrange("d r c -> r d c"), writes=[W2])
            def stream(d):
                Sst = T_("Sst", [128, 256]); SST = Buf()
                qT = T_("qT", [128, 128]); QTB = Buf()
                kT = T_("kT", [128, 128]); KTB = Buf()
                lrT = T_("lrT", [32, 128]); LRT = Buf()
                tm = T_("tm", [128, 544]); TM = Buf()
                z = T_("z", [128, 128]); Z = Buf()
                ebT = T_("ebT", [128, 128]); EBT = Buf()
                enbT = T_("enbT", [128, 128]); ENBT = Buf()
                qin = T_("qin", [128, 128]); QIN = Buf()
                kn = T_("kn", [128, 4, 128]); KN = Buf()
                ktok = T_("ktok", [128, 128]); KTOK = Buf()
                bcs = T_("bcs", [128, 128]); BCS = Buf()
                kend = T_("kend", [128, 2, 128]); KEND = Buf()
                aqk = T_("aqk", [128, 2, 512]); AQK = Buf()
                tmp = T_("tmp", [128, 256]); TMP = Buf()
                osb = T_("osb", [128, 256]); OSB = Buf()
                of = T_("of", [128, 256]); OF = Buf()
                zk = P_("zk", [128, 512]); ZP = Buf(); KP = ZP
                b3 = P_("b3", [128, 512]); B3 = [Buf()] * 3
                aq = P_("aq", [128, 512]); AQ = Buf()
                ops = P_("ops", [128, 512]); OPS = [Buf()] * 2
                sp = zk[:, 256:512]; SP = ZP
                fw.psum(ZP, B3, AQ, OPS)
                onescol = self.C("ones")[:, 0:1]
                b2o, _ = ROWS["gla_b2r"]
                nm = "f" if d == 0 else "r"
                order = list(range(NT)) if d == 0 else [1, 0] + list(range(NT - 1, 1, -1))
                if getattr(self, "nt_dbg", None):
                    order = [t for t in order if t < self.nt_dbg]
                fw.op("pool", lambda: nc.gpsimd.memset(Sst[:], 0.0), writes=[SST])
                for t in order:
                    tk = slice(t * 128, (t + 1) * 128)
                    fw.dma(qT[:], S["glaqT"][0][:, tk], reads=[S["glaqT"][1]], writes=[QTB])
                    fw.dma(kT[:], S["glakT"][0][:, tk], reads=[S["glakT"][1]], writes=[KTB])
                    fw.dma(lrT[:], S["glalrT"][0][:, tk], reads=[S["glalrT"][1]], writes=[LRT])
                    fw.dma(tm[:], S["tokmaj"][0][tk, 528:1072], reads=[S["tokmaj"][1]], writes=[TM])
                    fw.op("pe", lambda: nc.tensor.matmul(zk[:, 0:128], lhsT=lrT[:, :], rhs=w2[:, d, :], start=True, stop=True), reads=[LRT, W2], writes=[ZP])
                    fw.op("dve", lambda: nc.vector.tensor_tensor(out=z[:], in0=zk[:, 0:128], in1=self.rowp[:, b2o + d * 128:b2o + (d + 1) * 128], op=ALU.add), reads=[ZP, self.ROWP], writes=[Z])
                    fw.op("act", lambda: nc.scalar.activation(out=z[:], in_=z[:], func=AF.Exp, scale=-1.0), reads=[Z], writes=[Z])
                    fw.op("act", lambda: nc.scalar.activation(out=z[:], in_=z[:], func=AF.Ln, bias=onescol, scale=1.0), reads=[Z, self.CST], writes=[Z])
                    yield
                    fw.op("pe", lambda: nc.tensor.matmul(b3[:, 0:128], lhsT=self.C("tris_" + nm), rhs=z[:], start=True, stop=True), reads=[self.CST, Z], writes=[B3[0]])
                    fw.op("pe", lambda: nc.tensor.matmul(b3[:, 128:256], lhsT=z[:], rhs=self.C("tris_" + nm), start=True, stop=True), reads=[self.CST, Z], writes=[B3[1]])
                    fw.op("pe", lambda: nc.tensor.matmul(b3[:, 256:384], lhsT=self.C("blks"), rhs=z[:], start=True, stop=True), reads=[self.CST, Z], writes=[B3[2]])
                    fw.op("act", lambda: nc.scalar.activation(out=ebT[:], in_=b3[:, 128:256], func=AF.Exp), reads=[B3[1]], writes=[EBT])
                    fw.op("act", lambda: nc.scalar.activation(out=enbT[:], in_=b3[:, 128:256], func=AF.Exp, scale=-1.0), reads=[B3[1]], writes=[ENBT])
                    fw.op("dve", lambda: nc.vector.scalar_tensor_tensor(out=qin[:], in0=qT[:], scalar=scale, in1=ebT[:], op0=ALU.mult, op1=ALU.mult), reads=[QTB, EBT], writes=[QIN])
                    yield
                    hmo, _ = COFF["hm32"]
                    for h in range(4):
                        fw.op("dve", lambda: nc.vector.scalar_tensor_tensor(out=kn[:, h, :], in0=kT[:], scalar=self.cst[:, hmo + h:hmo + h + 1], in1=enbT[:], op0=ALU.mult, op1=ALU.mult),
                              reads=[KTB, ENBT, self.CST], writes=[KN])
                    fw.op("pe", lambda: nc.tensor.transpose(zk[:, 128:256], kT[:], self.C("ident")), reads=[KTB, self.CST], writes=[KP])
                    fw.op("act", lambda: nc.scalar.copy(out=ktok[:], in_=zk[:, 128:256]), reads=[KP], writes=[KTOK])
                    fw.op("act", lambda: nc.scalar.copy(out=bcs[:], in_=b3[:, 0:128]), reads=[B3[0]], writes=[BCS])
                    fw.op("dve", lambda: nc.vector.tensor_tensor(out=bcs[:], in0=b3[:, 256:384], in1=bcs[:], op=ALU.subtract), reads=[B3[2], BCS], writes=[BCS])
                    fw.op("act", lambda: nc.scalar.activation(out=bcs[:], in_=bcs[:], func=AF.Exp), reads=[BCS], writes=[BCS])
                    cio, _ = COFF["chunkind"]
                    for c in range(2):
                        fw.op("dve", lambda: nc.vector.scalar_tensor_tensor(out=kend[:, c, :], in0=ktok[:], scalar=self.cst[:, cio + c:cio + c + 1], in1=bcs[:], op0=ALU.mult, op1=ALU.mult),
                              reads=[KTOK, BCS, self.CST], writes=[KEND])
                    yield
                    for h in range(4):
                        fw.op("pe", lambda: nc.tensor.matmul(aq[:, h * 128:(h + 1) * 128], lhsT=kn[:, h, :], rhs=qin[:], start=True, stop=True), reads=[KN, QIN], writes=[AQ])
                    for c in range(2):
                        fw.op("dve", lambda: nc.vector.scalar_tensor_tensor(out=aqk[:, c, :], in0=aq[:], scalar=self.cst[:, cio + c:cio + c + 1], in1=self.C("cT4_" + nm), op0=ALU.mult, op1=ALU.mult),
                              reads=[AQ, self.CST], writes=[AQK])
                    yield
                    for c in ((0, 1) if d == 0 else (1, 0)):
                        r0 = 64 * c
                        dcol = (63 + 64 * c) if d == 0 else (64 * c)
                        for h in range(4):
                            fw.op("pe", lambda: nc.tensor.matmul(ops[:, c * 256 + h * 64:c * 256 + (h + 1) * 64], lhsT=qin[:], rhs=Sst[:, h * 64:(h + 1) * 64], start=True, stop=False),
                                  reads=[QIN, SST], writes=[OPS[c]])
                            fw.op("pe", lambda: nc.tensor.matmul(ops[:, c * 256 + h * 64:c * 256 + (h + 1) * 64], lhsT=aqk[:, c, h * 128:(h + 1) * 128],
                                                                 rhs=tm[:, h * 64:(h + 1) * 64], start=False, stop=True), reads=[AQK, TM], writes=[OPS[c]])
                        fw.op("pe", lambda: nc.tensor.matmul(sp, lhsT=kend[:, c, :], rhs=tm[:, 0:256], start=True, stop=True), reads=[KEND, TM], writes=[SP])
                        fw.op("dve", lambda: nc.vector.tensor_tensor(out=tmp[:], in0=sp, in1=self.C("bmask4"), op=ALU.mult), reads=[SP, self.CST], writes=[TMP])
                        fw.op("dve", lambda: nc.vector.scalar_tensor_tensor(out=Sst[:], in0=Sst[:], scalar=ebT[:, dcol:dcol + 1], in1=tmp[:], op0=ALU.mult, op1=ALU.add),
                              reads=[SST, EBT, TMP], writes=[SST])
                        fw.op("act", lambda: nc.scalar.copy(out=osb[r0:r0 + 64, :], in_=ops[r0:r0 + 64, c * 256:(c + 1) * 256]), reads=[OPS[c]], writes=[OSB])
                        yield
                    dst = S["ofwd"] if d == 0 else S["orev"]
                    fw.dma(dst[0][tk, :], osb[:], reads=[OSB], writes=[dst[1]])
                    yield

            gens = [stream(0), stream(1)]
            while gens:
                for g in list(gens):
                    try:
                        next(g)
                    except StopIteration:
                        gens.remove(g)
            fw.barrier()
        self.final_gated(272 + 544, "gla_on", 768)

    def final_gated(self, gate_col, on_name, row0):
        fw, nc = self.fw, self.nc
        S = self.scr
        with ExitStack() as st:
            T_ = lambda n, s, d=F32: st.enter_context(nc.sbuf_tensor(self.uniq(n), s, d))
            ident16, ID16 = self.common_tiles(st)
            gts = [self.gated_tiles(st, ident16, ID16) for i in range(2)]
            oa = [T_("oa%d" % i, [128, 256]) for i in range(2)]; OA = fw.bufs(2)
            ob_ = [T_("ob%d" % i, [128, 256]) for i in range(2)]; OB = fw.bufs(2)
            gg = [T_("gg%d" % i, [128, 256]) for i in range(2)]; GG = fw.bufs(2)
            tiles = list(range(NT))
            if getattr(self, "nt_dbg", None):
                tiles = [t for t in tiles if t < self.nt_dbg]
            for t in tiles:
                b = t % 2
                tk = slice(t * 128, (t + 1) * 128)
                fw.dma(oa[b][:], S["ofwd"][0][tk, :], reads=[S["ofwd"][1]], writes=[OA[b]])
                fw.dma(ob_[b][:], S["orev"][0][tk, :], reads=[S["orev"][1]], writes=[OB[b]])
                fw.dma(gg[b][:], S["tokmaj"][0][tk, gate_col:gate_col + 256], reads=[S["tokmaj"][1]], writes=[GG[b]])
                fw.op("pool", lambda: nc.gpsimd.tensor_tensor(out=oa[b][:], in0=oa[b][:], in1=ob_[b][:], op=ALU.add), reads=[OA[b], OB[b]], writes=[OA[b]])
                self.gated_out(st, gts[b], oa[b], OA[b], gg[b][:], GG[b], on_name, row0, t)
            fw.barrier()

    def phaseE(self, l):
        fw, nc, I = self.fw, self.nc, self.I
        S = self.scr
        with ExitStack() as st:
            T_ = lambda n, s, d=F32: st.enter_context(nc.sbuf_tensor(self.uniq(n), s, d))
            P_ = lambda n, s, d=F32: st.enter_context(nc.psum_tensor(self.uniq(n), s, d))
            self.common_tiles(st)
            raw = [T_("dnraw%d" % i, [128, TPD]) for i in range(2)]
            RAW = fw.bufs(2)
            acc = [T_("dnacc%d" % i, [128, TPD]) for i in range(2)]
            ACC = fw.bufs(2)
            sq = T_("dnsq", [128, 512]); SQ = Buf()
            rs = T_("dnrs", [128, 512]); RS = Buf()
            nrm = [T_("dnnrm%d" % i, [128, 512]) for i in range(2)]
            NRM = fw.bufs(2)
            tk_ = [T_("dntk%d" % i, [128, 512]) for i in range(2)]
            TK = fw.bufs(2)
            ssp = P_("dnssp", [128, 512]); SSP = Buf()
            trp = [P_("dntrp%d" % i, [128, 512]) for i in range(2)]
            TRP = fw.bufs(2)
            fw.psum(SSP, TRP)
            L = TPD - 2 * DNP
            wo, _ = COLS["dn_w"]
            blocks = [(DN_CTX0, 256, 0)] + [(DN_LAT0 + 512 * k, 512, CTX + 512 * k) for k in range(8)]
            bi = 0
            for ci in range(6):
                b = ci % 2
                kind = ci // 2
                half = ci % 2
                fw.dma(raw[b][:], S["dnqkv"][0][ci * 128:(ci + 1) * 128, :], reads=[S["dnqkv"][1]], writes=[RAW[b]])
                for (a, e_) in ((0, DNP), (DN_CTX0 + CTX, DN_LAT0), (DN_LAT0 + SEQ, TPD)):
                    fw.op("pool", lambda: nc.gpsimd.memset(raw[b][:, a:e_], 0.0), writes=[RAW[b]])
                for j in range(5):
                    wcol = self.colp[:, wo + ci * 5 + j:wo + ci * 5 + j + 1]
                    if j == 0:
                        fw.op("dve", lambda: nc.vector.tensor_scalar(out=acc[b][:, DNP:DNP + L], in0=raw[b][:, j:j + L], scalar1=wcol, scalar2=None, op0=ALU.mult),
                              reads=[RAW[b], self.COLP], writes=[ACC[b]])
                    else:
                        fw.op("dve", lambda: nc.vector.scalar_tensor_tensor(out=acc[b][:, DNP:DNP + L], in0=raw[b][:, j:j + L], scalar=wcol, in1=acc[b][:, DNP:DNP + L],
                                                                            op0=ALU.mult, op1=ALU.add), reads=[RAW[b], self.COLP, ACC[b]], writes=[ACC[b]])
                fw.op("act", lambda: nc.scalar.activation(out=acc[b][:, DNP:DNP + L], in_=acc[b][:, DNP:DNP + L], func=AF.Silu), reads=[ACC[b]], writes=[ACC[b]])
                for (p0, n, tok0) in blocks:
                    nb = bi % 2
                    bi += 1
                    if kind < 2:
                        fw.op("act", lambda: nc.scalar.activation(out=sq[:, 0:n], in_=acc[b][:, p0:p0 + n], func=AF.Square), reads=[ACC[b]], writes=[SQ])
                        fw.op("pe", lambda: nc.tensor.matmul(ssp[:, 0:n], lhsT=self.C("blk64"), rhs=sq[:, 0:n], start=True, stop=True), reads=[self.CST, SQ], writes=[SSP])
                        fw.op("act", lambda: nc.scalar.activation(out=rs[:, 0:n], in_=ssp[:, 0:n], func=AF.Sqrt, bias=self.epsc[:, 0:1], scale=1.0), reads=[SSP, self.EPSC], writes=[RS])
                        fw.op("dve", lambda: nc.vector.reciprocal(out=rs[:, 0:n], in_=rs[:, 0:n]), reads=[RS], writes=[RS])
                        fw.op("dve", lambda: nc.vector.scalar_tensor_tensor(out=nrm[nb][:, 0:n], in0=acc[b][:, p0:p0 + n], scalar=(0.125 if kind == 0 else 1.0), in1=rs[:, 0:n],
                                                                            op0=ALU.mult, op1=ALU.mult), reads=[ACC[b], RS], writes=[NRM[nb]])
                        dst = S["dn_qT"] if kind == 0 else S["dn_kT"]
                        fw.dma(dst[0][half * 128:(half + 1) * 128, tok0:tok0 + n], nrm[nb][:, 0:n], reads=[NRM[nb]], writes=[dst[1]])
                        src, SRC, soff = nrm[nb], NRM[nb], 0
                    else:
                        src, SRC, soff = acc[b], ACC[b], p0
                    if kind >= 1:
                        for sub in range(n // 128):
                            fw.op("pe", lambda: nc.tensor.transpose(trp[nb][:, sub * 128:(sub + 1) * 128], src[:, soff + sub * 128:soff + (sub + 1) * 128], self.C("ident")),
                                  reads=[SRC, self.CST], writes=[TRP[nb]])
                        fw.op("act", lambda: nc.scalar.copy(out=tk_[nb][:, 0:n], in_=trp[nb][:, 0:n]), reads=[TRP[nb]], writes=[TK[nb]])
                        dst = S["dn_ktok"] if kind == 1 else S["dn_vtok"]
                        fw.dma(dst[0][tok0:tok0 + n, half * 128:(half + 1) * 128].rearrange("(a p) c -> p a c", p=128),
                               tk_[nb][:, 0:n].rearrange("p (a c) -> p a c", c=128), reads=[TK[nb]], writes=[dst[1]])
            fw.barrier()
        if getattr(self, "d_level", 9) < 2:
            return
        with ExitStack() as st:
            T_ = lambda n, s, d=F32: st.enter_context(nc.sbuf_tensor(self.uniq(n), s, d))
            P_ = lambda n, s, d=F32: st.enter_context(nc.psum_tensor(self.uniq(n), s, d))
            ident16, ID16 = self.common_tiles(st)
            identf = self.C("ident")
            cio, _ = COFF["chunkind"]
            bc = lambda ap, shape, ax: ap.unsqueeze(ax).to_broadcast(shape)
            S4 = [128, 4, 128]
            negA = T_("negA", [128, 8]); NEGA = Buf()
            ao, _ = ROWS["a_log"]
            dto, _ = ROWS["dt_b"]
            fw.op("act", lambda: nc.scalar.activation(out=negA[:], in_=self.rowp[:, ao:ao + 8], func=AF.Exp), reads=[self.ROWP], writes=[NEGA])
            fw.op("dve", lambda: nc.vector.tensor_scalar(out=negA[:], in0=negA[:], scalar1=-1.0, scalar2=None, op0=ALU.mult), reads=[NEGA], writes=[NEGA])
            hm64 = self.C("chunkind")
            def stream(d):
                Sm = [T_("Sm%d" % i, [128, 128]) for i in range(2)]; SM = fw.bufs(2)
                qTp = T_("qTp", [128, 2, 128]); QTP = Buf()
                kTp = T_("kTp", [128, 2, 128]); KTP = Buf()
                kTm = T_("kTm", S4); KTM = Buf()
                ktok = T_("ktok", [128, 256]); KTOK = Buf()
                vtok = T_("vtok", [128, 256]); VTOK = Buf()
                bag = T_("bag", [128, 272]); BAG = Buf()
                sm = T_("sm", [128, 24]); SMB = Buf()
                G4 = T_("G4", S4); GB_ = Buf()
                gch = T_("gch", [128, 4, 4]); GCH = Buf()
                dm = T_("dm", S4); DM = Buf()
                dmT = T_("dmT", S4); DMT = Buf()
                dcs = T_("dcs", S4); DCS = Buf()
                e1b = T_("e1b", S4); E1B = Buf()
                gcs = T_("gcs", [128, 4, 12]); GCS = Buf()
                P0f = T_("P0f", S4); P0F = Buf()
                Q0f = T_("Q0f", S4); Q0F = Buf()
                Pm = [T_("Pm%d" % i, S4) for i in range(2)]; PM = fw.bufs(2)
                Qm = [T_("Qm%d" % i, S4) for i in range(2)]; QM = fw.bufs(2)
                Rm = [T_("Rm%d" % i, S4) for i in range(2)]; RM = fw.bufs(2)
                aqkc = T_("aqkc", [128, 2, 4, 128]); AQKC = Buf()
                vb = T_("vb", [128, 4, 64]); VBB = Buf()
                kbgm = T_("kbgm", S4); KBGM = Buf()
                kend = T_("kend", [128, 2, 256]); KEND = Buf()
                qin = T_("qin", [128, 2, 128]); QIN = Buf()
                gendp = T_("gendp", [128, 2, 2]); GENDP = Buf()
                usb = T_("usb", [128, 256]); USB = Buf()
                wT = T_("wT", [128, 2, 128]); WT = Buf()
                vnew = T_("vnew", [128, 256]); VNEW = Buf()
                tmp = T_("tmp", [128, 2, 128]); TMP = Buf()
                osb = T_("osb", [128, 256]); OSB = Buf()
                of = T_("of", [128, 256]); OF = Buf()
                p1 = P_("p1", S4); P1 = Buf()
                p2 = P_("p2", S4); P2 = Buf()
                p3f = P_("p3", [128, 512]); P3 = Buf()
                p3 = p3f[:].rearrange("p (h c) -> p h c", h=4)
                p4 = P_("p4", [128, 512]); P4 = Buf()
                fw.psum(P1, P2, P3, P4)
                fw.op("pool", lambda: nc.gpsimd.memset(kbgm[:], 0.0), writes=[KBGM])
                ones = self.C("ones")

                nm = "f" if d == 0 else "r"
                order = list(range(NT)) if d == 0 else [1, 0] + list(range(NT - 1, 1, -1))
                if getattr(self, "nt_dbg", None):
                    order = [t for t in order if t < self.nt_dbg]
                for hp in range(2):
                    fw.op("pool", lambda: nc.gpsimd.memset(Sm[hp][:], 0.0), writes=[SM[hp]])
                for t in order:
                    tk = slice(t * 128, (t + 1) * 128)
                    fw.dma(qTp[:], S["dn_qT"][0][:, tk].rearrange("(a p) t -> p a t", p=128), reads=[S["dn_qT"][1]], writes=[QTP])
                    fw.dma(kTp[:], S["dn_kT"][0][:, tk].rearrange("(a p) t -> p a t", p=128), reads=[S["dn_kT"][1]], writes=[KTP])
                    fw.dma(ktok[:], S["dn_ktok"][0][tk, :], reads=[S["dn_ktok"][1]], writes=[KTOK])
                    fw.dma(vtok[:], S["dn_vtok"][0][tk, :], reads=[S["dn_vtok"][1]], writes=[VTOK])
                    fw.dma(bag[:], S["tokmaj"][0][tk, 256:528], reads=[S["tokmaj"][1]], writes=[BAG])
                    fw.op("act", lambda: nc.scalar.activation(out=sm[:, 0:4], in_=bag[:, d * 4:d * 4 + 4], func=AF.Sigmoid), reads=[BAG], writes=[SMB])
                    fw.op("dve", lambda: nc.vector.tensor_scalar(out=sm[:, 4:8], in0=sm[:, 0:4], scalar1=-1.0, scalar2=None, op0=ALU.mult), reads=[SMB], writes=[SMB])
                    fw.op("dve", lambda: nc.vector.tensor_tensor(out=sm[:, 8:12], in0=bag[:, 8 + d * 4:12 + d * 4], in1=self.rowp[:, dto + d * 4:dto + d * 4 + 4], op=ALU.add),
                          reads=[BAG, self.ROWP], writes=[SMB])
                    fw.op("act", lambda: nc.scalar.activation(out=sm[:, 8:12], in_=sm[:, 8:12], func=AF.Exp), reads=[SMB], writes=[SMB])
                    fw.op("act", lambda: nc.scalar.activation(out=sm[:, 8:12], in_=sm[:, 8:12], func=AF.Ln, bias=ones[:, 0:1], scale=1.0), reads=[SMB, self.CST], writes=[SMB])
                    fw.op("dve", lambda: nc.vector.tensor_tensor(out=sm[:, 12:16], in0=sm[:, 8:12], in1=negA[:, d * 4:d * 4 + 4], op=ALU.mult), reads=[SMB, NEGA], writes=[SMB])
                    g4 = sm[:, 12:16]
                    yield
                    fw.op("dve", lambda: nc.vector.tensor_tensor(out=G4[:], in0=bc(self.C("tri_" + nm), S4, 1), in1=bc(g4, S4, 2), op=ALU.mult), reads=[self.CST, SMB], writes=[GB_])
                    cco, _ = COFF["cc4"]
                    fw.op("dve", lambda: nc.vector.tensor_tensor(out=gch[:], in0=bc(self.cst[:, cco:cco + 4], [128, 4, 4], 1), in1=bc(g4, [128, 4, 4], 2), op=ALU.mult),
                          reads=[self.CST, SMB], writes=[GCH])
                    for hp in range(2):
                        fw.op("pool", lambda: nc.gpsimd.tensor_tensor(out=kTm[:, 2 * hp:2 * hp + 2, :], in0=bc(kTp[:, hp, :], [128, 2, 128], 1), in1=bc(hm64, [128, 2, 128], 2), op=ALU.mult),
                              reads=[KTP, self.CST], writes=[KTM])
                    mm = lambda out, lhsT, rhs, st_, sp_, rd, W: fw.op("pe", lambda: nc.tensor.matmul(out, lhsT=lhsT, rhs=rhs, start=st_, stop=sp_), reads=rd, writes=[W])
                    for h in range(4):
                        mm(p1[:, h, :], ones, G4[:, h, :], True, True, [GB_, self.CST], P1)
                    for h in range(4):
                        mm(p3f[:, 256 + h * 2:256 + h * 2 + 2], G4[:, h, :], ones[:, 0:2], True, True, [GB_, self.CST], P3)
                    yield
                    for h in range(4):
                        mm(p2[:, h, :], kTm[:, h, :], kTp[:, h // 2, :], True, True, [KTM, KTP], P2)
                    yield
                    lastc = (63, 127) if d == 0 else (0, 64)
                    fw.op("dve", lambda: nc.vector.tensor_copy(out=gcs[:, :, 0], in_=p3f[:, 256:264].rearrange("p (h c) -> p h c", h=4)[:, :, 0]), reads=[P3], writes=[GCS])
                    for c in range(2):
                        fw.op("dve", lambda: nc.vector.tensor_copy(out=gcs[:, :, 2 + c], in_=p1[:, :, lastc[c]]), reads=[P1], writes=[GCS])
                        fw.op("dve", lambda: nc.vector.tensor_copy(out=gcs[64 * c:64 * c + 64, :, 1], in_=p1[64 * c:64 * c + 64, :, lastc[c]]), reads=[P1], writes=[GCS])
                    fw.op("dve", lambda: nc.vector.tensor_tensor(out=dm[:], in0=bc(gcs[:, :, 0], S4, 2), in1=p1[:], op=ALU.subtract), reads=[GCS, P1], writes=[DM])
                    fw.op("dve", lambda: nc.vector.tensor_scalar(out=dm[:], in0=dm[:], scalar1=0.0, scalar2=-40.0, op0=ALU.min, op1=ALU.max), reads=[DM], writes=[DM])
                    fw.op("pool", lambda: nc.gpsimd.tensor_tensor(out=dm[:], in0=dm[:], in1=bc(self.C("negc_" + nm), S4, 1), op=ALU.add), reads=[DM, self.CST], writes=[DM])
                    fw.op("act", lambda: nc.scalar.activation(out=dm[:], in_=dm[:], func=AF.Exp), reads=[DM], writes=[DM])
                    fw.op("pool", lambda: nc.gpsimd.tensor_tensor(out=dcs[:], in0=dm[:], in1=bc(self.C("strict_" + nm), S4, 1), op=ALU.mult), reads=[DM, self.CST], writes=[DCS])
                    fw.op("dve", lambda: nc.vector.tensor_tensor(out=dmT[:], in0=p1[:], in1=bc(gcs[:, :, 0], S4, 2), op=ALU.subtract), reads=[GCS, P1], writes=[DMT])
                    fw.op("dve", lambda: nc.vector.tensor_scalar(out=dmT[:], in0=dmT[:], scalar1=0.0, scalar2=-40.0, op0=ALU.min, op1=ALU.max), reads=[DMT], writes=[DMT])
                    fw.op("pool", lambda: nc.gpsimd.tensor_tensor(out=dmT[:], in0=dmT[:], in1=bc(self.C("negcT_" + nm), S4, 1), op=ALU.add), reads=[DMT, self.CST], writes=[DMT])
                    fw.op("act", lambda: nc.scalar.activation(out=dmT[:], in_=dmT[:], func=AF.Exp), reads=[DMT], writes=[DMT])
                    fw.op("dve", lambda: nc.vector.tensor_scalar(out=e1b[:], in0=p1[:], scalar1=-40.0, scalar2=None, op0=ALU.max), reads=[P1], writes=[E1B])
                    fw.op("act", lambda: nc.scalar.activation(out=e1b[:], in_=e1b[:], func=AF.Exp), reads=[E1B], writes=[E1B])
                    fw.op("dve", lambda: nc.vector.tensor_tensor(out=gcs[:, :, 6], in0=gcs[:, :, 1], in1=gcs[:, :, 0], op=ALU.subtract), reads=[GCS], writes=[GCS])
                    fw.op("dve", lambda: nc.vector.tensor_scalar(out=gcs[:, :, 6], in0=gcs[:, :, 6], scalar1=-40.0, scalar2=None, op0=ALU.max), reads=[GCS], writes=[GCS])
                    fw.op("dve", lambda: nc.vector.tensor_scalar(out=gcs[:, :, 0:4], in0=gcs[:, :, 0:4], scalar1=-40.0, scalar2=None, op0=ALU.max), reads=[GCS], writes=[GCS])
                    fw.op("act", lambda: nc.scalar.activation(out=gcs[:, :, 7], in_=gcs[:, :, 0], func=AF.Exp), reads=[GCS], writes=[GCS])
                    fw.op("act", lambda: nc.scalar.activation(out=gcs[:, :, 6], in_=gcs[:, :, 6], func=AF.Exp), reads=[GCS], writes=[GCS])
                    fw.op("act", lambda: nc.scalar.activation(out=gcs[:, :, 8:10], in_=gcs[:, :, 2:4], func=AF.Exp), reads=[GCS], writes=[GCS])
                    for h in range(4):
                        hp, hh = h // 2, h % 2
                        rows = slice(hh * 64, (hh + 1) * 64)
                        fw.op("pool", lambda: nc.gpsimd.tensor_copy(out=gendp[rows, hp, :], in_=gcs[rows, h, 8:10]), reads=[GCS], writes=[GENDP])
                    yield
                    fw.op("dve", lambda: nc.vector.tensor_tensor(out=P0f[:], in0=p2[:], in1=bc(sm[:, 4:8], S4, 2), op=ALU.mult), reads=[P2, SMB], writes=[P0F])
                    fw.op("pool", lambda: nc.gpsimd.tensor_tensor(out=P0f[:], in0=P0f[:], in1=dcs[:], op=ALU.mult), reads=[P0F, DCS], writes=[P0F])
                    for h in range(4):
                        mm(p2[:, h, :], kTm[:, h, :], qTp[:, h // 2, :], True, True, [KTM, QTP], P2)
                    for c in range(2):
                        fw.op("dve", lambda: nc.vector.scalar_tensor_tensor(out=aqkc[:, c, :, :], in0=p2[:], scalar=self.cst[:, cio + c:cio + c + 1], in1=dmT[:], op0=ALU.mult, op1=ALU.mult),
                              reads=[P2, self.CST, DMT], writes=[AQKC])
                    yield
                    for h in range(4):
                        fw.op("pe", lambda: nc.tensor.transpose(p2[:, h, :], P0f[:, h, :], identf), reads=[P0F, self.CST], writes=[P2])
                    fw.op("pool", lambda: nc.gpsimd.tensor_copy(out=Pm[0][:], in_=P0f[:]), reads=[P0F], writes=[PM[0]])
                    fw.op("act", lambda: nc.scalar.copy(out=Q0f[:], in_=p2[:]), reads=[P2], writes=[Q0F])
                    fw.op("dve", lambda: nc.vector.tensor_copy(out=Qm[0][:], in_=Q0f[:]), reads=[Q0F], writes=[QM[0]])
                    fw.op("pool", lambda: nc.gpsimd.tensor_tensor(out=Rm[0][:], in0=Q0f[:], in1=bc(identf, S4, 1), op=ALU.add), reads=[Q0F, self.CST], writes=[RM[0]])
                    cur = 0
                    rc = 0
                    for s_ in range(1, 6):
                        nx = 1 - cur
                        for h in range(4):
                            mm(p1[:, h, :], Qm[cur][:, h, :], Pm[cur][:, h, :], True, True, [QM[cur], PM[cur]], P1)
                        if s_ < 5:
                            for h in range(4):
                                mm(p2[:, h, :], Pm[cur][:, h, :], Qm[cur][:, h, :], True, True, [QM[cur], PM[cur]], P2)
                        fw.op("act", lambda: nc.scalar.copy(out=Pm[nx][:], in_=p1[:]), reads=[P1], writes=[PM[nx]])
                        if s_ < 5:
                            fw.op("dve", lambda: nc.vector.tensor_copy(out=Qm[nx][:], in_=p2[:]), reads=[P2], writes=[QM[nx]])
                        for h in range(4):
                            mm(p3[:, h, :], Pm[nx][:, h, :], Rm[rc][:, h, :], True, True, [PM[nx], RM[rc]], P3)
                        fw.op("dve", lambda: nc.vector.tensor_tensor(out=Rm[1 - rc][:], in0=p3[:], in1=Rm[rc][:], op=ALU.add), reads=[P3, RM[rc]], writes=[RM[1 - rc]])
                        yield
                        rc = 1 - rc
                        cur = nx
                    R = Rm[rc]; RB = RM[rc]
                    yield
                    v4 = vtok[:].rearrange("p (h d) -> p h d", h=4)
                    k4 = ktok[:].rearrange("p (h d) -> p h d", h=4)
                    fw.op("pool", lambda: nc.gpsimd.tensor_tensor(out=vb[:], in0=v4, in1=bc(sm[:, 0:4], [128, 4, 64], 2), op=ALU.mult), reads=[VTOK, SMB], writes=[VBB])
                    fw.op("dve", lambda: nc.vector.tensor_tensor(out=sm[:, 16:20], in0=sm[:, 0:4], in1=gcs[:, :, 7], op=ALU.mult), reads=[SMB, GCS], writes=[SMB])
                    for hh in range(2):
                        fw.op("dve", lambda: nc.vector.tensor_tensor(
                            out=kbgm[:].rearrange("p (a b) c -> p a b c", a=2)[:, :, hh, hh * 64:(hh + 1) * 64],
                            in0=ktok[:].rearrange("p (a b d) -> p a b d", a=2, b=2)[:, :, hh, :],
                            in1=bc(sm[:, 16:20].rearrange("p (a b) -> p a b", a=2)[:, :, hh], [128, 2, 64], 2), op=ALU.mult),
                            reads=[KTOK, SMB], writes=[KBGM])
                    for c in range(2):
                        fw.op("dve", lambda: nc.vector.scalar_tensor_tensor(out=kend[:, c, :].rearrange("p (h d) -> p h d", h=4), in0=k4, scalar=self.cst[:, cio + c:cio + c + 1],
                                                                            in1=bc(gcs[:, :, 6], [128, 4, 64], 2), op0=ALU.mult, op1=ALU.mult), reads=[KTOK, self.CST, GCS], writes=[KEND])
                    for h in range(4):
                        hp, hh = h // 2, h % 2
                        rows = slice(hh * 64, (hh + 1) * 64)
                        fw.op("pool", lambda: nc.gpsimd.tensor_tensor(out=qin[rows, hp, :], in0=qTp[rows, hp, :], in1=e1b[rows, h, :], op=ALU.mult), reads=[QTP, E1B], writes=[QIN])
                    for h in range(4):
                        mm(p3f[:, h * 64:(h + 1) * 64], R[:, h, :], vb[:, h, :], True, True, [RB, VBB], P3)
                    for h in range(4):
                        hp, hh = h // 2, h % 2
                        mm(p3f[:, 256 + hp * 128:256 + (hp + 1) * 128], kbgm[:, h, :], R[:, h, :], hh == 0, hh == 1, [KBGM, RB], P3)
                    fw.op("act", lambda: nc.scalar.copy(out=usb[:], in_=p3f[:, 0:256]), reads=[P3], writes=[USB])
                    fw.op("act", lambda: nc.scalar.copy(out=wT[:].rearrange("p a b -> p (a b)"), in_=p3f[:, 256:512]), reads=[P3], writes=[WT])
                    yield
                    for c in ((0, 1) if d == 0 else (1, 0)):
                        r0 = 64 * c
                        for hp in range(2):
                            mm(p3f[:, hp * 128:(hp + 1) * 128], wT[:, hp, :], Sm[hp][:], True, True, [WT, SM[hp]], P3)
                        fw.op("dve", lambda: nc.vector.tensor_tensor(out=vnew[:], in0=usb[:], in1=p3f[:, 0:256], op=ALU.subtract), reads=[USB, P3], writes=[VNEW])
                        for h in range(4):
                            hp, hh = h // 2, h % 2
                            hs = slice(h * 64, (h + 1) * 64)
                            mm(p4[:, hs], qin[:, hp, :], Sm[hp][:, hh * 64:(hh + 1) * 64], True, False, [QIN, SM[hp]], P4)
                            mm(p4[:, hs], aqkc[:, c, h, :], vnew[:, hs], False, True, [AQKC, VNEW], P4)
                        for hp in range(2):
                            mm(p4[:, 256 + hp * 128:256 + (hp + 1) * 128], kend[:, c, hp * 128:(hp + 1) * 128], vnew[:, hp * 128:(hp + 1) * 128], True, True, [KEND, VNEW], P4)
                        fw.op("dve", lambda: nc.vector.tensor_tensor(out=tmp[:], in0=p4[:, 256:512].rearrange("p (a b) -> p a b", a=2), in1=bc(self.C("bmask2"), [128, 2, 128], 1), op=ALU.mult),
                              reads=[P4, self.CST], writes=[TMP])
                        for hp in range(2):
                            fw.op("dve", lambda: nc.vector.scalar_tensor_tensor(out=Sm[hp][:], in0=Sm[hp][:], scalar=gendp[:, hp, c:c + 1], in1=tmp[:, hp, :], op0=ALU.mult, op1=ALU.add),
                                  reads=[SM[hp], GENDP, TMP], writes=[SM[hp]])
                        fw.op("act", lambda: nc.scalar.copy(out=osb[r0:r0 + 64, :], in_=p4[r0:r0 + 64, 0:256]), reads=[P4], writes=[OSB])
                        yield
                    dst = S["ofwd"] if d == 0 else S["orev"]
                    fw.dma(dst[0][tk, :], osb[:], reads=[OSB], writes=[dst[1]])
                    yield

            gens = [stream(0), stream(1)]
            while gens:
                for g in list(gens):
                    try:
                        next(g)
                    except StopIteration:
                        gens.remove(g)
            fw.barrier()
        self.final_gated(272, "dn_on", 512)

    def load_cast(self, st, name, src_ap_fn, nk, ncols, blk, eng="pool"):
        fw, nc = self.fw, self.nc
        w = st.enter_context(nc.sbuf_tensor(self.uniq(name), [128, nk, ncols], BF16))
        W = Buf()
        with ExitStack() as st2:
            stg = [st2.enter_context(nc.sbuf_tensor(self.uniq(name + "s"), [128, nk, blk], F32)) for i in range(2)]
            STG = self.fw.bufs(2)
            nb = (ncols + blk - 1) // blk
            for cb in range(nb):
                b = cb % 2
                c0 = cb * blk
                n = min(blk, ncols - c0)
                fw.dma(stg[b][:, :, 0:n], src_ap_fn(c0, n), writes=[STG[b]])
                e = ("pool", "dve")[cb % 2] if eng == "both" else eng
                if e == "pool":
                    fw.op("pool", lambda: nc.gpsimd.tensor_copy(out=w[:, :, c0:c0 + n], in_=stg[b][:, :, 0:n]), reads=[STG[b]], writes=[W])
                else:
                    fw.op("dve", lambda: nc.vector.tensor_copy(out=w[:, :, c0:c0 + n], in_=stg[b][:, :, 0:n]), reads=[STG[b]], writes=[W])
            fw.barrier()
        return w, W

    def phaseF(self, l, xsrc):
        fw, nc, I = self.fw, self.nc, self.I
        S = self.scr
        last = (l == DEPTH - 1)
        with ExitStack() as st:
            T_ = lambda n, s, d=F32: st.enter_context(nc.sbuf_tensor(self.uniq(n), s, d))
            P_ = lambda n, s, d=F32: st.enter_context(nc.psum_tensor(self.uniq(n), s, d))
            ident16, ID16 = self.common_tiles(st)
            wout, WOUT = self.load_cast(st, "wout", lambda c0, n: I["w_out"][l, :, c0:c0 + n].rearrange("(k p) c -> p k c", p=128), 8, D, 256)
            zt = T_("zt", [128, 8, 128], BF16)
            ZT = Buf()
            fw.op("pool", lambda: nc.gpsimd.memset(zt[:], 0.0), writes=[ZT])
            h2 = S["h2T"][0].rearrange("(k p) t -> p k t", p=128)
            H2 = S["h2T"][1]
            fw.dma(h2[:, :, 0:1], zt[:, :, 0:1], reads=[ZT], writes=[H2], allow_slow_non_contiguous=True)
            fw.dma(h2[:, :, 257:259], zt[:, :, 0:2], reads=[ZT], writes=[H2], allow_slow_non_contiguous=True)
            fw.dma(h2[:, :, 4355:4355 + 128], zt[:, :, :], reads=[ZT], writes=[H2])
            fw.dma(h2[:, :, 4483:4483 + 109], zt[:, :, 0:109], reads=[ZT], writes=[H2])
            mx = [T_("mx%d" % i, [128, 8, 128], BF16) for i in range(2)]
            MX = fw.bufs(2)
            xt = [T_("xt%d" % i, [128, 1024]) for i in range(2)]
            XT = fw.bufs(2)
            x1 = [T_("x1t%d" % i, [128, 1024]) for i in range(2)]
            X1 = fw.bufs(2)
            tmp = [T_("tmp%d" % i, [128, 512]) for i in range(2)]
            TMP = fw.bufs(2)
            hT = [T_("h2t%d" % i, [128, 8, 128], BF16) for i in range(2)]
            HT = fw.bufs(2)
            pst = [P_("pst%d" % i, [128, 1024], BF16) for i in range(2)]
            PST = fw.bufs(2)
            mp = [P_("mp%d" % i, [128, 512]) for i in range(4)]
            MP = fw.bufs(4)
            fw.psum(PST, MP)
            mixv = S["mixT"][0].rearrange("(k p) t -> p k t", p=128)
            for t in range(NT):
                b = t % 2
                s = 1 if t < 2 else 0
                fw.dma(mx[b][:], mixv[:, :, t * 128:(t + 1) * 128], reads=[S["mixT"][1]], writes=[MX[b]])
                fw.dma(xt[b][:], xsrc[t * 128:(t + 1) * 128, :], writes=[XT[b]])
                for hc in range(2):
                    pi = (2 * t + hc) % 4
                    for k in range(8):
                        fw.op("pe", lambda: nc.tensor.matmul(mp[pi][:, :], lhsT=mx[b][:, k, :], rhs=wout[:, k, hc * 512:(hc + 1) * 512],
                                                             start=(k == 0), stop=(k == 7)), reads=[MX[b], WOUT], writes=[MP[pi]])
                    fw.op("dve", lambda: nc.vector.tensor_tensor(out=tmp[hc][:], in0=mp[pi][:, :], in1=self.gb[:, s, hc * 512:(hc + 1) * 512], op=ALU.mult),
                          reads=[MP[pi], self.GB], writes=[TMP[hc]])
                    fw.op("pool", lambda: nc.gpsimd.tensor_tensor(out=x1[b][:, hc * 512:(hc + 1) * 512], in0=xt[b][:, hc * 512:(hc + 1) * 512], in1=tmp[hc][:], op=ALU.add),
                          reads=[TMP[hc], XT[b]], writes=[X1[b]])
                fw.dma(S["x1"][0][t * 128:(t + 1) * 128, :], x1[b][:], reads=[X1[b]], writes=[S["x1"][1]])
                self.norm_mod_T(st, 1, x1[b][:], X1[b], s, hT[b], HT[b], 0, "F", pst[b], PST[b], ident16, ID16)
                pos = (F_CTX0 + t * 128) if t < 2 else (F_LAT0 + (t - 2) * 128)
                fw.dma(h2[:, :, pos:pos + 128], hT[b][:], reads=[HT[b]], writes=[H2])
            fw.barrier()
        for half in range(2):
            with ExitStack() as st:
                T_ = lambda n, s, d=F32: st.enter_context(nc.sbuf_tensor(self.uniq(n), s, d))
                P_ = lambda n, s, d=F32: st.enter_context(nc.psum_tensor(self.uniq(n), s, d))
                NJ = 11
                a0 = half * NJ * 128
                g0 = DFF + half * NJ * 128
                wua, WUA = self.load_cast(st, "wua", lambda c0, n: I["w_up"][l, :, a0 + c0:a0 + c0 + n].rearrange("(k p) c -> p k c", p=128), 8, NJ * 128, 352, eng="both")
                wug, WUG = self.load_cast(st, "wug", lambda c0, n: I["w_up"][l, :, g0 + c0:g0 + c0 + n].rearrange("(k p) c -> p k c", p=128), 8, NJ * 128, 352, eng="both")
                wdn, WDN = self.load_cast(st, "wdn", lambda c0, n: I["w_down"][l, half * NJ * 128:(half + 1) * NJ * 128, c0:c0 + n].rearrange("(k p) c -> p k c", p=128), NJ, D, 256, eng="both")
                xin_ap = S["x1"][0] if half == 0 else S["xa"][0]
                XIN = S["x1"][1] if half == 0 else S["xa"][1]
                hw = [T_("hw%d" % i, [128, 8, 512], BF16) for i in range(2)]
                HW = fw.bufs(2)
                zT = T_("zT", [128, NJ, 512], BF16)
                ZT = Buf()
                ya = [T_("ya%d" % i, [128, 512]) for i in range(2)]
                YA = fw.bufs(2)
                yg = [T_("yg%d" % i, [128, 512]) for i in range(2)]
                YG = fw.bufs(2)
                sgt = [T_("sgt%d" % i, [128, 512]) for i in range(2)]
                SGT = fw.bufs(2)
                xt = [T_("fxt%d" % i, [128, 1024]) for i in range(2)]
                XT = fw.bufs(2)
                tmp = [T_("ftmp%d" % i, [128, 512]) for i in range(2)]
                TMP = fw.bufs(2)
                up = [P_("up%d" % i, [128, 512]) for i in range(4)]
                UP = fw.bufs(4)
                dp = [P_("dp%d" % i, [128, 512]) for i in range(2)]
                DP = fw.bufs(2)
                fw.psum(UP, DP)
                h2 = S["h2T"][0].rearrange("(k p) t -> p k t", p=128)
                fo, _ = COLS["ffn_w"]
                it = 0
                for fb in range(NFB):
                    b = fb % 2
                    c0 = fb * FB
                    fw.dma(hw[b][:], h2[:, :, c0:c0 + 512], reads=[S["h2T"][1]], writes=[HW[b]])
                    for j in range(NJ):
                        jb = j % 2
                        ja = half * NJ + j
                        for (which, wt, WT, y, Y, cj) in ((0, wua, WUA, ya[jb], YA[jb], ja), (1, wug, WUG, yg[jb], YG[jb], 22 + ja)):
                            pi = it % 4
                            it += 1
                            for k in range(8):
                                fw.op("pe", lambda: nc.tensor.matmul(up[pi][:, :], lhsT=wt[:, k, j * 128:(j + 1) * 128], rhs=hw[b][:, k, :],
                                                                     start=(k == 0), stop=(k == 7)), reads=[WT, HW[b]], writes=[UP[pi]])
                            wc = lambda tap: self.colp[:, fo + cj * 3 + tap:fo + cj * 3 + tap + 1]
                            fw.op("act", lambda: nc.scalar.activation(out=y[:, 0:FB], in_=up[pi][:, 0:FB], func=AF.Identity, bias=0.0, scale=wc(0)),
                                  reads=[UP[pi], self.COLP], writes=[Y])
                            fw.op("dve", lambda: nc.vector.scalar_tensor_tensor(out=y[:, 0:FB], in0=up[pi][:, 1:FB + 1], scalar=wc(1), in1=y[:, 0:FB], op0=ALU.mult, op1=ALU.add),
                                  reads=[UP[pi], self.COLP, Y], writes=[Y])
                            fw.op("dve", lambda: nc.vector.scalar_tensor_tensor(out=y[:, 0:FB], in0=up[pi][:, 2:FB + 2], scalar=wc(2), in1=y[:, 0:FB], op0=ALU.mult, op1=ALU.add),
                                  reads=[UP[pi], self.COLP, Y], writes=[Y])
                        fw.op("act", lambda: nc.scalar.activation(out=sgt[jb][:, 0:FB], in_=yg[jb][:, 0:FB], func=AF.Silu), reads=[YG[jb]], writes=[SGT[jb]])
                        fw.op("pool", lambda: nc.gpsimd.tensor_tensor(out=zT[:, j, 0:FB], in0=sgt[jb][:, 0:FB], in1=ya[jb][:, 0:FB], op=ALU.mult),
                              reads=[SGT[jb], YA[jb]], writes=[ZT])
                    for sub in range(4):
                        q0 = fb * FB + 1 + sub * 128
                        n = min(128, FB - sub * 128)
                        segs = []
                        for (p0, p1, t0) in ((F_CTX0, F_CTX0 + CTX, 0), (F_LAT0, F_LAT0 + SEQ, CTX)):
                            lo = max(q0, p0)
                            hi = min(q0 + n, p1)
                            if hi > lo:
                                segs.append((lo - q0, hi - lo, t0 + lo - p0))
                        if not segs:
                            continue
                        s = 1 if q0 < F_CTX0 + CTX else 0
                        xb = (fb * 4 + sub) % 2
                        for (r0, nr, tok0) in segs:
                            fw.dma(xt[xb][r0:r0 + nr, :], xin_ap[tok0:tok0 + nr, :], reads=[XIN], writes=[XT[xb]])
                        for hc in range(2):
                            for j in range(NJ):
                                fw.op("pe", lambda: nc.tensor.matmul(dp[hc][0:n, :], lhsT=zT[:, j, sub * 128:sub * 128 + n], rhs=wdn[:, j, hc * 512:(hc + 1) * 512],
                                                                     start=(j == 0), stop=(j == NJ - 1)), reads=[ZT, WDN], writes=[DP[hc]])
                            fw.op("dve", lambda: nc.vector.tensor_tensor(out=tmp[hc][0:n, :], in0=dp[hc][0:n, :], in1=self.gb[0:n, 2 + s, hc * 512:(hc + 1) * 512], op=ALU.mult),
                                  reads=[DP[hc], self.GB], writes=[TMP[hc]])
                            fw.op("pool", lambda: nc.gpsimd.tensor_tensor(out=xt[xb][0:n, hc * 512:(hc + 1) * 512], in0=xt[xb][0:n, hc * 512:(hc + 1) * 512], in1=tmp[hc][0:n, :], op=ALU.add),
                                  reads=[TMP[hc], XT[xb]], writes=[XT[xb]])
                        for (r0, nr, tok0) in segs:
                            if half == 0:
                                fw.dma(S["xa"][0][tok0:tok0 + nr, :], xt[xb][r0:r0 + nr, :], reads=[XT[xb]], writes=[S["xa"][1]])
                            elif not last:
                                fw.dma(S["xres"][0][tok0:tok0 + nr, :], xt[xb][r0:r0 + nr, :], reads=[XT[xb]], writes=[S["xres"][1]])
                            elif tok0 >= CTX:
                                fw.dma(self.out[tok0 - CTX:tok0 - CTX + nr, :], xt[xb][r0:r0 + nr, :], reads=[XT[xb]], writes=[self.OUTB])
                            if self.dbg and half == 1 and last:
                                fw.dma(S["xres"][0][tok0:tok0 + nr, :], xt[xb][r0:r0 + nr, :], reads=[XT[xb]], writes=[S["xres"][1]])
                fw.barrier()


def _host_inputs(inp):
    rope = _rope_tables()
    per_core = []
    packed = []
    for b in range(8):
        cvec = np.stack([inp["c"][b], inp["c_ctx"]], axis=0).astype(np.float32)
        cols, rows, w2p = [], [], []
        for l in range(DEPTH):
            c_, r_, w_ = _pack_params(inp, l, cvec)
            cols.append(c_); rows.append(r_); w2p.append(w_)
        xin = np.concatenate([inp["ctx"][b], inp["x"][b]], axis=0).astype(np.float32)
        per_core.append({
            "xin": np.ascontiguousarray(xin), "consts": CONSTS, "rope": rope,
            "cols": np.stack(cols), "rows": np.stack(rows), "w2p": np.stack(w2p),
            "w_mod": inp["w_mod"], "w_in": inp["w_in"], "w_out": inp["w_out"],
            "w_up": inp["ffn_w_up"], "w_down": inp["ffn_w_down"],
        })
    return per_core


def kernel(**inputs):
    inp = {k: np.asarray(v) for k, v in inputs.items()}
    bld = Builder(phases=PHASES)
    nc = bld.build()
    in_maps = _host_inputs(inp)
    res = run_bass_kernel_spmd(nc, in_maps, core_ids=list(range(8)))
    out = np.stack([np.asarray(r["out"]) for r in res.results], axis=0)
    return out.astype(np.float32)
```
